# Optimizing a Trainium2 kernel written in Bass

```python
import jax, jax.numpy as jnp
from jax import lax
import numpy as np


D_MODEL = 1024
BATCH = 8
SEQ = 2048
DEPTH = 2

BRANCH_WIDTH = 512
N_BRANCH = 4
EPS = 1e-6
S5_GROUP = 16
S5_GROUPS = BRANCH_WIDTH // S5_GROUP
S5_STATE = 64
S5_STEP_MIN = 1e-3
S5_STEP_MAX = 1e-1
SGU_CHUNK = 128
SGU_HEADS = 8
SGU_HEAD_DIM = BRANCH_WIDTH // SGU_HEADS
M2_HEAD_DIM = 64
M2_HEADS = BRANCH_WIDTH // M2_HEAD_DIM
M2_GROUPS = 2
M2_STATE = 128
M2_CONV = 4
M2_CHUNK = 128
M2_CONV_CH = BRANCH_WIDTH + 2 * M2_GROUPS * M2_STATE
M2_DT_MIN = 1e-3
M2_DT_MAX = 1e-1
SC_CONV = 3

IN_SIZES = (
    BRANCH_WIDTH, BRANCH_WIDTH,
    BRANCH_WIDTH, BRANCH_WIDTH, BRANCH_WIDTH,
    BRANCH_WIDTH, M2_CONV_CH, M2_HEADS,
    BRANCH_WIDTH, BRANCH_WIDTH, BRANCH_WIDTH, BRANCH_WIDTH,
    N_BRANCH * D_MODEL,
)
IN_DIM = int(sum(IN_SIZES))
IN_SPLITS = [int(v) for v in np.cumsum(IN_SIZES)[:-1]]

kernel_name = 'hybrid_s5_sgu_ssd_shortconv_gated_merge'


def rmsnorm(x, w):
    x32 = x.astype(jnp.float32)
    y = x32 * lax.rsqrt(jnp.mean(x32 * x32, axis=-1, keepdims=True) + EPS)
    return (y * w.astype(jnp.float32)).astype(x.dtype)


def causal_depthwise_conv(x, w):
    k, c = w.shape
    return lax.conv_general_dilated(
        x, w[:, None, :].astype(x.dtype), window_strides=(1,), padding=[(k - 1, 0)],
        dimension_numbers=('NWC', 'WIO', 'NWC'), feature_group_count=c)


def _complex_affine_combine(e1, e2):
    a1r, a1i, b1r, b1i = e1
    a2r, a2i, b2r, b2i = e2
    ar = a1r * a2r - a1i * a2i
    ai = a1r * a2i + a1i * a2r
    br = a2r * b1r - a2i * b1i + b2r
    bi = a2r * b1i + a2i * b1r + b2i
    return (ar, ai, br, bi)


def s5_branch(u, gate, lam_re, lam_im, b_re, b_im, c_re, c_im, d, log_step, w_glu):
    bsz, seq_len, _ = u.shape
    f32 = jnp.float32
    u32 = u.astype(f32).reshape(bsz, seq_len, S5_GROUPS, S5_GROUP)
    step = jnp.exp(log_step.astype(f32))[:, None]
    lr, li = lam_re.astype(f32), lam_im.astype(f32)
    mag = jnp.exp(lr * step)
    ab_re, ab_im = mag * jnp.cos(li * step), mag * jnp.sin(li * step)
    den = lr * lr + li * li
    nr = ab_re - 1.0
    coef_re = (nr * lr + ab_im * li) / den
    coef_im = (ab_im * lr - nr * li) / den
    br, bi = b_re.astype(f32), b_im.astype(f32)
    bb_re = coef_re[..., None] * br - coef_im[..., None] * bi
    bb_im = coef_re[..., None] * bi + coef_im[..., None] * br
    bu_re = jnp.einsum('blgp,gnp->blgn', u32, bb_re)
    bu_im = jnp.einsum('blgp,gnp->blgn', u32, bb_im)
    a_re = jnp.broadcast_to(ab_re, bu_re.shape)
    a_im = jnp.broadcast_to(ab_im, bu_re.shape)
    _, _, s_re, s_im = lax.associative_scan(
        _complex_affine_combine, (a_re, a_im, bu_re, bu_im), axis=1)
    y = (jnp.einsum('blgn,gpn->blgp', s_re, c_re.astype(f32))
         - jnp.einsum('blgn,gpn->blgp', s_im, c_im.astype(f32))
         + d.astype(f32) * u32)
    y = jax.nn.gelu(y.reshape(bsz, seq_len, BRANCH_WIDTH))
    y = y * jax.nn.sigmoid(y @ w_glu.astype(f32))
    return (y * jax.nn.silu(gate.astype(f32))).astype(u.dtype)


def sgu_branch(u, v, gate, ln_w, ln_b, w_s, b_s):
    bsz, seq_len, _ = u.shape
    f32 = jnp.float32
    u32 = jax.nn.gelu(u.astype(f32))
    v32 = jax.nn.gelu(v.astype(f32))
    mu = jnp.mean(v32, axis=-1, keepdims=True)
    var = jnp.mean(jnp.square(v32 - mu), axis=-1, keepdims=True)
    vn = (v32 - mu) * lax.rsqrt(var + EPS) * ln_w.astype(f32) + ln_b.astype(f32)
    vn = vn.reshape(bsz, seq_len // SGU_CHUNK, SGU_CHUNK, SGU_HEADS, SGU_HEAD_DIM)
    mask = jnp.tril(jnp.ones((SGU_CHUNK, SGU_CHUNK), dtype=bool))
    w_m = jnp.where(mask, w_s.astype(f32), 0.0)
    s = jnp.einsum('hts,bcshe->bcthe', w_m, vn) + b_s.astype(f32).T[:, :, None]
    out = u32 * s.reshape(bsz, seq_len, BRANCH_WIDTH)
    return (out * jax.nn.silu(gate.astype(f32))).astype(u.dtype)


def segsum(a):
    t = a.shape[-1]
    cs = jnp.cumsum(a, axis=-1)
    diff = cs[..., :, None] - cs[..., None, :]
    mask = jnp.tril(jnp.ones((t, t), dtype=bool))
    return jnp.where(mask, diff, -jnp.inf)


def mamba2_branch(z, xbc, dt_raw, conv_w, conv_b, dt_bias, a_log, d, norm_w):
    bsz, seq_len, _ = z.shape
    f32 = jnp.float32
    nc, q = seq_len // M2_CHUNK, M2_CHUNK
    xbc = jax.nn.silu((causal_depthwise_conv(xbc, conv_w) + conv_b).astype(f32))
    x, bm, cm = jnp.split(xbc, [BRANCH_WIDTH, BRANCH_WIDTH + M2_GROUPS * M2_STATE], axis=-1)
    rep = M2_HEADS // M2_GROUPS
    x = x.reshape(bsz, nc, q, M2_HEADS, M2_HEAD_DIM)
    bm = jnp.repeat(bm.reshape(bsz, seq_len, M2_GROUPS, M2_STATE), rep, axis=2).reshape(bsz, nc, q, M2_HEADS, M2_STATE)
    cm = jnp.repeat(cm.reshape(bsz, seq_len, M2_GROUPS, M2_STATE), rep, axis=2).reshape(bsz, nc, q, M2_HEADS, M2_STATE)
    dt = jax.nn.softplus(dt_raw.astype(f32) + dt_bias.astype(f32))
    a = -jnp.exp(a_log.astype(f32))
    da = (dt * a).reshape(bsz, nc, q, M2_HEADS).transpose(0, 3, 1, 2)
    a_cs = jnp.cumsum(da, axis=-1)
    xdt = x * dt.reshape(bsz, nc, q, M2_HEADS)[..., None]
    scores = jnp.einsum('bclhn,bcshn->bhcls', cm, bm) * jnp.exp(segsum(da))
    y_diag = jnp.einsum('bhcls,bcshp->bclhp', scores, xdt)
    decay_states = jnp.exp(a_cs[..., -1:] - a_cs)
    states = jnp.einsum('bclhn,bhcl,bclhp->bchpn', bm, decay_states, xdt)
    states = jnp.concatenate([jnp.zeros_like(states[:, :1]), states], axis=1)
    decay_chunk = jnp.exp(segsum(jnp.pad(a_cs[..., -1], ((0, 0), (0, 0), (1, 0)))))
    states = jnp.einsum('bhzc,bchpn->bzhpn', decay_chunk, states)[:, :-1]
    y_off = jnp.einsum('bclhn,bchpn,bhcl->bclhp', cm, states, jnp.exp(a_cs))
    y = y_diag + y_off + d.astype(f32)[:, None] * x
    y = y.reshape(bsz, seq_len, BRANCH_WIDTH) * jax.nn.silu(z.astype(f32))
    y = y * lax.rsqrt(jnp.mean(y * y, axis=-1, keepdims=True) + EPS) * norm_w.astype(f32)
    return y.astype(z.dtype)


def shortconv_branch(bg, cg, h, gate, conv_w):
    y = bg * causal_depthwise_conv(cg * h, conv_w)
    return y * jax.nn.silu(gate)


def setup_inputs(seed: int = 0) -> dict:
    key = jax.random.key(seed)
    ks = jax.random.split(key, 32)
    f32 = jnp.float32
    W, D, G, N, P = BRANCH_WIDTH, D_MODEL, S5_GROUPS, S5_STATE, S5_GROUP
    nrm = lambda k, shape, s: jax.random.normal(k, shape, f32) * s
    x = jax.random.normal(ks[0], (BATCH, SEQ, D), f32)
    norm_w = 1.0 + nrm(ks[1], (DEPTH, D), 0.01)
    w_in = nrm(ks[2], (DEPTH, D, IN_DIM), D ** -0.5)
    s5_lambda_re = -0.5 + nrm(ks[3], (DEPTH, G, N), 0.01)
    s5_lambda_im = jnp.pi * jnp.arange(N, dtype=f32)[None, None, :] + nrm(ks[4], (DEPTH, G, N), 0.01)
    s5_b_re = nrm(ks[5], (DEPTH, G, N, P), (2.0 * P) ** -0.5)
    s5_b_im = nrm(ks[6], (DEPTH, G, N, P), (2.0 * P) ** -0.5)
    s5_c_re = nrm(ks[7], (DEPTH, G, P, N), (2.0 * N) ** -0.5)
    s5_c_im = nrm(ks[8], (DEPTH, G, P, N), (2.0 * N) ** -0.5)
    s5_d = nrm(ks[9], (DEPTH, G, P), 1.0)
    s5_log_step = jax.random.uniform(ks[10], (DEPTH, G), f32, np.log(S5_STEP_MIN), np.log(S5_STEP_MAX))
    s5_w_glu = nrm(ks[11], (DEPTH, W, W), W ** -0.5)
    sgu_ln_w = 1.0 + nrm(ks[12], (DEPTH, W), 0.01)
    sgu_ln_b = nrm(ks[13], (DEPTH, W), 0.01)
    sgu_w = nrm(ks[14], (DEPTH, SGU_HEADS, SGU_CHUNK, SGU_CHUNK), SGU_CHUNK ** -0.5)
    sgu_b = 1.0 + nrm(ks[15], (DEPTH, SGU_HEADS, SGU_CHUNK), 0.1)
    m2_conv_w = nrm(ks[16], (DEPTH, M2_CONV, M2_CONV_CH), M2_CONV ** -0.5)
    m2_conv_b = nrm(ks[17], (DEPTH, M2_CONV_CH), 0.01)
    dt0 = jnp.exp(jax.random.uniform(ks[18], (DEPTH, M2_HEADS), f32, np.log(M2_DT_MIN), np.log(M2_DT_MAX)))
    m2_dt_bias = dt0 + jnp.log(-jnp.expm1(-dt0))
    m2_a_log = jnp.log(jax.random.uniform(ks[19], (DEPTH, M2_HEADS), f32, 1.0, 16.0))
    m2_d = 1.0 + nrm(ks[20], (DEPTH, M2_HEADS), 0.01)
    m2_norm_w = 1.0 + nrm(ks[21], (DEPTH, W), 0.01)
    sc_conv_w = nrm(ks[22], (DEPTH, SC_CONV, W), SC_CONV ** -0.5)
    merge_b = nrm(ks[23], (DEPTH, N_BRANCH, D), 0.01)
    w_branch = nrm(ks[24], (DEPTH, N_BRANCH, W, D), W ** -0.5)
    w_out = nrm(ks[25], (DEPTH, D, D), D ** -0.5)
    final_norm_w = 1.0 + nrm(ks[26], (D,), 0.01)
    return {'x': x, 'norm_w': norm_w, 'w_in': w_in,
            's5_lambda_re': s5_lambda_re, 's5_lambda_im': s5_lambda_im,
            's5_b_re': s5_b_re, 's5_b_im': s5_b_im, 's5_c_re': s5_c_re, 's5_c_im': s5_c_im,
            's5_d': s5_d, 's5_log_step': s5_log_step, 's5_w_glu': s5_w_glu,
            'sgu_ln_w': sgu_ln_w, 'sgu_ln_b': sgu_ln_b, 'sgu_w': sgu_w, 'sgu_b': sgu_b,
            'm2_conv_w': m2_conv_w, 'm2_conv_b': m2_conv_b, 'm2_dt_bias': m2_dt_bias,
            'm2_a_log': m2_a_log, 'm2_d': m2_d, 'm2_norm_w': m2_norm_w,
            'sc_conv_w': sc_conv_w, 'merge_b': merge_b, 'w_branch': w_branch,
            'w_out': w_out, 'final_norm_w': final_norm_w}


def reference(x, norm_w, w_in, s5_lambda_re, s5_lambda_im, s5_b_re, s5_b_im, s5_c_re, s5_c_im,
              s5_d, s5_log_step, s5_w_glu, sgu_ln_w, sgu_ln_b, sgu_w, sgu_b,
              m2_conv_w, m2_conv_b, m2_dt_bias, m2_a_log, m2_d, m2_norm_w,
              sc_conv_w, merge_b, w_branch, w_out, final_norm_w):
    bsz, seq_len, _ = x.shape
    for i in range(DEPTH):
        h = rmsnorm(x, norm_w[i])
        (s5_u, s5_g, sgu_u, sgu_v, sgu_g, m2_z, m2_xbc, m2_dt,
         sc_b, sc_c, sc_h, sc_g, merge_logits) = jnp.split(h @ w_in[i], IN_SPLITS, axis=-1)
        y_a = s5_branch(s5_u, s5_g, s5_lambda_re[i], s5_lambda_im[i], s5_b_re[i], s5_b_im[i],
                        s5_c_re[i], s5_c_im[i], s5_d[i], s5_log_step[i], s5_w_glu[i])
        y_b = sgu_branch(sgu_u, sgu_v, sgu_g, sgu_ln_w[i], sgu_ln_b[i], sgu_w[i], sgu_b[i])
        y_c = mamba2_branch(m2_z, m2_xbc, m2_dt, m2_conv_w[i], m2_conv_b[i], m2_dt_bias[i],
                            m2_a_log[i], m2_d[i], m2_norm_w[i])
        y_d = shortconv_branch(sc_b, sc_c, sc_h, sc_g, sc_conv_w[i])
        branches = jnp.stack([y_a, y_b, y_c, y_d], axis=2)
        branch_out = jnp.einsum('blkw,kwd->blkd', branches, w_branch[i])
        gates = jax.nn.sigmoid(
            merge_logits.reshape(bsz, seq_len, N_BRANCH, D_MODEL).astype(jnp.float32)
            + merge_b[i].astype(jnp.float32))
        merged = jnp.einsum('blkd,blkd->bld', gates, branch_out.astype(jnp.float32)).astype(x.dtype)
        x = x + merged @ w_out[i]
    return rmsnorm(x, final_norm_w)
```

```python
import os
import numpy as np
import concourse.bass as bass
import concourse.mybir as mybir
from concourse.bass_utils import run_bass_kernel_spmd

F32 = mybir.dt.float32
BF16 = mybir.dt.bfloat16
ALU = mybir.AluOpType
AF = mybir.ActivationFunctionType

D = 1024
SEQ = 2048
DEPTH = 2
W = 512
IN_DIM = 10248
TP = 1024
NPASS = SEQ // TP
ST = 512
NSUB = TP // ST
EPS = 1e-6

O_S5U, O_S5G = 0, 512
O_SGU, O_SGV, O_SGG = 1024, 1536, 2048
O_M2Z, O_M2X, O_M2DT = 2560, 3072, 4096
O_SCB, O_SCC, O_SCH, O_SCG = 4104, 4616, 5128, 5640
O_MG = 6152


class Buf:
    __slots__ = ("w", "r")

    def __init__(self):
        self.w = None
        self.r = {}


class Prog:
    def __init__(self, nc):
        self.nc = nc
        self.eng = {"pe": nc.tensor, "act": nc.scalar, "dve": nc.vector, "pool": nc.gpsimd, "sp": nc.sync}
        self.sem = {}
        self.cnt = {}
        self.seen = {e: {} for e in self.eng}
        self.pend = {e: [] for e in self.eng}
        for e in self.eng:
            self.sem[e] = nc.alloc_semaphore("s_" + e)
            self.cnt[e] = 0
        self.nwaits = 0
        self.nins = 0

    def new_sem(self, name):
        self.sem[name] = self.nc.alloc_semaphore(name)
        self.cnt[name] = 0
        return name

    def _wait(self, e, key, val):
        if key not in self.eng:
            val = self.cnt[key]
        if self.seen[e].get(key, 0) >= val:
            return
        self.seen[e][key] = val
        self.eng[e].wait_ge(self.sem[key], val)
        self.nwaits += 1

    def _deps(self, e, reads, writes):
        deps = {}
        for b in reads:
            if b.w is not None:
                k, v = b.w
                if deps.get(k, 0) < v:
                    deps[k] = v
        for b in writes:
            if b.w is not None:
                k, v = b.w
                if deps.get(k, 0) < v:
                    deps[k] = v
            for k, v in b.r.items():
                if deps.get(k, 0) < v:
                    deps[k] = v
        for k, v in deps.items():
            if k == e and (e == "pe" or v > self.cnt[e]):
                continue
            self._wait(e, k, v)

    def _commit(self, k, v, reads, writes):
        for b in reads:
            b.r[k] = v
        for b in writes:
            b.w = (k, v)
            b.r = {}

    def op(self, e, fn, reads=(), writes=(), inc=True):
        self._deps(e, reads, writes)
        ins = fn(self.eng[e])
        self.nins += 1
        if inc:
            self.cnt[e] += 1
            ins.then_inc(self.sem[e], 1)
            self._commit(e, self.cnt[e], reads, writes)
        else:
            v = self.cnt[e] + 1
            self._commit(e, v, reads, writes)

    def barrier(self):
        for e in self.eng:
            for k, v in self.cnt.items():
                if k != e and v > 0:
                    self._wait(e, k, v)

    def dma(self, q, semkey, out, in_, reads=(), writes=(), **kw):
        self._deps(q, reads, writes)
        ins = self.eng[q].dma_start(out=out, in_=in_, **kw)
        self.cnt[semkey] += 16
        ins.then_inc(self.sem[semkey], 16)
        self.nins += 1
        self._commit(semkey, self.cnt[semkey], reads, writes)


def col_layout():
    off = {}
    n = 0

    def add(name, w):
        nonlocal n
        off[name] = n
        n += w
    add("nw", 8)
    add("mb", 32)
    add("scw", 12)
    add("fw", 8)
    add("m2cw", 32)
    add("m2cb", 8)
    add("m2d", 4)
    add("m2nw", 4)
    add("s5d", 4)
    add("mkB", 8)
    add("mkC", 2)
    add("mkCn", 2)
    return off, n


COLOFF, NCOL = col_layout()


def host_cols(inp, l):
    c = np.zeros((128, NCOL), np.float32)
    c[:, COLOFF["nw"]:COLOFF["nw"] + 8] = inp["norm_w"][l].reshape(8, 128).T
    c[:, COLOFF["mb"]:COLOFF["mb"] + 32] = inp["merge_b"][l].reshape(32, 128).T
    c[:, COLOFF["scw"]:COLOFF["scw"] + 12] = inp["sc_conv_w"][l].reshape(12, 128).T
    c[:, COLOFF["fw"]:COLOFF["fw"] + 8] = inp["final_norm_w"].reshape(8, 128).T
    c[:, COLOFF["m2cw"]:COLOFF["m2cw"] + 32] = inp["m2_conv_w"][l].reshape(32, 128).T
    c[:, COLOFF["m2cb"]:COLOFF["m2cb"] + 8] = inp["m2_conv_b"][l].reshape(8, 128).T
    c[:, COLOFF["m2d"]:COLOFF["m2d"] + 4] = np.repeat(inp["m2_d"][l], 64).reshape(4, 128).T
    c[:, COLOFF["m2nw"]:COLOFF["m2nw"] + 4] = inp["m2_norm_w"][l].reshape(4, 128).T
    c[:, COLOFF["s5d"]:COLOFF["s5d"] + 4] = inp["s5_d"][l].reshape(4, 128).T
    gl = np.arange(128) // 16
    for jj in range(4):
        for g2 in range(2):
            c[:, COLOFF["mkB"] + jj * 2 + g2] = (gl == 2 * jj + g2)
    g2p = np.arange(128) // 64
    for g2 in range(2):
        c[:, COLOFF["mkC"] + g2] = (g2p == g2)
        c[:, COLOFF["mkCn"] + g2] = -1.0 * (g2p == g2)
    return c


def host_consts():
    i = np.arange(128)
    k = np.zeros((128, 5, 128), np.float32)
    k[:, 0] = np.eye(128)
    k[:, 1] = (i[:, None] <= i[None, :])
    k[:, 2] = (i[:, None] > i[None, :])
    k[:, 3] = 1.0
    k[:, 4] = (i[:, None] // 16 == i[None, :] // 16)
    return k


def host_rows(inp, l):
    r = np.zeros((128, 1040), np.float32)
    r[:, 0:512] = inp["sgu_ln_w"][l][None, :]
    r[:, 512:1024] = inp["sgu_ln_b"][l][None, :]
    r[:, 1024:1032] = inp["m2_dt_bias"][l][None, :]
    r[:, 1032:1040] = inp["m2_a_log"][l][None, :]
    return r


def host_s5(inp, l):
    G, N, Pq = 32, 64, 16
    def L2(a_gn):
        a = a_gn.reshape(4, 8, N)
        a = np.repeat(a[:, :, None, :], 16, axis=2)
        return a.transpose(1, 2, 0, 3).reshape(128, 256)
    def L2b(b_gnq):
        a = b_gnq.reshape(4, 8, N, Pq)
        return a.transpose(1, 3, 0, 2).reshape(128, 256)
    p5 = np.stack([L2(inp["s5_lambda_re"][l]), L2(inp["s5_lambda_im"][l]),
                   L2(np.repeat(inp["s5_log_step"][l][:, None], N, 1)),
                   L2b(inp["s5_b_re"][l]), L2b(inp["s5_b_im"][l])], 1)
    def PL(a_gn):
        return a_gn.reshape(16, 2, N).transpose(1, 2, 0).reshape(128, 16)
    pq = np.stack([PL(inp["s5_lambda_re"][l]), PL(inp["s5_lambda_im"][l]),
                   PL(np.repeat(inp["s5_log_step"][l][:, None], N, 1))], 1)
    def PLc(c_gpn):
        return c_gpn.reshape(16, 2, Pq, N).transpose(1, 3, 2, 0).reshape(128, 256)
    def PLb(b_gnq):
        return b_gnq.reshape(16, 2, N, Pq).transpose(1, 2, 3, 0).reshape(128, 256)
    pc = np.stack([PLc(inp["s5_c_re"][l]), PLc(inp["s5_c_im"][l]),
                   PLb(inp["s5_b_re"][l]), PLb(inp["s5_b_im"][l])], 1)
    return p5.astype(np.float32), pq.astype(np.float32), pc.astype(np.float32)


def build_nc(branches=("a", "b", "c", "d"), nlayers=DEPTH):
    nc = bass.Bass("TRN2", target_bir_lowering=False)
    xT = nc.dram_tensor("xT", [D, SEQ], F32, kind="ExternalInput").ap()
    w_in = nc.dram_tensor("w_in", [DEPTH, D, IN_DIM], F32, kind="ExternalInput").ap()
    w_br = nc.dram_tensor("w_branch", [DEPTH, 4, W, D], F32, kind="ExternalInput").ap()
    w_out = nc.dram_tensor("w_out", [DEPTH, D, D], F32, kind="ExternalInput").ap()
    w_glu = nc.dram_tensor("w_glu", [DEPTH, W, W], F32, kind="ExternalInput").ap()
    cols_d = nc.dram_tensor("cols", [DEPTH, 128, NCOL], F32, kind="ExternalInput").ap()
    consts_d = nc.dram_tensor("consts", [128, 5, 128], F32, kind="ExternalInput").ap()
    rows_d = nc.dram_tensor("rows", [DEPTH, 128, 1040], F32, kind="ExternalInput").ap()
    sguw_d = nc.dram_tensor("sguw", [DEPTH, 128, 8, 128], F32, kind="ExternalInput").ap()
    sgub_d = nc.dram_tensor("sgub", [DEPTH, 128, 512], F32, kind="ExternalInput").ap()
    s5p_d = nc.dram_tensor("s5p", [DEPTH, 128, 5, 256], F32, kind="ExternalInput").ap()
    s5q_d = nc.dram_tensor("s5q", [DEPTH, 128, 3, 16], F32, kind="ExternalInput").ap()
    s5c_d = nc.dram_tensor("s5c", [DEPTH, 128, 4, 256], F32, kind="ExternalInput").ap()
    yT = nc.dram_tensor("yT", [D, SEQ], F32, kind="ExternalOutput").ap()

    P = Prog(nc)
    sb = nc.alloc_sbuf_tensor
    X = sb("X", [128, 8, TP], F32)
    H = sb("H", [128, 8, TP], BF16)
    ACC = sb("ACC", [128, 8, TP], F32)
    Y = sb("Y", [128, 4, TP], BF16)
    cols = sb("colsb", [128, DEPTH, NCOL], F32)
    consts = sb("constsb", [128, 5, 128], F32)
    identb = sb("identb", [128, 128], BF16)
    mask01b = sb("mask01b", [128, 128], BF16)
    bX = [[Buf() for _ in range(NSUB)] for _ in range(8)]
    bH = [Buf() for _ in range(NSUB)]
    bACC = [[Buf() for _ in range(NSUB)] for _ in range(8)]
    bY = [[Buf() for _ in range(NSUB)] for _ in range(4)]
    bcols, bconsts = Buf(), Buf()
    ident, triu, ltstrict, ones = consts[:, 0, :], consts[:, 1, :], consts[:, 2, :], consts[:, 3, :]
    blockmask = consts[:, 4, :]
    zerob = sb("zerob", [128, 128], BF16)
    bzerob = Buf()
    bones = bconsts

    NS = 4
    slots = [sb("slot%d" % i, [128, 8 * 512], BF16) for i in range(NS)]
    bslot = [Buf() for _ in range(NS)]
    sslot = [P.new_sem("dslot%d" % i) for i in range(NS)]
    slot_rr = [0]

    psum = [nc.alloc_psum_tensor("ps%d" % i, [128, 512], F32) for i in range(7)]
    psT = nc.alloc_psum_tensor("psT", [128, 512], F32)
    bps = [Buf() for _ in range(7)]
    bpsT = Buf()

    SCRF = 16600
    scrF = sb("scrF", [128, SCRF], F32)
    scr_pos = [0]

    def scratch_reset(pos=0):
        P.barrier()
        scr_pos[0] = pos

    def falloc(n, parts=128):
        a = scrF[0:parts, scr_pos[0]:scr_pos[0] + n]
        scr_pos[0] += n
        assert scr_pos[0] <= SCRF, scr_pos
        return a, Buf()

    def balloc(n):
        m = (n + 1) // 2
        a = scrF[:, scr_pos[0]:scr_pos[0] + m].bitcast(BF16)[:, 0:n]
        scr_pos[0] += m
        assert scr_pos[0] <= SCRF, scr_pos
        return a, Buf()

    sx = P.new_sem("dx")
    sy = P.new_sem("dy")
    sc = P.new_sem("dc")
    sm = P.new_sem("dm")
    sw8 = P.new_sem("dw8")

    P.dma("sp", sc, cols[:, :, :], cols_d.rearrange("l p n -> p l n"), writes=[bcols])
    P.dma("sp", sc, consts[:, :, :], consts_d, writes=[bconsts])
    bidb = Buf()
    P.op("dve", lambda e: e.tensor_copy(out=identb[:, :], in_=ident), reads=[bconsts], writes=[bidb])
    P.op("dve", lambda e: e.tensor_copy(out=mask01b[:, :], in_=triu), reads=[bconsts], writes=[bidb])
    epsb = sb("epsb", [128, 1], F32)
    bepsb = Buf()
    P.op("pool", lambda e: e.memset(epsb[:, :], EPS), writes=[bepsb])
    P.op("pool", lambda e: e.memset(zerob[:, :], 0.0), writes=[bzerob])

    def col(l, name, j=0):
        o = COLOFF[name] + j
        return cols[:, l, o:o + 1]

    def load_slot(pieces):
        i = slot_rr[0] % NS
        slot_rr[0] += 1
        s = slots[i]
        for src, kt, off, n in pieces:
            dst = s[:, :].rearrange("p (k n) -> p k n", k=8)[:, 0:kt, off:off + n]
            P.dma("pool", sslot[i], dst, src.rearrange("(k p) n -> p k n", p=128), writes=[bslot[i]])
        return s[:, :].rearrange("p (k n) -> p k n", k=8), bslot[i]

    def mm_group(out_ap, bout, lhs_fn, rhs_fn, nk, reads):
        for k in range(nk):
            P.op("pe", lambda e: e.matmul(out_ap, lhs_fn(k), rhs_fn(k), start=(k == 0), stop=(k == nk - 1)),
                 reads=reads, writes=[bout], inc=(k == nk - 1))

    def hs(j):
        return slice(j * ST, (j + 1) * ST)

    carry_sc = [[sb("csc%d_%d" % (l, c), [128, 2], F32) for c in range(4)] for l in range(DEPTH)]
    bcarry_sc = [[Buf() for c in range(4)] for l in range(DEPTH)]
    carry_m2 = [[sb("cm2%d_%d" % (l, c), [128, 3], F32) for c in range(8)] for l in range(DEPTH)]
    bcarry_m2 = [[Buf() for c in range(8)] for l in range(DEPTH)]
    ST_m2 = [sb("stm2_%d" % l, [128, 512], F32) for l in range(DEPTH)]
    bST_m2 = [Buf() for l in range(DEPTH)]
    carry_s5 = [sb("cs5_%d" % l, [128, 2, 16], F32) for l in range(DEPTH)]
    bcarry_s5 = [Buf() for l in range(DEPTH)]
    for l in range(DEPTH):
        for c in range(4):
            P.op("pool", lambda e: e.memset(carry_sc[l][c][:, :], 0.0), writes=[bcarry_sc[l][c]])
        for c in range(8):
            P.op("pool", lambda e: e.memset(carry_m2[l][c][:, :], 0.0), writes=[bcarry_m2[l][c]])
        P.op("pool", lambda e: e.memset(ST_m2[l][:, :], 0.0), writes=[bST_m2[l]])
        P.op("pool", lambda e: e.memset(carry_s5[l][:, :, :], 0.0), writes=[bcarry_s5[l]])

    def rms_stats(src_fn, breads, nk, scale, rstd, brstd, n=ST):
        sq, bsq = falloc_sq[0]
        for k in range(nk):
            P.op("act", lambda e: e.activation(out=sq[:, 0:n], in_=src_fn(k), func=AF.Square), reads=breads(k), writes=[bsq])
            P.op("pe", lambda e: e.matmul(psum[6][:, 0:n], ones, sq[:, 0:n], start=(k == 0), stop=(k == nk - 1)),
                 reads=[bsq, bones], writes=[bps[6]])
        P.op("act", lambda e: e.activation(out=rstd[:, 0:n], in_=psum[6][:, 0:n], func=AF.Sqrt, bias=epsb[:, 0:1], scale=scale),
             reads=[bps[6], bepsb], writes=[brstd])
        P.op("dve", lambda e: e.reciprocal(out=rstd[:, 0:n], in_=rstd[:, 0:n]), reads=[brstd], writes=[brstd])

    sq_t = sb("sq_t", [128, ST], F32)
    falloc_sq = [(sq_t, Buf())]
    rstd_t = sb("rstd_t", [128, ST], F32)
    brstd_t = Buf()

    def phase_norm(l):
        for j in range(NSUB):
            rms_stats(lambda k: X[:, k, hs(j)], lambda k: [bX[k][j]], 8, 1.0 / D, rstd_t, brstd_t)
            for k in range(8):
                P.op("dve", lambda e: e.scalar_tensor_tensor(
                    out=H[:, k, hs(j)], in0=X[:, k, hs(j)], scalar=col(l, "nw", k),
                    in1=rstd_t[:, :], op0=ALU.mult, op1=ALU.mult),
                    reads=[bX[k][j], brstd_t, bcols], writes=[bH[j]])

    def phase_d(l):
        scratch_reset()
        pbuf, bp = falloc(2 + TP)
        hsb, bhsb = falloc(ST)
        q, bq = falloc(ST)
        yv, byv = falloc(ST)
        sg, bsg = falloc(ST)
        for c in range(4):
            sl, bsl = load_slot([(w_in[l, :, o + c * 128:o + (c + 1) * 128], 8, i * 128, 128)
                                 for i, o in enumerate((O_SCB, O_SCC, O_SCH, O_SCG))])
            P.op("pool", lambda e: e.tensor_copy(out=pbuf[:, 0:2], in_=carry_sc[l][c][:, :]), reads=[bcarry_sc[l][c]], writes=[bp])
            for j in range(NSUB):
                pb = 3 * (j % 2)
                pgate = 6
                for i in range(3):
                    mm_group(psum[pb + i][:, :], bps[pb + i], lambda k: sl[:, k, i * 128:(i + 1) * 128], lambda k: H[:, k, hs(j)], 8, [bsl, bH[j]])
                mm_group(psum[pgate][:, :], bps[pgate], lambda k: sl[:, k, 384:512], lambda k: H[:, k, hs(j)], 8, [bsl, bH[j]])
                P.op("act", lambda e: e.activation(out=hsb, in_=psum[pb + 2][:, :], func=AF.Copy), reads=[bps[pb + 2]], writes=[bhsb])
                P.op("dve", lambda e: e.tensor_tensor(out=pbuf[:, 2 + j * ST:2 + (j + 1) * ST], in0=psum[pb + 1][:, :], in1=hsb, op=ALU.mult),
                     reads=[bps[pb + 1], bhsb], writes=[bp])
                P.op("dve", lambda e: e.tensor_scalar(out=q, in0=pbuf[:, 2 + j * ST:2 + (j + 1) * ST], scalar1=col(l, "scw", 8 + c), scalar2=None, op0=ALU.mult),
                     reads=[bp, bcols], writes=[bq])
                P.op("dve", lambda e: e.scalar_tensor_tensor(out=q, in0=pbuf[:, 1 + j * ST:1 + (j + 1) * ST], scalar=col(l, "scw", 4 + c), in1=q, op0=ALU.mult, op1=ALU.add),
                     reads=[bp, bq, bcols], writes=[bq])
                P.op("dve", lambda e: e.scalar_tensor_tensor(out=q, in0=pbuf[:, j * ST:(j + 1) * ST], scalar=col(l, "scw", c), in1=q, op0=ALU.mult, op1=ALU.add),
                     reads=[bp, bq, bcols], writes=[bq])
                P.op("dve", lambda e: e.tensor_tensor(out=yv, in0=psum[pb][:, :], in1=q, op=ALU.mult), reads=[bps[pb], bq], writes=[byv])
                P.op("act", lambda e: e.activation(out=sg, in_=psum[pgate][:, :], func=AF.Silu), reads=[bps[pgate]], writes=[bsg])
                P.op("pool", lambda e: e.tensor_tensor(out=Y[:, c, hs(j)], in0=yv, in1=sg, op=ALU.mult),
                     reads=[byv, bsg], writes=[bY[c][j]])
            P.op("pool", lambda e: e.tensor_copy(out=carry_sc[l][c][:, :], in_=pbuf[:, TP:TP + 2]), reads=[bp], writes=[bcarry_sc[l][c]])

    def phase_b(l):
        scratch_reset()
        lnw, blnw = falloc(512)
        lnb, blnb = falloc(512)
        wraw, bwraw = falloc(1024)
        bsrow, bbsrow = falloc(512)
        v32, bv32 = falloc(512)
        vn, bvn = falloc(512)
        st6, bst6 = falloc(6)
        mv, bmv = falloc(2)
        rs, brs = falloc(1)
        gu, bgu = falloc(ST)
        sg, bsg = falloc(ST)
        t1, bt1 = falloc(ST)
        wmT, bwmT = balloc(1024)
        VN, bVN = balloc(8 * 512)
        bVNq = [Buf() for _ in range(8)]
        P.dma("sp", sm, lnw, rows_d[l, :, 0:512], writes=[blnw])
        P.dma("sp", sm, lnb, rows_d[l, :, 512:1024], writes=[blnb])
        P.dma("sp", sm, wraw, sguw_d[l].rearrange("s h t -> s (h t)"), writes=[bwraw])
        P.dma("sp", sm, bsrow, sgub_d[l], writes=[bbsrow])
        bsv = bsrow.rearrange("p (c t) -> p c t", c=4)
        P.op("dve", lambda e: e.tensor_tensor(out=wmT.rearrange("p (h t) -> p h t", h=8), in0=wraw.rearrange("p (h t) -> p h t", h=8),
                                             in1=triu.unsqueeze(1).broadcast_to([128, 8, 128]), op=ALU.mult),
             reads=[bwraw, bconsts], writes=[bwmT])
        slv, bslv = load_slot([(w_in[l, :, O_SGV:O_SGV + 512], 8, 0, 512)])
        for qc in range(TP // 128):
            pi = qc % 2
            mm_group(psum[pi][:, :], bps[pi], lambda k: H[:, k, qc * 128:(qc + 1) * 128], lambda k: slv[:, k, 0:512], 8, [bslv, bH[qc // 4]])
            P.op("act", lambda e: e.activation(out=v32, in_=psum[pi][:, :], func=AF.Gelu), reads=[bps[pi]], writes=[bv32])
            P.op("dve", lambda e: e.bn_stats(out=st6, in_=v32), reads=[bv32], writes=[bst6])
            P.op("dve", lambda e: e.bn_aggr(out=mv, in_=st6), reads=[bst6], writes=[bmv])
            P.op("act", lambda e: e.activation(out=rs, in_=mv[:, 1:2], func=AF.Sqrt, bias=epsb[:, 0:1], scale=1.0), reads=[bmv, bepsb], writes=[brs])
            P.op("dve", lambda e: e.reciprocal(out=rs, in_=rs), reads=[brs], writes=[brs])
            P.op("dve", lambda e: e.tensor_scalar(out=vn, in0=v32, scalar1=mv[:, 0:1], scalar2=rs, op0=ALU.subtract, op1=ALU.mult),
                 reads=[bv32, bmv, brs], writes=[bvn])
            P.op("pool", lambda e: e.tensor_tensor(out=vn, in0=vn, in1=lnw, op=ALU.mult), reads=[bvn, blnw], writes=[bvn])
            P.op("pool", lambda e: e.tensor_tensor(out=VN[:, qc * 512:(qc + 1) * 512], in0=vn, in1=lnb, op=ALU.add), reads=[bvn, blnb], writes=[bVNq[qc]])
        for c in range(4):
            sl, bsl = load_slot([(w_in[l, :, O_SGU + c * 128:O_SGU + (c + 1) * 128], 8, 0, 128),
                                 (w_in[l, :, O_SGG + c * 128:O_SGG + (c + 1) * 128], 8, 128, 128)])
            for j in range(NSUB):
                pu, pg, pss = 2, 3, 4 + (j % 2)
                mm_group(psum[pu][:, :], bps[pu], lambda k: sl[:, k, 0:128], lambda k: H[:, k, hs(j)], 8, [bsl, bH[j]])
                mm_group(psum[pg][:, :], bps[pg], lambda k: sl[:, k, 128:256], lambda k: H[:, k, hs(j)], 8, [bsl, bH[j]])
                P.op("act", lambda e: e.activation(out=gu, in_=psum[pu][:, :], func=AF.Gelu), reads=[bps[pu]], writes=[bgu])
                P.op("act", lambda e: e.activation(out=sg, in_=psum[pg][:, :], func=AF.Silu), reads=[bps[pg]], writes=[bsg])
                for qq in range(4):
                    qc = j * 4 + qq
                    for h2 in range(2):
                        h = 2 * c + h2
                        o = psum[pss][h2 * 64:(h2 + 1) * 64, qq * 128:(qq + 1) * 128]
                        P.op("pe", lambda e: e.matmul(o, VN[:, qc * 512 + h * 64:qc * 512 + (h + 1) * 64], wmT[:, h * 128:(h + 1) * 128], start=True, stop=True),
                             reads=[bVNq[qc], bwmT], writes=[bps[pss]], inc=True)
                P.op("dve", lambda e: e.tensor_tensor(out=t1.rearrange("p (q t) -> p q t", q=4), in0=psum[pss][:, :].rearrange("p (q t) -> p q t", q=4),
                                                     in1=bsv[:, c, :].unsqueeze(1).broadcast_to([128, 4, 128]), op=ALU.add), reads=[bps[pss], bbsrow], writes=[bt1])
                P.op("dve", lambda e: e.tensor_tensor(out=t1, in0=t1, in1=gu, op=ALU.mult), reads=[bt1, bgu], writes=[bt1])
                P.op("pool", lambda e: e.tensor_tensor(out=Y[:, c, hs(j)], in0=t1, in1=sg, op=ALU.mult), reads=[bt1, bsg], writes=[bY[c][j]])

    def phase_c(l):
        scratch_reset()
        NQ = TP // 128
        dtb, bdtb = falloc(8)
        alog, balog = falloc(8)
        a_t, ba_t = falloc(8)
        dt, bdt = falloc(64)
        da, bda = falloc(64)
        csc, bcsc = falloc(64)
        dec, bdec = falloc(64)
        eA, beA = falloc(64)
        dtdec, bdtdec = falloc(64)
        cbuf, bcbuf = falloc(3 + TP)
        qv, bqv = falloc(ST)
        ltl, bltl = falloc(128)
        rep, brep = falloc(128)
        erow, berow = falloc(128)
        t1, bt1 = falloc(128)
        yq, byq = falloc(512)
        rstd, brstd = falloc(128)
        wdt, bwdt = balloc(64)
        xTb, bxT = balloc(4 * TP)
        BTb, bBT = balloc(2 * TP)
        CTb, bCT = balloc(2 * TP)
        ZS, bZS = balloc(4 * TP)
        xdt, bxdt = balloc(512)
        xdd, bxdd = balloc(512)
        Btok, bBtok = balloc(256)
        scm, bscm = balloc(256)
        LTm, bLTm = balloc(128)
        MT, bMT = balloc(128)
        STb, bSTb = balloc(512)
        P.dma("sp", sm, dtb, rows_d[l, :, 1024:1032], writes=[bdtb])
        P.dma("sp", sm, alog, rows_d[l, :, 1032:1040], writes=[balog])
        slw8, bwdt = load_slot([(w_in[l, :, O_M2DT - 120:O_M2DT + 8], 8, 0, 128)])
        P.op("act", lambda e: e.activation(out=a_t, in_=alog, func=AF.Exp), reads=[balog], writes=[ba_t])
        P.op("dve", lambda e: e.tensor_scalar(out=a_t, in0=a_t, scalar1=-1.0, scalar2=None, op0=ALU.mult), reads=[ba_t], writes=[ba_t])
        P.op("act", lambda e: e.activation(out=STb, in_=ST_m2[l][:, :], func=AF.Copy), reads=[bST_m2[l]], writes=[bSTb])
        wdtv = slw8[:, :, 120:128]
        for qc in range(NQ):
            mm_group(psum[0][:, qc * 8:(qc + 1) * 8], bps[0], lambda k: H[:, k, qc * 128:(qc + 1) * 128], lambda k: wdtv[:, k, :], 8, [bwdt, bH[qc // 4]])
        P.op("dve", lambda e: e.tensor_tensor(out=dt.rearrange("p (q h) -> p q h", h=8), in0=psum[0][:, 0:64].rearrange("p (q h) -> p q h", h=8),
                                             in1=dtb.unsqueeze(1).broadcast_to([128, NQ, 8]), op=ALU.add), reads=[bps[0], bdtb], writes=[bdt])
        P.op("act", lambda e: e.activation(out=dt, in_=dt, func=AF.Exp), reads=[bdt], writes=[bdt])
        P.op("act", lambda e: e.activation(out=dt, in_=dt, func=AF.Ln, bias=1.0), reads=[bdt], writes=[bdt])
        P.op("dve", lambda e: e.tensor_tensor(out=da.rearrange("p (q h) -> p q h", h=8), in0=dt.rearrange("p (q h) -> p q h", h=8),
                                             in1=a_t.unsqueeze(1).broadcast_to([128, NQ, 8]), op=ALU.mult), reads=[bdt, ba_t], writes=[bda])
        P.op("pe", lambda e: e.matmul(psum[0][:, 64:128], triu, da, start=True, stop=True), reads=[bda, bconsts], writes=[bps[0]])
        P.op("pe", lambda e: e.matmul(psum[0][:, 128:192], ones, da, start=True, stop=True), reads=[bda, bconsts], writes=[bps[0]])
        P.op("act", lambda e: e.activation(out=csc, in_=psum[0][:, 64:128], func=AF.Copy), reads=[bps[0]], writes=[bcsc])
        P.op("dve", lambda e: e.tensor_tensor(out=dec, in0=psum[0][:, 128:192], in1=csc, op=ALU.subtract), reads=[bps[0], bcsc], writes=[bdec])
        P.op("act", lambda e: e.activation(out=dec, in_=dec, func=AF.Exp), reads=[bdec], writes=[bdec])
        P.op("act", lambda e: e.activation(out=eA, in_=psum[0][:, 128:192], func=AF.Exp), reads=[bps[0]], writes=[beA])
        P.op("dve", lambda e: e.tensor_tensor(out=dtdec, in0=dt, in1=dec, op=ALU.mult), reads=[bdt, bdec], writes=[bdtdec])
        CSTOP = int(os.environ.get("CSTOP", "9"))
        if CSTOP <= 1:
            return
        slz, bslz = load_slot([(w_in[l, :, O_M2Z:O_M2Z + 512], 8, 0, 512)])
        n = 0
        for ct in range(4):
            for j in range(NSUB):
                pi = 1 + n % 2
                n += 1
                mm_group(psum[pi][:, :], bps[pi], lambda k: slz[:, k, ct * 128:(ct + 1) * 128], lambda k: H[:, k, hs(j)], 8, [bslz, bH[j]])
                P.op("act", lambda e: e.activation(out=ZS[:, ct * TP + j * ST:ct * TP + (j + 1) * ST], in_=psum[pi][:, :], func=AF.Silu), reads=[bps[pi]], writes=[bZS])
        slx = [load_slot([(w_in[l, :, O_M2X + hh * 512:O_M2X + (hh + 1) * 512], 8, 0, 512)]) for hh in range(2)]
        for ct in range(8):
            sl, bsl = slx[ct // 4]
            P.op("pool", lambda e: e.tensor_copy(out=cbuf[:, 0:3], in_=carry_m2[l][ct][:, :]), reads=[bcarry_m2[l][ct]], writes=[bcbuf])
            for j in range(NSUB):
                pi = 1 + n % 2
                n += 1
                mm_group(psum[pi][:, :], bps[pi], lambda k: sl[:, k, (ct % 4) * 128:(ct % 4 + 1) * 128], lambda k: H[:, k, hs(j)], 8, [bsl, bH[j]])
                P.op("act", lambda e: e.activation(out=cbuf[:, 3 + j * ST:3 + (j + 1) * ST], in_=psum[pi][:, :], func=AF.Copy), reads=[bps[pi]], writes=[bcbuf])
            for j in range(NSUB):
                P.op("dve", lambda e: e.tensor_scalar(out=qv, in0=cbuf[:, 3 + j * ST:3 + (j + 1) * ST], scalar1=col(l, "m2cw", 24 + ct), scalar2=col(l, "m2cb", ct), op0=ALU.mult, op1=ALU.add),
                     reads=[bcbuf, bcols], writes=[bqv])
                for tap in range(3):
                    P.op("dve", lambda e: e.scalar_tensor_tensor(out=qv, in0=cbuf[:, tap + j * ST:tap + (j + 1) * ST], scalar=col(l, "m2cw", tap * 8 + ct), in1=qv, op0=ALU.mult, op1=ALU.add),
                         reads=[bcbuf, bqv, bcols], writes=[bqv])
                if ct < 4:
                    dst, bd = xTb[:, ct * TP + j * ST:ct * TP + (j + 1) * ST], bxT
                elif ct < 6:
                    dst, bd = BTb[:, (ct - 4) * TP + j * ST:(ct - 4) * TP + (j + 1) * ST], bBT
                else:
                    dst, bd = CTb[:, (ct - 6) * TP + j * ST:(ct - 6) * TP + (j + 1) * ST], bCT
                P.op("act", lambda e: e.activation(out=dst, in_=qv, func=AF.Silu), reads=[bqv], writes=[bd])
            P.op("pool", lambda e: e.tensor_copy(out=carry_m2[l][ct][:, :], in_=cbuf[:, TP:TP + 3]), reads=[bcbuf], writes=[bcarry_m2[l][ct]])
        if CSTOP <= 2:
            return
        for qc in range(NQ):
            cs = slice(qc * 128, (qc + 1) * 128)
            for ct in range(4):
                P.op("pe", lambda e: e.matmul(psT[:, ct * 128:(ct + 1) * 128], xTb[:, ct * TP + qc * 128:ct * TP + (qc + 1) * 128], identb[:, :], start=True, stop=True),
                     reads=[bxT, bidb], writes=[bpsT])
            for g in range(2):
                P.op("pe", lambda e: e.matmul(psum[0][:, g * 128:(g + 1) * 128], BTb[:, g * TP + qc * 128:g * TP + (qc + 1) * 128], identb[:, :], start=True, stop=True),
                     reads=[bBT, bidb], writes=[bps[0]])
            for h in range(8):
                P.op("dve", lambda e: e.tensor_scalar(out=xdt[:, h * 64:(h + 1) * 64], in0=psT[:, h * 64:(h + 1) * 64], scalar1=dt[:, qc * 8 + h:qc * 8 + h + 1], scalar2=None, op0=ALU.mult),
                     reads=[bpsT, bdt], writes=[bxdt])
                P.op("dve", lambda e: e.tensor_scalar(out=xdd[:, h * 64:(h + 1) * 64], in0=psT[:, h * 64:(h + 1) * 64], scalar1=dtdec[:, qc * 8 + h:qc * 8 + h + 1], scalar2=None, op0=ALU.mult),
                     reads=[bpsT, bdtdec], writes=[bxdd])
            P.op("act", lambda e: e.activation(out=Btok, in_=psum[0][:, 0:256], func=AF.Copy), reads=[bps[0]], writes=[bBtok])
            if CSTOP <= 3:
                continue
            for g in range(2):
                P.op("pe", lambda e: e.matmul(psum[1][:, g * 128:(g + 1) * 128], BTb[:, g * TP + qc * 128:g * TP + (qc + 1) * 128],
                                              CTb[:, g * TP + qc * 128:g * TP + (qc + 1) * 128], start=True, stop=True), reads=[bBT, bCT], writes=[bps[1]])
            P.op("dve", lambda e: e.tensor_tensor(out=scm.rearrange("p (g t) -> p g t", g=2), in0=psum[1][:, 0:256].rearrange("p (g t) -> p g t", g=2),
                                                 in1=triu.unsqueeze(1).broadcast_to([128, 2, 128]), op=ALU.mult), reads=[bps[1], bconsts], writes=[bscm])
            for h in range(8):
                g = h // 4
                h2 = h % 2
                dacol = da[:, qc * 8 + h:qc * 8 + h + 1]
                P.op("pool", lambda e: e.tensor_scalar(out=ltl, in0=ltstrict, scalar1=dacol, scalar2=1.0, op0=ALU.mult, op1=ALU.mult), reads=[bda, bconsts], writes=[bltl])
                P.op("pe", lambda e: e.matmul(psum[2][:, (h % 4) * 128:(h % 4 + 1) * 128], ltl, triu, start=True, stop=True), reads=[bltl, bconsts], writes=[bps[2]])
                P.op("act", lambda e: e.activation(out=LTm, in_=psum[2][:, (h % 4) * 128:(h % 4 + 1) * 128], func=AF.Exp), reads=[bps[2]], writes=[bLTm])
                P.op("dve", lambda e: e.tensor_tensor(out=MT, in0=scm[:, g * 128:(g + 1) * 128], in1=LTm, op=ALU.mult), reads=[bscm, bLTm], writes=[bMT])
                P.op("pool", lambda e: e.tensor_scalar(out=rep[:, h2 * 64:(h2 + 1) * 64], in0=ones[:, 0:64], scalar1=dacol, scalar2=1.0, op0=ALU.mult, op1=ALU.mult), reads=[bda, bconsts], writes=[brep])
                if h2 == 1:
                    P.op("pe", lambda e: e.matmul(psum[3][:, 0:128], rep, triu, start=True, stop=True), reads=[brep, bconsts], writes=[bps[3]])
                P.op("pe", lambda e: e.matmul(psum[4][h2 * 64:(h2 + 1) * 64, 0:128], xdt[:, h * 64:(h + 1) * 64], MT, start=True, stop=True), reads=[bxdt, bMT], writes=[bps[4]])
                P.op("pe", lambda e: e.matmul(psum[5][h2 * 64:(h2 + 1) * 64, 0:128], STb[:, h * 64:(h + 1) * 64], CTb[:, g * TP + qc * 128:g * TP + (qc + 1) * 128], start=True, stop=True),
                     reads=[bSTb, bCT], writes=[bps[5]])
                if h2 == 1:
                    ct = h // 2
                    P.op("act", lambda e: e.activation(out=erow, in_=psum[3][:, 0:128], func=AF.Exp), reads=[bps[3]], writes=[berow])
                    P.op("dve", lambda e: e.tensor_tensor(out=t1, in0=psum[5][:, 0:128], in1=erow, op=ALU.mult), reads=[bps[5], berow], writes=[bt1])
                    P.op("dve", lambda e: e.tensor_tensor(out=t1, in0=psum[4][:, 0:128], in1=t1, op=ALU.add), reads=[bps[4], bt1], writes=[bt1])
                    P.op("dve", lambda e: e.scalar_tensor_tensor(out=yq[:, ct * 128:(ct + 1) * 128], in0=xTb[:, ct * TP + qc * 128:ct * TP + (qc + 1) * 128], scalar=col(l, "m2d", ct),
                                                                in1=t1, op0=ALU.mult, op1=ALU.add), reads=[bxT, bt1, bcols], writes=[byq])
                    P.op("pool", lambda e: e.tensor_tensor(out=yq[:, ct * 128:(ct + 1) * 128], in0=yq[:, ct * 128:(ct + 1) * 128],
                                                          in1=ZS[:, ct * TP + qc * 128:ct * TP + (qc + 1) * 128], op=ALU.mult), reads=[byq, bZS], writes=[byq])
            for g in range(2):
                P.op("pe", lambda e: e.matmul(psum[1][:, g * 256:(g + 1) * 256], Btok[:, g * 128:(g + 1) * 128], xdd[:, g * 256:(g + 1) * 256], start=True, stop=True),
                     reads=[bBtok, bxdd], writes=[bps[1]])
            for h in range(8):
                P.op("dve", lambda e: e.scalar_tensor_tensor(out=ST_m2[l][:, h * 64:(h + 1) * 64], in0=ST_m2[l][:, h * 64:(h + 1) * 64], scalar=eA[:, qc * 8 + h:qc * 8 + h + 1],
                                                            in1=psum[1][:, h * 64:(h + 1) * 64], op0=ALU.mult, op1=ALU.add), reads=[bST_m2[l], beA, bps[1]], writes=[bST_m2[l]])
            P.op("act", lambda e: e.activation(out=STb, in_=ST_m2[l][:, :], func=AF.Copy), reads=[bST_m2[l]], writes=[bSTb])
            rms_stats(lambda k: yq[:, k * 128:(k + 1) * 128], lambda k: [byq], 4, 1.0 / W, rstd, brstd, n=128)
            for ct in range(4):
                P.op("dve", lambda e: e.scalar_tensor_tensor(out=Y[:, ct, cs], in0=yq[:, ct * 128:(ct + 1) * 128], scalar=col(l, "m2nw", ct), in1=rstd[:, 0:128],
                                                            op0=ALU.mult, op1=ALU.mult), reads=[byq, brstd, bcols], writes=[bY[ct][qc // 4]])


    def sincos(ang, bang, n, out_sin, out_cos, tmp, tmpi, btmp):
        for shift, dst in ((0.0, out_sin), (0.5 * np.pi, out_cos)):
            P.op("dve", lambda e: e.tensor_scalar(out=tmp[:, 0:n], in0=ang, scalar1=float(shift), scalar2=float(1.0 / (2 * np.pi)), op0=ALU.add, op1=ALU.mult),
                 reads=[bang], writes=[btmp])
            P.op("dve", lambda e: e.tensor_copy(out=tmpi[:, 0:n], in_=tmp[:, 0:n]), reads=[btmp], writes=[btmp])
            P.op("dve", lambda e: e.tensor_copy(out=tmp[:, n:2 * n], in_=tmpi[:, 0:n]), reads=[btmp], writes=[btmp])
            P.op("dve", lambda e: e.tensor_tensor(out=tmp[:, 0:n], in0=tmp[:, 0:n], in1=tmp[:, n:2 * n], op=ALU.subtract), reads=[btmp], writes=[btmp])
            P.op("dve", lambda e: e.tensor_scalar(out=tmp[:, n:2 * n], in0=tmp[:, 0:n], scalar1=0.5, scalar2=None, op0=ALU.is_gt), reads=[btmp], writes=[btmp])
            P.op("dve", lambda e: e.tensor_tensor(out=tmp[:, 0:n], in0=tmp[:, 0:n], in1=tmp[:, n:2 * n], op=ALU.subtract), reads=[btmp], writes=[btmp])
            P.op("dve", lambda e: e.tensor_scalar(out=tmp[:, n:2 * n], in0=tmp[:, 0:n], scalar1=-0.5, scalar2=None, op0=ALU.is_lt), reads=[btmp], writes=[btmp])
            P.op("dve", lambda e: e.tensor_tensor(out=tmp[:, 0:n], in0=tmp[:, 0:n], in1=tmp[:, n:2 * n], op=ALU.add), reads=[btmp], writes=[btmp])
            P.op("act", lambda e: e.activation(out=dst, in_=tmp[:, 0:n], func=AF.Sin, scale=float(2 * np.pi * (1 - 1e-6))), reads=[btmp], writes=[btmp])

    def s5_lambda(lre, lim, lst, n, bsrc, abre, abim, tmp, tmpi, btmp, scr, bscr):
        step, lrs, lis, mag = scr[:, 0:n], scr[:, n:2 * n], scr[:, 2 * n:3 * n], scr[:, 3 * n:4 * n]
        P.op("act", lambda e: e.activation(out=step, in_=lst, func=AF.Exp), reads=[bsrc], writes=[bscr])
        P.op("dve", lambda e: e.tensor_tensor(out=lrs, in0=lre, in1=step, op=ALU.mult), reads=[bsrc, bscr], writes=[bscr])
        P.op("dve", lambda e: e.tensor_tensor(out=lis, in0=lim, in1=step, op=ALU.mult), reads=[bsrc, bscr], writes=[bscr])
        P.op("act", lambda e: e.activation(out=mag, in_=lrs, func=AF.Exp), reads=[bscr], writes=[bscr])
        sincos(lis, bscr, n, abim, abre, tmp, tmpi, btmp)
        P.op("dve", lambda e: e.tensor_tensor(out=abre, in0=abre, in1=mag, op=ALU.mult), reads=[btmp, bscr], writes=[btmp])
        P.op("dve", lambda e: e.tensor_tensor(out=abim, in0=abim, in1=mag, op=ALU.mult), reads=[btmp, bscr], writes=[btmp])

    def coef_calc(lre, lim, abre, abim, n, cre_, cim_, scr, rd, bscr, bout):
        nr, den, u1, u2 = scr[:, 0:n], scr[:, n:2 * n], scr[:, 2 * n:3 * n], scr[:, 3 * n:4 * n]
        TT = lambda o, a, b, op, r_, w_: P.op("dve", lambda e: e.tensor_tensor(out=o, in0=a, in1=b, op=op), reads=r_, writes=w_)
        P.op("dve", lambda e: e.tensor_scalar(out=nr, in0=abre, scalar1=-1.0, scalar2=None, op0=ALU.add), reads=rd, writes=[bscr])
        TT(den, lre, lre, ALU.mult, rd, [bscr])
        TT(u1, lim, lim, ALU.mult, rd, [bscr])
        TT(den, den, u1, ALU.add, [bscr], [bscr])
        P.op("dve", lambda e: e.reciprocal(out=den, in_=den), reads=[bscr], writes=[bscr])
        TT(u1, nr, lre, ALU.mult, [bscr] + rd, [bscr])
        TT(u2, abim, lim, ALU.mult, rd, [bscr])
        TT(u1, u1, u2, ALU.add, [bscr], [bscr])
        TT(cre_, u1, den, ALU.mult, [bscr], [bout])
        TT(u1, abim, lre, ALU.mult, rd, [bscr])
        TT(u2, nr, lim, ALU.mult, [bscr] + rd, [bscr])
        TT(u1, u1, u2, ALU.subtract, [bscr], [bscr])
        TT(cim_, u1, den, ALU.mult, [bscr], [bout])

    def phase_a(l):
        scratch_reset()
        NSC = 7
        Q8 = 8
        CC = TP // Q8
        TT = lambda o, a, b, op, rd, wr: P.op("dve", lambda e: e.tensor_tensor(out=o, in0=a, in1=b, op=op), reads=rd, writes=wr)
        STT = lambda o, a, sc_, b, rd, wr: P.op("dve", lambda e: e.scalar_tensor_tensor(out=o, in0=a, scalar=sc_, in1=b, op0=ALU.mult, op1=ALU.add), reads=rd, writes=wr)
        TS = lambda o, a, sc_, rd, wr: P.op("dve", lambda e: e.tensor_scalar(out=o, in0=a, scalar1=sc_, scalar2=None, op0=ALU.mult), reads=rd, writes=wr)
        pwc, bpwc = falloc(9 * 3 * 16)
        pws, bpws = falloc(NSC * 3 * 16)
        pcC, bpcC = falloc(2 * 256)
        BD, bBD = balloc(2 * 2048)
        CD, bCD = balloc(2 * 2048)
        BDp, bBDp = balloc(2 * 2048)
        pwcv = pwc.rearrange("p (k a g) -> p k a g", k=9, a=3)
        pwsv = pws.rearrange("p (k a g) -> p k a g", k=NSC, a=3)
        mark = scr_pos[0]
        p5, bp5 = falloc(5 * 256)
        pq, bpq = falloc(3 * 16)
        pbp, bpbp = falloc(2 * 256)
        tmp, btmp = falloc(512)
        tmpi_f, _ = falloc(256)
        tmpi = tmpi_f.bitcast(mybir.dt.int32)
        scr, bscr = falloc(1024)
        ab, bab = falloc(512)
        cf, bcf = falloc(512)
        bb, bbb = falloc(512)
        abp, babp = falloc(32)
        cfp, bcfp = falloc(32)
        bbp, bbbp = falloc(512)
        P.dma("sp", sm, p5.rearrange("p (a n) -> p a n", a=5), s5p_d[l], writes=[bp5])
        P.dma("sp", sm, pq.rearrange("p (a n) -> p a n", a=3), s5q_d[l], writes=[bpq])
        P.dma("sp", sm, pcC.rearrange("p (a n) -> p a n", a=2), s5c_d[l, :, 0:2, :], writes=[bpcC])
        P.dma("sp", sm, pbp.rearrange("p (a n) -> p a n", a=2), s5c_d[l, :, 2:4, :], writes=[bpbp])
        lre, lim, lst, bre, bim = (p5[:, i * 256:(i + 1) * 256] for i in range(5))
        abre, abim = ab[:, 0:256], ab[:, 256:512]
        s5_lambda(lre, lim, lst, 256, bp5, abre, abim, tmp, tmpi, btmp, scr, bscr)
        cre_, cim_ = cf[:, 0:256], cf[:, 256:512]
        coef_calc(lre, lim, abre, abim, 256, cre_, cim_, scr, [bp5, btmp], bscr, bcf)
        u1, u2 = scr[:, 512:768], scr[:, 768:1024]
        bbre, bbim = bb[:, 0:256], bb[:, 256:512]
        TT(u1, cre_, bre, ALU.mult, [bcf, bp5], [bscr])
        TT(u2, cim_, bim, ALU.mult, [bcf, bp5], [bscr])
        TT(bbre, u1, u2, ALU.subtract, [bscr], [bbb])
        TT(u1, cre_, bim, ALU.mult, [bcf, bp5], [bscr])
        TT(u2, cim_, bre, ALU.mult, [bcf, bp5], [bscr])
        TT(bbim, u1, u2, ALU.add, [bscr], [bbb])
        for ri, src in enumerate((bbre, bbim)):
            dstv = BD[:, ri * 2048:(ri + 1) * 2048].rearrange("p (c j g n) -> p c j g n", c=4, j=4, g=2)
            for jj in range(4):
                for g2 in range(2):
                    TS(dstv[:, :, jj, g2, :], src.rearrange("p (c n) -> p c n", c=4), col(l, "mkB", jj * 2 + g2), [bbb, bcols], [bBD])
        P.op("pool", lambda e: e.memset(CD, 0.0), writes=[bCD])
        for ri, nm in enumerate(("mkC", "mkCn")):
            dstv = CD[:, ri * 2048:(ri + 1) * 2048].rearrange("p (c j m) -> p c j m", c=4, j=4)
            srcv = pcC[:, ri * 256:(ri + 1) * 256].rearrange("p (q c j) -> p c j q", q=16, c=4, j=4)
            for jj in range(4):
                for g2 in range(2):
                    gl = 2 * jj + g2
                    TS(dstv[:, :, jj, gl * 16:(gl + 1) * 16], srcv[:, :, jj, :], col(l, nm, g2), [bpcC, bcols, bCD], [bCD])
        s5_lambda(pq[:, 0:16], pq[:, 16:32], pq[:, 32:48], 16, bpq, abp[:, 0:16], abp[:, 16:32], tmp, tmpi, btmp, scr, bscr)
        coef_calc(pq[:, 0:16], pq[:, 16:32], abp[:, 0:16], abp[:, 16:32], 16, cfp[:, 0:16], cfp[:, 16:32], scr, [bpq, btmp], bscr, bcfp)
        bq_re = pbp[:, 0:256].rearrange("p (q g) -> p q g", q=16)
        bq_im = pbp[:, 256:512].rearrange("p (q g) -> p q g", q=16)
        cfr = cfp[:, 0:16].unsqueeze(1).broadcast_to([128, 16, 16])
        cfi = cfp[:, 16:32].unsqueeze(1).broadcast_to([128, 16, 16])
        w1 = scr[:, 0:256].rearrange("p (q g) -> p q g", q=16)
        w2 = scr[:, 256:512].rearrange("p (q g) -> p q g", q=16)
        bbp_re = bbp[:, 0:256].rearrange("p (q g) -> p q g", q=16)
        bbp_im = bbp[:, 256:512].rearrange("p (q g) -> p q g", q=16)
        TT(w1, bq_re, cfr, ALU.mult, [bpbp, bcfp], [bscr])
        TT(w2, bq_im, cfi, ALU.mult, [bpbp, bcfp], [bscr])
        TT(bbp_re, w1, w2, ALU.subtract, [bscr], [bbbp])
        TT(w1, bq_im, cfr, ALU.mult, [bpbp, bcfp], [bscr])
        TT(w2, bq_re, cfi, ALU.mult, [bpbp, bcfp], [bscr])
        TT(bbp_im, w1, w2, ALU.add, [bscr], [bbbp])
        P.op("pool", lambda e: e.memset(BDp, 0.0), writes=[bBDp])
        for ri in range(2):
            dstv = BDp[:, ri * 2048:(ri + 1) * 2048].rearrange("p (c j m) -> p c j m", c=4, j=4)
            srcv = bbp[:, ri * 256:(ri + 1) * 256].rearrange("p (q c j) -> p c j q", q=16, c=4, j=4)
            for jj in range(4):
                for g2 in range(2):
                    gl = 2 * jj + g2
                    TS(dstv[:, :, jj, gl * 16:(gl + 1) * 16], srcv[:, :, jj, :], col(l, "mkC", g2), [bbbp, bcols, bBDp], [bBDp])
        P.op("pool", lambda e: e.memset(pwcv[:, 0, 0, :], 1.0), writes=[bpwc])
        P.op("pool", lambda e: e.memset(pwcv[:, 0, 1:3, :], 0.0), reads=[bpwc], writes=[bpwc])
        P.op("dve", lambda e: e.tensor_copy(out=pwcv[:, 1, 0, :], in_=abp[:, 0:16]), reads=[btmp, bpwc], writes=[bpwc])
        P.op("dve", lambda e: e.tensor_copy(out=pwcv[:, 1, 1, :], in_=abp[:, 16:32]), reads=[btmp, bpwc], writes=[bpwc])
        lr_, li_ = abp[:, 0:16], abp[:, 16:32]
        for k in range(2, 9):
            a_, b_ = pwcv[:, k - 1, 0, :], pwcv[:, k - 1, 1, :]
            TT(scr[:, 0:16], a_, lr_, ALU.mult, [bpwc, btmp], [bscr])
            TT(scr[:, 16:32], b_, li_, ALU.mult, [bpwc, btmp], [bscr])
            TT(pwcv[:, k, 0, :], scr[:, 0:16], scr[:, 16:32], ALU.subtract, [bscr, bpwc], [bpwc])
            TT(scr[:, 32:48], a_, li_, ALU.mult, [bpwc, btmp], [bscr])
            TT(scr[:, 48:64], b_, lr_, ALU.mult, [bpwc, btmp], [bscr])
            TT(pwcv[:, k, 1, :], scr[:, 32:48], scr[:, 48:64], ALU.add, [bscr, bpwc], [bpwc])
        for k in range(1, 9):
            TS(pwcv[:, k, 2, :], pwcv[:, k, 1, :], -1.0, [bpwc], [bpwc])
        P.op("dve", lambda e: e.tensor_copy(out=pwsv[:, 0, :, :], in_=pwcv[:, 8, :, :]), reads=[bpwc], writes=[bpws])
        for k in range(1, NSC):
            a_, b_ = pwsv[:, k - 1, 0, :], pwsv[:, k - 1, 1, :]
            TT(scr[:, 0:16], a_, a_, ALU.mult, [bpws], [bscr])
            TT(scr[:, 16:32], b_, b_, ALU.mult, [bpws], [bscr])
            TT(pwsv[:, k, 0, :], scr[:, 0:16], scr[:, 16:32], ALU.subtract, [bscr, bpws], [bpws])
            TT(scr[:, 32:48], a_, b_, ALU.mult, [bpws], [bscr])
            TS(pwsv[:, k, 1, :], scr[:, 32:48], 2.0, [bscr, bpws], [bpws])
            TS(pwsv[:, k, 2, :], scr[:, 32:48], -2.0, [bscr, bpws], [bpws])
        scratch_reset(mark)
        t2, bt2 = falloc(TP)
        XS, _ = falloc(4 * CC)
        bXS = [[Buf(), Buf()], [Buf(), Buf()]]
        XSv = [[XS[:, (b * 2 + ri) * CC:(b * 2 + ri + 1) * CC] for ri in range(2)] for b in range(2)]
        SP, _ = falloc(2 * CC)
        bSP = [Buf(), Buf()]
        SPv = [SP[:, 0:CC], SP[:, CC:2 * CC]]
        stmp, _ = falloc(4 * CC)
        bstmp = [Buf() for _ in range(4)]
        m12, bm12 = falloc(128)
        U, bU = balloc(4 * TP)
        G1, bG1 = balloc(4 * TP)
        Kc, bKc = balloc(8 * 128)
        Mc, bMc = balloc(8 * 2 * 64)
        SX, bSX = balloc(8 * 2 * CC)
        slu, bslu = load_slot([(w_in[l, :, O_S5U:O_S5U + 512], 8, 0, 512)])
        n = 0
        for c in range(4):
            for j in range(NSUB):
                pi = n % 2
                n += 1
                mm_group(psum[pi][:, :], bps[pi], lambda k: slu[:, k, c * 128:(c + 1) * 128], lambda k: H[:, k, hs(j)], 8, [bslu, bH[j]])
                P.op("act", lambda e: e.activation(out=U[:, c * TP + j * ST:c * TP + (j + 1) * ST], in_=psum[pi][:, :], func=AF.Copy), reads=[bps[pi]], writes=[bU])
        bmv = blockmask.rearrange("p (g q) -> p g q", g=8)
        for c in range(4):
            Uc = U[:, c * TP:(c + 1) * TP]
            Ucv = Uc.rearrange("p (cc r) -> p cc r", r=8)
            Cre = pcC[:, 0:256].rearrange("q (p g) -> q p g", p=16)[:, :, 4 * c:4 * c + 4]
            Cim = pcC[:, 256:512].rearrange("q (p g) -> q p g", p=16)[:, :, 4 * c:4 * c + 4]
            m1 = m12[:, 0:64].rearrange("q (p j) -> q p j", p=16)
            m2 = m12[:, 64:128].rearrange("q (p j) -> q p j", p=16)
            for tau in range(8):
                Mre = Mc[:, (tau * 2) * 64:(tau * 2 + 1) * 64].rearrange("q (p j) -> q p j", p=16)
                Mim = Mc[:, (tau * 2 + 1) * 64:(tau * 2 + 2) * 64].rearrange("q (p j) -> q p j", p=16)
                if tau == 0:
                    P.op("dve", lambda e: e.tensor_copy(out=Mre, in_=Cre), reads=[bpcC, bMc], writes=[bMc])
                    TS(Mim, Cim, -1.0, [bpcC, bMc], [bMc])
                    continue
                Pre = pwcv[:, tau, 0, 4 * c:4 * c + 4].unsqueeze(1).broadcast_to([128, 16, 4])
                Pim = pwcv[:, tau, 1, 4 * c:4 * c + 4].unsqueeze(1).broadcast_to([128, 16, 4])
                nPim = pwcv[:, tau, 2, 4 * c:4 * c + 4].unsqueeze(1).broadcast_to([128, 16, 4])
                TT(m1, Cre, Pre, ALU.mult, [bpcC, bpwc, bm12], [bm12])
                TT(m2, Cim, Pim, ALU.mult, [bpcC, bpwc, bm12], [bm12])
                TT(Mre, m1, m2, ALU.subtract, [bm12, bMc], [bMc])
                TT(m1, Cre, nPim, ALU.mult, [bpcC, bpwc, bm12], [bm12])
                TT(m2, Cim, Pre, ALU.mult, [bpcC, bpwc, bm12], [bm12])
                TT(Mim, m1, m2, ALU.subtract, [bm12, bMc], [bMc])
            for tau in range(8):
                nmm = 0
                for jj in range(4):
                    gp = 4 * c + jj
                    for ri in range(2):
                        rhs = Mc[:, (tau * 2 + ri) * 64:(tau * 2 + ri + 1) * 64].rearrange("q (p j) -> q p j", p=16)[:, :, jj]
                        P.op("pe", lambda e: e.matmul(psum[6][:, tau * 16:(tau + 1) * 16], BDp[:, ri * 2048 + gp * 128:ri * 2048 + (gp + 1) * 128], rhs,
                                                      start=(nmm == 0), stop=(nmm == 7)), reads=[bBDp, bMc], writes=[bps[6]], inc=(nmm == 7))
                        nmm += 1
            for tau in range(8):
                P.op("dve", lambda e: e.tensor_tensor(out=Kc[:, tau * 128:(tau + 1) * 128].rearrange("p (g q) -> p g q", g=8), in0=bmv,
                                                     in1=psum[6][:, tau * 16:(tau + 1) * 16].unsqueeze(1).broadcast_to([128, 8, 16]), op=ALU.mult),
                     reads=[bps[6], bconsts, bKc], writes=[bKc])
            for bk in (4, 5):
                P.op("pe", lambda e: e.matmul(psum[bk][:, :], zerob[:, :], Uc[:, 0:512], start=True, stop=False, skip_group_check=True),
                     reads=[bzerob, bU], writes=[bps[bk]], inc=True)
            for r in range(8):
                for rp in range(r + 1):
                    last = (r == 7 and rp == 7)
                    P.op("pe", lambda e: e.matmul(psum[4 + r // 4][:, (r % 4) * 128:(r % 4 + 1) * 128], Kc[:, (r - rp) * 128:(r - rp + 1) * 128], Ucv[:, :, rp],
                                                  start=False, stop=False, skip_group_check=True), reads=[bKc, bU], writes=[bps[4 + r // 4]], inc=last)
            for jj in range(4):
                gp = 4 * c + jj
                for j in range(NSUB):
                    for ri in range(2):
                        pi = 2 * j + ri
                        P.op("pe", lambda e: e.matmul(psum[pi][:, :], BD[:, ri * 2048 + gp * 128:ri * 2048 + (gp + 1) * 128], Uc[:, j * ST:(j + 1) * ST], start=True, stop=True),
                             reads=[bBD, bU], writes=[bps[pi]])
                for j in range(NSUB):
                    bv = [psum[2 * j + ri][:, :].rearrange("p (cc r) -> p cc r", r=8) for ri in range(2)]
                    acc = [XSv[0][ri][:, j * 64:(j + 1) * 64] for ri in range(2)]
                    for ri in range(2):
                        P.op("act", lambda e: e.activation(out=acc[ri], in_=bv[ri][:, :, 7], func=AF.Copy), reads=[bps[2 * j + ri], bXS[0][ri]], writes=[bXS[0][ri]])
                    for r in range(7):
                        k = 7 - r
                        pr, pi_, npi = pwcv[:, k, 0, gp:gp + 1], pwcv[:, k, 1, gp:gp + 1], pwcv[:, k, 2, gp:gp + 1]
                        STT(acc[0], bv[0][:, :, r], pr, acc[0], [bps[2 * j], bpwc, bXS[0][0]], [bXS[0][0]])
                        STT(acc[1], bv[1][:, :, r], pr, acc[1], [bps[2 * j + 1], bpwc, bXS[0][1]], [bXS[0][1]])
                        STT(acc[0], bv[1][:, :, r], npi, acc[0], [bps[2 * j + 1], bpwc, bXS[0][0]], [bXS[0][0]])
                        STT(acc[1], bv[0][:, :, r], pi_, acc[1], [bps[2 * j], bpwc, bXS[0][1]], [bXS[0][1]])
                cr, ci = carry_s5[l][:, 0, gp:gp + 1], carry_s5[l][:, 1, gp:gp + 1]
                p8r, p8i, p8n = pwcv[:, 8, 0, gp:gp + 1], pwcv[:, 8, 1, gp:gp + 1], pwcv[:, 8, 2, gp:gp + 1]
                STT(XSv[0][0][:, 0:1], cr, p8r, XSv[0][0][:, 0:1], [bcarry_s5[l], bpwc, bXS[0][0]], [bXS[0][0]])
                STT(XSv[0][1][:, 0:1], ci, p8r, XSv[0][1][:, 0:1], [bcarry_s5[l], bpwc, bXS[0][1]], [bXS[0][1]])
                STT(XSv[0][0][:, 0:1], ci, p8n, XSv[0][0][:, 0:1], [bcarry_s5[l], bpwc, bXS[0][0]], [bXS[0][0]])
                STT(XSv[0][1][:, 0:1], cr, p8i, XSv[0][1][:, 0:1], [bcarry_s5[l], bpwc, bXS[0][1]], [bXS[0][1]])
                sbuf_i = 0
                for k in range(NSC):
                    sh = 1 << k
                    src, dst = XSv[sbuf_i], XSv[1 - sbuf_i]
                    bs_, bd_ = bXS[sbuf_i], bXS[1 - sbuf_i]
                    ar, ai, nai = pwsv[:, k, 0, gp:gp + 1], pwsv[:, k, 1, gp:gp + 1], pwsv[:, k, 2, gp:gp + 1]
                    STT(dst[0][:, sh:CC], src[0][:, 0:CC - sh], ar, src[0][:, sh:CC], [bs_[0], bpws, bd_[0]], [bd_[0]])
                    STT(dst[1][:, sh:CC], src[1][:, 0:CC - sh], ar, src[1][:, sh:CC], [bs_[1], bpws, bd_[1]], [bd_[1]])
                    STT(dst[0][:, sh:CC], src[1][:, 0:CC - sh], nai, dst[0][:, sh:CC], [bs_[1], bpws, bd_[0]], [bd_[0]])
                    STT(dst[1][:, sh:CC], src[0][:, 0:CC - sh], ai, dst[1][:, sh:CC], [bs_[0], bpws, bd_[1]], [bd_[1]])
                    for ri in range(2):
                        P.op("act", lambda e: e.activation(out=dst[ri][:, 0:sh], in_=src[ri][:, 0:sh], func=AF.Copy), reads=[bs_[ri], bd_[ri]], writes=[bd_[ri]])
                    sbuf_i = 1 - sbuf_i
                S_, bS_ = XSv[sbuf_i], bXS[sbuf_i]
                for ri in range(2):
                    P.op("pool", lambda e: e.tensor_copy(out=SPv[ri][:, 1:CC], in_=S_[ri][:, 0:CC - 1]), reads=[bS_[ri], bSP[ri]], writes=[bSP[ri]])
                    P.op("pool", lambda e: e.tensor_copy(out=SPv[ri][:, 0:1], in_=carry_s5[l][:, ri, gp:gp + 1]), reads=[bcarry_s5[l], bSP[ri]], writes=[bSP[ri]])
                for ri in range(2):
                    P.op("pool", lambda e: e.tensor_copy(out=carry_s5[l][:, ri, gp:gp + 1], in_=S_[ri][:, CC - 1:CC]), reads=[bS_[ri], bcarry_s5[l]], writes=[bcarry_s5[l]])
                for x in range(1, 9):
                    pr, pi_, npi = pwcv[:, x, 0, gp:gp + 1], pwcv[:, x, 1, gp:gp + 1], pwcv[:, x, 2, gp:gp + 1]
                    sb2 = (x % 2) * 2
                    t_re, t_im = stmp[:, sb2 * CC:(sb2 + 1) * CC], stmp[:, (sb2 + 1) * CC:(sb2 + 2) * CC]
                    TS(t_re, SPv[0], pr, [bSP[0], bpwc, bstmp[sb2]], [bstmp[sb2]])
                    TS(t_im, SPv[1], pr, [bSP[1], bpwc, bstmp[sb2 + 1]], [bstmp[sb2 + 1]])
                    STT(SX[:, ((x - 1) * 2) * CC:((x - 1) * 2 + 1) * CC], SPv[1], npi, t_re, [bSP[1], bpwc, bstmp[sb2], bSX], [bSX])
                    STT(SX[:, ((x - 1) * 2 + 1) * CC:((x - 1) * 2 + 2) * CC], SPv[0], pi_, t_im, [bSP[0], bpwc, bstmp[sb2 + 1], bSX], [bSX])
                for r in range(8):
                    for ri in range(2):
                        last = (r == 7 and ri == 1)
                        P.op("pe", lambda e: e.matmul(psum[4 + r // 4][:, (r % 4) * 128:(r % 4 + 1) * 128], CD[:, ri * 2048 + gp * 128:ri * 2048 + (gp + 1) * 128],
                                                      SX[:, (r * 2 + ri) * CC:(r * 2 + ri + 1) * CC], start=False, stop=(jj == 3 and last), skip_group_check=True),
                             reads=[bCD, bSX], writes=[bps[4 + r // 4]], inc=last)
            t2v = t2.rearrange("p (cc r) -> p cc r", r=8)
            for bk in range(2):
                P.op("dve", lambda e: e.scalar_tensor_tensor(out=t2v[:, :, 4 * bk:4 * bk + 4], in0=Ucv[:, :, 4 * bk:4 * bk + 4], scalar=col(l, "s5d", c),
                                                            in1=psum[4 + bk][:, :].rearrange("p (r cc) -> p cc r", r=4), op0=ALU.mult, op1=ALU.add),
                     reads=[bU, bcols, bps[4 + bk], bt2], writes=[bt2])
            for j in range(NSUB):
                P.op("act", lambda e: e.activation(out=G1[:, c * TP + j * ST:c * TP + (j + 1) * ST], in_=t2[:, hs(j)], func=AF.Gelu), reads=[bt2], writes=[bG1])
        P.barrier()
        sig, bsig = XS, Buf()
        gate_s, bgs = stmp, Buf()
        slw, bslw = load_slot([(w_glu[l, :, :], 4, 0, 512)])
        slg, bslg = load_slot([(w_in[l, :, O_S5G:O_S5G + 512], 8, 0, 512)])
        for co in range(4):
            for j in range(NSUB):
                p1, p2 = (0, 1) if (co * NSUB + j) % 2 == 0 else (2, 3)
                mm_group(psum[p1][:, :], bps[p1], lambda k: slw[:, k, co * 128:(co + 1) * 128], lambda k: G1[:, k * TP + j * ST:k * TP + (j + 1) * ST], 4, [bslw, bG1])
                mm_group(psum[p2][:, :], bps[p2], lambda k: slg[:, k, co * 128:(co + 1) * 128], lambda k: H[:, k, hs(j)], 8, [bslg, bH[j]])
                P.op("act", lambda e: e.activation(out=sig, in_=psum[p1][:, :], func=AF.Sigmoid), reads=[bps[p1]], writes=[bsig])
                P.op("act", lambda e: e.activation(out=gate_s, in_=psum[p2][:, :], func=AF.Silu), reads=[bps[p2]], writes=[bgs])
                P.op("dve", lambda e: e.tensor_tensor(out=t2[:, 0:ST], in0=G1[:, co * TP + j * ST:co * TP + (j + 1) * ST], in1=sig, op=ALU.mult), reads=[bG1, bsig, bt2], writes=[bt2])
                P.op("pool", lambda e: e.tensor_tensor(out=Y[:, co, hs(j)], in0=t2[:, 0:ST], in1=gate_s, op=ALU.mult), reads=[bt2, bgs], writes=[bY[co][j]])

    first_merge = [True]
    mg_t = sb("mg_t", [128, ST], F32)
    mt_t = sb("mt_t", [128, ST], F32)
    bmg, bmt = Buf(), Buf()

    def phase_merge(l, kb):
        g, bg, t, bt = mg_t[:, :], bmg, mt_t[:, :], bmt
        for hh in range(2):
            slg, bslg = load_slot([(w_in[l, :, O_MG + kb * D + hh * 512:O_MG + kb * D + (hh + 1) * 512], 8, 0, 512)])
            slb, bslb = load_slot([(w_br[l, kb, :, hh * 512:(hh + 1) * 512], 4, 0, 512)])
            for dt_ in range(4):
                d = hh * 4 + dt_
                for j in range(NSUB):
                    pg, pbk = (0, 1) if (dt_ * NSUB + j) % 2 == 0 else (2, 3)
                    mm_group(psum[pg][:, :], bps[pg], lambda k: slg[:, k, dt_ * 128:(dt_ + 1) * 128], lambda k: H[:, k, hs(j)], 8, [bslg, bH[j]])
                    mm_group(psum[pbk][:, :], bps[pbk], lambda k: slb[:, k, dt_ * 128:(dt_ + 1) * 128], lambda k: Y[:, k, hs(j)], 4,
                             [bslb] + [bY[k][j] for k in range(4)])
                    P.op("act", lambda e: e.activation(out=g, in_=psum[pg][:, :], func=AF.Sigmoid, bias=col(l, "mb", kb * 8 + d)),
                         reads=[bps[pg], bcols], writes=[bg])
                    if first_merge[0]:
                        P.op("dve", lambda e: e.tensor_tensor(out=ACC[:, d, hs(j)], in0=psum[pbk][:, :], in1=g, op=ALU.mult),
                             reads=[bps[pbk], bg], writes=[bACC[d][j]])
                    else:
                        P.op("dve", lambda e: e.tensor_tensor(out=t, in0=psum[pbk][:, :], in1=g, op=ALU.mult),
                             reads=[bps[pbk], bg], writes=[bt])
                        P.op("pool", lambda e: e.tensor_tensor(out=ACC[:, d, hs(j)], in0=ACC[:, d, hs(j)], in1=t, op=ALU.add),
                             reads=[bt, bACC[d][j]], writes=[bACC[d][j]])
        first_merge[0] = False

    def phase_out(l):
        for j in range(NSUB):
            for k in range(8):
                P.op("act", lambda e: e.activation(out=H[:, k, hs(j)], in_=ACC[:, k, hs(j)], func=AF.Copy),
                     reads=[bACC[k][j]], writes=[bH[j]])
        for hh in range(2):
            sl, bsl = load_slot([(w_out[l, :, hh * 512:(hh + 1) * 512], 8, 0, 512)])
            for dt_ in range(4):
                d = hh * 4 + dt_
                for j in range(NSUB):
                    pi = 2 + (dt_ * NSUB + j) % 4
                    mm_group(psum[pi][:, :], bps[pi], lambda k: sl[:, k, dt_ * 128:(dt_ + 1) * 128], lambda k: H[:, k, hs(j)], 8, [bsl, bH[j]])
                    P.op("dve", lambda e: e.tensor_tensor(out=X[:, d, hs(j)], in0=psum[pi][:, :], in1=X[:, d, hs(j)], op=ALU.add),
                         reads=[bps[pi], bX[d][j]], writes=[bX[d][j]])

    fo_t = sb("fo_t", [128, 2, ST], F32)
    bfo = [Buf(), Buf()]

    def phase_final(p):
        n = 0
        for j in range(NSUB):
            rms_stats(lambda k: X[:, k, hs(j)], lambda k: [bX[k][j]], 8, 1.0 / D, rstd_t, brstd_t)
            for k in range(8):
                i = n % 2
                n += 1
                P.op("dve", lambda e: e.scalar_tensor_tensor(
                    out=fo_t[:, i, :], in0=X[:, k, hs(j)], scalar=col(0, "fw", k),
                    in1=rstd_t[:, :], op0=ALU.mult, op1=ALU.mult),
                    reads=[bX[k][j], brstd_t, bcols], writes=[bfo[i]])
                P.dma("sp", sy, yT[k * 128:(k + 1) * 128, p * TP + j * ST:p * TP + (j + 1) * ST], fo_t[:, i, :], reads=[bfo[i]])

    phases = {"a": phase_a, "b": phase_b, "c": phase_c, "d": phase_d}
    for p in range(NPASS):
        for k in range(8):
            for j in range(NSUB):
                P.dma("sp", sx, X[:, k, hs(j)], xT[k * 128:(k + 1) * 128, p * TP + j * ST:p * TP + (j + 1) * ST], writes=[bX[k][j]])
        for l in range(nlayers):
            phase_norm(l)
            first_merge[0] = True
            for kb, name in enumerate("abcd"):
                if name not in branches:
                    continue
                phases[name](l)
                phase_merge(l, kb)
            phase_out(l)
        phase_final(p)
    P._wait("sp", sy, P.cnt[sy])
    print("program: nins=%d nwaits=%d" % (P.nins, P.nwaits), {k: v for k, v in P.cnt.items() if k in P.eng})
    return nc


_NC_CACHE = {}


def run(inputs, branches=("a", "b", "c", "d"), nlayers=DEPTH, trace=False):
    key = (tuple(branches), nlayers)
    if key not in _NC_CACHE:
        _NC_CACHE[key] = build_nc(branches, nlayers)
    nc = _NC_CACHE[key]
    inp = {k: np.asarray(v) for k, v in inputs.items()}
    x = inp["x"].astype(np.float32)
    s5 = [host_s5(inp, l) for l in range(DEPTH)]
    shared = {
        "w_in": np.ascontiguousarray(inp["w_in"], dtype=np.float32),
        "w_branch": np.ascontiguousarray(inp["w_branch"], dtype=np.float32),
        "w_out": np.ascontiguousarray(inp["w_out"], dtype=np.float32),
        "w_glu": np.ascontiguousarray(inp["s5_w_glu"], dtype=np.float32),
        "cols": np.stack([host_cols(inp, l) for l in range(DEPTH)], 0),
        "consts": host_consts(),
        "rows": np.stack([host_rows(inp, l) for l in range(DEPTH)], 0),
        "sguw": np.ascontiguousarray(inp["sgu_w"].transpose(0, 3, 1, 2), dtype=np.float32),
        "sgub": np.ascontiguousarray(np.repeat(inp["sgu_b"].reshape(DEPTH, 4, 2, 1, 128), 64, axis=3).transpose(0, 2, 3, 1, 4).reshape(DEPTH, 128, 512), dtype=np.float32),
        "s5p": np.stack([s[0] for s in s5], 0),
        "s5q": np.stack([s[1] for s in s5], 0),
        "s5c": np.stack([s[2] for s in s5], 0),
    }
    in_maps = []
    for b in range(8):
        m = dict(shared)
        m["xT"] = np.ascontiguousarray(x[b].T)
        in_maps.append(m)
    res = run_bass_kernel_spmd(nc, in_maps, core_ids=list(range(8)), trace=trace)
    out = np.stack([np.ascontiguousarray(res.results[b]["yT"].T) for b in range(8)], 0).astype(np.float32)
    return out, res


def kernel(**inputs):
    out, _ = run(inputs)
    return out
```

```python
import os
import numpy as np
import concourse.bass as bass
import concourse.mybir as mybir
from concourse.bass_utils import run_bass_kernel_spmd

F32 = mybir.dt.float32
BF16 = mybir.dt.bfloat16
ALU = mybir.AluOpType
AF = mybir.ActivationFunctionType

D = 1024
SEQ = 2048
DEPTH = 2
W = 512
IN_DIM = 10248
TP = 1024
NPASS = SEQ // TP
ST = 512
NSUB = TP // ST
EPS = 1e-6

O_S5U, O_S5G = 0, 512
O_SGU, O_SGV, O_SGG = 1024, 1536, 2048
O_M2Z, O_M2X, O_M2DT = 2560, 3072, 4096
O_SCB, O_SCC, O_SCH, O_SCG = 4104, 4616, 5128, 5640
O_MG = 6152


class Buf:
    __slots__ = ("w", "r")

    def __init__(self):
        self.w = None
        self.r = {}


class Prog:
    def __init__(self, nc):
        self.nc = nc
        self.eng = {"pe": nc.tensor, "act": nc.scalar, "dve": nc.vector, "pool": nc.gpsimd, "sp": nc.sync}
        self.sem = {}
        self.cnt = {}
        self.seen = {e: {} for e in self.eng}
        self.pend = {e: [] for e in self.eng}
        for e in self.eng:
            self.sem[e] = nc.alloc_semaphore("s_" + e)
            self.cnt[e] = 0
        self.nwaits = 0
        self.nins = 0

    def new_sem(self, name):
        self.sem[name] = self.nc.alloc_semaphore(name)
        self.cnt[name] = 0
        return name

    def _wait(self, e, key, val):
        if key not in self.eng:
            val = self.cnt[key]
        if self.seen[e].get(key, 0) >= val:
            return
        self.seen[e][key] = val
        self.eng[e].wait_ge(self.sem[key], val)
        self.nwaits += 1

    def _deps(self, e, reads, writes):
        deps = {}
        for b in reads:
            if b.w is not None:
                k, v = b.w
                if deps.get(k, 0) < v:
                    deps[k] = v
        for b in writes:
            if b.w is not None:
                k, v = b.w
                if deps.get(k, 0) < v:
                    deps[k] = v
            for k, v in b.r.items():
                if deps.get(k, 0) < v:
                    deps[k] = v
        for k, v in deps.items():
            if k == e and (e == "pe" or v > self.cnt[e]):
                continue
            self._wait(e, k, v)

    def _commit(self, k, v, reads, writes):
        for b in reads:
            b.r[k] = v
        for b in writes:
            b.w = (k, v)
            b.r = {}

    def op(self, e, fn, reads=(), writes=(), inc=True):
        self._deps(e, reads, writes)
        ins = fn(self.eng[e])
        self.nins += 1
        if inc:
            self.cnt[e] += 1
            ins.then_inc(self.sem[e], 1)
            self._commit(e, self.cnt[e], reads, writes)
        else:
            v = self.cnt[e] + 1
            self._commit(e, v, reads, writes)

    def barrier(self):
        for e in self.eng:
            for k, v in self.cnt.items():
                if k != e and v > 0:
                    self._wait(e, k, v)

    def dma(self, q, semkey, out, in_, reads=(), writes=(), **kw):
        self._deps(q, reads, writes)
        ins = self.eng[q].dma_start(out=out, in_=in_, **kw)
        self.cnt[semkey] += 16
        ins.then_inc(self.sem[semkey], 16)
        self.nins += 1
        self._commit(semkey, self.cnt[semkey], reads, writes)


def col_layout():
    off = {}
    n = 0

    def add(name, w):
        nonlocal n
        off[name] = n
        n += w
    add("nw", 8)
    add("mb", 32)
    add("scw", 12)
    add("fw", 8)
    add("m2cw", 32)
    add("m2cb", 8)
    add("m2d", 4)
    add("m2nw", 4)
    add("s5d", 4)
    add("mkB", 8)
    add("mkC", 2)
    add("mkCn", 2)
    return off, n


COLOFF, NCOL = col_layout()


def host_cols(inp, l):
    c = np.zeros((128, NCOL), np.float32)
    c[:, COLOFF["nw"]:COLOFF["nw"] + 8] = inp["norm_w"][l].reshape(8, 128).T
    c[:, COLOFF["mb"]:COLOFF["mb"] + 32] = inp["merge_b"][l].reshape(32, 128).T
    c[:, COLOFF["scw"]:COLOFF["scw"] + 12] = inp["sc_conv_w"][l].reshape(12, 128).T
    c[:, COLOFF["fw"]:COLOFF["fw"] + 8] = inp["final_norm_w"].reshape(8, 128).T
    c[:, COLOFF["m2cw"]:COLOFF["m2cw"] + 32] = inp["m2_conv_w"][l].reshape(32, 128).T
    c[:, COLOFF["m2cb"]:COLOFF["m2cb"] + 8] = inp["m2_conv_b"][l].reshape(8, 128).T
    c[:, COLOFF["m2d"]:COLOFF["m2d"] + 4] = np.repeat(inp["m2_d"][l], 64).reshape(4, 128).T
    c[:, COLOFF["m2nw"]:COLOFF["m2nw"] + 4] = inp["m2_norm_w"][l].reshape(4, 128).T
    c[:, COLOFF["s5d"]:COLOFF["s5d"] + 4] = inp["s5_d"][l].reshape(4, 128).T
    gl = np.arange(128) // 16
    for jj in range(4):
        for g2 in range(2):
            c[:, COLOFF["mkB"] + jj * 2 + g2] = (gl == 2 * jj + g2)
    g2p = np.arange(128) // 64
    for g2 in range(2):
        c[:, COLOFF["mkC"] + g2] = (g2p == g2)
        c[:, COLOFF["mkCn"] + g2] = -1.0 * (g2p == g2)
    return c


def host_consts():
    i = np.arange(128)
    k = np.zeros((128, 5, 128), np.float32)
    k[:, 0] = np.eye(128)
    k[:, 1] = (i[:, None] <= i[None, :])
    k[:, 2] = (i[:, None] > i[None, :])
    k[:, 3] = 1.0
    k[:, 4] = (i[:, None] // 16 == i[None, :] // 16)
    return k


def host_rows(inp, l):
    r = np.zeros((128, 1040), np.float32)
    r[:, 0:512] = inp["sgu_ln_w"][l][None, :]
    r[:, 512:1024] = inp["sgu_ln_b"][l][None, :]
    r[:, 1024:1032] = inp["m2_dt_bias"][l][None, :]
    r[:, 1032:1040] = inp["m2_a_log"][l][None, :]
    return r


def host_s5(inp, l):
    G, N, Pq = 32, 64, 16
    def L2(a_gn):
        a = a_gn.reshape(4, 8, N)
        a = np.repeat(a[:, :, None, :], 16, axis=2)
        return a.transpose(1, 2, 0, 3).reshape(128, 256)
    def L2b(b_gnq):
        a = b_gnq.reshape(4, 8, N, Pq)
        return a.transpose(1, 3, 0, 2).reshape(128, 256)
    p5 = np.stack([L2(inp["s5_lambda_re"][l]), L2(inp["s5_lambda_im"][l]),
                   L2(np.repeat(inp["s5_log_step"][l][:, None], N, 1)),
                   L2b(inp["s5_b_re"][l]), L2b(inp["s5_b_im"][l])], 1)
    def PL(a_gn):
        return a_gn.reshape(16, 2, N).transpose(1, 2, 0).reshape(128, 16)
    pq = np.stack([PL(inp["s5_lambda_re"][l]), PL(inp["s5_lambda_im"][l]),
                   PL(np.repeat(inp["s5_log_step"][l][:, None], N, 1))], 1)
    def PLc(c_gpn):
        return c_gpn.reshape(16, 2, Pq, N).transpose(1, 3, 2, 0).reshape(128, 256)
    def PLb(b_gnq):
        return b_gnq.reshape(16, 2, N, Pq).transpose(1, 2, 3, 0).reshape(128, 256)
    pc = np.stack([PLc(inp["s5_c_re"][l]), PLc(inp["s5_c_im"][l]),
                   PLb(inp["s5_b_re"][l]), PLb(inp["s5_b_im"][l])], 1)
    return p5.astype(np.float32), pq.astype(np.float32), pc.astype(np.float32)


def build_nc(branches=("a", "b", "c", "d"), nlayers=DEPTH):
    nc = bass.Bass("TRN2", target_bir_lowering=False)
    xT = nc.dram_tensor("xT", [D, SEQ], F32, kind="ExternalInput").ap()
    w_in = nc.dram_tensor("w_in", [DEPTH, D, IN_DIM], F32, kind="ExternalInput").ap()
    w_br = nc.dram_tensor("w_branch", [DEPTH, 4, W, D], F32, kind="ExternalInput").ap()
    w_out = nc.dram_tensor("w_out", [DEPTH, D, D], F32, kind="ExternalInput").ap()
    w_glu = nc.dram_tensor("w_glu", [DEPTH, W, W], F32, kind="ExternalInput").ap()
    cols_d = nc.dram_tensor("cols", [DEPTH, 128, NCOL], F32, kind="ExternalInput").ap()
    consts_d = nc.dram_tensor("consts", [128, 5, 128], F32, kind="ExternalInput").ap()
    rows_d = nc.dram_tensor("rows", [DEPTH, 128, 1040], F32, kind="ExternalInput").ap()
    sguw_d = nc.dram_tensor("sguw", [DEPTH, 128, 8, 128], F32, kind="ExternalInput").ap()
    sgub_d = nc.dram_tensor("sgub", [DEPTH, 128, 512], F32, kind="ExternalInput").ap()
    s5p_d = nc.dram_tensor("s5p", [DEPTH, 128, 5, 256], F32, kind="ExternalInput").ap()
    s5q_d = nc.dram_tensor("s5q", [DEPTH, 128, 3, 16], F32, kind="ExternalInput").ap()
    s5c_d = nc.dram_tensor("s5c", [DEPTH, 128, 4, 256], F32, kind="ExternalInput").ap()
    yT = nc.dram_tensor("yT", [D, SEQ], F32, kind="ExternalOutput").ap()

    P = Prog(nc)
    sb = nc.alloc_sbuf_tensor
    X = sb("X", [128, 8, TP], F32)
    H = sb("H", [128, 8, TP], BF16)
    ACC = sb("ACC", [128, 8, TP], F32)
    Y = sb("Y", [128, 4, TP], BF16)
    cols = sb("colsb", [128, DEPTH, NCOL], F32)
    consts = sb("constsb", [128, 5, 128], F32)
    identb = sb("identb", [128, 128], BF16)
    mask01b = sb("mask01b", [128, 128], BF16)
    bX = [[Buf() for _ in range(NSUB)] for _ in range(8)]
    bH = [Buf() for _ in range(NSUB)]
    bACC = [[Buf() for _ in range(NSUB)] for _ in range(8)]
    bY = [[Buf() for _ in range(NSUB)] for _ in range(4)]
    bcols, bconsts = Buf(), Buf()
    ident, triu, ltstrict, ones = consts[:, 0, :], consts[:, 1, :], consts[:, 2, :], consts[:, 3, :]
    blockmask = consts[:, 4, :]
    zerob = sb("zerob", [128, 128], BF16)
    bzerob = Buf()
    bones = bconsts

    NS = 4
    slots = [sb("slot%d" % i, [128, 8 * 512], BF16) for i in range(NS)]
    bslot = [Buf() for _ in range(NS)]
    sslot = [P.new_sem("dslot%d" % i) for i in range(NS)]
    slot_rr = [0]

    psbig = nc.alloc_psum_tensor("psbig", [128, 7 * 512], F32)
    psum = [psbig[:, i * 512:(i + 1) * 512] for i in range(7)]
    psT = nc.alloc_psum_tensor("psT", [128, 512], F32)
    bps = [Buf() for _ in range(7)]
    bpsT = Buf()
    psum.append(psT[:, :])
    bps.append(bpsT)

    SCRF = 16600
    scrF = sb("scrF", [128, SCRF], F32)
    scr_pos = [0]

    def scratch_reset(pos=0):
        P.barrier()
        scr_pos[0] = pos

    def falloc(n, parts=128):
        a = scrF[0:parts, scr_pos[0]:scr_pos[0] + n]
        scr_pos[0] += n
        assert scr_pos[0] <= SCRF, scr_pos
        return a, Buf()

    def balloc(n):
        m = (n + 1) // 2
        a = scrF[:, scr_pos[0]:scr_pos[0] + m].bitcast(BF16)[:, 0:n]
        scr_pos[0] += m
        assert scr_pos[0] <= SCRF, scr_pos
        return a, Buf()

    sx = P.new_sem("dx")
    sy = P.new_sem("dy")
    sc = P.new_sem("dc")
    sm = P.new_sem("dm")
    sw8 = P.new_sem("dw8")

    P.dma("sp", sc, cols[:, :, :], cols_d.rearrange("l p n -> p l n"), writes=[bcols])
    P.dma("sp", sc, consts[:, :, :], consts_d, writes=[bconsts])
    bidb = Buf()
    P.op("dve", lambda e: e.tensor_copy(out=identb[:, :], in_=ident), reads=[bconsts], writes=[bidb])
    P.op("dve", lambda e: e.tensor_copy(out=mask01b[:, :], in_=triu), reads=[bconsts], writes=[bidb])
    epsb = sb("epsb", [128, 1], F32)
    bepsb = Buf()
    P.op("pool", lambda e: e.memset(epsb[:, :], EPS), writes=[bepsb])
    P.op("pool", lambda e: e.memset(zerob[:, :], 0.0), writes=[bzerob])

    def col(l, name, j=0):
        o = COLOFF[name] + j
        return cols[:, l, o:o + 1]

    def load_slot(pieces):
        i = slot_rr[0] % NS
        slot_rr[0] += 1
        s = slots[i]
        for src, kt, off, n in pieces:
            dst = s[:, :].rearrange("p (k n) -> p k n", k=8)[:, 0:kt, off:off + n]
            P.dma("pool", sslot[i], dst, src.rearrange("(k p) n -> p k n", p=128), writes=[bslot[i]])
        return s[:, :].rearrange("p (k n) -> p k n", k=8), bslot[i]

    def mm_group(out_ap, bout, lhs_fn, rhs_fn, nk, reads):
        for k in range(nk):
            P.op("pe", lambda e: e.matmul(out_ap, lhs_fn(k), rhs_fn(k), start=(k == 0), stop=(k == nk - 1)),
                 reads=reads, writes=[bout], inc=(k == nk - 1))

    def hs(j):
        return slice(j * ST, (j + 1) * ST)

    carry_sc = [[sb("csc%d_%d" % (l, c), [128, 2], F32) for c in range(4)] for l in range(DEPTH)]
    bcarry_sc = [[Buf() for c in range(4)] for l in range(DEPTH)]
    carry_m2 = [[sb("cm2%d_%d" % (l, c), [128, 3], F32) for c in range(8)] for l in range(DEPTH)]
    bcarry_m2 = [[Buf() for c in range(8)] for l in range(DEPTH)]
    ST_m2 = [sb("stm2_%d" % l, [128, 512], F32) for l in range(DEPTH)]
    bST_m2 = [Buf() for l in range(DEPTH)]
    carry_s5 = [sb("cs5_%d" % l, [128, 2, 16], F32) for l in range(DEPTH)]
    bcarry_s5 = [Buf() for l in range(DEPTH)]
    for l in range(DEPTH):
        for c in range(4):
            P.op("pool", lambda e: e.memset(carry_sc[l][c][:, :], 0.0), writes=[bcarry_sc[l][c]])
        for c in range(8):
            P.op("pool", lambda e: e.memset(carry_m2[l][c][:, :], 0.0), writes=[bcarry_m2[l][c]])
        P.op("pool", lambda e: e.memset(ST_m2[l][:, :], 0.0), writes=[bST_m2[l]])
        P.op("pool", lambda e: e.memset(carry_s5[l][:, :, :], 0.0), writes=[bcarry_s5[l]])

    def rms_stats(src_fn, breads, nk, scale, rstd, brstd, n=ST):
        sq, bsq = falloc_sq[0]
        for k in range(nk):
            P.op("act", lambda e: e.activation(out=sq[:, 0:n], in_=src_fn(k), func=AF.Square), reads=breads(k), writes=[bsq])
            P.op("pe", lambda e: e.matmul(psum[6][:, 0:n], ones, sq[:, 0:n], start=(k == 0), stop=(k == nk - 1)),
                 reads=[bsq, bones], writes=[bps[6]])
        P.op("act", lambda e: e.activation(out=rstd[:, 0:n], in_=psum[6][:, 0:n], func=AF.Sqrt, bias=epsb[:, 0:1], scale=scale),
             reads=[bps[6], bepsb], writes=[brstd])
        P.op("dve", lambda e: e.reciprocal(out=rstd[:, 0:n], in_=rstd[:, 0:n]), reads=[brstd], writes=[brstd])

    sq_t = sb("sq_t", [128, ST], F32)
    falloc_sq = [(sq_t, Buf())]
    rstd_t = sb("rstd_t", [128, ST], F32)
    brstd_t = Buf()

    def phase_norm(l):
        for j in range(NSUB):
            rms_stats(lambda k: X[:, k, hs(j)], lambda k: [bX[k][j]], 8, 1.0 / D, rstd_t, brstd_t)
            for k in range(8):
                P.op("dve", lambda e: e.scalar_tensor_tensor(
                    out=H[:, k, hs(j)], in0=X[:, k, hs(j)], scalar=col(l, "nw", k),
                    in1=rstd_t[:, :], op0=ALU.mult, op1=ALU.mult),
                    reads=[bX[k][j], brstd_t, bcols], writes=[bH[j]])

    def phase_d(l):
        scratch_reset()
        pbufs = [falloc(2 + TP) for _ in range(2)]
        hsbs = [falloc(ST) for _ in range(2)]
        qs = [falloc(ST) for _ in range(2)]
        yvs = [falloc(ST) for _ in range(2)]
        sgs = [falloc(ST) for _ in range(2)]
        for c in range(4):
            sl, bsl = load_slot([(w_in[l, :, o + c * 128:o + (c + 1) * 128], 8, i * 128, 128)
                                 for i, o in enumerate((O_SCB, O_SCC, O_SCH, O_SCG))])
            pbuf, bp = pbufs[c % 2]
            P.op("pool", lambda e: e.tensor_copy(out=pbuf[:, 0:2], in_=carry_sc[l][c][:, :]), reads=[bcarry_sc[l][c]], writes=[bp])
            for j in range(NSUB):
                pb = 3 * (j % 2)
                pgate = 6 + (j % 2)
                (hsb, bhsb), (q, bq), (yv, byv), (sg, bsg) = hsbs[j % 2], qs[j % 2], yvs[j % 2], sgs[j % 2]
                for i in range(3):
                    mm_group(psum[pb + i][:, :], bps[pb + i], lambda k: sl[:, k, i * 128:(i + 1) * 128], lambda k: H[:, k, hs(j)], 8, [bsl, bH[j]])
                mm_group(psum[pgate][:, :], bps[pgate], lambda k: sl[:, k, 384:512], lambda k: H[:, k, hs(j)], 8, [bsl, bH[j]])
                P.op("act", lambda e: e.activation(out=hsb, in_=psum[pb + 2][:, :], func=AF.Copy), reads=[bps[pb + 2]], writes=[bhsb])
                P.op("dve", lambda e: e.tensor_tensor(out=pbuf[:, 2 + j * ST:2 + (j + 1) * ST], in0=psum[pb + 1][:, :], in1=hsb, op=ALU.mult),
                     reads=[bps[pb + 1], bhsb], writes=[bp])
                P.op("dve", lambda e: e.tensor_scalar(out=q, in0=pbuf[:, 2 + j * ST:2 + (j + 1) * ST], scalar1=col(l, "scw", 8 + c), scalar2=None, op0=ALU.mult),
                     reads=[bp, bcols], writes=[bq])
                P.op("dve", lambda e: e.scalar_tensor_tensor(out=q, in0=pbuf[:, 1 + j * ST:1 + (j + 1) * ST], scalar=col(l, "scw", 4 + c), in1=q, op0=ALU.mult, op1=ALU.add),
                     reads=[bp, bq, bcols], writes=[bq])
                P.op("dve", lambda e: e.scalar_tensor_tensor(out=q, in0=pbuf[:, j * ST:(j + 1) * ST], scalar=col(l, "scw", c), in1=q, op0=ALU.mult, op1=ALU.add),
                     reads=[bp, bq, bcols], writes=[bq])
                P.op("dve", lambda e: e.tensor_tensor(out=yv, in0=psum[pb][:, :], in1=q, op=ALU.mult), reads=[bps[pb], bq], writes=[byv])
                P.op("act", lambda e: e.activation(out=sg, in_=psum[pgate][:, :], func=AF.Silu), reads=[bps[pgate]], writes=[bsg])
                P.op("pool", lambda e: e.tensor_tensor(out=Y[:, c, hs(j)], in0=yv, in1=sg, op=ALU.mult),
                     reads=[byv, bsg], writes=[bY[c][j]])
            P.op("pool", lambda e: e.tensor_copy(out=carry_sc[l][c][:, :], in_=pbuf[:, TP:TP + 2]), reads=[bp], writes=[bcarry_sc[l][c]])

    def phase_b(l):
        scratch_reset()
        lnw, blnw = falloc(512)
        lnb, blnb = falloc(512)
        wraw, bwraw = falloc(1024)
        bsrow, bbsrow = falloc(512)
        v32s = [falloc(512) for _ in range(2)]
        vns = [falloc(512) for _ in range(2)]
        st6s = [falloc(6) for _ in range(2)]
        mvs = [falloc(2) for _ in range(2)]
        rss = [falloc(1) for _ in range(2)]
        gus = [falloc(ST) for _ in range(2)]
        sgs = [falloc(ST) for _ in range(2)]
        t1s = [falloc(ST) for _ in range(2)]
        wmT, bwmT = balloc(1024)
        VN, bVN = balloc(8 * 512)
        bVNq = [Buf() for _ in range(8)]
        P.dma("sp", sm, lnw, rows_d[l, :, 0:512], writes=[blnw])
        P.dma("sp", sm, lnb, rows_d[l, :, 512:1024], writes=[blnb])
        P.dma("sp", sm, wraw, sguw_d[l].rearrange("s h t -> s (h t)"), writes=[bwraw])
        P.dma("sp", sm, bsrow, sgub_d[l], writes=[bbsrow])
        bsv = bsrow.rearrange("p (c t) -> p c t", c=4)
        P.op("dve", lambda e: e.tensor_tensor(out=wmT.rearrange("p (h t) -> p h t", h=8), in0=wraw.rearrange("p (h t) -> p h t", h=8),
                                             in1=triu.unsqueeze(1).broadcast_to([128, 8, 128]), op=ALU.mult),
             reads=[bwraw, bconsts], writes=[bwmT])
        slv, bslv = load_slot([(w_in[l, :, O_SGV:O_SGV + 512], 8, 0, 512)])
        for qc in range(TP // 128):
            pi = qc % 2
            (v32, bv32), (vn, bvn), (st6, bst6), (mv, bmv), (rs, brs) = v32s[pi], vns[pi], st6s[pi], mvs[pi], rss[pi]
            mm_group(psum[pi][:, :], bps[pi], lambda k: H[:, k, qc * 128:(qc + 1) * 128], lambda k: slv[:, k, 0:512], 8, [bslv, bH[qc // 4]])
            P.op("act", lambda e: e.activation(out=v32, in_=psum[pi][:, :], func=AF.Gelu), reads=[bps[pi]], writes=[bv32])
            P.op("dve", lambda e: e.bn_stats(out=st6, in_=v32), reads=[bv32], writes=[bst6])
            P.op("dve", lambda e: e.bn_aggr(out=mv, in_=st6), reads=[bst6], writes=[bmv])
            P.op("act", lambda e: e.activation(out=rs, in_=mv[:, 1:2], func=AF.Sqrt, bias=epsb[:, 0:1], scale=1.0), reads=[bmv, bepsb], writes=[brs])
            P.op("dve", lambda e: e.reciprocal(out=rs, in_=rs), reads=[brs], writes=[brs])
            P.op("dve", lambda e: e.tensor_scalar(out=vn, in0=v32, scalar1=mv[:, 0:1], scalar2=rs, op0=ALU.subtract, op1=ALU.mult),
                 reads=[bv32, bmv, brs], writes=[bvn])
            P.op("pool", lambda e: e.tensor_tensor(out=vn, in0=vn, in1=lnw, op=ALU.mult), reads=[bvn, blnw], writes=[bvn])
            P.op("pool", lambda e: e.tensor_tensor(out=VN[:, qc * 512:(qc + 1) * 512], in0=vn, in1=lnb, op=ALU.add), reads=[bvn, blnb], writes=[bVNq[qc]])
        for c in range(4):
            sl, bsl = load_slot([(w_in[l, :, O_SGU + c * 128:O_SGU + (c + 1) * 128], 8, 0, 128),
                                 (w_in[l, :, O_SGG + c * 128:O_SGG + (c + 1) * 128], 8, 128, 128)])
            for j in range(NSUB):
                pu, pg, pss = (2, 3, 4) if j % 2 == 0 else (6, 7, 5)
                (gu, bgu), (sg, bsg), (t1, bt1) = gus[j % 2], sgs[j % 2], t1s[j % 2]
                mm_group(psum[pu][:, :], bps[pu], lambda k: sl[:, k, 0:128], lambda k: H[:, k, hs(j)], 8, [bsl, bH[j]])
                mm_group(psum[pg][:, :], bps[pg], lambda k: sl[:, k, 128:256], lambda k: H[:, k, hs(j)], 8, [bsl, bH[j]])
                P.op("act", lambda e: e.activation(out=gu, in_=psum[pu][:, :], func=AF.Gelu), reads=[bps[pu]], writes=[bgu])
                P.op("act", lambda e: e.activation(out=sg, in_=psum[pg][:, :], func=AF.Silu), reads=[bps[pg]], writes=[bsg])
                for qq in range(4):
                    qc = j * 4 + qq
                    for h2 in range(2):
                        h = 2 * c + h2
                        o = psum[pss][h2 * 64:(h2 + 1) * 64, qq * 128:(qq + 1) * 128]
                        P.op("pe", lambda e: e.matmul(o, VN[:, qc * 512 + h * 64:qc * 512 + (h + 1) * 64], wmT[:, h * 128:(h + 1) * 128], start=True, stop=True),
                             reads=[bVNq[qc], bwmT], writes=[bps[pss]], inc=True)
                P.op("dve", lambda e: e.tensor_tensor(out=t1.rearrange("p (q t) -> p q t", q=4), in0=psum[pss][:, :].rearrange("p (q t) -> p q t", q=4),
                                                     in1=bsv[:, c, :].unsqueeze(1).broadcast_to([128, 4, 128]), op=ALU.add), reads=[bps[pss], bbsrow], writes=[bt1])
                P.op("dve", lambda e: e.tensor_tensor(out=t1, in0=t1, in1=gu, op=ALU.mult), reads=[bt1, bgu], writes=[bt1])
                P.op("pool", lambda e: e.tensor_tensor(out=Y[:, c, hs(j)], in0=t1, in1=sg, op=ALU.mult), reads=[bt1, bsg], writes=[bY[c][j]])

    def phase_c(l):
        scratch_reset()
        NQ = TP // 128
        dtb, bdtb = falloc(8)
        alog, balog = falloc(8)
        a_t, ba_t = falloc(8)
        dt, bdt = falloc(64)
        da, bda = falloc(64)
        csc, bcsc = falloc(64)
        dec, bdec = falloc(64)
        eA, beA = falloc(64)
        dtdec, bdtdec = falloc(64)
        cbufs = [falloc(3 + TP) for _ in range(2)]
        qvs = [falloc(ST) for _ in range(2)]
        ltls = [falloc(128) for _ in range(2)]
        reps = [falloc(128) for _ in range(2)]
        erows = [falloc(128) for _ in range(2)]
        t1s = [falloc(128) for _ in range(2)]
        yqs = [falloc(512) for _ in range(2)]
        rstds = [falloc(128) for _ in range(2)]
        xTb, bxT = balloc(4 * TP)
        BTb, bBT = balloc(2 * TP)
        CTb, bCT = balloc(2 * TP)
        ZS, bZS = balloc(4 * TP)
        xdts = [balloc(512) for _ in range(2)]
        xdds = [balloc(512) for _ in range(2)]
        Btoks = [balloc(256) for _ in range(2)]
        scms = [balloc(256) for _ in range(2)]
        LTms = [balloc(128) for _ in range(2)]
        MTs = [balloc(128) for _ in range(2)]
        STbs = [balloc(512) for _ in range(2)]
        STb, bSTb = STbs[0]
        P.dma("sp", sm, dtb, rows_d[l, :, 1024:1032], writes=[bdtb])
        P.dma("sp", sm, alog, rows_d[l, :, 1032:1040], writes=[balog])
        slw8, bwdt = load_slot([(w_in[l, :, O_M2DT - 120:O_M2DT + 8], 8, 0, 128)])
        P.op("act", lambda e: e.activation(out=a_t, in_=alog, func=AF.Exp), reads=[balog], writes=[ba_t])
        P.op("dve", lambda e: e.tensor_scalar(out=a_t, in0=a_t, scalar1=-1.0, scalar2=None, op0=ALU.mult), reads=[ba_t], writes=[ba_t])
        P.op("act", lambda e: e.activation(out=STb, in_=ST_m2[l][:, :], func=AF.Copy), reads=[bST_m2[l]], writes=[bSTb])
        wdtv = slw8[:, :, 120:128]
        for qc in range(NQ):
            mm_group(psum[0][:, qc * 8:(qc + 1) * 8], bps[0], lambda k: H[:, k, qc * 128:(qc + 1) * 128], lambda k: wdtv[:, k, :], 8, [bwdt, bH[qc // 4]])
        P.op("dve", lambda e: e.tensor_tensor(out=dt.rearrange("p (q h) -> p q h", h=8), in0=psum[0][:, 0:64].rearrange("p (q h) -> p q h", h=8),
                                             in1=dtb.unsqueeze(1).broadcast_to([128, NQ, 8]), op=ALU.add), reads=[bps[0], bdtb], writes=[bdt])
        P.op("act", lambda e: e.activation(out=dt, in_=dt, func=AF.Exp), reads=[bdt], writes=[bdt])
        P.op("act", lambda e: e.activation(out=dt, in_=dt, func=AF.Ln, bias=1.0), reads=[bdt], writes=[bdt])
        P.op("dve", lambda e: e.tensor_tensor(out=da.rearrange("p (q h) -> p q h", h=8), in0=dt.rearrange("p (q h) -> p q h", h=8),
                                             in1=a_t.unsqueeze(1).broadcast_to([128, NQ, 8]), op=ALU.mult), reads=[bdt, ba_t], writes=[bda])
        P.op("pe", lambda e: e.matmul(psum[0][:, 64:128], triu, da, start=True, stop=True), reads=[bda, bconsts], writes=[bps[0]])
        P.op("pe", lambda e: e.matmul(psum[0][:, 128:192], ones, da, start=True, stop=True), reads=[bda, bconsts], writes=[bps[0]])
        P.op("act", lambda e: e.activation(out=csc, in_=psum[0][:, 64:128], func=AF.Copy), reads=[bps[0]], writes=[bcsc])
        P.op("dve", lambda e: e.tensor_tensor(out=dec, in0=psum[0][:, 128:192], in1=csc, op=ALU.subtract), reads=[bps[0], bcsc], writes=[bdec])
        P.op("act", lambda e: e.activation(out=dec, in_=dec, func=AF.Exp), reads=[bdec], writes=[bdec])
        P.op("act", lambda e: e.activation(out=eA, in_=psum[0][:, 128:192], func=AF.Exp), reads=[bps[0]], writes=[beA])
        P.op("dve", lambda e: e.tensor_tensor(out=dtdec, in0=dt, in1=dec, op=ALU.mult), reads=[bdt, bdec], writes=[bdtdec])
        CSTOP = int(os.environ.get("CSTOP", "9"))
        if CSTOP <= 1:
            return
        slz, bslz = load_slot([(w_in[l, :, O_M2Z:O_M2Z + 512], 8, 0, 512)])
        n = 0
        for ct in range(4):
            for j in range(NSUB):
                pi = 1 + n % 2
                n += 1
                mm_group(psum[pi][:, :], bps[pi], lambda k: slz[:, k, ct * 128:(ct + 1) * 128], lambda k: H[:, k, hs(j)], 8, [bslz, bH[j]])
                P.op("act", lambda e: e.activation(out=ZS[:, ct * TP + j * ST:ct * TP + (j + 1) * ST], in_=psum[pi][:, :], func=AF.Silu), reads=[bps[pi]], writes=[bZS])
        slx = [load_slot([(w_in[l, :, O_M2X + hh * 512:O_M2X + (hh + 1) * 512], 8, 0, 512)]) for hh in range(2)]
        for ct in range(8):
            sl, bsl = slx[ct // 4]
            cbuf, bcbuf = cbufs[ct % 2]
            P.op("pool", lambda e: e.tensor_copy(out=cbuf[:, 0:3], in_=carry_m2[l][ct][:, :]), reads=[bcarry_m2[l][ct]], writes=[bcbuf])
            for j in range(NSUB):
                pi = 1 + n % 2
                n += 1
                mm_group(psum[pi][:, :], bps[pi], lambda k: sl[:, k, (ct % 4) * 128:(ct % 4 + 1) * 128], lambda k: H[:, k, hs(j)], 8, [bsl, bH[j]])
                P.op("act", lambda e: e.activation(out=cbuf[:, 3 + j * ST:3 + (j + 1) * ST], in_=psum[pi][:, :], func=AF.Copy), reads=[bps[pi]], writes=[bcbuf])
            for j in range(NSUB):
                qv, bqv = qvs[j % 2]
                P.op("dve", lambda e: e.tensor_scalar(out=qv, in0=cbuf[:, 3 + j * ST:3 + (j + 1) * ST], scalar1=col(l, "m2cw", 24 + ct), scalar2=col(l, "m2cb", ct), op0=ALU.mult, op1=ALU.add),
                     reads=[bcbuf, bcols], writes=[bqv])
                for tap in range(3):
                    P.op("dve", lambda e: e.scalar_tensor_tensor(out=qv, in0=cbuf[:, tap + j * ST:tap + (j + 1) * ST], scalar=col(l, "m2cw", tap * 8 + ct), in1=qv, op0=ALU.mult, op1=ALU.add),
                         reads=[bcbuf, bqv, bcols], writes=[bqv])
                if ct < 4:
                    dst, bd = xTb[:, ct * TP + j * ST:ct * TP + (j + 1) * ST], bxT
                elif ct < 6:
                    dst, bd = BTb[:, (ct - 4) * TP + j * ST:(ct - 4) * TP + (j + 1) * ST], bBT
                else:
                    dst, bd = CTb[:, (ct - 6) * TP + j * ST:(ct - 6) * TP + (j + 1) * ST], bCT
                P.op("act", lambda e: e.activation(out=dst, in_=qv, func=AF.Silu), reads=[bqv], writes=[bd])
            P.op("pool", lambda e: e.tensor_copy(out=carry_m2[l][ct][:, :], in_=cbuf[:, TP:TP + 3]), reads=[bcbuf], writes=[bcarry_m2[l][ct]])
        if CSTOP <= 2:
            return
        P.barrier()
        rB0, rSc = Buf(), Buf()
        rL = [Buf() for _ in range(4)]
        rE, rY, rO = [Buf(), Buf()], [Buf(), Buf()], [Buf(), Buf()]
        for qc in range(NQ):
            cs = slice(qc * 128, (qc + 1) * 128)
            q2 = qc % 2
            (xdt, bxdt), (xdd, bxdd), (Btok, bBtok), (scm, bscm) = xdts[q2], xdds[q2], Btoks[q2], scms[q2]
            (yq, byq), (rstd, brstd) = yqs[q2], rstds[q2]
            (STc, bSTc), (STn, bSTn) = STbs[q2], STbs[1 - q2]
            for ct in range(4):
                P.op("pe", lambda e: e.matmul(psum[7][:, ct * 128:(ct + 1) * 128], xTb[:, ct * TP + qc * 128:ct * TP + (qc + 1) * 128], identb[:, :], start=True, stop=True),
                     reads=[bxT, bidb], writes=[bps[7]])
            for g in range(2):
                P.op("pe", lambda e: e.matmul(psum[0][:, g * 128:(g + 1) * 128], BTb[:, g * TP + qc * 128:g * TP + (qc + 1) * 128], identb[:, :], start=True, stop=True),
                     reads=[bBT, bidb], writes=[rB0])
            for h in range(8):
                P.op("dve", lambda e: e.tensor_scalar(out=xdt[:, h * 64:(h + 1) * 64], in0=psum[7][:, h * 64:(h + 1) * 64], scalar1=dt[:, qc * 8 + h:qc * 8 + h + 1], scalar2=None, op0=ALU.mult),
                     reads=[bps[7], bdt], writes=[bxdt])
                P.op("dve", lambda e: e.tensor_scalar(out=xdd[:, h * 64:(h + 1) * 64], in0=psum[7][:, h * 64:(h + 1) * 64], scalar1=dtdec[:, qc * 8 + h:qc * 8 + h + 1], scalar2=None, op0=ALU.mult),
                     reads=[bps[7], bdtdec], writes=[bxdd])
            P.op("act", lambda e: e.activation(out=Btok, in_=psum[0][:, 0:256], func=AF.Copy), reads=[rB0], writes=[bBtok])
            for g in range(2):
                P.op("pe", lambda e: e.matmul(psum[0][:, 256 + g * 128:256 + (g + 1) * 128], BTb[:, g * TP + qc * 128:g * TP + (qc + 1) * 128],
                                              CTb[:, g * TP + qc * 128:g * TP + (qc + 1) * 128], start=True, stop=True), reads=[bBT, bCT], writes=[rSc])
            P.op("dve", lambda e: e.tensor_tensor(out=scm.rearrange("p (g t) -> p g t", g=2), in0=psum[0][:, 256:512].rearrange("p (g t) -> p g t", g=2),
                                                 in1=triu.unsqueeze(1).broadcast_to([128, 2, 128]), op=ALU.mult), reads=[rSc, bconsts], writes=[bscm])
            for g in range(2):
                P.op("pe", lambda e: e.matmul(psum[1][:, g * 256:(g + 1) * 256], Btok[:, g * 128:(g + 1) * 128], xdd[:, g * 256:(g + 1) * 256], start=True, stop=True),
                     reads=[bBtok, bxdd], writes=[bps[1]])
            for h in range(8):
                g = h // 4
                hp = h % 2
                pp = (h // 2) % 2
                (ltl, bltl), (LTm, bLTm), (MT, bMT) = ltls[hp], LTms[hp], MTs[hp]
                (rep, brep), (erow, berow), (t1, bt1) = reps[pp], erows[pp], t1s[pp]
                dacol = da[:, qc * 8 + h:qc * 8 + h + 1]
                P.op("pool", lambda e: e.tensor_scalar(out=ltl, in0=ltstrict, scalar1=dacol, scalar2=1.0, op0=ALU.mult, op1=ALU.mult), reads=[bda, bconsts], writes=[bltl])
                P.op("pe", lambda e: e.matmul(psum[2][:, (h % 4) * 128:(h % 4 + 1) * 128], ltl, triu, start=True, stop=True), reads=[bltl, bconsts], writes=[rL[h % 4]])
                P.op("act", lambda e: e.activation(out=LTm, in_=psum[2][:, (h % 4) * 128:(h % 4 + 1) * 128], func=AF.Exp), reads=[rL[h % 4]], writes=[bLTm])
                P.op("dve", lambda e: e.tensor_tensor(out=MT, in0=scm[:, g * 128:(g + 1) * 128], in1=LTm, op=ALU.mult), reads=[bscm, bLTm], writes=[bMT])
                P.op("pool", lambda e: e.tensor_scalar(out=rep[:, hp * 64:(hp + 1) * 64], in0=ones[:, 0:64], scalar1=dacol, scalar2=1.0, op0=ALU.mult, op1=ALU.mult), reads=[bda, bconsts], writes=[brep])
                if hp == 1:
                    P.op("pe", lambda e: e.matmul(psum[3][:, pp * 128:(pp + 1) * 128], rep, triu, start=True, stop=True), reads=[brep, bconsts], writes=[rE[pp]])
                P.op("pe", lambda e: e.matmul(psum[4][hp * 64:(hp + 1) * 64, pp * 128:(pp + 1) * 128], xdt[:, h * 64:(h + 1) * 64], MT, start=True, stop=True), reads=[bxdt, bMT], writes=[rY[pp]])
                P.op("pe", lambda e: e.matmul(psum[5][hp * 64:(hp + 1) * 64, pp * 128:(pp + 1) * 128], STc[:, h * 64:(h + 1) * 64], CTb[:, g * TP + qc * 128:g * TP + (qc + 1) * 128], start=True, stop=True),
                     reads=[bSTc, bCT], writes=[rO[pp]])
                if hp == 1:
                    ct = h // 2
                    P.op("act", lambda e: e.activation(out=erow, in_=psum[3][:, pp * 128:(pp + 1) * 128], func=AF.Exp), reads=[rE[pp]], writes=[berow])
                    P.op("dve", lambda e: e.tensor_tensor(out=t1, in0=psum[5][:, pp * 128:(pp + 1) * 128], in1=erow, op=ALU.mult), reads=[rO[pp], berow], writes=[bt1])
                    P.op("dve", lambda e: e.tensor_tensor(out=t1, in0=psum[4][:, pp * 128:(pp + 1) * 128], in1=t1, op=ALU.add), reads=[rY[pp], bt1], writes=[bt1])
                    P.op("dve", lambda e: e.scalar_tensor_tensor(out=yq[:, ct * 128:(ct + 1) * 128], in0=xTb[:, ct * TP + qc * 128:ct * TP + (qc + 1) * 128], scalar=col(l, "m2d", ct),
                                                                in1=t1, op0=ALU.mult, op1=ALU.add), reads=[bxT, bt1, bcols, byq], writes=[byq])
                    P.op("pool", lambda e: e.tensor_tensor(out=yq[:, ct * 128:(ct + 1) * 128], in0=yq[:, ct * 128:(ct + 1) * 128],
                                                          in1=ZS[:, ct * TP + qc * 128:ct * TP + (qc + 1) * 128], op=ALU.mult), reads=[byq, bZS], writes=[byq])
            for h in range(8):
                P.op("dve", lambda e: e.scalar_tensor_tensor(out=ST_m2[l][:, h * 64:(h + 1) * 64], in0=ST_m2[l][:, h * 64:(h + 1) * 64], scalar=eA[:, qc * 8 + h:qc * 8 + h + 1],
                                                            in1=psum[1][:, h * 64:(h + 1) * 64], op0=ALU.mult, op1=ALU.add), reads=[bST_m2[l], beA, bps[1]], writes=[bST_m2[l]])
            P.op("act", lambda e: e.activation(out=STn, in_=ST_m2[l][:, :], func=AF.Copy), reads=[bST_m2[l], bSTn], writes=[bSTn])
            rms_stats(lambda k: yq[:, k * 128:(k + 1) * 128], lambda k: [byq], 4, 1.0 / W, rstd, brstd, n=128)
            for ct in range(4):
                P.op("dve", lambda e: e.scalar_tensor_tensor(out=Y[:, ct, cs], in0=yq[:, ct * 128:(ct + 1) * 128], scalar=col(l, "m2nw", ct), in1=rstd[:, 0:128],
                                                            op0=ALU.mult, op1=ALU.mult), reads=[byq, brstd, bcols], writes=[bY[ct][qc // 4]])
        P.barrier()

    def sincos(ang, bang, n, out_sin, out_cos, tmp, tmpi, btmp):
        for shift, dst in ((0.0, out_sin), (0.5 * np.pi, out_cos)):
            P.op("dve", lambda e: e.tensor_scalar(out=tmp[:, 0:n], in0=ang, scalar1=float(shift), scalar2=float(1.0 / (2 * np.pi)), op0=ALU.add, op1=ALU.mult),
                 reads=[bang], writes=[btmp])
            P.op("dve", lambda e: e.tensor_copy(out=tmpi[:, 0:n], in_=tmp[:, 0:n]), reads=[btmp], writes=[btmp])
            P.op("dve", lambda e: e.tensor_copy(out=tmp[:, n:2 * n], in_=tmpi[:, 0:n]), reads=[btmp], writes=[btmp])
            P.op("dve", lambda e: e.tensor_tensor(out=tmp[:, 0:n], in0=tmp[:, 0:n], in1=tmp[:, n:2 * n], op=ALU.subtract), reads=[btmp], writes=[btmp])
            P.op("dve", lambda e: e.tensor_scalar(out=tmp[:, n:2 * n], in0=tmp[:, 0:n], scalar1=0.5, scalar2=None, op0=ALU.is_gt), reads=[btmp], writes=[btmp])
            P.op("dve", lambda e: e.tensor_tensor(out=tmp[:, 0:n], in0=tmp[:, 0:n], in1=tmp[:, n:2 * n], op=ALU.subtract), reads=[btmp], writes=[btmp])
            P.op("dve", lambda e: e.tensor_scalar(out=tmp[:, n:2 * n], in0=tmp[:, 0:n], scalar1=-0.5, scalar2=None, op0=ALU.is_lt), reads=[btmp], writes=[btmp])
            P.op("dve", lambda e: e.tensor_tensor(out=tmp[:, 0:n], in0=tmp[:, 0:n], in1=tmp[:, n:2 * n], op=ALU.add), reads=[btmp], writes=[btmp])
            P.op("act", lambda e: e.activation(out=dst, in_=tmp[:, 0:n], func=AF.Sin, scale=float(2 * np.pi * (1 - 1e-6))), reads=[btmp], writes=[btmp])

    def s5_lambda(lre, lim, lst, n, bsrc, abre, abim, tmp, tmpi, btmp, scr, bscr):
        step, lrs, lis, mag = scr[:, 0:n], scr[:, n:2 * n], scr[:, 2 * n:3 * n], scr[:, 3 * n:4 * n]
        P.op("act", lambda e: e.activation(out=step, in_=lst, func=AF.Exp), reads=[bsrc], writes=[bscr])
        P.op("dve", lambda e: e.tensor_tensor(out=lrs, in0=lre, in1=step, op=ALU.mult), reads=[bsrc, bscr], writes=[bscr])
        P.op("dve", lambda e: e.tensor_tensor(out=lis, in0=lim, in1=step, op=ALU.mult), reads=[bsrc, bscr], writes=[bscr])
        P.op("act", lambda e: e.activation(out=mag, in_=lrs, func=AF.Exp), reads=[bscr], writes=[bscr])
        sincos(lis, bscr, n, abim, abre, tmp, tmpi, btmp)
        P.op("dve", lambda e: e.tensor_tensor(out=abre, in0=abre, in1=mag, op=ALU.mult), reads=[btmp, bscr], writes=[btmp])
        P.op("dve", lambda e: e.tensor_tensor(out=abim, in0=abim, in1=mag, op=ALU.mult), reads=[btmp, bscr], writes=[btmp])

    def coef_calc(lre, lim, abre, abim, n, cre_, cim_, scr, rd, bscr, bout):
        nr, den, u1, u2 = scr[:, 0:n], scr[:, n:2 * n], scr[:, 2 * n:3 * n], scr[:, 3 * n:4 * n]
        TT = lambda o, a, b, op, r_, w_: P.op("dve", lambda e: e.tensor_tensor(out=o, in0=a, in1=b, op=op), reads=r_, writes=w_)
        P.op("dve", lambda e: e.tensor_scalar(out=nr, in0=abre, scalar1=-1.0, scalar2=None, op0=ALU.add), reads=rd, writes=[bscr])
        TT(den, lre, lre, ALU.mult, rd, [bscr])
        TT(u1, lim, lim, ALU.mult, rd, [bscr])
        TT(den, den, u1, ALU.add, [bscr], [bscr])
        P.op("dve", lambda e: e.reciprocal(out=den, in_=den), reads=[bscr], writes=[bscr])
        TT(u1, nr, lre, ALU.mult, [bscr] + rd, [bscr])
        TT(u2, abim, lim, ALU.mult, rd, [bscr])
        TT(u1, u1, u2, ALU.add, [bscr], [bscr])
        TT(cre_, u1, den, ALU.mult, [bscr], [bout])
        TT(u1, abim, lre, ALU.mult, rd, [bscr])
        TT(u2, nr, lim, ALU.mult, [bscr] + rd, [bscr])
        TT(u1, u1, u2, ALU.subtract, [bscr], [bscr])
        TT(cim_, u1, den, ALU.mult, [bscr], [bout])

    def phase_a(l):
        scratch_reset()
        NSC = 7
        Q8 = 8
        CC = TP // Q8
        TT = lambda o, a, b, op, rd, wr: P.op("dve", lambda e: e.tensor_tensor(out=o, in0=a, in1=b, op=op), reads=rd, writes=wr)
        STT = lambda o, a, sc_, b, rd, wr: P.op("dve", lambda e: e.scalar_tensor_tensor(out=o, in0=a, scalar=sc_, in1=b, op0=ALU.mult, op1=ALU.add), reads=rd, writes=wr)
        TS = lambda o, a, sc_, rd, wr: P.op("dve", lambda e: e.tensor_scalar(out=o, in0=a, scalar1=sc_, scalar2=None, op0=ALU.mult), reads=rd, writes=wr)
        pwc, bpwc = falloc(9 * 3 * 16)
        pws, bpws = falloc(NSC * 3 * 16)
        pcC, bpcC = falloc(2 * 256)
        BD, bBD = balloc(2 * 2048)
        CD, bCD = balloc(2 * 2048)
        BDp, bBDp = balloc(2 * 2048)
        pwcv = pwc.rearrange("p (k a g) -> p k a g", k=9, a=3)
        pwsv = pws.rearrange("p (k a g) -> p k a g", k=NSC, a=3)
        mark = scr_pos[0]
        p5, bp5 = falloc(5 * 256)
        pq, bpq = falloc(3 * 16)
        pbp, bpbp = falloc(2 * 256)
        tmp, btmp = falloc(512)
        tmpi_f, _ = falloc(256)
        tmpi = tmpi_f.bitcast(mybir.dt.int32)
        scr, bscr = falloc(1024)
        ab, bab = falloc(512)
        cf, bcf = falloc(512)
        bb, bbb = falloc(512)
        abp, babp = falloc(32)
        cfp, bcfp = falloc(32)
        bbp, bbbp = falloc(512)
        P.dma("sp", sm, p5.rearrange("p (a n) -> p a n", a=5), s5p_d[l], writes=[bp5])
        P.dma("sp", sm, pq.rearrange("p (a n) -> p a n", a=3), s5q_d[l], writes=[bpq])
        P.dma("sp", sm, pcC.rearrange("p (a n) -> p a n", a=2), s5c_d[l, :, 0:2, :], writes=[bpcC])
        P.dma("sp", sm, pbp.rearrange("p (a n) -> p a n", a=2), s5c_d[l, :, 2:4, :], writes=[bpbp])
        lre, lim, lst, bre, bim = (p5[:, i * 256:(i + 1) * 256] for i in range(5))
        abre, abim = ab[:, 0:256], ab[:, 256:512]
        s5_lambda(lre, lim, lst, 256, bp5, abre, abim, tmp, tmpi, btmp, scr, bscr)
        cre_, cim_ = cf[:, 0:256], cf[:, 256:512]
        coef_calc(lre, lim, abre, abim, 256, cre_, cim_, scr, [bp5, btmp], bscr, bcf)
        u1, u2 = scr[:, 512:768], scr[:, 768:1024]
        bbre, bbim = bb[:, 0:256], bb[:, 256:512]
        TT(u1, cre_, bre, ALU.mult, [bcf, bp5], [bscr])
        TT(u2, cim_, bim, ALU.mult, [bcf, bp5], [bscr])
        TT(bbre, u1, u2, ALU.subtract, [bscr], [bbb])
        TT(u1, cre_, bim, ALU.mult, [bcf, bp5], [bscr])
        TT(u2, cim_, bre, ALU.mult, [bcf, bp5], [bscr])
        TT(bbim, u1, u2, ALU.add, [bscr], [bbb])
        for ri, src in enumerate((bbre, bbim)):
            dstv = BD[:, ri * 2048:(ri + 1) * 2048].rearrange("p (c j g n) -> p c j g n", c=4, j=4, g=2)
            for jj in range(4):
                for g2 in range(2):
                    TS(dstv[:, :, jj, g2, :], src.rearrange("p (c n) -> p c n", c=4), col(l, "mkB", jj * 2 + g2), [bbb, bcols], [bBD])
        P.op("pool", lambda e: e.memset(CD, 0.0), writes=[bCD])
        for ri, nm in enumerate(("mkC", "mkCn")):
            dstv = CD[:, ri * 2048:(ri + 1) * 2048].rearrange("p (c j m) -> p c j m", c=4, j=4)
            srcv = pcC[:, ri * 256:(ri + 1) * 256].rearrange("p (q c j) -> p c j q", q=16, c=4, j=4)
            for jj in range(4):
                for g2 in range(2):
                    gl = 2 * jj + g2
                    TS(dstv[:, :, jj, gl * 16:(gl + 1) * 16], srcv[:, :, jj, :], col(l, nm, g2), [bpcC, bcols, bCD], [bCD])
        s5_lambda(pq[:, 0:16], pq[:, 16:32], pq[:, 32:48], 16, bpq, abp[:, 0:16], abp[:, 16:32], tmp, tmpi, btmp, scr, bscr)
        coef_calc(pq[:, 0:16], pq[:, 16:32], abp[:, 0:16], abp[:, 16:32], 16, cfp[:, 0:16], cfp[:, 16:32], scr, [bpq, btmp], bscr, bcfp)
        bq_re = pbp[:, 0:256].rearrange("p (q g) -> p q g", q=16)
        bq_im = pbp[:, 256:512].rearrange("p (q g) -> p q g", q=16)
        cfr = cfp[:, 0:16].unsqueeze(1).broadcast_to([128, 16, 16])
        cfi = cfp[:, 16:32].unsqueeze(1).broadcast_to([128, 16, 16])
        w1 = scr[:, 0:256].rearrange("p (q g) -> p q g", q=16)
        w2 = scr[:, 256:512].rearrange("p (q g) -> p q g", q=16)
        bbp_re = bbp[:, 0:256].rearrange("p (q g) -> p q g", q=16)
        bbp_im = bbp[:, 256:512].rearrange("p (q g) -> p q g", q=16)
        TT(w1, bq_re, cfr, ALU.mult, [bpbp, bcfp], [bscr])
        TT(w2, bq_im, cfi, ALU.mult, [bpbp, bcfp], [bscr])
        TT(bbp_re, w1, w2, ALU.subtract, [bscr], [bbbp])
        TT(w1, bq_im, cfr, ALU.mult, [bpbp, bcfp], [bscr])
        TT(w2, bq_re, cfi, ALU.mult, [bpbp, bcfp], [bscr])
        TT(bbp_im, w1, w2, ALU.add, [bscr], [bbbp])
        P.op("pool", lambda e: e.memset(BDp, 0.0), writes=[bBDp])
        for ri in range(2):
            dstv = BDp[:, ri * 2048:(ri + 1) * 2048].rearrange("p (c j m) -> p c j m", c=4, j=4)
            srcv = bbp[:, ri * 256:(ri + 1) * 256].rearrange("p (q c j) -> p c j q", q=16, c=4, j=4)
            for jj in range(4):
                for g2 in range(2):
                    gl = 2 * jj + g2
                    TS(dstv[:, :, jj, gl * 16:(gl + 1) * 16], srcv[:, :, jj, :], col(l, "mkC", g2), [bbbp, bcols, bBDp], [bBDp])
        P.op("pool", lambda e: e.memset(pwcv[:, 0, 0, :], 1.0), writes=[bpwc])
        P.op("pool", lambda e: e.memset(pwcv[:, 0, 1:3, :], 0.0), reads=[bpwc], writes=[bpwc])
        P.op("dve", lambda e: e.tensor_copy(out=pwcv[:, 1, 0, :], in_=abp[:, 0:16]), reads=[btmp, bpwc], writes=[bpwc])
        P.op("dve", lambda e: e.tensor_copy(out=pwcv[:, 1, 1, :], in_=abp[:, 16:32]), reads=[btmp, bpwc], writes=[bpwc])
        lr_, li_ = abp[:, 0:16], abp[:, 16:32]
        for k in range(2, 9):
            a_, b_ = pwcv[:, k - 1, 0, :], pwcv[:, k - 1, 1, :]
            TT(scr[:, 0:16], a_, lr_, ALU.mult, [bpwc, btmp], [bscr])
            TT(scr[:, 16:32], b_, li_, ALU.mult, [bpwc, btmp], [bscr])
            TT(pwcv[:, k, 0, :], scr[:, 0:16], scr[:, 16:32], ALU.subtract, [bscr, bpwc], [bpwc])
            TT(scr[:, 32:48], a_, li_, ALU.mult, [bpwc, btmp], [bscr])
            TT(scr[:, 48:64], b_, lr_, ALU.mult, [bpwc, btmp], [bscr])
            TT(pwcv[:, k, 1, :], scr[:, 32:48], scr[:, 48:64], ALU.add, [bscr, bpwc], [bpwc])
        for k in range(1, 9):
            TS(pwcv[:, k, 2, :], pwcv[:, k, 1, :], -1.0, [bpwc], [bpwc])
        P.op("dve", lambda e: e.tensor_copy(out=pwsv[:, 0, :, :], in_=pwcv[:, 8, :, :]), reads=[bpwc], writes=[bpws])
        for k in range(1, NSC):
            a_, b_ = pwsv[:, k - 1, 0, :], pwsv[:, k - 1, 1, :]
            TT(scr[:, 0:16], a_, a_, ALU.mult, [bpws], [bscr])
            TT(scr[:, 16:32], b_, b_, ALU.mult, [bpws], [bscr])
            TT(pwsv[:, k, 0, :], scr[:, 0:16], scr[:, 16:32], ALU.subtract, [bscr, bpws], [bpws])
            TT(scr[:, 32:48], a_, b_, ALU.mult, [bpws], [bscr])
            TS(pwsv[:, k, 1, :], scr[:, 32:48], 2.0, [bscr, bpws], [bpws])
            TS(pwsv[:, k, 2, :], scr[:, 32:48], -2.0, [bscr, bpws], [bpws])
        scratch_reset(mark)
        t2, bt2 = falloc(TP)
        XS, _ = falloc(4 * CC)
        bXS = [[Buf(), Buf()], [Buf(), Buf()]]
        XSv = [[XS[:, (b * 2 + ri) * CC:(b * 2 + ri + 1) * CC] for ri in range(2)] for b in range(2)]
        SP, _ = falloc(2 * CC)
        bSP = [Buf(), Buf()]
        SPv = [SP[:, 0:CC], SP[:, CC:2 * CC]]
        stmp, _ = falloc(4 * CC)
        bstmp = [Buf() for _ in range(4)]
        m12, bm12 = falloc(128)
        U, bU = balloc(4 * TP)
        G1, bG1 = balloc(4 * TP)
        Kc, bKc = balloc(8 * 128)
        Mc, bMc = balloc(8 * 2 * 64)
        SX, bSX = balloc(8 * 2 * CC)
        slu, bslu = load_slot([(w_in[l, :, O_S5U:O_S5U + 512], 8, 0, 512)])
        n = 0
        for c in range(4):
            for j in range(NSUB):
                pi = n % 2
                n += 1
                mm_group(psum[pi][:, :], bps[pi], lambda k: slu[:, k, c * 128:(c + 1) * 128], lambda k: H[:, k, hs(j)], 8, [bslu, bH[j]])
                P.op("act", lambda e: e.activation(out=U[:, c * TP + j * ST:c * TP + (j + 1) * ST], in_=psum[pi][:, :], func=AF.Copy), reads=[bps[pi]], writes=[bU])
        bmv = blockmask.rearrange("p (g q) -> p g q", g=8)
        for c in range(4):
            Uc = U[:, c * TP:(c + 1) * TP]
            Ucv = Uc.rearrange("p (cc r) -> p cc r", r=8)
            Cre = pcC[:, 0:256].rearrange("q (p g) -> q p g", p=16)[:, :, 4 * c:4 * c + 4]
            Cim = pcC[:, 256:512].rearrange("q (p g) -> q p g", p=16)[:, :, 4 * c:4 * c + 4]
            m1 = m12[:, 0:64].rearrange("q (p j) -> q p j", p=16)
            m2 = m12[:, 64:128].rearrange("q (p j) -> q p j", p=16)
            for tau in range(8):
                Mre = Mc[:, (tau * 2) * 64:(tau * 2 + 1) * 64].rearrange("q (p j) -> q p j", p=16)
                Mim = Mc[:, (tau * 2 + 1) * 64:(tau * 2 + 2) * 64].rearrange("q (p j) -> q p j", p=16)
                if tau == 0:
                    P.op("dve", lambda e: e.tensor_copy(out=Mre, in_=Cre), reads=[bpcC, bMc], writes=[bMc])
                    TS(Mim, Cim, -1.0, [bpcC, bMc], [bMc])
                    continue
                Pre = pwcv[:, tau, 0, 4 * c:4 * c + 4].unsqueeze(1).broadcast_to([128, 16, 4])
                Pim = pwcv[:, tau, 1, 4 * c:4 * c + 4].unsqueeze(1).broadcast_to([128, 16, 4])
                nPim = pwcv[:, tau, 2, 4 * c:4 * c + 4].unsqueeze(1).broadcast_to([128, 16, 4])
                TT(m1, Cre, Pre, ALU.mult, [bpcC, bpwc, bm12], [bm12])
                TT(m2, Cim, Pim, ALU.mult, [bpcC, bpwc, bm12], [bm12])
                TT(Mre, m1, m2, ALU.subtract, [bm12, bMc], [bMc])
                TT(m1, Cre, nPim, ALU.mult, [bpcC, bpwc, bm12], [bm12])
                TT(m2, Cim, Pre, ALU.mult, [bpcC, bpwc, bm12], [bm12])
                TT(Mim, m1, m2, ALU.subtract, [bm12, bMc], [bMc])
            for tau in range(8):
                nmm = 0
                for jj in range(4):
                    gp = 4 * c + jj
                    for ri in range(2):
                        rhs = Mc[:, (tau * 2 + ri) * 64:(tau * 2 + ri + 1) * 64].rearrange("q (p j) -> q p j", p=16)[:, :, jj]
                        P.op("pe", lambda e: e.matmul(psum[6][:, tau * 16:(tau + 1) * 16], BDp[:, ri * 2048 + gp * 128:ri * 2048 + (gp + 1) * 128], rhs,
                                                      start=(nmm == 0), stop=(nmm == 7)), reads=[bBDp, bMc], writes=[bps[6]], inc=(nmm == 7))
                        nmm += 1
            for tau in range(8):
                P.op("dve", lambda e: e.tensor_tensor(out=Kc[:, tau * 128:(tau + 1) * 128].rearrange("p (g q) -> p g q", g=8), in0=bmv,
                                                     in1=psum[6][:, tau * 16:(tau + 1) * 16].unsqueeze(1).broadcast_to([128, 8, 16]), op=ALU.mult),
                     reads=[bps[6], bconsts, bKc], writes=[bKc])
            for bk in (4, 5):
                P.op("pe", lambda e: e.matmul(psum[bk][:, :], zerob[:, :], Uc[:, 0:512], start=True, stop=False, skip_group_check=True),
                     reads=[bzerob, bU], writes=[bps[bk]], inc=True)
            for r in range(8):
                for rp in range(r + 1):
                    last = (r == 7 and rp == 7)
                    P.op("pe", lambda e: e.matmul(psum[4 + r // 4][:, (r % 4) * 128:(r % 4 + 1) * 128], Kc[:, (r - rp) * 128:(r - rp + 1) * 128], Ucv[:, :, rp],
                                                  start=False, stop=False, skip_group_check=True), reads=[bKc, bU], writes=[bps[4 + r // 4]], inc=last)
            for jj in range(4):
                gp = 4 * c + jj
                for j in range(NSUB):
                    for ri in range(2):
                        pi = 2 * ri + j
                        P.op("pe", lambda e: e.matmul(psum[pi][:, :], BD[:, ri * 2048 + gp * 128:ri * 2048 + (gp + 1) * 128], Uc[:, j * ST:(j + 1) * ST], start=True, stop=True),
                             reads=[bBD, bU], writes=[bps[pi]])
                bv = [psbig[:, ri * 1024:(ri + 1) * 1024].rearrange("p (cc r) -> p cc r", r=8) for ri in range(2)]
                bpb = [[bps[0], bps[1]], [bps[2], bps[3]]]
                acc = [XSv[0][ri][:, 0:CC] for ri in range(2)]
                for ri in range(2):
                    P.op("act", lambda e: e.activation(out=acc[ri], in_=bv[ri][:, :, 7], func=AF.Copy), reads=bpb[ri] + [bXS[0][ri]], writes=[bXS[0][ri]])
                for r in range(7):
                    k = 7 - r
                    pr, pi_, npi = pwcv[:, k, 0, gp:gp + 1], pwcv[:, k, 1, gp:gp + 1], pwcv[:, k, 2, gp:gp + 1]
                    STT(acc[0], bv[0][:, :, r], pr, acc[0], bpb[0] + [bpwc, bXS[0][0]], [bXS[0][0]])
                    STT(acc[1], bv[1][:, :, r], pr, acc[1], bpb[1] + [bpwc, bXS[0][1]], [bXS[0][1]])
                    STT(acc[0], bv[1][:, :, r], npi, acc[0], bpb[1] + [bpwc, bXS[0][0]], [bXS[0][0]])
                    STT(acc[1], bv[0][:, :, r], pi_, acc[1], bpb[0] + [bpwc, bXS[0][1]], [bXS[0][1]])
                cr, ci = carry_s5[l][:, 0, gp:gp + 1], carry_s5[l][:, 1, gp:gp + 1]
                p8r, p8i, p8n = pwcv[:, 8, 0, gp:gp + 1], pwcv[:, 8, 1, gp:gp + 1], pwcv[:, 8, 2, gp:gp + 1]
                STT(XSv[0][0][:, 0:1], cr, p8r, XSv[0][0][:, 0:1], [bcarry_s5[l], bpwc, bXS[0][0]], [bXS[0][0]])
                STT(XSv[0][1][:, 0:1], ci, p8r, XSv[0][1][:, 0:1], [bcarry_s5[l], bpwc, bXS[0][1]], [bXS[0][1]])
                STT(XSv[0][0][:, 0:1], ci, p8n, XSv[0][0][:, 0:1], [bcarry_s5[l], bpwc, bXS[0][0]], [bXS[0][0]])
                STT(XSv[0][1][:, 0:1], cr, p8i, XSv[0][1][:, 0:1], [bcarry_s5[l], bpwc, bXS[0][1]], [bXS[0][1]])
                sbuf_i = 0
                for k in range(NSC):
                    sh = 1 << k
                    src, dst = XSv[sbuf_i], XSv[1 - sbuf_i]
                    bs_, bd_ = bXS[sbuf_i], bXS[1 - sbuf_i]
                    ar, ai, nai = pwsv[:, k, 0, gp:gp + 1], pwsv[:, k, 1, gp:gp + 1], pwsv[:, k, 2, gp:gp + 1]
                    STT(dst[0][:, sh:CC], src[0][:, 0:CC - sh], ar, src[0][:, sh:CC], [bs_[0], bpws, bd_[0]], [bd_[0]])
                    STT(dst[1][:, sh:CC], src[1][:, 0:CC - sh], ar, src[1][:, sh:CC], [bs_[1], bpws, bd_[1]], [bd_[1]])
                    STT(dst[0][:, sh:CC], src[1][:, 0:CC - sh], nai, dst[0][:, sh:CC], [bs_[1], bpws, bd_[0]], [bd_[0]])
                    STT(dst[1][:, sh:CC], src[0][:, 0:CC - sh], ai, dst[1][:, sh:CC], [bs_[0], bpws, bd_[1]], [bd_[1]])
                    for ri in range(2):
                        P.op("act", lambda e: e.activation(out=dst[ri][:, 0:sh], in_=src[ri][:, 0:sh], func=AF.Copy), reads=[bs_[ri], bd_[ri]], writes=[bd_[ri]])
                    sbuf_i = 1 - sbuf_i
                S_, bS_ = XSv[sbuf_i], bXS[sbuf_i]
                for ri in range(2):
                    P.op("pool", lambda e: e.tensor_copy(out=SPv[ri][:, 1:CC], in_=S_[ri][:, 0:CC - 1]), reads=[bS_[ri], bSP[ri]], writes=[bSP[ri]])
                    P.op("pool", lambda e: e.tensor_copy(out=SPv[ri][:, 0:1], in_=carry_s5[l][:, ri, gp:gp + 1]), reads=[bcarry_s5[l], bSP[ri]], writes=[bSP[ri]])
                for ri in range(2):
                    P.op("pool", lambda e: e.tensor_copy(out=carry_s5[l][:, ri, gp:gp + 1], in_=S_[ri][:, CC - 1:CC]), reads=[bS_[ri], bcarry_s5[l]], writes=[bcarry_s5[l]])
                for x in range(1, 9):
                    pr, pi_, npi = pwcv[:, x, 0, gp:gp + 1], pwcv[:, x, 1, gp:gp + 1], pwcv[:, x, 2, gp:gp + 1]
                    sb2 = (x % 2) * 2
                    t_re, t_im = stmp[:, sb2 * CC:(sb2 + 1) * CC], stmp[:, (sb2 + 1) * CC:(sb2 + 2) * CC]
                    P.op("pool", lambda e: e.tensor_scalar(out=t_re, in0=SPv[0], scalar1=pr, scalar2=1.0, op0=ALU.mult, op1=ALU.mult), reads=[bSP[0], bpwc, bstmp[sb2]], writes=[bstmp[sb2]])
                    P.op("pool", lambda e: e.tensor_scalar(out=t_im, in0=SPv[1], scalar1=pr, scalar2=1.0, op0=ALU.mult, op1=ALU.mult), reads=[bSP[1], bpwc, bstmp[sb2 + 1]], writes=[bstmp[sb2 + 1]])
                    STT(SX[:, ((x - 1) * 2) * CC:((x - 1) * 2 + 1) * CC], SPv[1], npi, t_re, [bSP[1], bpwc, bstmp[sb2], bSX], [bSX])
                    STT(SX[:, ((x - 1) * 2 + 1) * CC:((x - 1) * 2 + 2) * CC], SPv[0], pi_, t_im, [bSP[0], bpwc, bstmp[sb2 + 1], bSX], [bSX])
                for r in range(8):
                    for ri in range(2):
                        last = (r == 7 and ri == 1)
                        P.op("pe", lambda e: e.matmul(psum[4 + r // 4][:, (r % 4) * 128:(r % 4 + 1) * 128], CD[:, ri * 2048 + gp * 128:ri * 2048 + (gp + 1) * 128],
                                                      SX[:, (r * 2 + ri) * CC:(r * 2 + ri + 1) * CC], start=False, stop=(jj == 3 and last), skip_group_check=True),
                             reads=[bCD, bSX], writes=[bps[4 + r // 4]], inc=last)
            t2v = t2.rearrange("p (cc r) -> p cc r", r=8)
            for bk in range(2):
                P.op("dve", lambda e: e.scalar_tensor_tensor(out=t2v[:, :, 4 * bk:4 * bk + 4], in0=Ucv[:, :, 4 * bk:4 * bk + 4], scalar=col(l, "s5d", c),
                                                            in1=psum[4 + bk][:, :].rearrange("p (r cc) -> p cc r", r=4), op0=ALU.mult, op1=ALU.add),
                     reads=[bU, bcols, bps[4 + bk], bt2], writes=[bt2])
            for j in range(NSUB):
                P.op("act", lambda e: e.activation(out=G1[:, c * TP + j * ST:c * TP + (j + 1) * ST], in_=t2[:, hs(j)], func=AF.Gelu), reads=[bt2], writes=[bG1])
        P.barrier()
        sig, bsig = XS, Buf()
        gate_s, bgs = stmp, Buf()
        slw, bslw = load_slot([(w_glu[l, :, :], 4, 0, 512)])
        slg, bslg = load_slot([(w_in[l, :, O_S5G:O_S5G + 512], 8, 0, 512)])
        for co in range(4):
            for j in range(NSUB):
                p1, p2 = (0, 1) if (co * NSUB + j) % 2 == 0 else (2, 3)
                mm_group(psum[p1][:, :], bps[p1], lambda k: slw[:, k, co * 128:(co + 1) * 128], lambda k: G1[:, k * TP + j * ST:k * TP + (j + 1) * ST], 4, [bslw, bG1])
                mm_group(psum[p2][:, :], bps[p2], lambda k: slg[:, k, co * 128:(co + 1) * 128], lambda k: H[:, k, hs(j)], 8, [bslg, bH[j]])
                P.op("act", lambda e: e.activation(out=sig, in_=psum[p1][:, :], func=AF.Sigmoid), reads=[bps[p1]], writes=[bsig])
                P.op("act", lambda e: e.activation(out=gate_s, in_=psum[p2][:, :], func=AF.Silu), reads=[bps[p2]], writes=[bgs])
                P.op("dve", lambda e: e.tensor_tensor(out=t2[:, 0:ST], in0=G1[:, co * TP + j * ST:co * TP + (j + 1) * ST], in1=sig, op=ALU.mult), reads=[bG1, bsig, bt2], writes=[bt2])
                P.op("pool", lambda e: e.tensor_tensor(out=Y[:, co, hs(j)], in0=t2[:, 0:ST], in1=gate_s, op=ALU.mult), reads=[bt2, bgs], writes=[bY[co][j]])

    first_merge = [True]
    mg_t = sb("mg_t", [128, ST], F32)
    mt_t = sb("mt_t", [128, ST], F32)
    bmg, bmt = Buf(), Buf()

    def phase_merge(l, kb):
        g, bg, t, bt = mg_t[:, :], bmg, mt_t[:, :], bmt
        for hh in range(2):
            slg, bslg = load_slot([(w_in[l, :, O_MG + kb * D + hh * 512:O_MG + kb * D + (hh + 1) * 512], 8, 0, 512)])
            slb, bslb = load_slot([(w_br[l, kb, :, hh * 512:(hh + 1) * 512], 4, 0, 512)])
            for dt_ in range(4):
                d = hh * 4 + dt_
                for j in range(NSUB):
                    pg, pbk = (0, 1) if (dt_ * NSUB + j) % 2 == 0 else (2, 3)
                    mm_group(psum[pg][:, :], bps[pg], lambda k: slg[:, k, dt_ * 128:(dt_ + 1) * 128], lambda k: H[:, k, hs(j)], 8, [bslg, bH[j]])
                    mm_group(psum[pbk][:, :], bps[pbk], lambda k: slb[:, k, dt_ * 128:(dt_ + 1) * 128], lambda k: Y[:, k, hs(j)], 4,
                             [bslb] + [bY[k][j] for k in range(4)])
                    P.op("act", lambda e: e.activation(out=g, in_=psum[pg][:, :], func=AF.Sigmoid, bias=col(l, "mb", kb * 8 + d)),
                         reads=[bps[pg], bcols], writes=[bg])
                    if first_merge[0]:
                        P.op("dve", lambda e: e.tensor_tensor(out=ACC[:, d, hs(j)], in0=psum[pbk][:, :], in1=g, op=ALU.mult),
                             reads=[bps[pbk], bg], writes=[bACC[d][j]])
                    else:
                        P.op("dve", lambda e: e.tensor_tensor(out=t, in0=psum[pbk][:, :], in1=g, op=ALU.mult),
                             reads=[bps[pbk], bg], writes=[bt])
                        P.op("pool", lambda e: e.tensor_tensor(out=ACC[:, d, hs(j)], in0=ACC[:, d, hs(j)], in1=t, op=ALU.add),
                             reads=[bt, bACC[d][j]], writes=[bACC[d][j]])
        first_merge[0] = False

    def phase_out(l):
        for j in range(NSUB):
            for k in range(8):
                P.op("act", lambda e: e.activation(out=H[:, k, hs(j)], in_=ACC[:, k, hs(j)], func=AF.Copy),
                     reads=[bACC[k][j]], writes=[bH[j]])
        for hh in range(2):
            sl, bsl = load_slot([(w_out[l, :, hh * 512:(hh + 1) * 512], 8, 0, 512)])
            for dt_ in range(4):
                d = hh * 4 + dt_
                for j in range(NSUB):
                    pi = 2 + (dt_ * NSUB + j) % 4
                    mm_group(psum[pi][:, :], bps[pi], lambda k: sl[:, k, dt_ * 128:(dt_ + 1) * 128], lambda k: H[:, k, hs(j)], 8, [bsl, bH[j]])
                    P.op("dve", lambda e: e.tensor_tensor(out=X[:, d, hs(j)], in0=psum[pi][:, :], in1=X[:, d, hs(j)], op=ALU.add),
                         reads=[bps[pi], bX[d][j]], writes=[bX[d][j]])

    fo_t = sb("fo_t", [128, 2, ST], F32)
    bfo = [Buf(), Buf()]

    def phase_final(p):
        n = 0
        for j in range(NSUB):
            rms_stats(lambda k: X[:, k, hs(j)], lambda k: [bX[k][j]], 8, 1.0 / D, rstd_t, brstd_t)
            for k in range(8):
                i = n % 2
                n += 1
                P.op("dve", lambda e: e.scalar_tensor_tensor(
                    out=fo_t[:, i, :], in0=X[:, k, hs(j)], scalar=col(0, "fw", k),
                    in1=rstd_t[:, :], op0=ALU.mult, op1=ALU.mult),
                    reads=[bX[k][j], brstd_t, bcols], writes=[bfo[i]])
                P.dma("sp", sy, yT[k * 128:(k + 1) * 128, p * TP + j * ST:p * TP + (j + 1) * ST], fo_t[:, i, :], reads=[bfo[i]])

    phases = {"a": phase_a, "b": phase_b, "c": phase_c, "d": phase_d}
    for p in range(NPASS):
        for k in range(8):
            for j in range(NSUB):
                P.dma("sp", sx, X[:, k, hs(j)], xT[k * 128:(k + 1) * 128, p * TP + j * ST:p * TP + (j + 1) * ST], writes=[bX[k][j]])
        for l in range(nlayers):
            phase_norm(l)
            first_merge[0] = True
            for kb, name in enumerate("abcd"):
                if name not in branches:
                    continue
                phases[name](l)
                phase_merge(l, kb)
            phase_out(l)
        phase_final(p)
    P._wait("sp", sy, P.cnt[sy])
    print("program: nins=%d nwaits=%d" % (P.nins, P.nwaits), {k: v for k, v in P.cnt.items() if k in P.eng})
    return nc


_NC_CACHE = {}


def run(inputs, branches=("a", "b", "c", "d"), nlayers=DEPTH, trace=False):
    key = (tuple(branches), nlayers)
    if key not in _NC_CACHE:
        _NC_CACHE[key] = build_nc(branches, nlayers)
    nc = _NC_CACHE[key]
    inp = {k: np.asarray(v) for k, v in inputs.items()}
    x = inp["x"].astype(np.float32)
    s5 = [host_s5(inp, l) for l in range(DEPTH)]
    shared = {
        "w_in": np.ascontiguousarray(inp["w_in"], dtype=np.float32),
        "w_branch": np.ascontiguousarray(inp["w_branch"], dtype=np.float32),
        "w_out": np.ascontiguousarray(inp["w_out"], dtype=np.float32),
        "w_glu": np.ascontiguousarray(inp["s5_w_glu"], dtype=np.float32),
        "cols": np.stack([host_cols(inp, l) for l in range(DEPTH)], 0),
        "consts": host_consts(),
        "rows": np.stack([host_rows(inp, l) for l in range(DEPTH)], 0),
        "sguw": np.ascontiguousarray(inp["sgu_w"].transpose(0, 3, 1, 2), dtype=np.float32),
        "sgub": np.ascontiguousarray(np.repeat(inp["sgu_b"].reshape(DEPTH, 4, 2, 1, 128), 64, axis=3).transpose(0, 2, 3, 1, 4).reshape(DEPTH, 128, 512), dtype=np.float32),
        "s5p": np.stack([s[0] for s in s5], 0),
        "s5q": np.stack([s[1] for s in s5], 0),
        "s5c": np.stack([s[2] for s in s5], 0),
    }
    in_maps = []
    for b in range(8):
        m = dict(shared)
        m["xT"] = np.ascontiguousarray(x[b].T)
        in_maps.append(m)
    res = run_bass_kernel_spmd(nc, in_maps, core_ids=list(range(8)), trace=trace)
    out = np.stack([np.ascontiguousarray(res.results[b]["yT"].T) for b in range(8)], 0).astype(np.float32)
    return out, res


def kernel(**inputs):
    out, _ = run(inputs)
    return out
```

```python
import os
import numpy as np
import concourse.bass as bass
import concourse.mybir as mybir
from concourse.bass_utils import run_bass_kernel_spmd

F32 = mybir.dt.float32
BF16 = mybir.dt.bfloat16
ALU = mybir.AluOpType
AF = mybir.ActivationFunctionType

D = 1024
SEQ = 2048
DEPTH = 2
W = 512
IN_DIM = 10248
TP = 1024
NPASS = SEQ // TP
ST = 512
NSUB = TP // ST
EPS = 1e-6

O_S5U, O_S5G = 0, 512
O_SGU, O_SGV, O_SGG = 1024, 1536, 2048
O_M2Z, O_M2X, O_M2DT = 2560, 3072, 4096
O_SCB, O_SCC, O_SCH, O_SCG = 4104, 4616, 5128, 5640
O_MG = 6152


class Buf:
    __slots__ = ("w", "r")

    def __init__(self):
        self.w = None
        self.r = {}


class Prog:
    def __init__(self, nc):
        self.nc = nc
        self.eng = {"pe": nc.tensor, "act": nc.scalar, "dve": nc.vector, "pool": nc.gpsimd, "sp": nc.sync}
        self.sem = {}
        self.cnt = {}
        self.seen = {e: {} for e in self.eng}
        self.pend = {e: [] for e in self.eng}
        for e in self.eng:
            self.sem[e] = nc.alloc_semaphore("s_" + e)
            self.cnt[e] = 0
        self.nwaits = 0
        self.nins = 0

    def new_sem(self, name):
        self.sem[name] = self.nc.alloc_semaphore(name)
        self.cnt[name] = 0
        return name

    def _wait(self, e, key, val):
        if key not in self.eng:
            val = self.cnt[key]
        if self.seen[e].get(key, 0) >= val:
            return
        self.seen[e][key] = val
        self.eng[e].wait_ge(self.sem[key], val)
        self.nwaits += 1

    def _deps(self, e, reads, writes):
        deps = {}
        for b in reads:
            if b.w is not None:
                k, v = b.w
                if deps.get(k, 0) < v:
                    deps[k] = v
        for b in writes:
            if b.w is not None:
                k, v = b.w
                if deps.get(k, 0) < v:
                    deps[k] = v
            for k, v in b.r.items():
                if deps.get(k, 0) < v:
                    deps[k] = v
        for k, v in deps.items():
            if k == e and (e == "pe" or v > self.cnt[e]):
                continue
            self._wait(e, k, v)

    def _commit(self, k, v, reads, writes):
        for b in reads:
            b.r[k] = v
        for b in writes:
            b.w = (k, v)
            b.r = {}

    def op(self, e, fn, reads=(), writes=(), inc=True):
        self._deps(e, reads, writes)
        ins = fn(self.eng[e])
        self.nins += 1
        if inc:
            self.cnt[e] += 1
            ins.then_inc(self.sem[e], 1)
            self._commit(e, self.cnt[e], reads, writes)
        else:
            v = self.cnt[e] + 1
            self._commit(e, v, reads, writes)

    def barrier(self):
        for e in self.eng:
            for k, v in self.cnt.items():
                if k != e and v > 0:
                    self._wait(e, k, v)

    def dma(self, q, semkey, out, in_, reads=(), writes=(), **kw):
        self._deps(q, reads, writes)
        ins = self.eng[q].dma_start(out=out, in_=in_, **kw)
        self.cnt[semkey] += 16
        ins.then_inc(self.sem[semkey], 16)
        self.nins += 1
        self._commit(semkey, self.cnt[semkey], reads, writes)


def col_layout():
    off = {}
    n = 0

    def add(name, w):
        nonlocal n
        off[name] = n
        n += w
    add("nw", 8)
    add("mb", 32)
    add("scw", 12)
    add("fw", 8)
    add("m2cw", 32)
    add("m2cb", 8)
    add("m2d", 4)
    add("m2nw", 4)
    add("s5d", 4)
    add("mkB", 8)
    add("mkC", 2)
    add("mkCn", 2)
    return off, n


COLOFF, NCOL = col_layout()


def host_cols(inp, l):
    c = np.zeros((128, NCOL), np.float32)
    c[:, COLOFF["nw"]:COLOFF["nw"] + 8] = inp["norm_w"][l].reshape(8, 128).T
    c[:, COLOFF["mb"]:COLOFF["mb"] + 32] = inp["merge_b"][l].reshape(32, 128).T
    c[:, COLOFF["scw"]:COLOFF["scw"] + 12] = inp["sc_conv_w"][l].reshape(12, 128).T
    c[:, COLOFF["fw"]:COLOFF["fw"] + 8] = inp["final_norm_w"].reshape(8, 128).T
    c[:, COLOFF["m2cw"]:COLOFF["m2cw"] + 32] = inp["m2_conv_w"][l].reshape(32, 128).T
    c[:, COLOFF["m2cb"]:COLOFF["m2cb"] + 8] = inp["m2_conv_b"][l].reshape(8, 128).T
    c[:, COLOFF["m2d"]:COLOFF["m2d"] + 4] = np.repeat(inp["m2_d"][l], 64).reshape(4, 128).T
    c[:, COLOFF["m2nw"]:COLOFF["m2nw"] + 4] = inp["m2_norm_w"][l].reshape(4, 128).T
    c[:, COLOFF["s5d"]:COLOFF["s5d"] + 4] = inp["s5_d"][l].reshape(4, 128).T
    gl = np.arange(128) // 16
    for jj in range(4):
        for g2 in range(2):
            c[:, COLOFF["mkB"] + jj * 2 + g2] = (gl == 2 * jj + g2)
    g2p = np.arange(128) // 64
    for g2 in range(2):
        c[:, COLOFF["mkC"] + g2] = (g2p == g2)
        c[:, COLOFF["mkCn"] + g2] = -1.0 * (g2p == g2)
    return c


def host_consts():
    i = np.arange(128)
    k = np.zeros((128, 5, 128), np.float32)
    k[:, 0] = np.eye(128)
    k[:, 1] = (i[:, None] <= i[None, :])
    k[:, 2] = (i[:, None] > i[None, :])
    k[:, 3] = 1.0
    k[:, 4] = (i[:, None] // 16 == i[None, :] // 16)
    return k


def host_rows(inp, l):
    r = np.zeros((128, 1040), np.float32)
    r[:, 0:512] = inp["sgu_ln_w"][l][None, :]
    r[:, 512:1024] = inp["sgu_ln_b"][l][None, :]
    r[:, 1024:1032] = inp["m2_dt_bias"][l][None, :]
    r[:, 1032:1040] = inp["m2_a_log"][l][None, :]
    return r


def host_s5(inp, l):
    G, N, Pq = 32, 64, 16
    def L2(a_gn):
        a = a_gn.reshape(4, 8, N)
        a = np.repeat(a[:, :, None, :], 16, axis=2)
        return a.transpose(1, 2, 0, 3).reshape(128, 256)
    def L2b(b_gnq):
        a = b_gnq.reshape(4, 8, N, Pq)
        return a.transpose(1, 3, 0, 2).reshape(128, 256)
    p5 = np.stack([L2(inp["s5_lambda_re"][l]), L2(inp["s5_lambda_im"][l]),
                   L2(np.repeat(inp["s5_log_step"][l][:, None], N, 1)),
                   L2b(inp["s5_b_re"][l]), L2b(inp["s5_b_im"][l])], 1)
    def PL(a_gn):
        return a_gn.reshape(16, 2, N).transpose(1, 2, 0).reshape(128, 16)
    pq = np.stack([PL(inp["s5_lambda_re"][l]), PL(inp["s5_lambda_im"][l]),
                   PL(np.repeat(inp["s5_log_step"][l][:, None], N, 1))], 1)
    def PLc(c_gpn):
        return c_gpn.reshape(16, 2, Pq, N).transpose(1, 3, 2, 0).reshape(128, 256)
    def PLb(b_gnq):
        return b_gnq.reshape(16, 2, N, Pq).transpose(1, 2, 3, 0).reshape(128, 256)
    pc = np.stack([PLc(inp["s5_c_re"][l]), PLc(inp["s5_c_im"][l]),
                   PLb(inp["s5_b_re"][l]), PLb(inp["s5_b_im"][l])], 1)
    return p5.astype(np.float32), pq.astype(np.float32), pc.astype(np.float32)


def build_nc(branches=("a", "b", "c", "d"), nlayers=DEPTH):
    nc = bass.Bass("TRN2", target_bir_lowering=False)
    xT = nc.dram_tensor("xT", [D, SEQ], F32, kind="ExternalInput").ap()
    w_in = nc.dram_tensor("w_in", [DEPTH, D, IN_DIM], F32, kind="ExternalInput").ap()
    w_br = nc.dram_tensor("w_branch", [DEPTH, 4, W, D], F32, kind="ExternalInput").ap()
    w_out = nc.dram_tensor("w_out", [DEPTH, D, D], F32, kind="ExternalInput").ap()
    w_glu = nc.dram_tensor("w_glu", [DEPTH, W, W], F32, kind="ExternalInput").ap()
    cols_d = nc.dram_tensor("cols", [DEPTH, 128, NCOL], F32, kind="ExternalInput").ap()
    consts_d = nc.dram_tensor("consts", [128, 5, 128], F32, kind="ExternalInput").ap()
    rows_d = nc.dram_tensor("rows", [DEPTH, 128, 1040], F32, kind="ExternalInput").ap()
    sguw_d = nc.dram_tensor("sguw", [DEPTH, 128, 8, 128], F32, kind="ExternalInput").ap()
    sgub_d = nc.dram_tensor("sgub", [DEPTH, 128, 512], F32, kind="ExternalInput").ap()
    s5p_d = nc.dram_tensor("s5p", [DEPTH, 128, 5, 256], F32, kind="ExternalInput").ap()
    s5q_d = nc.dram_tensor("s5q", [DEPTH, 128, 3, 16], F32, kind="ExternalInput").ap()
    s5c_d = nc.dram_tensor("s5c", [DEPTH, 128, 4, 256], F32, kind="ExternalInput").ap()
    yT = nc.dram_tensor("yT", [D, SEQ], F32, kind="ExternalOutput").ap()

    P = Prog(nc)
    sb = nc.alloc_sbuf_tensor
    X = sb("X", [128, 8, TP], F32)
    H = sb("H", [128, 8, TP], BF16)
    ACC = sb("ACC", [128, 8, TP], F32)
    Y = sb("Y", [128, 4, TP], BF16)
    cols = sb("colsb", [128, DEPTH, NCOL], F32)
    consts = sb("constsb", [128, 5, 128], F32)
    identb = sb("identb", [128, 128], BF16)
    mask01b = sb("mask01b", [128, 128], BF16)
    bX = [[Buf() for _ in range(NSUB)] for _ in range(8)]
    bH = [Buf() for _ in range(NSUB)]
    bACC = [[Buf() for _ in range(NSUB)] for _ in range(8)]
    bY = [[Buf() for _ in range(NSUB)] for _ in range(4)]
    bcols, bconsts = Buf(), Buf()
    ident, triu, ltstrict, ones = consts[:, 0, :], consts[:, 1, :], consts[:, 2, :], consts[:, 3, :]
    blockmask = consts[:, 4, :]
    zerob = sb("zerob", [128, 128], BF16)
    bzerob = Buf()
    bones = bconsts

    NS = 4
    slots = [sb("slot%d" % i, [128, 8 * 512], BF16) for i in range(NS)]
    bslot = [Buf() for _ in range(NS)]
    sslot = [P.new_sem("dslot%d" % i) for i in range(NS)]
    slot_rr = [0]

    psbig = nc.alloc_psum_tensor("psbig", [128, 7 * 512], F32)
    psum = [psbig[:, i * 512:(i + 1) * 512] for i in range(7)]
    psT = nc.alloc_psum_tensor("psT", [128, 512], F32)
    bps = [Buf() for _ in range(7)]
    bpsT = Buf()
    psum.append(psT[:, :])
    bps.append(bpsT)

    SCRF = 16600
    scrF = sb("scrF", [128, SCRF], F32)
    scr_pos = [0]

    def scratch_reset(pos=0):
        P.barrier()
        scr_pos[0] = pos

    def falloc(n, parts=128):
        a = scrF[0:parts, scr_pos[0]:scr_pos[0] + n]
        scr_pos[0] += n
        assert scr_pos[0] <= SCRF, scr_pos
        return a, Buf()

    def balloc(n):
        m = (n + 1) // 2
        a = scrF[:, scr_pos[0]:scr_pos[0] + m].bitcast(BF16)[:, 0:n]
        scr_pos[0] += m
        assert scr_pos[0] <= SCRF, scr_pos
        return a, Buf()

    sx = P.new_sem("dx")
    sy = P.new_sem("dy")
    sc = P.new_sem("dc")
    sm = P.new_sem("dm")
    sw8 = P.new_sem("dw8")

    P.dma("sp", sc, cols[:, :, :], cols_d.rearrange("l p n -> p l n"), writes=[bcols])
    P.dma("sp", sc, consts[:, :, :], consts_d, writes=[bconsts])
    bidb = Buf()
    P.op("dve", lambda e: e.tensor_copy(out=identb[:, :], in_=ident), reads=[bconsts], writes=[bidb])
    P.op("dve", lambda e: e.tensor_copy(out=mask01b[:, :], in_=triu), reads=[bconsts], writes=[bidb])
    epsb = sb("epsb", [128, 1], F32)
    bepsb = Buf()
    P.op("pool", lambda e: e.memset(epsb[:, :], EPS), writes=[bepsb])
    P.op("pool", lambda e: e.memset(zerob[:, :], 0.0), writes=[bzerob])

    def col(l, name, j=0):
        o = COLOFF[name] + j
        return cols[:, l, o:o + 1]

    def load_slot(pieces):
        i = slot_rr[0] % NS
        slot_rr[0] += 1
        s = slots[i]
        for src, kt, off, n in pieces:
            dst = s[:, :].rearrange("p (k n) -> p k n", k=8)[:, 0:kt, off:off + n]
            P.dma("pool", sslot[i], dst, src.rearrange("(k p) n -> p k n", p=128), writes=[bslot[i]])
        return s[:, :].rearrange("p (k n) -> p k n", k=8), bslot[i]

    def mm_group(out_ap, bout, lhs_fn, rhs_fn, nk, reads):
        for k in range(nk):
            P.op("pe", lambda e: e.matmul(out_ap, lhs_fn(k), rhs_fn(k), start=(k == 0), stop=(k == nk - 1)),
                 reads=reads, writes=[bout], inc=(k == nk - 1))

    def hs(j):
        return slice(j * ST, (j + 1) * ST)

    carry_sc = [[sb("csc%d_%d" % (l, c), [128, 2], F32) for c in range(4)] for l in range(DEPTH)]
    bcarry_sc = [[Buf() for c in range(4)] for l in range(DEPTH)]
    carry_m2 = [[sb("cm2%d_%d" % (l, c), [128, 3], F32) for c in range(8)] for l in range(DEPTH)]
    bcarry_m2 = [[Buf() for c in range(8)] for l in range(DEPTH)]
    ST_m2 = [sb("stm2_%d" % l, [128, 512], F32) for l in range(DEPTH)]
    bST_m2 = [Buf() for l in range(DEPTH)]
    carry_s5 = [sb("cs5_%d" % l, [128, 2, 16], F32) for l in range(DEPTH)]
    bcarry_s5 = [Buf() for l in range(DEPTH)]
    for l in range(DEPTH):
        for c in range(4):
            P.op("pool", lambda e: e.memset(carry_sc[l][c][:, :], 0.0), writes=[bcarry_sc[l][c]])
        for c in range(8):
            P.op("pool", lambda e: e.memset(carry_m2[l][c][:, :], 0.0), writes=[bcarry_m2[l][c]])
        P.op("pool", lambda e: e.memset(ST_m2[l][:, :], 0.0), writes=[bST_m2[l]])
        P.op("pool", lambda e: e.memset(carry_s5[l][:, :, :], 0.0), writes=[bcarry_s5[l]])

    def rms_stats(src_fn, breads, nk, scale, rstd, brstd, n=ST, lnexp=False):
        sq, bsq = falloc_sq[0]
        for k in range(nk):
            P.op("act", lambda e: e.activation(out=sq[:, 0:n], in_=src_fn(k), func=AF.Square), reads=breads(k), writes=[bsq])
            P.op("pe", lambda e: e.matmul(psum[6][:, 0:n], ones, sq[:, 0:n], start=(k == 0), stop=(k == nk - 1)),
                 reads=[bsq, bones], writes=[bps[6]])
        if lnexp:
            P.op("act", lambda e: e.activation(out=rstd[:, 0:n], in_=psum[6][:, 0:n], func=AF.Ln, bias=epsb[:, 0:1], scale=scale),
                 reads=[bps[6], bepsb], writes=[brstd])
            P.op("act", lambda e: e.activation(out=rstd[:, 0:n], in_=rstd[:, 0:n], func=AF.Exp, scale=-0.5), reads=[brstd], writes=[brstd])
            return
        P.op("act", lambda e: e.activation(out=rstd[:, 0:n], in_=psum[6][:, 0:n], func=AF.Sqrt, bias=epsb[:, 0:1], scale=scale),
             reads=[bps[6], bepsb], writes=[brstd])
        P.op("dve", lambda e: e.reciprocal(out=rstd[:, 0:n], in_=rstd[:, 0:n]), reads=[brstd], writes=[brstd])

    sq_t = sb("sq_t", [128, ST], F32)
    falloc_sq = [(sq_t, Buf())]
    rstd_t = sb("rstd_t", [128, ST], F32)
    brstd_t = Buf()

    def phase_norm(l):
        for j in range(NSUB):
            rms_stats(lambda k: X[:, k, hs(j)], lambda k: [bX[k][j]], 8, 1.0 / D, rstd_t, brstd_t)
            for k in range(8):
                P.op("dve", lambda e: e.scalar_tensor_tensor(
                    out=H[:, k, hs(j)], in0=X[:, k, hs(j)], scalar=col(l, "nw", k),
                    in1=rstd_t[:, :], op0=ALU.mult, op1=ALU.mult),
                    reads=[bX[k][j], brstd_t, bcols], writes=[bH[j]])

    def phase_d(l):
        scratch_reset()
        pbufs = [falloc(2 + TP) for _ in range(2)]
        hsbs = [falloc(ST) for _ in range(2)]
        qs = [falloc(ST) for _ in range(2)]
        yvs = [falloc(ST) for _ in range(2)]
        sgs = [falloc(ST) for _ in range(2)]
        for c in range(4):
            sl, bsl = load_slot([(w_in[l, :, o + c * 128:o + (c + 1) * 128], 8, i * 128, 128)
                                 for i, o in enumerate((O_SCB, O_SCC, O_SCH, O_SCG))])
            pbuf, bp = pbufs[c % 2]
            P.op("pool", lambda e: e.tensor_copy(out=pbuf[:, 0:2], in_=carry_sc[l][c][:, :]), reads=[bcarry_sc[l][c]], writes=[bp])
            for j in range(NSUB):
                pb = 3 * (j % 2)
                pgate = 6 + (j % 2)
                (hsb, bhsb), (q, bq), (yv, byv), (sg, bsg) = hsbs[j % 2], qs[j % 2], yvs[j % 2], sgs[j % 2]
                for i in range(3):
                    mm_group(psum[pb + i][:, :], bps[pb + i], lambda k: sl[:, k, i * 128:(i + 1) * 128], lambda k: H[:, k, hs(j)], 8, [bsl, bH[j]])
                mm_group(psum[pgate][:, :], bps[pgate], lambda k: sl[:, k, 384:512], lambda k: H[:, k, hs(j)], 8, [bsl, bH[j]])
                P.op("act", lambda e: e.activation(out=hsb, in_=psum[pb + 2][:, :], func=AF.Copy), reads=[bps[pb + 2]], writes=[bhsb])
                P.op("dve", lambda e: e.tensor_tensor(out=pbuf[:, 2 + j * ST:2 + (j + 1) * ST], in0=psum[pb + 1][:, :], in1=hsb, op=ALU.mult),
                     reads=[bps[pb + 1], bhsb], writes=[bp])
                P.op("dve", lambda e: e.tensor_scalar(out=q, in0=pbuf[:, 2 + j * ST:2 + (j + 1) * ST], scalar1=col(l, "scw", 8 + c), scalar2=None, op0=ALU.mult),
                     reads=[bp, bcols], writes=[bq])
                P.op("dve", lambda e: e.scalar_tensor_tensor(out=q, in0=pbuf[:, 1 + j * ST:1 + (j + 1) * ST], scalar=col(l, "scw", 4 + c), in1=q, op0=ALU.mult, op1=ALU.add),
                     reads=[bp, bq, bcols], writes=[bq])
                P.op("dve", lambda e: e.scalar_tensor_tensor(out=q, in0=pbuf[:, j * ST:(j + 1) * ST], scalar=col(l, "scw", c), in1=q, op0=ALU.mult, op1=ALU.add),
                     reads=[bp, bq, bcols], writes=[bq])
                P.op("dve", lambda e: e.tensor_tensor(out=yv, in0=psum[pb][:, :], in1=q, op=ALU.mult), reads=[bps[pb], bq], writes=[byv])
                P.op("act", lambda e: e.activation(out=sg, in_=psum[pgate][:, :], func=AF.Silu), reads=[bps[pgate]], writes=[bsg])
                P.op("pool", lambda e: e.tensor_tensor(out=Y[:, c, hs(j)], in0=yv, in1=sg, op=ALU.mult),
                     reads=[byv, bsg], writes=[bY[c][j]])
            P.op("pool", lambda e: e.tensor_copy(out=carry_sc[l][c][:, :], in_=pbuf[:, TP:TP + 2]), reads=[bp], writes=[bcarry_sc[l][c]])

    def phase_b(l):
        scratch_reset()
        lnw, blnw = falloc(512)
        lnb, blnb = falloc(512)
        wraw, bwraw = falloc(1024)
        bsrow, bbsrow = falloc(512)
        v32s = [falloc(512) for _ in range(2)]
        vns = [falloc(512) for _ in range(2)]
        st6s = [falloc(6) for _ in range(2)]
        mvs = [falloc(2) for _ in range(2)]
        rss = [falloc(1) for _ in range(2)]
        gus = [falloc(ST) for _ in range(2)]
        sgs = [falloc(ST) for _ in range(2)]
        t1s = [falloc(ST) for _ in range(2)]
        wmT, bwmT = balloc(1024)
        VN, bVN = balloc(8 * 512)
        bVNq = [Buf() for _ in range(8)]
        P.dma("sp", sm, lnw, rows_d[l, :, 0:512], writes=[blnw])
        P.dma("sp", sm, lnb, rows_d[l, :, 512:1024], writes=[blnb])
        P.dma("sp", sm, wraw, sguw_d[l].rearrange("s h t -> s (h t)"), writes=[bwraw])
        P.dma("sp", sm, bsrow, sgub_d[l], writes=[bbsrow])
        bsv = bsrow.rearrange("p (c t) -> p c t", c=4)
        P.op("dve", lambda e: e.tensor_tensor(out=wmT.rearrange("p (h t) -> p h t", h=8), in0=wraw.rearrange("p (h t) -> p h t", h=8),
                                             in1=triu.unsqueeze(1).broadcast_to([128, 8, 128]), op=ALU.mult),
             reads=[bwraw, bconsts], writes=[bwmT])
        slv, bslv = load_slot([(w_in[l, :, O_SGV:O_SGV + 512], 8, 0, 512)])
        for qc in range(TP // 128):
            pi = qc % 2
            (v32, bv32), (vn, bvn), (st6, bst6), (mv, bmv), (rs, brs) = v32s[pi], vns[pi], st6s[pi], mvs[pi], rss[pi]
            mm_group(psum[pi][:, :], bps[pi], lambda k: H[:, k, qc * 128:(qc + 1) * 128], lambda k: slv[:, k, 0:512], 8, [bslv, bH[qc // 4]])
            P.op("act", lambda e: e.activation(out=v32, in_=psum[pi][:, :], func=AF.Gelu), reads=[bps[pi]], writes=[bv32])
            P.op("dve", lambda e: e.bn_stats(out=st6, in_=v32), reads=[bv32], writes=[bst6])
            P.op("dve", lambda e: e.bn_aggr(out=mv, in_=st6), reads=[bst6], writes=[bmv])
            P.op("act", lambda e: e.activation(out=rs, in_=mv[:, 1:2], func=AF.Sqrt, bias=epsb[:, 0:1], scale=1.0), reads=[bmv, bepsb], writes=[brs])
            P.op("dve", lambda e: e.reciprocal(out=rs, in_=rs), reads=[brs], writes=[brs])
            P.op("dve", lambda e: e.tensor_scalar(out=vn, in0=v32, scalar1=mv[:, 0:1], scalar2=rs, op0=ALU.subtract, op1=ALU.mult),
                 reads=[bv32, bmv, brs], writes=[bvn])
            P.op("pool", lambda e: e.tensor_tensor(out=vn, in0=vn, in1=lnw, op=ALU.mult), reads=[bvn, blnw], writes=[bvn])
            P.op("pool", lambda e: e.tensor_tensor(out=VN[:, qc * 512:(qc + 1) * 512], in0=vn, in1=lnb, op=ALU.add), reads=[bvn, blnb], writes=[bVNq[qc]])
        for c in range(4):
            sl, bsl = load_slot([(w_in[l, :, O_SGU + c * 128:O_SGU + (c + 1) * 128], 8, 0, 128),
                                 (w_in[l, :, O_SGG + c * 128:O_SGG + (c + 1) * 128], 8, 128, 128)])
            for j in range(NSUB):
                pu, pg, pss = (2, 3, 4) if j % 2 == 0 else (6, 7, 5)
                (gu, bgu), (sg, bsg), (t1, bt1) = gus[j % 2], sgs[j % 2], t1s[j % 2]
                mm_group(psum[pu][:, :], bps[pu], lambda k: sl[:, k, 0:128], lambda k: H[:, k, hs(j)], 8, [bsl, bH[j]])
                mm_group(psum[pg][:, :], bps[pg], lambda k: sl[:, k, 128:256], lambda k: H[:, k, hs(j)], 8, [bsl, bH[j]])
                P.op("act", lambda e: e.activation(out=gu, in_=psum[pu][:, :], func=AF.Gelu), reads=[bps[pu]], writes=[bgu])
                P.op("act", lambda e: e.activation(out=sg, in_=psum[pg][:, :], func=AF.Silu), reads=[bps[pg]], writes=[bsg])
                for qq in range(4):
                    qc = j * 4 + qq
                    for h2 in range(2):
                        h = 2 * c + h2
                        o = psum[pss][h2 * 64:(h2 + 1) * 64, qq * 128:(qq + 1) * 128]
                        P.op("pe", lambda e: e.matmul(o, VN[:, qc * 512 + h * 64:qc * 512 + (h + 1) * 64], wmT[:, h * 128:(h + 1) * 128], start=True, stop=True),
                             reads=[bVNq[qc], bwmT], writes=[bps[pss]], inc=True)
                P.op("dve", lambda e: e.tensor_tensor(out=t1.rearrange("p (q t) -> p q t", q=4), in0=psum[pss][:, :].rearrange("p (q t) -> p q t", q=4),
                                                     in1=bsv[:, c, :].unsqueeze(1).broadcast_to([128, 4, 128]), op=ALU.add), reads=[bps[pss], bbsrow], writes=[bt1])
                P.op("dve", lambda e: e.tensor_tensor(out=t1, in0=t1, in1=gu, op=ALU.mult), reads=[bt1, bgu], writes=[bt1])
                P.op("pool", lambda e: e.tensor_tensor(out=Y[:, c, hs(j)], in0=t1, in1=sg, op=ALU.mult), reads=[bt1, bsg], writes=[bY[c][j]])

    def phase_c(l):
        scratch_reset()
        NQ = TP // 128
        dtb, bdtb = falloc(8)
        alog, balog = falloc(8)
        a_t, ba_t = falloc(8)
        dt, bdt = falloc(64)
        da, bda = falloc(64)
        csc, bcsc = falloc(64)
        dec, bdec = falloc(64)
        eA, beA = falloc(64)
        dtdec, bdtdec = falloc(64)
        cbufs = [falloc(3 + TP) for _ in range(2)]
        qvs = [falloc(ST) for _ in range(2)]
        ltls = [falloc(128) for _ in range(8)]
        reps = [falloc(128) for _ in range(4)]
        erows = [falloc(128) for _ in range(2)]
        t1s = [falloc(128) for _ in range(2)]
        yqs = [falloc(512) for _ in range(2)]
        rstds = [falloc(128) for _ in range(2)]
        xTb, bxT = balloc(4 * TP)
        BTb, bBT = balloc(2 * TP)
        CTb, bCT = balloc(2 * TP)
        ZS, bZS = balloc(4 * TP)
        xdts = [balloc(512) for _ in range(2)]
        xdds = [balloc(512) for _ in range(2)]
        Btoks = [balloc(256) for _ in range(2)]
        scms = [balloc(256) for _ in range(2)]
        LTms = [balloc(128) for _ in range(2)]
        MTs = [balloc(128) for _ in range(2)]
        STbs = [balloc(512) for _ in range(2)]
        STb, bSTb = STbs[0]
        P.dma("sp", sm, dtb, rows_d[l, :, 1024:1032], writes=[bdtb])
        P.dma("sp", sm, alog, rows_d[l, :, 1032:1040], writes=[balog])
        slw8, bwdt = load_slot([(w_in[l, :, O_M2DT - 120:O_M2DT + 8], 8, 0, 128)])
        P.op("act", lambda e: e.activation(out=a_t, in_=alog, func=AF.Exp), reads=[balog], writes=[ba_t])
        P.op("dve", lambda e: e.tensor_scalar(out=a_t, in0=a_t, scalar1=-1.0, scalar2=None, op0=ALU.mult), reads=[ba_t], writes=[ba_t])
        P.op("act", lambda e: e.activation(out=STb, in_=ST_m2[l][:, :], func=AF.Copy), reads=[bST_m2[l]], writes=[bSTb])
        wdtv = slw8[:, :, 120:128]
        for qc in range(NQ):
            mm_group(psum[0][:, qc * 8:(qc + 1) * 8], bps[0], lambda k: H[:, k, qc * 128:(qc + 1) * 128], lambda k: wdtv[:, k, :], 8, [bwdt, bH[qc // 4]])
        P.op("dve", lambda e: e.tensor_tensor(out=dt.rearrange("p (q h) -> p q h", h=8), in0=psum[0][:, 0:64].rearrange("p (q h) -> p q h", h=8),
                                             in1=dtb.unsqueeze(1).broadcast_to([128, NQ, 8]), op=ALU.add), reads=[bps[0], bdtb], writes=[bdt])
        P.op("act", lambda e: e.activation(out=dt, in_=dt, func=AF.Exp), reads=[bdt], writes=[bdt])
        P.op("act", lambda e: e.activation(out=dt, in_=dt, func=AF.Ln, bias=1.0), reads=[bdt], writes=[bdt])
        P.op("dve", lambda e: e.tensor_tensor(out=da.rearrange("p (q h) -> p q h", h=8), in0=dt.rearrange("p (q h) -> p q h", h=8),
                                             in1=a_t.unsqueeze(1).broadcast_to([128, NQ, 8]), op=ALU.mult), reads=[bdt, ba_t], writes=[bda])
        P.op("pe", lambda e: e.matmul(psum[0][:, 64:128], triu, da, start=True, stop=True), reads=[bda, bconsts], writes=[bps[0]])
        P.op("pe", lambda e: e.matmul(psum[0][:, 128:192], ones, da, start=True, stop=True), reads=[bda, bconsts], writes=[bps[0]])
        P.op("act", lambda e: e.activation(out=csc, in_=psum[0][:, 64:128], func=AF.Copy), reads=[bps[0]], writes=[bcsc])
        P.op("dve", lambda e: e.tensor_tensor(out=dec, in0=psum[0][:, 128:192], in1=csc, op=ALU.subtract), reads=[bps[0], bcsc], writes=[bdec])
        P.op("act", lambda e: e.activation(out=dec, in_=dec, func=AF.Exp), reads=[bdec], writes=[bdec])
        P.op("act", lambda e: e.activation(out=eA, in_=psum[0][:, 128:192], func=AF.Exp), reads=[bps[0]], writes=[beA])
        P.op("dve", lambda e: e.tensor_tensor(out=dtdec, in0=dt, in1=dec, op=ALU.mult), reads=[bdt, bdec], writes=[bdtdec])
        CSTOP = int(os.environ.get("CSTOP", "9"))
        if CSTOP <= 1:
            return
        slz, bslz = load_slot([(w_in[l, :, O_M2Z:O_M2Z + 512], 8, 0, 512)])
        n = 0
        for ct in range(4):
            for j in range(NSUB):
                pi = 1 + n % 2
                n += 1
                mm_group(psum[pi][:, :], bps[pi], lambda k: slz[:, k, ct * 128:(ct + 1) * 128], lambda k: H[:, k, hs(j)], 8, [bslz, bH[j]])
                P.op("act", lambda e: e.activation(out=ZS[:, ct * TP + j * ST:ct * TP + (j + 1) * ST], in_=psum[pi][:, :], func=AF.Silu), reads=[bps[pi]], writes=[bZS])
        slx = [load_slot([(w_in[l, :, O_M2X + hh * 512:O_M2X + (hh + 1) * 512], 8, 0, 512)]) for hh in range(2)]
        for ct in range(8):
            sl, bsl = slx[ct // 4]
            cbuf, bcbuf = cbufs[ct % 2]
            P.op("pool", lambda e: e.tensor_copy(out=cbuf[:, 0:3], in_=carry_m2[l][ct][:, :]), reads=[bcarry_m2[l][ct]], writes=[bcbuf])
            for j in range(NSUB):
                pi = 1 + n % 2
                n += 1
                mm_group(psum[pi][:, :], bps[pi], lambda k: sl[:, k, (ct % 4) * 128:(ct % 4 + 1) * 128], lambda k: H[:, k, hs(j)], 8, [bsl, bH[j]])
                P.op("act", lambda e: e.activation(out=cbuf[:, 3 + j * ST:3 + (j + 1) * ST], in_=psum[pi][:, :], func=AF.Copy), reads=[bps[pi]], writes=[bcbuf])
            for j in range(NSUB):
                qv, bqv = qvs[j % 2]
                P.op("dve", lambda e: e.tensor_scalar(out=qv, in0=cbuf[:, 3 + j * ST:3 + (j + 1) * ST], scalar1=col(l, "m2cw", 24 + ct), scalar2=col(l, "m2cb", ct), op0=ALU.mult, op1=ALU.add),
                     reads=[bcbuf, bcols], writes=[bqv])
                for tap in range(3):
                    P.op("dve", lambda e: e.scalar_tensor_tensor(out=qv, in0=cbuf[:, tap + j * ST:tap + (j + 1) * ST], scalar=col(l, "m2cw", tap * 8 + ct), in1=qv, op0=ALU.mult, op1=ALU.add),
                         reads=[bcbuf, bqv, bcols], writes=[bqv])
                if ct < 4:
                    dst, bd = xTb[:, ct * TP + j * ST:ct * TP + (j + 1) * ST], bxT
                elif ct < 6:
                    dst, bd = BTb[:, (ct - 4) * TP + j * ST:(ct - 4) * TP + (j + 1) * ST], bBT
                else:
                    dst, bd = CTb[:, (ct - 6) * TP + j * ST:(ct - 6) * TP + (j + 1) * ST], bCT
                P.op("act", lambda e: e.activation(out=dst, in_=qv, func=AF.Silu), reads=[bqv], writes=[bd])
            P.op("pool", lambda e: e.tensor_copy(out=carry_m2[l][ct][:, :], in_=cbuf[:, TP:TP + 3]), reads=[bcbuf], writes=[bcarry_m2[l][ct]])
        if CSTOP <= 2:
            return
        P.barrier()
        rB0, rSc = Buf(), Buf()
        rL = [Buf() for _ in range(8)]
        rE = [Buf() for _ in range(4)]
        rY, rO = [Buf(), Buf()], [Buf(), Buf()]

        def c_front(qc):
            q2 = qc % 2
            (xdt, bxdt), (xdd, bxdd), (Btok, bBtok), (scm, bscm) = xdts[q2], xdds[q2], Btoks[q2], scms[q2]
            (STn, bSTn) = STbs[1 - q2]
            for ct in range(4):
                P.op("pe", lambda e: e.matmul(psum[7][:, ct * 128:(ct + 1) * 128], xTb[:, ct * TP + qc * 128:ct * TP + (qc + 1) * 128], identb[:, :], start=True, stop=True),
                     reads=[bxT, bidb], writes=[bps[7]])
            for g in range(2):
                P.op("pe", lambda e: e.matmul(psum[0][:, g * 128:(g + 1) * 128], BTb[:, g * TP + qc * 128:g * TP + (qc + 1) * 128], identb[:, :], start=True, stop=True),
                     reads=[bBT, bidb], writes=[rB0])
            for g in range(2):
                P.op("pe", lambda e: e.matmul(psum[0][:, 256 + g * 128:256 + (g + 1) * 128], BTb[:, g * TP + qc * 128:g * TP + (qc + 1) * 128],
                                              CTb[:, g * TP + qc * 128:g * TP + (qc + 1) * 128], start=True, stop=True), reads=[bBT, bCT], writes=[rSc])
            for h in range(8):
                dacol = da[:, qc * 8 + h:qc * 8 + h + 1]
                ltl, bltl = ltls[h]
                rep, brep = reps[h // 2]
                P.op("pool", lambda e: e.tensor_scalar(out=ltl, in0=ltstrict, scalar1=dacol, scalar2=1.0, op0=ALU.mult, op1=ALU.mult), reads=[bda, bconsts], writes=[bltl])
                P.op("pool", lambda e: e.tensor_scalar(out=rep[:, (h % 2) * 64:(h % 2 + 1) * 64], in0=ones[:, 0:64], scalar1=dacol, scalar2=1.0, op0=ALU.mult, op1=ALU.mult), reads=[bda, bconsts], writes=[brep])
            for h in range(8):
                P.op("dve", lambda e: e.tensor_scalar(out=xdt[:, h * 64:(h + 1) * 64], in0=psum[7][:, h * 64:(h + 1) * 64], scalar1=dt[:, qc * 8 + h:qc * 8 + h + 1], scalar2=None, op0=ALU.mult),
                     reads=[bps[7], bdt], writes=[bxdt])
                P.op("dve", lambda e: e.tensor_scalar(out=xdd[:, h * 64:(h + 1) * 64], in0=psum[7][:, h * 64:(h + 1) * 64], scalar1=dtdec[:, qc * 8 + h:qc * 8 + h + 1], scalar2=None, op0=ALU.mult),
                     reads=[bps[7], bdtdec], writes=[bxdd])
            P.op("act", lambda e: e.activation(out=Btok, in_=psum[0][:, 0:256], func=AF.Copy), reads=[rB0], writes=[bBtok])
            P.op("dve", lambda e: e.tensor_tensor(out=scm.rearrange("p (g t) -> p g t", g=2), in0=psum[0][:, 256:512].rearrange("p (g t) -> p g t", g=2),
                                                 in1=triu.unsqueeze(1).broadcast_to([128, 2, 128]), op=ALU.mult), reads=[rSc, bconsts], writes=[bscm])
            for g in range(2):
                P.op("pe", lambda e: e.matmul(psum[1][:, g * 256:(g + 1) * 256], Btok[:, g * 128:(g + 1) * 128], xdd[:, g * 256:(g + 1) * 256], start=True, stop=True),
                     reads=[bBtok, bxdd], writes=[bps[1]])
            for h in range(8):
                P.op("dve", lambda e: e.scalar_tensor_tensor(out=ST_m2[l][:, h * 64:(h + 1) * 64], in0=ST_m2[l][:, h * 64:(h + 1) * 64], scalar=eA[:, qc * 8 + h:qc * 8 + h + 1],
                                                            in1=psum[1][:, h * 64:(h + 1) * 64], op0=ALU.mult, op1=ALU.add), reads=[bST_m2[l], beA, bps[1]], writes=[bST_m2[l]])
            P.op("act", lambda e: e.activation(out=STn, in_=ST_m2[l][:, :], func=AF.Copy), reads=[bST_m2[l], bSTn], writes=[bSTn])

        def c_heads(qc):
            q2 = qc % 2
            (xdt, bxdt), (scm, bscm) = xdts[q2], scms[q2]
            (yq, byq) = yqs[q2]
            (STc, bSTc) = STbs[q2]
            for h in range(8):
                g = h // 4
                hp = h % 2
                pair = h // 2
                pp = pair % 2
                (ltl, bltl), (LTm, bLTm), (MT, bMT) = ltls[h], LTms[hp], MTs[hp]
                (rep, brep), (erow, berow), (t1, bt1) = reps[pair], erows[pp], t1s[pp]
                Lreg = psum[2 + h // 4][:, (h % 4) * 128:(h % 4 + 1) * 128]
                P.op("pe", lambda e: e.matmul(Lreg, ltl, triu, start=True, stop=True), reads=[bltl, bconsts], writes=[rL[h]])
                P.op("act", lambda e: e.activation(out=LTm, in_=Lreg, func=AF.Exp), reads=[rL[h]], writes=[bLTm])
                P.op("dve", lambda e: e.tensor_tensor(out=MT, in0=scm[:, g * 128:(g + 1) * 128], in1=LTm, op=ALU.mult), reads=[bscm, bLTm], writes=[bMT])
                if hp == 1:
                    P.op("pe", lambda e: e.matmul(psum[4][:, pair * 128:(pair + 1) * 128], rep, triu, start=True, stop=True), reads=[brep, bconsts], writes=[rE[pair]])
                P.op("pe", lambda e: e.matmul(psum[5][hp * 64:(hp + 1) * 64, pp * 128:(pp + 1) * 128], xdt[:, h * 64:(h + 1) * 64], MT, start=True, stop=True), reads=[bxdt, bMT], writes=[rY[pp]])
                P.op("pe", lambda e: e.matmul(psum[5][hp * 64:(hp + 1) * 64, 256 + pp * 128:256 + (pp + 1) * 128], STc[:, h * 64:(h + 1) * 64], CTb[:, g * TP + qc * 128:g * TP + (qc + 1) * 128], start=True, stop=True),
                     reads=[bSTc, bCT], writes=[rO[pp]])
                if hp == 1:
                    ct = pair
                    P.op("act", lambda e: e.activation(out=erow, in_=psum[4][:, pair * 128:(pair + 1) * 128], func=AF.Exp), reads=[rE[pair]], writes=[berow])
                    P.op("dve", lambda e: e.tensor_tensor(out=t1, in0=psum[5][:, 256 + pp * 128:256 + (pp + 1) * 128], in1=erow, op=ALU.mult), reads=[rO[pp], berow], writes=[bt1])
                    P.op("dve", lambda e: e.tensor_tensor(out=t1, in0=psum[5][:, pp * 128:(pp + 1) * 128], in1=t1, op=ALU.add), reads=[rY[pp], bt1], writes=[bt1])
                    P.op("dve", lambda e: e.scalar_tensor_tensor(out=yq[:, ct * 128:(ct + 1) * 128], in0=xTb[:, ct * TP + qc * 128:ct * TP + (qc + 1) * 128], scalar=col(l, "m2d", ct),
                                                                in1=t1, op0=ALU.mult, op1=ALU.add), reads=[bxT, bt1, bcols, byq], writes=[byq])
                    P.op("pool", lambda e: e.tensor_tensor(out=yq[:, ct * 128:(ct + 1) * 128], in0=yq[:, ct * 128:(ct + 1) * 128],
                                                          in1=ZS[:, ct * TP + qc * 128:ct * TP + (qc + 1) * 128], op=ALU.mult), reads=[byq, bZS], writes=[byq])

        def c_tail(qc):
            q2 = qc % 2
            cs = slice(qc * 128, (qc + 1) * 128)
            (yq, byq), (rstd, brstd) = yqs[q2], rstds[q2]
            rms_stats(lambda k: yq[:, k * 128:(k + 1) * 128], lambda k: [byq], 4, 1.0 / W, rstd, brstd, n=128, lnexp=True)
            for ct in range(4):
                P.op("dve", lambda e: e.scalar_tensor_tensor(out=Y[:, ct, cs], in0=yq[:, ct * 128:(ct + 1) * 128], scalar=col(l, "m2nw", ct), in1=rstd[:, 0:128],
                                                            op0=ALU.mult, op1=ALU.mult), reads=[byq, brstd, bcols], writes=[bY[ct][qc // 4]])

        c_front(0)
        for qc in range(NQ):
            c_heads(qc)
            if qc + 1 < NQ:
                c_front(qc + 1)
            c_tail(qc)
        P.barrier()

    def sincos(ang, bang, n, out_sin, out_cos, tmp, tmpi, btmp):
        for shift, dst in ((0.0, out_sin), (0.5 * np.pi, out_cos)):
            P.op("dve", lambda e: e.tensor_scalar(out=tmp[:, 0:n], in0=ang, scalar1=float(shift), scalar2=float(1.0 / (2 * np.pi)), op0=ALU.add, op1=ALU.mult),
                 reads=[bang], writes=[btmp])
            P.op("dve", lambda e: e.tensor_copy(out=tmpi[:, 0:n], in_=tmp[:, 0:n]), reads=[btmp], writes=[btmp])
            P.op("dve", lambda e: e.tensor_copy(out=tmp[:, n:2 * n], in_=tmpi[:, 0:n]), reads=[btmp], writes=[btmp])
            P.op("dve", lambda e: e.tensor_tensor(out=tmp[:, 0:n], in0=tmp[:, 0:n], in1=tmp[:, n:2 * n], op=ALU.subtract), reads=[btmp], writes=[btmp])
            P.op("dve", lambda e: e.tensor_scalar(out=tmp[:, n:2 * n], in0=tmp[:, 0:n], scalar1=0.5, scalar2=None, op0=ALU.is_gt), reads=[btmp], writes=[btmp])
            P.op("dve", lambda e: e.tensor_tensor(out=tmp[:, 0:n], in0=tmp[:, 0:n], in1=tmp[:, n:2 * n], op=ALU.subtract), reads=[btmp], writes=[btmp])
            P.op("dve", lambda e: e.tensor_scalar(out=tmp[:, n:2 * n], in0=tmp[:, 0:n], scalar1=-0.5, scalar2=None, op0=ALU.is_lt), reads=[btmp], writes=[btmp])
            P.op("dve", lambda e: e.tensor_tensor(out=tmp[:, 0:n], in0=tmp[:, 0:n], in1=tmp[:, n:2 * n], op=ALU.add), reads=[btmp], writes=[btmp])
            P.op("act", lambda e: e.activation(out=dst, in_=tmp[:, 0:n], func=AF.Sin, scale=float(2 * np.pi * (1 - 1e-6))), reads=[btmp], writes=[btmp])

    def s5_lambda(lre, lim, lst, n, bsrc, abre, abim, tmp, tmpi, btmp, scr, bscr):
        step, lrs, lis, mag = scr[:, 0:n], scr[:, n:2 * n], scr[:, 2 * n:3 * n], scr[:, 3 * n:4 * n]
        P.op("act", lambda e: e.activation(out=step, in_=lst, func=AF.Exp), reads=[bsrc], writes=[bscr])
        P.op("dve", lambda e: e.tensor_tensor(out=lrs, in0=lre, in1=step, op=ALU.mult), reads=[bsrc, bscr], writes=[bscr])
        P.op("dve", lambda e: e.tensor_tensor(out=lis, in0=lim, in1=step, op=ALU.mult), reads=[bsrc, bscr], writes=[bscr])
        P.op("act", lambda e: e.activation(out=mag, in_=lrs, func=AF.Exp), reads=[bscr], writes=[bscr])
        sincos(lis, bscr, n, abim, abre, tmp, tmpi, btmp)
        P.op("dve", lambda e: e.tensor_tensor(out=abre, in0=abre, in1=mag, op=ALU.mult), reads=[btmp, bscr], writes=[btmp])
        P.op("dve", lambda e: e.tensor_tensor(out=abim, in0=abim, in1=mag, op=ALU.mult), reads=[btmp, bscr], writes=[btmp])

    def coef_calc(lre, lim, abre, abim, n, cre_, cim_, scr, rd, bscr, bout):
        nr, den, u1, u2 = scr[:, 0:n], scr[:, n:2 * n], scr[:, 2 * n:3 * n], scr[:, 3 * n:4 * n]
        TT = lambda o, a, b, op, r_, w_: P.op("dve", lambda e: e.tensor_tensor(out=o, in0=a, in1=b, op=op), reads=r_, writes=w_)
        P.op("dve", lambda e: e.tensor_scalar(out=nr, in0=abre, scalar1=-1.0, scalar2=None, op0=ALU.add), reads=rd, writes=[bscr])
        TT(den, lre, lre, ALU.mult, rd, [bscr])
        TT(u1, lim, lim, ALU.mult, rd, [bscr])
        TT(den, den, u1, ALU.add, [bscr], [bscr])
        P.op("dve", lambda e: e.reciprocal(out=den, in_=den), reads=[bscr], writes=[bscr])
        TT(u1, nr, lre, ALU.mult, [bscr] + rd, [bscr])
        TT(u2, abim, lim, ALU.mult, rd, [bscr])
        TT(u1, u1, u2, ALU.add, [bscr], [bscr])
        TT(cre_, u1, den, ALU.mult, [bscr], [bout])
        TT(u1, abim, lre, ALU.mult, rd, [bscr])
        TT(u2, nr, lim, ALU.mult, [bscr] + rd, [bscr])
        TT(u1, u1, u2, ALU.subtract, [bscr], [bscr])
        TT(cim_, u1, den, ALU.mult, [bscr], [bout])

    def phase_a(l):
        scratch_reset()
        NSC = 7
        Q8 = 8
        CC = TP // Q8
        TT = lambda o, a, b, op, rd, wr: P.op("dve", lambda e: e.tensor_tensor(out=o, in0=a, in1=b, op=op), reads=rd, writes=wr)
        STT = lambda o, a, sc_, b, rd, wr: P.op("dve", lambda e: e.scalar_tensor_tensor(out=o, in0=a, scalar=sc_, in1=b, op0=ALU.mult, op1=ALU.add), reads=rd, writes=wr)
        TS = lambda o, a, sc_, rd, wr: P.op("dve", lambda e: e.tensor_scalar(out=o, in0=a, scalar1=sc_, scalar2=None, op0=ALU.mult), reads=rd, writes=wr)
        pwc, bpwc = falloc(9 * 3 * 16)
        pws, bpws = falloc(NSC * 3 * 16)
        pcC, bpcC = falloc(2 * 256)
        BD, bBD = balloc(2 * 2048)
        CD, bCD = balloc(2 * 2048)
        BDp, bBDp = balloc(2 * 2048)
        pwcv = pwc.rearrange("p (k a g) -> p k a g", k=9, a=3)
        pwsv = pws.rearrange("p (k a g) -> p k a g", k=NSC, a=3)
        mark = scr_pos[0]
        p5, bp5 = falloc(5 * 256)
        pq, bpq = falloc(3 * 16)
        pbp, bpbp = falloc(2 * 256)
        tmp, btmp = falloc(512)
        tmpi_f, _ = falloc(256)
        tmpi = tmpi_f.bitcast(mybir.dt.int32)
        scr, bscr = falloc(1024)
        ab, bab = falloc(512)
        cf, bcf = falloc(512)
        bb, bbb = falloc(512)
        abp, babp = falloc(32)
        cfp, bcfp = falloc(32)
        bbp, bbbp = falloc(512)
        P.dma("sp", sm, p5.rearrange("p (a n) -> p a n", a=5), s5p_d[l], writes=[bp5])
        P.dma("sp", sm, pq.rearrange("p (a n) -> p a n", a=3), s5q_d[l], writes=[bpq])
        P.dma("sp", sm, pcC.rearrange("p (a n) -> p a n", a=2), s5c_d[l, :, 0:2, :], writes=[bpcC])
        P.dma("sp", sm, pbp.rearrange("p (a n) -> p a n", a=2), s5c_d[l, :, 2:4, :], writes=[bpbp])
        lre, lim, lst, bre, bim = (p5[:, i * 256:(i + 1) * 256] for i in range(5))
        abre, abim = ab[:, 0:256], ab[:, 256:512]
        s5_lambda(lre, lim, lst, 256, bp5, abre, abim, tmp, tmpi, btmp, scr, bscr)
        cre_, cim_ = cf[:, 0:256], cf[:, 256:512]
        coef_calc(lre, lim, abre, abim, 256, cre_, cim_, scr, [bp5, btmp], bscr, bcf)
        u1, u2 = scr[:, 512:768], scr[:, 768:1024]
        bbre, bbim = bb[:, 0:256], bb[:, 256:512]
        TT(u1, cre_, bre, ALU.mult, [bcf, bp5], [bscr])
        TT(u2, cim_, bim, ALU.mult, [bcf, bp5], [bscr])
        TT(bbre, u1, u2, ALU.subtract, [bscr], [bbb])
        TT(u1, cre_, bim, ALU.mult, [bcf, bp5], [bscr])
        TT(u2, cim_, bre, ALU.mult, [bcf, bp5], [bscr])
        TT(bbim, u1, u2, ALU.add, [bscr], [bbb])
        for ri, src in enumerate((bbre, bbim)):
            dstv = BD[:, ri * 2048:(ri + 1) * 2048].rearrange("p (c j g n) -> p c j g n", c=4, j=4, g=2)
            for jj in range(4):
                for g2 in range(2):
                    TS(dstv[:, :, jj, g2, :], src.rearrange("p (c n) -> p c n", c=4), col(l, "mkB", jj * 2 + g2), [bbb, bcols], [bBD])
        P.op("pool", lambda e: e.memset(CD, 0.0), writes=[bCD])
        for ri, nm in enumerate(("mkC", "mkCn")):
            dstv = CD[:, ri * 2048:(ri + 1) * 2048].rearrange("p (c j m) -> p c j m", c=4, j=4)
            srcv = pcC[:, ri * 256:(ri + 1) * 256].rearrange("p (q c j) -> p c j q", q=16, c=4, j=4)
            for jj in range(4):
                for g2 in range(2):
                    gl = 2 * jj + g2
                    TS(dstv[:, :, jj, gl * 16:(gl + 1) * 16], srcv[:, :, jj, :], col(l, nm, g2), [bpcC, bcols, bCD], [bCD])
        s5_lambda(pq[:, 0:16], pq[:, 16:32], pq[:, 32:48], 16, bpq, abp[:, 0:16], abp[:, 16:32], tmp, tmpi, btmp, scr, bscr)
        coef_calc(pq[:, 0:16], pq[:, 16:32], abp[:, 0:16], abp[:, 16:32], 16, cfp[:, 0:16], cfp[:, 16:32], scr, [bpq, btmp], bscr, bcfp)
        bq_re = pbp[:, 0:256].rearrange("p (q g) -> p q g", q=16)
        bq_im = pbp[:, 256:512].rearrange("p (q g) -> p q g", q=16)
        cfr = cfp[:, 0:16].unsqueeze(1).broadcast_to([128, 16, 16])
        cfi = cfp[:, 16:32].unsqueeze(1).broadcast_to([128, 16, 16])
        w1 = scr[:, 0:256].rearrange("p (q g) -> p q g", q=16)
        w2 = scr[:, 256:512].rearrange("p (q g) -> p q g", q=16)
        bbp_re = bbp[:, 0:256].rearrange("p (q g) -> p q g", q=16)
        bbp_im = bbp[:, 256:512].rearrange("p (q g) -> p q g", q=16)
        TT(w1, bq_re, cfr, ALU.mult, [bpbp, bcfp], [bscr])
        TT(w2, bq_im, cfi, ALU.mult, [bpbp, bcfp], [bscr])
        TT(bbp_re, w1, w2, ALU.subtract, [bscr], [bbbp])
        TT(w1, bq_im, cfr, ALU.mult, [bpbp, bcfp], [bscr])
        TT(w2, bq_re, cfi, ALU.mult, [bpbp, bcfp], [bscr])
        TT(bbp_im, w1, w2, ALU.add, [bscr], [bbbp])
        P.op("pool", lambda e: e.memset(BDp, 0.0), writes=[bBDp])
        for ri in range(2):
            dstv = BDp[:, ri * 2048:(ri + 1) * 2048].rearrange("p (c j m) -> p c j m", c=4, j=4)
            srcv = bbp[:, ri * 256:(ri + 1) * 256].rearrange("p (q c j) -> p c j q", q=16, c=4, j=4)
            for jj in range(4):
                for g2 in range(2):
                    gl = 2 * jj + g2
                    TS(dstv[:, :, jj, gl * 16:(gl + 1) * 16], srcv[:, :, jj, :], col(l, "mkC", g2), [bbbp, bcols, bBDp], [bBDp])
        P.op("pool", lambda e: e.memset(pwcv[:, 0, 0, :], 1.0), writes=[bpwc])
        P.op("pool", lambda e: e.memset(pwcv[:, 0, 1:3, :], 0.0), reads=[bpwc], writes=[bpwc])
        P.op("dve", lambda e: e.tensor_copy(out=pwcv[:, 1, 0, :], in_=abp[:, 0:16]), reads=[btmp, bpwc], writes=[bpwc])
        P.op("dve", lambda e: e.tensor_copy(out=pwcv[:, 1, 1, :], in_=abp[:, 16:32]), reads=[btmp, bpwc], writes=[bpwc])
        lr_, li_ = abp[:, 0:16], abp[:, 16:32]
        for k in range(2, 9):
            a_, b_ = pwcv[:, k - 1, 0, :], pwcv[:, k - 1, 1, :]
            TT(scr[:, 0:16], a_, lr_, ALU.mult, [bpwc, btmp], [bscr])
            TT(scr[:, 16:32], b_, li_, ALU.mult, [bpwc, btmp], [bscr])
            TT(pwcv[:, k, 0, :], scr[:, 0:16], scr[:, 16:32], ALU.subtract, [bscr, bpwc], [bpwc])
            TT(scr[:, 32:48], a_, li_, ALU.mult, [bpwc, btmp], [bscr])
            TT(scr[:, 48:64], b_, lr_, ALU.mult, [bpwc, btmp], [bscr])
            TT(pwcv[:, k, 1, :], scr[:, 32:48], scr[:, 48:64], ALU.add, [bscr, bpwc], [bpwc])
        for k in range(1, 9):
            TS(pwcv[:, k, 2, :], pwcv[:, k, 1, :], -1.0, [bpwc], [bpwc])
        P.op("dve", lambda e: e.tensor_copy(out=pwsv[:, 0, :, :], in_=pwcv[:, 8, :, :]), reads=[bpwc], writes=[bpws])
        for k in range(1, NSC):
            a_, b_ = pwsv[:, k - 1, 0, :], pwsv[:, k - 1, 1, :]
            TT(scr[:, 0:16], a_, a_, ALU.mult, [bpws], [bscr])
            TT(scr[:, 16:32], b_, b_, ALU.mult, [bpws], [bscr])
            TT(pwsv[:, k, 0, :], scr[:, 0:16], scr[:, 16:32], ALU.subtract, [bscr, bpws], [bpws])
            TT(scr[:, 32:48], a_, b_, ALU.mult, [bpws], [bscr])
            TS(pwsv[:, k, 1, :], scr[:, 32:48], 2.0, [bscr, bpws], [bpws])
            TS(pwsv[:, k, 2, :], scr[:, 32:48], -2.0, [bscr, bpws], [bpws])
        scratch_reset(mark)
        t2, bt2 = falloc(TP)
        XS, _ = falloc(4 * CC)
        bXS = [[Buf(), Buf()], [Buf(), Buf()]]
        XSv = [[XS[:, (b * 2 + ri) * CC:(b * 2 + ri + 1) * CC] for ri in range(2)] for b in range(2)]
        SP, _ = falloc(2 * CC)
        bSP = [Buf(), Buf()]
        SPv = [SP[:, 0:CC], SP[:, CC:2 * CC]]
        stmp, _ = falloc(4 * CC)
        bstmp = [Buf() for _ in range(4)]
        m12, bm12 = falloc(128)
        U, bU = balloc(4 * TP)
        G1, bG1 = balloc(4 * TP)
        Kc, bKc = balloc(8 * 128)
        Mc, bMc = balloc(8 * 2 * 64)
        SX, bSX = balloc(8 * 2 * CC)
        slu, bslu = load_slot([(w_in[l, :, O_S5U:O_S5U + 512], 8, 0, 512)])
        n = 0
        for c in range(4):
            for j in range(NSUB):
                pi = n % 2
                n += 1
                mm_group(psum[pi][:, :], bps[pi], lambda k: slu[:, k, c * 128:(c + 1) * 128], lambda k: H[:, k, hs(j)], 8, [bslu, bH[j]])
                P.op("act", lambda e: e.activation(out=U[:, c * TP + j * ST:c * TP + (j + 1) * ST], in_=psum[pi][:, :], func=AF.Copy), reads=[bps[pi]], writes=[bU])
        bmv = blockmask.rearrange("p (g q) -> p g q", g=8)
        for c in range(4):
            Uc = U[:, c * TP:(c + 1) * TP]
            Ucv = Uc.rearrange("p (cc r) -> p cc r", r=8)
            Cre = pcC[:, 0:256].rearrange("q (p g) -> q p g", p=16)[:, :, 4 * c:4 * c + 4]
            Cim = pcC[:, 256:512].rearrange("q (p g) -> q p g", p=16)[:, :, 4 * c:4 * c + 4]
            m1 = m12[:, 0:64].rearrange("q (p j) -> q p j", p=16)
            m2 = m12[:, 64:128].rearrange("q (p j) -> q p j", p=16)
            for tau in range(8):
                Mre = Mc[:, (tau * 2) * 64:(tau * 2 + 1) * 64].rearrange("q (p j) -> q p j", p=16)
                Mim = Mc[:, (tau * 2 + 1) * 64:(tau * 2 + 2) * 64].rearrange("q (p j) -> q p j", p=16)
                if tau == 0:
                    P.op("dve", lambda e: e.tensor_copy(out=Mre, in_=Cre), reads=[bpcC, bMc], writes=[bMc])
                    TS(Mim, Cim, -1.0, [bpcC, bMc], [bMc])
                    continue
                Pre = pwcv[:, tau, 0, 4 * c:4 * c + 4].unsqueeze(1).broadcast_to([128, 16, 4])
                Pim = pwcv[:, tau, 1, 4 * c:4 * c + 4].unsqueeze(1).broadcast_to([128, 16, 4])
                nPim = pwcv[:, tau, 2, 4 * c:4 * c + 4].unsqueeze(1).broadcast_to([128, 16, 4])
                TT(m1, Cre, Pre, ALU.mult, [bpcC, bpwc, bm12], [bm12])
                TT(m2, Cim, Pim, ALU.mult, [bpcC, bpwc, bm12], [bm12])
                TT(Mre, m1, m2, ALU.subtract, [bm12, bMc], [bMc])
                TT(m1, Cre, nPim, ALU.mult, [bpcC, bpwc, bm12], [bm12])
                TT(m2, Cim, Pre, ALU.mult, [bpcC, bpwc, bm12], [bm12])
                TT(Mim, m1, m2, ALU.subtract, [bm12, bMc], [bMc])
            for tau in range(8):
                nmm = 0
                for jj in range(4):
                    gp = 4 * c + jj
                    for ri in range(2):
                        rhs = Mc[:, (tau * 2 + ri) * 64:(tau * 2 + ri + 1) * 64].rearrange("q (p j) -> q p j", p=16)[:, :, jj]
                        P.op("pe", lambda e: e.matmul(psum[6][:, tau * 16:(tau + 1) * 16], BDp[:, ri * 2048 + gp * 128:ri * 2048 + (gp + 1) * 128], rhs,
                                                      start=(nmm == 0), stop=(nmm == 7)), reads=[bBDp, bMc], writes=[bps[6]], inc=(nmm == 7))
                        nmm += 1
            for tau in range(8):
                P.op("dve", lambda e: e.tensor_tensor(out=Kc[:, tau * 128:(tau + 1) * 128].rearrange("p (g q) -> p g q", g=8), in0=bmv,
                                                     in1=psum[6][:, tau * 16:(tau + 1) * 16].unsqueeze(1).broadcast_to([128, 8, 16]), op=ALU.mult),
                     reads=[bps[6], bconsts, bKc], writes=[bKc])
            for bk in (4, 5):
                P.op("pe", lambda e: e.matmul(psum[bk][:, :], zerob[:, :], Uc[:, 0:512], start=True, stop=False, skip_group_check=True),
                     reads=[bzerob, bU], writes=[bps[bk]], inc=True)
            for r in range(8):
                for rp in range(r + 1):
                    last = (r == 7 and rp == 7)
                    P.op("pe", lambda e: e.matmul(psum[4 + r // 4][:, (r % 4) * 128:(r % 4 + 1) * 128], Kc[:, (r - rp) * 128:(r - rp + 1) * 128], Ucv[:, :, rp],
                                                  start=False, stop=False, skip_group_check=True), reads=[bKc, bU], writes=[bps[4 + r // 4]], inc=last)
            for jj in range(4):
                gp = 4 * c + jj
                for j in range(NSUB):
                    for ri in range(2):
                        pi = 2 * ri + j
                        P.op("pe", lambda e: e.matmul(psum[pi][:, :], BD[:, ri * 2048 + gp * 128:ri * 2048 + (gp + 1) * 128], Uc[:, j * ST:(j + 1) * ST], start=True, stop=True),
                             reads=[bBD, bU], writes=[bps[pi]])
                bv = [psbig[:, ri * 1024:(ri + 1) * 1024].rearrange("p (cc r) -> p cc r", r=8) for ri in range(2)]
                bpb = [[bps[0], bps[1]], [bps[2], bps[3]]]
                acc = [XSv[0][ri][:, 0:CC] for ri in range(2)]
                for ri in range(2):
                    P.op("act", lambda e: e.activation(out=acc[ri], in_=bv[ri][:, :, 7], func=AF.Copy), reads=bpb[ri] + [bXS[0][ri]], writes=[bXS[0][ri]])
                for r in range(7):
                    k = 7 - r
                    pr, pi_, npi = pwcv[:, k, 0, gp:gp + 1], pwcv[:, k, 1, gp:gp + 1], pwcv[:, k, 2, gp:gp + 1]
                    STT(acc[0], bv[0][:, :, r], pr, acc[0], bpb[0] + [bpwc, bXS[0][0]], [bXS[0][0]])
                    STT(acc[1], bv[1][:, :, r], pr, acc[1], bpb[1] + [bpwc, bXS[0][1]], [bXS[0][1]])
                    STT(acc[0], bv[1][:, :, r], npi, acc[0], bpb[1] + [bpwc, bXS[0][0]], [bXS[0][0]])
                    STT(acc[1], bv[0][:, :, r], pi_, acc[1], bpb[0] + [bpwc, bXS[0][1]], [bXS[0][1]])
                cr, ci = carry_s5[l][:, 0, gp:gp + 1], carry_s5[l][:, 1, gp:gp + 1]
                p8r, p8i, p8n = pwcv[:, 8, 0, gp:gp + 1], pwcv[:, 8, 1, gp:gp + 1], pwcv[:, 8, 2, gp:gp + 1]
                STT(XSv[0][0][:, 0:1], cr, p8r, XSv[0][0][:, 0:1], [bcarry_s5[l], bpwc, bXS[0][0]], [bXS[0][0]])
                STT(XSv[0][1][:, 0:1], ci, p8r, XSv[0][1][:, 0:1], [bcarry_s5[l], bpwc, bXS[0][1]], [bXS[0][1]])
                STT(XSv[0][0][:, 0:1], ci, p8n, XSv[0][0][:, 0:1], [bcarry_s5[l], bpwc, bXS[0][0]], [bXS[0][0]])
                STT(XSv[0][1][:, 0:1], cr, p8i, XSv[0][1][:, 0:1], [bcarry_s5[l], bpwc, bXS[0][1]], [bXS[0][1]])
                sbuf_i = 0
                for k in range(NSC):
                    sh = 1 << k
                    src, dst = XSv[sbuf_i], XSv[1 - sbuf_i]
                    bs_, bd_ = bXS[sbuf_i], bXS[1 - sbuf_i]
                    ar, ai, nai = pwsv[:, k, 0, gp:gp + 1], pwsv[:, k, 1, gp:gp + 1], pwsv[:, k, 2, gp:gp + 1]
                    STT(dst[0][:, sh:CC], src[0][:, 0:CC - sh], ar, src[0][:, sh:CC], [bs_[0], bpws, bd_[0]], [bd_[0]])
                    STT(dst[1][:, sh:CC], src[1][:, 0:CC - sh], ar, src[1][:, sh:CC], [bs_[1], bpws, bd_[1]], [bd_[1]])
                    STT(dst[0][:, sh:CC], src[1][:, 0:CC - sh], nai, dst[0][:, sh:CC], [bs_[1], bpws, bd_[0]], [bd_[0]])
                    STT(dst[1][:, sh:CC], src[0][:, 0:CC - sh], ai, dst[1][:, sh:CC], [bs_[0], bpws, bd_[1]], [bd_[1]])
                    for ri in range(2):
                        P.op("act", lambda e: e.activation(out=dst[ri][:, 0:sh], in_=src[ri][:, 0:sh], func=AF.Copy), reads=[bs_[ri], bd_[ri]], writes=[bd_[ri]])
                    sbuf_i = 1 - sbuf_i
                S_, bS_ = XSv[sbuf_i], bXS[sbuf_i]
                for ri in range(2):
                    P.op("pool", lambda e: e.tensor_copy(out=SPv[ri][:, 1:CC], in_=S_[ri][:, 0:CC - 1]), reads=[bS_[ri], bSP[ri]], writes=[bSP[ri]])
                    P.op("pool", lambda e: e.tensor_copy(out=SPv[ri][:, 0:1], in_=carry_s5[l][:, ri, gp:gp + 1]), reads=[bcarry_s5[l], bSP[ri]], writes=[bSP[ri]])
                for ri in range(2):
                    P.op("pool", lambda e: e.tensor_copy(out=carry_s5[l][:, ri, gp:gp + 1], in_=S_[ri][:, CC - 1:CC]), reads=[bS_[ri], bcarry_s5[l]], writes=[bcarry_s5[l]])
                for x in range(1, 9):
                    pr, pi_, npi = pwcv[:, x, 0, gp:gp + 1], pwcv[:, x, 1, gp:gp + 1], pwcv[:, x, 2, gp:gp + 1]
                    sb2 = (x % 2) * 2
                    t_re, t_im = stmp[:, sb2 * CC:(sb2 + 1) * CC], stmp[:, (sb2 + 1) * CC:(sb2 + 2) * CC]
                    P.op("pool", lambda e: e.tensor_scalar(out=t_re, in0=SPv[0], scalar1=pr, scalar2=1.0, op0=ALU.mult, op1=ALU.mult), reads=[bSP[0], bpwc, bstmp[sb2]], writes=[bstmp[sb2]])
                    P.op("pool", lambda e: e.tensor_scalar(out=t_im, in0=SPv[1], scalar1=pr, scalar2=1.0, op0=ALU.mult, op1=ALU.mult), reads=[bSP[1], bpwc, bstmp[sb2 + 1]], writes=[bstmp[sb2 + 1]])
                    STT(SX[:, ((x - 1) * 2) * CC:((x - 1) * 2 + 1) * CC], SPv[1], npi, t_re, [bSP[1], bpwc, bstmp[sb2], bSX], [bSX])
                    STT(SX[:, ((x - 1) * 2 + 1) * CC:((x - 1) * 2 + 2) * CC], SPv[0], pi_, t_im, [bSP[0], bpwc, bstmp[sb2 + 1], bSX], [bSX])
                for r in range(8):
                    for ri in range(2):
                        last = (r == 7 and ri == 1)
                        P.op("pe", lambda e: e.matmul(psum[4 + r // 4][:, (r % 4) * 128:(r % 4 + 1) * 128], CD[:, ri * 2048 + gp * 128:ri * 2048 + (gp + 1) * 128],
                                                      SX[:, (r * 2 + ri) * CC:(r * 2 + ri + 1) * CC], start=False, stop=(jj == 3 and last), skip_group_check=True),
                             reads=[bCD, bSX], writes=[bps[4 + r // 4]], inc=last)
            t2v = t2.rearrange("p (cc r) -> p cc r", r=8)
            for bk in range(2):
                P.op("dve", lambda e: e.scalar_tensor_tensor(out=t2v[:, :, 4 * bk:4 * bk + 4], in0=Ucv[:, :, 4 * bk:4 * bk + 4], scalar=col(l, "s5d", c),
                                                            in1=psum[4 + bk][:, :].rearrange("p (r cc) -> p cc r", r=4), op0=ALU.mult, op1=ALU.add),
                     reads=[bU, bcols, bps[4 + bk], bt2], writes=[bt2])
            for j in range(NSUB):
                P.op("act", lambda e: e.activation(out=G1[:, c * TP + j * ST:c * TP + (j + 1) * ST], in_=t2[:, hs(j)], func=AF.Gelu), reads=[bt2], writes=[bG1])
        P.barrier()
        sig, bsig = XS, Buf()
        gate_s, bgs = stmp, Buf()
        slw, bslw = load_slot([(w_glu[l, :, :], 4, 0, 512)])
        slg, bslg = load_slot([(w_in[l, :, O_S5G:O_S5G + 512], 8, 0, 512)])
        for co in range(4):
            for j in range(NSUB):
                p1, p2 = (0, 1) if (co * NSUB + j) % 2 == 0 else (2, 3)
                mm_group(psum[p1][:, :], bps[p1], lambda k: slw[:, k, co * 128:(co + 1) * 128], lambda k: G1[:, k * TP + j * ST:k * TP + (j + 1) * ST], 4, [bslw, bG1])
                mm_group(psum[p2][:, :], bps[p2], lambda k: slg[:, k, co * 128:(co + 1) * 128], lambda k: H[:, k, hs(j)], 8, [bslg, bH[j]])
                P.op("act", lambda e: e.activation(out=sig, in_=psum[p1][:, :], func=AF.Sigmoid), reads=[bps[p1]], writes=[bsig])
                P.op("act", lambda e: e.activation(out=gate_s, in_=psum[p2][:, :], func=AF.Silu), reads=[bps[p2]], writes=[bgs])
                P.op("dve", lambda e: e.tensor_tensor(out=t2[:, 0:ST], in0=G1[:, co * TP + j * ST:co * TP + (j + 1) * ST], in1=sig, op=ALU.mult), reads=[bG1, bsig, bt2], writes=[bt2])
                P.op("pool", lambda e: e.tensor_tensor(out=Y[:, co, hs(j)], in0=t2[:, 0:ST], in1=gate_s, op=ALU.mult), reads=[bt2, bgs], writes=[bY[co][j]])

    first_merge = [True]
    mg_t = sb("mg_t", [128, ST], F32)
    mt_t = sb("mt_t", [128, ST], F32)
    bmg, bmt = Buf(), Buf()

    def phase_merge(l, kb):
        g, bg, t, bt = mg_t[:, :], bmg, mt_t[:, :], bmt
        for hh in range(2):
            slg, bslg = load_slot([(w_in[l, :, O_MG + kb * D + hh * 512:O_MG + kb * D + (hh + 1) * 512], 8, 0, 512)])
            slb, bslb = load_slot([(w_br[l, kb, :, hh * 512:(hh + 1) * 512], 4, 0, 512)])
            for dt_ in range(4):
                d = hh * 4 + dt_
                for j in range(NSUB):
                    pg, pbk = (0, 1) if (dt_ * NSUB + j) % 2 == 0 else (2, 3)
                    mm_group(psum[pg][:, :], bps[pg], lambda k: slg[:, k, dt_ * 128:(dt_ + 1) * 128], lambda k: H[:, k, hs(j)], 8, [bslg, bH[j]])
                    mm_group(psum[pbk][:, :], bps[pbk], lambda k: slb[:, k, dt_ * 128:(dt_ + 1) * 128], lambda k: Y[:, k, hs(j)], 4,
                             [bslb] + [bY[k][j] for k in range(4)])
                    P.op("act", lambda e: e.activation(out=g, in_=psum[pg][:, :], func=AF.Sigmoid, bias=col(l, "mb", kb * 8 + d)),
                         reads=[bps[pg], bcols], writes=[bg])
                    if first_merge[0]:
                        P.op("dve", lambda e: e.tensor_tensor(out=ACC[:, d, hs(j)], in0=psum[pbk][:, :], in1=g, op=ALU.mult),
                             reads=[bps[pbk], bg], writes=[bACC[d][j]])
                    else:
                        P.op("dve", lambda e: e.tensor_tensor(out=t, in0=psum[pbk][:, :], in1=g, op=ALU.mult),
                             reads=[bps[pbk], bg], writes=[bt])
                        P.op("pool", lambda e: e.tensor_tensor(out=ACC[:, d, hs(j)], in0=ACC[:, d, hs(j)], in1=t, op=ALU.add),
                             reads=[bt, bACC[d][j]], writes=[bACC[d][j]])
        first_merge[0] = False

    def phase_out(l):
        for j in range(NSUB):
            for k in range(8):
                P.op("act", lambda e: e.activation(out=H[:, k, hs(j)], in_=ACC[:, k, hs(j)], func=AF.Copy),
                     reads=[bACC[k][j]], writes=[bH[j]])
        for hh in range(2):
            sl, bsl = load_slot([(w_out[l, :, hh * 512:(hh + 1) * 512], 8, 0, 512)])
            for dt_ in range(4):
                d = hh * 4 + dt_
                for j in range(NSUB):
                    pi = 2 + (dt_ * NSUB + j) % 4
                    mm_group(psum[pi][:, :], bps[pi], lambda k: sl[:, k, dt_ * 128:(dt_ + 1) * 128], lambda k: H[:, k, hs(j)], 8, [bsl, bH[j]])
                    P.op("dve", lambda e: e.tensor_tensor(out=X[:, d, hs(j)], in0=psum[pi][:, :], in1=X[:, d, hs(j)], op=ALU.add),
                         reads=[bps[pi], bX[d][j]], writes=[bX[d][j]])

    fo_t = sb("fo_t", [128, 2, ST], F32)
    bfo = [Buf(), Buf()]

    def phase_final(p):
        n = 0
        for j in range(NSUB):
            rms_stats(lambda k: X[:, k, hs(j)], lambda k: [bX[k][j]], 8, 1.0 / D, rstd_t, brstd_t)
            for k in range(8):
                i = n % 2
                n += 1
                P.op("dve", lambda e: e.scalar_tensor_tensor(
                    out=fo_t[:, i, :], in0=X[:, k, hs(j)], scalar=col(0, "fw", k),
                    in1=rstd_t[:, :], op0=ALU.mult, op1=ALU.mult),
                    reads=[bX[k][j], brstd_t, bcols], writes=[bfo[i]])
                P.dma("sp", sy, yT[k * 128:(k + 1) * 128, p * TP + j * ST:p * TP + (j + 1) * ST], fo_t[:, i, :], reads=[bfo[i]])

    phases = {"a": phase_a, "b": phase_b, "c": phase_c, "d": phase_d}
    for p in range(NPASS):
        for k in range(8):
            for j in range(NSUB):
                P.dma("sp", sx, X[:, k, hs(j)], xT[k * 128:(k + 1) * 128, p * TP + j * ST:p * TP + (j + 1) * ST], writes=[bX[k][j]])
        for l in range(nlayers):
            phase_norm(l)
            first_merge[0] = True
            for kb, name in enumerate("abcd"):
                if name not in branches:
                    continue
                phases[name](l)
                phase_merge(l, kb)
            phase_out(l)
        phase_final(p)
    P._wait("sp", sy, P.cnt[sy])
    print("program: nins=%d nwaits=%d" % (P.nins, P.nwaits), {k: v for k, v in P.cnt.items() if k in P.eng})
    return nc


_NC_CACHE = {}


def run(inputs, branches=("a", "b", "c", "d"), nlayers=DEPTH, trace=False):
    key = (tuple(branches), nlayers)
    if key not in _NC_CACHE:
        _NC_CACHE[key] = build_nc(branches, nlayers)
    nc = _NC_CACHE[key]
    inp = {k: np.asarray(v) for k, v in inputs.items()}
    x = inp["x"].astype(np.float32)
    s5 = [host_s5(inp, l) for l in range(DEPTH)]
    shared = {
        "w_in": np.ascontiguousarray(inp["w_in"], dtype=np.float32),
        "w_branch": np.ascontiguousarray(inp["w_branch"], dtype=np.float32),
        "w_out": np.ascontiguousarray(inp["w_out"], dtype=np.float32),
        "w_glu": np.ascontiguousarray(inp["s5_w_glu"], dtype=np.float32),
        "cols": np.stack([host_cols(inp, l) for l in range(DEPTH)], 0),
        "consts": host_consts(),
        "rows": np.stack([host_rows(inp, l) for l in range(DEPTH)], 0),
        "sguw": np.ascontiguousarray(inp["sgu_w"].transpose(0, 3, 1, 2), dtype=np.float32),
        "sgub": np.ascontiguousarray(np.repeat(inp["sgu_b"].reshape(DEPTH, 4, 2, 1, 128), 64, axis=3).transpose(0, 2, 3, 1, 4).reshape(DEPTH, 128, 512), dtype=np.float32),
        "s5p": np.stack([s[0] for s in s5], 0),
        "s5q": np.stack([s[1] for s in s5], 0),
        "s5c": np.stack([s[2] for s in s5], 0),
    }
    in_maps = []
    for b in range(8):
        m = dict(shared)
        m["xT"] = np.ascontiguousarray(x[b].T)
        in_maps.append(m)
    res = run_bass_kernel_spmd(nc, in_maps, core_ids=list(range(8)), trace=trace)
    out = np.stack([np.ascontiguousarray(res.results[b]["yT"].T) for b in range(8)], 0).astype(np.float32)
    return out, res


def kernel(**inputs):
    out, _ = run(inputs)
    return out
```

```python
import os
import numpy as np
import concourse.bass as bass
import concourse.mybir as mybir
from concourse.bass_utils import run_bass_kernel_spmd

F32 = mybir.dt.float32
BF16 = mybir.dt.bfloat16
ALU = mybir.AluOpType
AF = mybir.ActivationFunctionType

D = 1024
SEQ = 2048
DEPTH = 2
W = 512
IN_DIM = 10248
TP = 1024
NPASS = SEQ // TP
ST = 512
NSUB = TP // ST
EPS = 1e-6

O_S5U, O_S5G = 0, 512
O_SGU, O_SGV, O_SGG = 1024, 1536, 2048
O_M2Z, O_M2X, O_M2DT = 2560, 3072, 4096
O_SCB, O_SCC, O_SCH, O_SCG = 4104, 4616, 5128, 5640
O_MG = 6152


class Buf:
    __slots__ = ("w", "r")

    def __init__(self):
        self.w = None
        self.r = {}


class Prog:
    def __init__(self, nc):
        self.nc = nc
        self.eng = {"pe": nc.tensor, "act": nc.scalar, "dve": nc.vector, "pool": nc.gpsimd, "sp": nc.sync}
        self.sem = {}
        self.cnt = {}
        self.seen = {e: {} for e in self.eng}
        self.pend = {e: [] for e in self.eng}
        for e in self.eng:
            self.sem[e] = nc.alloc_semaphore("s_" + e)
            self.cnt[e] = 0
        self.nwaits = 0
        self.nins = 0

    def new_sem(self, name):
        self.sem[name] = self.nc.alloc_semaphore(name)
        self.cnt[name] = 0
        return name

    def _wait(self, e, key, val):
        if key not in self.eng:
            val = self.cnt[key]
        if self.seen[e].get(key, 0) >= val:
            return
        self.seen[e][key] = val
        self.eng[e].wait_ge(self.sem[key], val)
        self.nwaits += 1

    def _deps(self, e, reads, writes):
        deps = {}
        for b in reads:
            if b.w is not None:
                k, v = b.w
                if deps.get(k, 0) < v:
                    deps[k] = v
        for b in writes:
            if b.w is not None:
                k, v = b.w
                if deps.get(k, 0) < v:
                    deps[k] = v
            for k, v in b.r.items():
                if deps.get(k, 0) < v:
                    deps[k] = v
        for k, v in deps.items():
            if k == e and (e == "pe" or v > self.cnt[e]):
                continue
            self._wait(e, k, v)

    def _commit(self, k, v, reads, writes):
        for b in reads:
            b.r[k] = v
        for b in writes:
            b.w = (k, v)
            b.r = {}

    def op(self, e, fn, reads=(), writes=(), inc=True):
        self._deps(e, reads, writes)
        ins = fn(self.eng[e])
        self.nins += 1
        if inc:
            self.cnt[e] += 1
            ins.then_inc(self.sem[e], 1)
            self._commit(e, self.cnt[e], reads, writes)
        else:
            v = self.cnt[e] + 1
            self._commit(e, v, reads, writes)

    def barrier(self):
        for e in self.eng:
            for k, v in self.cnt.items():
                if k != e and v > 0:
                    self._wait(e, k, v)

    def dma(self, q, semkey, out, in_, reads=(), writes=(), **kw):
        self._deps(q, reads, writes)
        ins = self.eng[q].dma_start(out=out, in_=in_, **kw)
        self.cnt[semkey] += 16
        ins.then_inc(self.sem[semkey], 16)
        self.nins += 1
        self._commit(semkey, self.cnt[semkey], reads, writes)


def col_layout():
    off = {}
    n = 0

    def add(name, w):
        nonlocal n
        off[name] = n
        n += w
    add("nw", 8)
    add("mb", 32)
    add("scw", 12)
    add("fw", 8)
    add("m2cw", 32)
    add("m2cb", 8)
    add("m2d", 4)
    add("m2nw", 4)
    add("s5d", 4)
    add("mkB", 8)
    add("mkC", 2)
    add("mkCn", 2)
    return off, n


COLOFF, NCOL = col_layout()


def host_cols(inp, l):
    c = np.zeros((128, NCOL), np.float32)
    c[:, COLOFF["nw"]:COLOFF["nw"] + 8] = inp["norm_w"][l].reshape(8, 128).T
    c[:, COLOFF["mb"]:COLOFF["mb"] + 32] = inp["merge_b"][l].reshape(32, 128).T
    c[:, COLOFF["scw"]:COLOFF["scw"] + 12] = inp["sc_conv_w"][l].reshape(12, 128).T
    c[:, COLOFF["fw"]:COLOFF["fw"] + 8] = inp["final_norm_w"].reshape(8, 128).T
    c[:, COLOFF["m2cw"]:COLOFF["m2cw"] + 32] = inp["m2_conv_w"][l].reshape(32, 128).T
    c[:, COLOFF["m2cb"]:COLOFF["m2cb"] + 8] = inp["m2_conv_b"][l].reshape(8, 128).T
    c[:, COLOFF["m2d"]:COLOFF["m2d"] + 4] = np.repeat(inp["m2_d"][l], 64).reshape(4, 128).T
    c[:, COLOFF["m2nw"]:COLOFF["m2nw"] + 4] = inp["m2_norm_w"][l].reshape(4, 128).T
    c[:, COLOFF["s5d"]:COLOFF["s5d"] + 4] = inp["s5_d"][l].reshape(4, 128).T
    gl = np.arange(128) // 16
    for jj in range(4):
        for g2 in range(2):
            c[:, COLOFF["mkB"] + jj * 2 + g2] = (gl == 2 * jj + g2)
    g2p = np.arange(128) // 64
    for g2 in range(2):
        c[:, COLOFF["mkC"] + g2] = (g2p == g2)
        c[:, COLOFF["mkCn"] + g2] = -1.0 * (g2p == g2)
    return c


def host_consts():
    i = np.arange(128)
    k = np.zeros((128, 5, 128), np.float32)
    k[:, 0] = np.eye(128)
    k[:, 1] = (i[:, None] <= i[None, :])
    k[:, 2] = (i[:, None] > i[None, :])
    k[:, 3] = 1.0
    k[:, 4] = (i[:, None] // 16 == i[None, :] // 16)
    return k


def host_rows(inp, l):
    r = np.zeros((128, 1040), np.float32)
    r[:, 0:512] = inp["sgu_ln_w"][l][None, :]
    r[:, 512:1024] = inp["sgu_ln_b"][l][None, :]
    r[:, 1024:1032] = inp["m2_dt_bias"][l][None, :]
    r[:, 1032:1040] = inp["m2_a_log"][l][None, :]
    return r


def host_s5(inp, l):
    G, N, Pq = 32, 64, 16
    def L2(a_gn):
        a = a_gn.reshape(4, 8, N)
        a = np.repeat(a[:, :, None, :], 16, axis=2)
        return a.transpose(1, 2, 0, 3).reshape(128, 256)
    def L2b(b_gnq):
        a = b_gnq.reshape(4, 8, N, Pq)
        return a.transpose(1, 3, 0, 2).reshape(128, 256)
    p5 = np.stack([L2(inp["s5_lambda_re"][l]), L2(inp["s5_lambda_im"][l]),
                   L2(np.repeat(inp["s5_log_step"][l][:, None], N, 1)),
                   L2b(inp["s5_b_re"][l]), L2b(inp["s5_b_im"][l])], 1)
    def PL(a_gn):
        return a_gn.reshape(16, 2, N).transpose(1, 2, 0).reshape(128, 16)
    pq = np.stack([PL(inp["s5_lambda_re"][l]), PL(inp["s5_lambda_im"][l]),
                   PL(np.repeat(inp["s5_log_step"][l][:, None], N, 1))], 1)
    def PLc(c_gpn):
        return c_gpn.reshape(16, 2, Pq, N).transpose(1, 3, 2, 0).reshape(128, 256)
    def PLb(b_gnq):
        return b_gnq.reshape(16, 2, N, Pq).transpose(1, 2, 3, 0).reshape(128, 256)
    pc = np.stack([PLc(inp["s5_c_re"][l]), PLc(inp["s5_c_im"][l]),
                   PLb(inp["s5_b_re"][l]), PLb(inp["s5_b_im"][l])], 1)
    return p5.astype(np.float32), pq.astype(np.float32), pc.astype(np.float32)


def build_nc(branches=("a", "b", "c", "d"), nlayers=DEPTH):
    nc = bass.Bass("TRN2", target_bir_lowering=False)
    xT = nc.dram_tensor("xT", [D, SEQ], F32, kind="ExternalInput").ap()
    w_in = nc.dram_tensor("w_in", [DEPTH, D, IN_DIM], F32, kind="ExternalInput").ap()
    w_br = nc.dram_tensor("w_branch", [DEPTH, 4, W, D], F32, kind="ExternalInput").ap()
    w_out = nc.dram_tensor("w_out", [DEPTH, D, D], F32, kind="ExternalInput").ap()
    w_glu = nc.dram_tensor("w_glu", [DEPTH, W, W], F32, kind="ExternalInput").ap()
    cols_d = nc.dram_tensor("cols", [DEPTH, 128, NCOL], F32, kind="ExternalInput").ap()
    consts_d = nc.dram_tensor("consts", [128, 5, 128], F32, kind="ExternalInput").ap()
    rows_d = nc.dram_tensor("rows", [DEPTH, 128, 1040], F32, kind="ExternalInput").ap()
    sguw_d = nc.dram_tensor("sguw", [DEPTH, 128, 8, 128], F32, kind="ExternalInput").ap()
    sgub_d = nc.dram_tensor("sgub", [DEPTH, 128, 512], F32, kind="ExternalInput").ap()
    s5p_d = nc.dram_tensor("s5p", [DEPTH, 128, 5, 256], F32, kind="ExternalInput").ap()
    s5q_d = nc.dram_tensor("s5q", [DEPTH, 128, 3, 16], F32, kind="ExternalInput").ap()
    s5c_d = nc.dram_tensor("s5c", [DEPTH, 128, 4, 256], F32, kind="ExternalInput").ap()
    yT = nc.dram_tensor("yT", [D, SEQ], F32, kind="ExternalOutput").ap()

    P = Prog(nc)
    sb = nc.alloc_sbuf_tensor
    X = sb("X", [128, 8, TP], F32)
    H = sb("H", [128, 8, TP], BF16)
    ACC = sb("ACC", [128, 8, TP], F32)
    Y = sb("Y", [128, 4, TP], BF16)
    cols = sb("colsb", [128, DEPTH, NCOL], F32)
    consts = sb("constsb", [128, 5, 128], F32)
    identb = sb("identb", [128, 128], BF16)
    mask01b = sb("mask01b", [128, 128], BF16)
    bX = [[Buf() for _ in range(NSUB)] for _ in range(8)]
    bH = [Buf() for _ in range(NSUB)]
    bACC = [[Buf() for _ in range(NSUB)] for _ in range(8)]
    bY = [[Buf() for _ in range(NSUB)] for _ in range(4)]
    bcols, bconsts = Buf(), Buf()
    ident, triu, ltstrict, ones = consts[:, 0, :], consts[:, 1, :], consts[:, 2, :], consts[:, 3, :]
    blockmask = consts[:, 4, :]
    zerob = sb("zerob", [128, 128], BF16)
    bzerob = Buf()
    bones = bconsts

    NS = 4
    slots = [sb("slot%d" % i, [128, 8 * 512], BF16) for i in range(NS)]
    bslot = [Buf() for _ in range(NS)]
    sslot = [P.new_sem("dslot%d" % i) for i in range(NS)]
    slot_rr = [0]

    psbig = nc.alloc_psum_tensor("psbig", [128, 7 * 512], F32)
    psum = [psbig[:, i * 512:(i + 1) * 512] for i in range(7)]
    psT = nc.alloc_psum_tensor("psT", [128, 512], F32)
    bps = [Buf() for _ in range(7)]
    bpsT = Buf()
    psum.append(psT[:, :])
    bps.append(bpsT)

    SCRF = 16600
    scrF = sb("scrF", [128, SCRF], F32)
    scr_pos = [0]

    def scratch_reset(pos=0):
        P.barrier()
        scr_pos[0] = pos

    def falloc(n, parts=128):
        a = scrF[0:parts, scr_pos[0]:scr_pos[0] + n]
        scr_pos[0] += n
        assert scr_pos[0] <= SCRF, scr_pos
        return a, Buf()

    def balloc(n):
        m = (n + 1) // 2
        a = scrF[:, scr_pos[0]:scr_pos[0] + m].bitcast(BF16)[:, 0:n]
        scr_pos[0] += m
        assert scr_pos[0] <= SCRF, scr_pos
        return a, Buf()

    sx = P.new_sem("dx")
    sy = P.new_sem("dy")
    sc = P.new_sem("dc")
    sm = P.new_sem("dm")
    sw8 = P.new_sem("dw8")

    P.dma("sp", sc, cols[:, :, :], cols_d.rearrange("l p n -> p l n"), writes=[bcols])
    P.dma("sp", sc, consts[:, :, :], consts_d, writes=[bconsts])
    bidb = Buf()
    P.op("dve", lambda e: e.tensor_copy(out=identb[:, :], in_=ident), reads=[bconsts], writes=[bidb])
    P.op("dve", lambda e: e.tensor_copy(out=mask01b[:, :], in_=triu), reads=[bconsts], writes=[bidb])
    epsb = sb("epsb", [128, 1], F32)
    bepsb = Buf()
    P.op("pool", lambda e: e.memset(epsb[:, :], EPS), writes=[bepsb])
    P.op("pool", lambda e: e.memset(zerob[:, :], 0.0), writes=[bzerob])

    def col(l, name, j=0):
        o = COLOFF[name] + j
        return cols[:, l, o:o + 1]

    def sched_layer(l):
        out = []

        def mg(kb):
            for hh in range(2):
                out.append([(w_in[l, :, O_MG + kb * D + hh * 512:O_MG + kb * D + (hh + 1) * 512], 8, 0, 512)])
                out.append([(w_br[l, kb, :, hh * 512:(hh + 1) * 512], 4, 0, 512)])
        if "a" in branches:
            out.append([(w_in[l, :, O_S5U:O_S5U + 512], 8, 0, 512)])
            out.append([(w_glu[l, :, :], 4, 0, 512)])
            out.append([(w_in[l, :, O_S5G:O_S5G + 512], 8, 0, 512)])
            mg(0)
        if "b" in branches:
            out.append([(w_in[l, :, O_SGV:O_SGV + 512], 8, 0, 512)])
            for c in range(4):
                out.append([(w_in[l, :, O_SGU + c * 128:O_SGU + (c + 1) * 128], 8, 0, 128),
                            (w_in[l, :, O_SGG + c * 128:O_SGG + (c + 1) * 128], 8, 128, 128)])
            mg(1)
        if "c" in branches:
            out.append([(w_in[l, :, O_M2DT - 120:O_M2DT + 8], 8, 0, 128)])
            out.append([(w_in[l, :, O_M2Z:O_M2Z + 512], 8, 0, 512)])
            for hh in range(2):
                out.append([(w_in[l, :, O_M2X + hh * 512:O_M2X + (hh + 1) * 512], 8, 0, 512)])
            mg(2)
        if "d" in branches:
            for c in range(4):
                out.append([(w_in[l, :, o + c * 128:o + (c + 1) * 128], 8, i * 128, 128) for i, o in enumerate((O_SCB, O_SCC, O_SCH, O_SCG))])
            mg(3)
        for hh in range(2):
            out.append([(w_out[l, :, hh * 512:(hh + 1) * 512], 8, 0, 512)])
        return out

    schedule = [e for _p in range(NPASS) for l in range(nlayers) for e in sched_layer(l)]
    LOOK = 2
    emitted = [0]

    def _emit_load(j):
        i = j % NS
        s = slots[i]
        for src, kt, off, n in schedule[j]:
            dst = s[:, :].rearrange("p (k n) -> p k n", k=8)[:, 0:kt, off:off + n]
            P.dma("pool", sslot[i], dst, src.rearrange("(k p) n -> p k n", p=128), writes=[bslot[i]])

    def load_slot(pieces):
        j = slot_rr[0]
        slot_rr[0] += 1
        assert [(kt, off, n) for _, kt, off, n in pieces] == [(kt, off, n) for _, kt, off, n in schedule[j]], j
        while emitted[0] < min(len(schedule), j + 1 + LOOK):
            _emit_load(emitted[0])
            emitted[0] += 1
        i = j % NS
        return slots[i][:, :].rearrange("p (k n) -> p k n", k=8), bslot[i]

    def mm_group(out_ap, bout, lhs_fn, rhs_fn, nk, reads):
        for k in range(nk):
            P.op("pe", lambda e: e.matmul(out_ap, lhs_fn(k), rhs_fn(k), start=(k == 0), stop=(k == nk - 1)),
                 reads=reads, writes=[bout], inc=(k == nk - 1))

    def hs(j):
        return slice(j * ST, (j + 1) * ST)

    carry_sc = [[sb("csc%d_%d" % (l, c), [128, 2], F32) for c in range(4)] for l in range(DEPTH)]
    bcarry_sc = [[Buf() for c in range(4)] for l in range(DEPTH)]
    carry_m2 = [[sb("cm2%d_%d" % (l, c), [128, 3], F32) for c in range(8)] for l in range(DEPTH)]
    bcarry_m2 = [[Buf() for c in range(8)] for l in range(DEPTH)]
    ST_m2 = [sb("stm2_%d" % l, [128, 512], F32) for l in range(DEPTH)]
    bST_m2 = [Buf() for l in range(DEPTH)]
    carry_s5 = [sb("cs5_%d" % l, [128, 2, 16], F32) for l in range(DEPTH)]
    bcarry_s5 = [Buf() for l in range(DEPTH)]
    for l in range(DEPTH):
        for c in range(4):
            P.op("pool", lambda e: e.memset(carry_sc[l][c][:, :], 0.0), writes=[bcarry_sc[l][c]])
        for c in range(8):
            P.op("pool", lambda e: e.memset(carry_m2[l][c][:, :], 0.0), writes=[bcarry_m2[l][c]])
        P.op("pool", lambda e: e.memset(ST_m2[l][:, :], 0.0), writes=[bST_m2[l]])
        P.op("pool", lambda e: e.memset(carry_s5[l][:, :, :], 0.0), writes=[bcarry_s5[l]])

    def rms_stats(src_fn, breads, nk, scale, rstd, brstd, n=ST, lnexp=False):
        sq, bsq = falloc_sq[0]
        for k in range(nk):
            P.op("act", lambda e: e.activation(out=sq[:, 0:n], in_=src_fn(k), func=AF.Square), reads=breads(k), writes=[bsq])
            P.op("pe", lambda e: e.matmul(psum[6][:, 0:n], ones, sq[:, 0:n], start=(k == 0), stop=(k == nk - 1)),
                 reads=[bsq, bones], writes=[bps[6]])
        if lnexp:
            P.op("act", lambda e: e.activation(out=rstd[:, 0:n], in_=psum[6][:, 0:n], func=AF.Ln, bias=epsb[:, 0:1], scale=scale),
                 reads=[bps[6], bepsb], writes=[brstd])
            P.op("act", lambda e: e.activation(out=rstd[:, 0:n], in_=rstd[:, 0:n], func=AF.Exp, scale=-0.5), reads=[brstd], writes=[brstd])
            return
        P.op("act", lambda e: e.activation(out=rstd[:, 0:n], in_=psum[6][:, 0:n], func=AF.Sqrt, bias=epsb[:, 0:1], scale=scale),
             reads=[bps[6], bepsb], writes=[brstd])
        P.op("dve", lambda e: e.reciprocal(out=rstd[:, 0:n], in_=rstd[:, 0:n]), reads=[brstd], writes=[brstd])

    sq_t = sb("sq_t", [128, ST], F32)
    falloc_sq = [(sq_t, Buf())]
    rstd_t = sb("rstd_t", [128, ST], F32)
    brstd_t = Buf()

    def phase_norm(l):
        for j in range(NSUB):
            rms_stats(lambda k: X[:, k, hs(j)], lambda k: [bX[k][j]], 8, 1.0 / D, rstd_t, brstd_t)
            for k in range(8):
                P.op("dve", lambda e: e.scalar_tensor_tensor(
                    out=H[:, k, hs(j)], in0=X[:, k, hs(j)], scalar=col(l, "nw", k),
                    in1=rstd_t[:, :], op0=ALU.mult, op1=ALU.mult),
                    reads=[bX[k][j], brstd_t, bcols], writes=[bH[j]])

    def phase_d(l):
        scratch_reset()
        pbufs = [falloc(2 + TP) for _ in range(2)]
        hsbs = [falloc(ST) for _ in range(2)]
        qs = [falloc(ST) for _ in range(2)]
        yvs = [falloc(ST) for _ in range(2)]
        sgs = [falloc(ST) for _ in range(2)]
        for c in range(4):
            sl, bsl = load_slot([(w_in[l, :, o + c * 128:o + (c + 1) * 128], 8, i * 128, 128)
                                 for i, o in enumerate((O_SCB, O_SCC, O_SCH, O_SCG))])
            pbuf, bp = pbufs[c % 2]
            P.op("pool", lambda e: e.tensor_copy(out=pbuf[:, 0:2], in_=carry_sc[l][c][:, :]), reads=[bcarry_sc[l][c]], writes=[bp])
            for j in range(NSUB):
                pb = 3 * (j % 2)
                pgate = 6 + (j % 2)
                (hsb, bhsb), (q, bq), (yv, byv), (sg, bsg) = hsbs[j % 2], qs[j % 2], yvs[j % 2], sgs[j % 2]
                for i in range(3):
                    mm_group(psum[pb + i][:, :], bps[pb + i], lambda k: sl[:, k, i * 128:(i + 1) * 128], lambda k: H[:, k, hs(j)], 8, [bsl, bH[j]])
                mm_group(psum[pgate][:, :], bps[pgate], lambda k: sl[:, k, 384:512], lambda k: H[:, k, hs(j)], 8, [bsl, bH[j]])
                P.op("act", lambda e: e.activation(out=hsb, in_=psum[pb + 2][:, :], func=AF.Copy), reads=[bps[pb + 2]], writes=[bhsb])
                P.op("dve", lambda e: e.tensor_tensor(out=pbuf[:, 2 + j * ST:2 + (j + 1) * ST], in0=psum[pb + 1][:, :], in1=hsb, op=ALU.mult),
                     reads=[bps[pb + 1], bhsb], writes=[bp])
                P.op("dve", lambda e: e.tensor_scalar(out=q, in0=pbuf[:, 2 + j * ST:2 + (j + 1) * ST], scalar1=col(l, "scw", 8 + c), scalar2=None, op0=ALU.mult),
                     reads=[bp, bcols], writes=[bq])
                P.op("dve", lambda e: e.scalar_tensor_tensor(out=q, in0=pbuf[:, 1 + j * ST:1 + (j + 1) * ST], scalar=col(l, "scw", 4 + c), in1=q, op0=ALU.mult, op1=ALU.add),
                     reads=[bp, bq, bcols], writes=[bq])
                P.op("dve", lambda e: e.scalar_tensor_tensor(out=q, in0=pbuf[:, j * ST:(j + 1) * ST], scalar=col(l, "scw", c), in1=q, op0=ALU.mult, op1=ALU.add),
                     reads=[bp, bq, bcols], writes=[bq])
                P.op("dve", lambda e: e.tensor_tensor(out=yv, in0=psum[pb][:, :], in1=q, op=ALU.mult), reads=[bps[pb], bq], writes=[byv])
                P.op("act", lambda e: e.activation(out=sg, in_=psum[pgate][:, :], func=AF.Silu), reads=[bps[pgate]], writes=[bsg])
                P.op("pool", lambda e: e.tensor_tensor(out=Y[:, c, hs(j)], in0=yv, in1=sg, op=ALU.mult),
                     reads=[byv, bsg], writes=[bY[c][j]])
            P.op("pool", lambda e: e.tensor_copy(out=carry_sc[l][c][:, :], in_=pbuf[:, TP:TP + 2]), reads=[bp], writes=[bcarry_sc[l][c]])

    def phase_b(l):
        scratch_reset()
        lnw, blnw = falloc(512)
        lnb, blnb = falloc(512)
        wraw, bwraw = falloc(1024)
        bsrow, bbsrow = falloc(512)
        v32s = [falloc(512) for _ in range(2)]
        vns = [falloc(512) for _ in range(2)]
        st6s = [falloc(6) for _ in range(2)]
        mvs = [falloc(2) for _ in range(2)]
        rss = [falloc(1) for _ in range(2)]
        gus = [falloc(ST) for _ in range(2)]
        sgs = [falloc(ST) for _ in range(2)]
        t1s = [falloc(ST) for _ in range(2)]
        wmT, bwmT = balloc(1024)
        VN, bVN = balloc(8 * 512)
        bVNq = [Buf() for _ in range(8)]
        P.dma("sp", sm, lnw, rows_d[l, :, 0:512], writes=[blnw])
        P.dma("sp", sm, lnb, rows_d[l, :, 512:1024], writes=[blnb])
        P.dma("sp", sm, wraw, sguw_d[l].rearrange("s h t -> s (h t)"), writes=[bwraw])
        P.dma("sp", sm, bsrow, sgub_d[l], writes=[bbsrow])
        bsv = bsrow.rearrange("p (c t) -> p c t", c=4)
        P.op("dve", lambda e: e.tensor_tensor(out=wmT.rearrange("p (h t) -> p h t", h=8), in0=wraw.rearrange("p (h t) -> p h t", h=8),
                                             in1=triu.unsqueeze(1).broadcast_to([128, 8, 128]), op=ALU.mult),
             reads=[bwraw, bconsts], writes=[bwmT])
        slv, bslv = load_slot([(w_in[l, :, O_SGV:O_SGV + 512], 8, 0, 512)])
        for qc in range(TP // 128):
            pi = qc % 2
            (v32, bv32), (vn, bvn), (st6, bst6), (mv, bmv), (rs, brs) = v32s[pi], vns[pi], st6s[pi], mvs[pi], rss[pi]
            mm_group(psum[pi][:, :], bps[pi], lambda k: H[:, k, qc * 128:(qc + 1) * 128], lambda k: slv[:, k, 0:512], 8, [bslv, bH[qc // 4]])
            P.op("act", lambda e: e.activation(out=v32, in_=psum[pi][:, :], func=AF.Gelu), reads=[bps[pi]], writes=[bv32])
            P.op("dve", lambda e: e.bn_stats(out=st6, in_=v32), reads=[bv32], writes=[bst6])
            P.op("dve", lambda e: e.bn_aggr(out=mv, in_=st6), reads=[bst6], writes=[bmv])
            P.op("act", lambda e: e.activation(out=rs, in_=mv[:, 1:2], func=AF.Sqrt, bias=epsb[:, 0:1], scale=1.0), reads=[bmv, bepsb], writes=[brs])
            P.op("dve", lambda e: e.reciprocal(out=rs, in_=rs), reads=[brs], writes=[brs])
            P.op("dve", lambda e: e.tensor_scalar(out=vn, in0=v32, scalar1=mv[:, 0:1], scalar2=rs, op0=ALU.subtract, op1=ALU.mult),
                 reads=[bv32, bmv, brs], writes=[bvn])
            P.op("pool", lambda e: e.tensor_tensor(out=vn, in0=vn, in1=lnw, op=ALU.mult), reads=[bvn, blnw], writes=[bvn])
            P.op("pool", lambda e: e.tensor_tensor(out=VN[:, qc * 512:(qc + 1) * 512], in0=vn, in1=lnb, op=ALU.add), reads=[bvn, blnb], writes=[bVNq[qc]])
        for c in range(4):
            sl, bsl = load_slot([(w_in[l, :, O_SGU + c * 128:O_SGU + (c + 1) * 128], 8, 0, 128),
                                 (w_in[l, :, O_SGG + c * 128:O_SGG + (c + 1) * 128], 8, 128, 128)])
            for j in range(NSUB):
                pu, pg, pss = (2, 3, 4) if j % 2 == 0 else (6, 7, 5)
                (gu, bgu), (sg, bsg), (t1, bt1) = gus[j % 2], sgs[j % 2], t1s[j % 2]
                mm_group(psum[pu][:, :], bps[pu], lambda k: sl[:, k, 0:128], lambda k: H[:, k, hs(j)], 8, [bsl, bH[j]])
                mm_group(psum[pg][:, :], bps[pg], lambda k: sl[:, k, 128:256], lambda k: H[:, k, hs(j)], 8, [bsl, bH[j]])
                P.op("act", lambda e: e.activation(out=gu, in_=psum[pu][:, :], func=AF.Gelu), reads=[bps[pu]], writes=[bgu])
                P.op("act", lambda e: e.activation(out=sg, in_=psum[pg][:, :], func=AF.Silu), reads=[bps[pg]], writes=[bsg])
                for qq in range(4):
                    qc = j * 4 + qq
                    for h2 in range(2):
                        h = 2 * c + h2
                        o = psum[pss][h2 * 64:(h2 + 1) * 64, qq * 128:(qq + 1) * 128]
                        P.op("pe", lambda e: e.matmul(o, VN[:, qc * 512 + h * 64:qc * 512 + (h + 1) * 64], wmT[:, h * 128:(h + 1) * 128], start=True, stop=True),
                             reads=[bVNq[qc], bwmT], writes=[bps[pss]], inc=True)
                P.op("dve", lambda e: e.tensor_tensor(out=t1.rearrange("p (q t) -> p q t", q=4), in0=psum[pss][:, :].rearrange("p (q t) -> p q t", q=4),
                                                     in1=bsv[:, c, :].unsqueeze(1).broadcast_to([128, 4, 128]), op=ALU.add), reads=[bps[pss], bbsrow], writes=[bt1])
                P.op("dve", lambda e: e.tensor_tensor(out=t1, in0=t1, in1=gu, op=ALU.mult), reads=[bt1, bgu], writes=[bt1])
                P.op("pool", lambda e: e.tensor_tensor(out=Y[:, c, hs(j)], in0=t1, in1=sg, op=ALU.mult), reads=[bt1, bsg], writes=[bY[c][j]])

    def phase_c(l):
        scratch_reset()
        NQ = TP // 128
        dtb, bdtb = falloc(8)
        alog, balog = falloc(8)
        a_t, ba_t = falloc(8)
        dt, bdt = falloc(64)
        da, bda = falloc(64)
        csc, bcsc = falloc(64)
        dec, bdec = falloc(64)
        eA, beA = falloc(64)
        dtdec, bdtdec = falloc(64)
        cbufs = [falloc(3 + TP) for _ in range(2)]
        qvs = [falloc(ST) for _ in range(2)]
        ltls = [falloc(128) for _ in range(8)]
        reps = [falloc(128) for _ in range(4)]
        erows = [falloc(128) for _ in range(2)]
        t1s = [falloc(128) for _ in range(2)]
        yqs = [falloc(512) for _ in range(2)]
        rstds = [falloc(128) for _ in range(2)]
        xTb, bxT = balloc(4 * TP)
        BTb, bBT = balloc(2 * TP)
        CTb, bCT = balloc(2 * TP)
        ZS, bZS = balloc(4 * TP)
        xdts = [balloc(512) for _ in range(2)]
        xdds = [balloc(512) for _ in range(2)]
        Btoks = [balloc(256) for _ in range(2)]
        scms = [balloc(256) for _ in range(2)]
        LTms = [balloc(128) for _ in range(2)]
        MTs = [balloc(128) for _ in range(2)]
        STbs = [balloc(512) for _ in range(2)]
        STb, bSTb = STbs[0]
        P.dma("sp", sm, dtb, rows_d[l, :, 1024:1032], writes=[bdtb])
        P.dma("sp", sm, alog, rows_d[l, :, 1032:1040], writes=[balog])
        slw8, bwdt = load_slot([(w_in[l, :, O_M2DT - 120:O_M2DT + 8], 8, 0, 128)])
        P.op("act", lambda e: e.activation(out=a_t, in_=alog, func=AF.Exp), reads=[balog], writes=[ba_t])
        P.op("dve", lambda e: e.tensor_scalar(out=a_t, in0=a_t, scalar1=-1.0, scalar2=None, op0=ALU.mult), reads=[ba_t], writes=[ba_t])
        P.op("act", lambda e: e.activation(out=STb, in_=ST_m2[l][:, :], func=AF.Copy), reads=[bST_m2[l]], writes=[bSTb])
        wdtv = slw8[:, :, 120:128]
        for qc in range(NQ):
            mm_group(psum[0][:, qc * 8:(qc + 1) * 8], bps[0], lambda k: H[:, k, qc * 128:(qc + 1) * 128], lambda k: wdtv[:, k, :], 8, [bwdt, bH[qc // 4]])
        P.op("dve", lambda e: e.tensor_tensor(out=dt.rearrange("p (q h) -> p q h", h=8), in0=psum[0][:, 0:64].rearrange("p (q h) -> p q h", h=8),
                                             in1=dtb.unsqueeze(1).broadcast_to([128, NQ, 8]), op=ALU.add), reads=[bps[0], bdtb], writes=[bdt])
        P.op("act", lambda e: e.activation(out=dt, in_=dt, func=AF.Exp), reads=[bdt], writes=[bdt])
        P.op("act", lambda e: e.activation(out=dt, in_=dt, func=AF.Ln, bias=1.0), reads=[bdt], writes=[bdt])
        P.op("dve", lambda e: e.tensor_tensor(out=da.rearrange("p (q h) -> p q h", h=8), in0=dt.rearrange("p (q h) -> p q h", h=8),
                                             in1=a_t.unsqueeze(1).broadcast_to([128, NQ, 8]), op=ALU.mult), reads=[bdt, ba_t], writes=[bda])
        P.op("pe", lambda e: e.matmul(psum[0][:, 64:128], triu, da, start=True, stop=True), reads=[bda, bconsts], writes=[bps[0]])
        P.op("pe", lambda e: e.matmul(psum[0][:, 128:192], ones, da, start=True, stop=True), reads=[bda, bconsts], writes=[bps[0]])
        P.op("act", lambda e: e.activation(out=csc, in_=psum[0][:, 64:128], func=AF.Copy), reads=[bps[0]], writes=[bcsc])
        P.op("dve", lambda e: e.tensor_tensor(out=dec, in0=psum[0][:, 128:192], in1=csc, op=ALU.subtract), reads=[bps[0], bcsc], writes=[bdec])
        P.op("act", lambda e: e.activation(out=dec, in_=dec, func=AF.Exp), reads=[bdec], writes=[bdec])
        P.op("act", lambda e: e.activation(out=eA, in_=psum[0][:, 128:192], func=AF.Exp), reads=[bps[0]], writes=[beA])
        P.op("dve", lambda e: e.tensor_tensor(out=dtdec, in0=dt, in1=dec, op=ALU.mult), reads=[bdt, bdec], writes=[bdtdec])
        CSTOP = int(os.environ.get("CSTOP", "9"))
        if CSTOP <= 1:
            return
        slz, bslz = load_slot([(w_in[l, :, O_M2Z:O_M2Z + 512], 8, 0, 512)])
        n = 0
        for ct in range(4):
            for j in range(NSUB):
                pi = 1 + n % 2
                n += 1
                mm_group(psum[pi][:, :], bps[pi], lambda k: slz[:, k, ct * 128:(ct + 1) * 128], lambda k: H[:, k, hs(j)], 8, [bslz, bH[j]])
                P.op("act", lambda e: e.activation(out=ZS[:, ct * TP + j * ST:ct * TP + (j + 1) * ST], in_=psum[pi][:, :], func=AF.Silu), reads=[bps[pi]], writes=[bZS])
        slx = [load_slot([(w_in[l, :, O_M2X + hh * 512:O_M2X + (hh + 1) * 512], 8, 0, 512)]) for hh in range(2)]
        for ct in range(8):
            sl, bsl = slx[ct // 4]
            cbuf, bcbuf = cbufs[ct % 2]
            P.op("pool", lambda e: e.tensor_copy(out=cbuf[:, 0:3], in_=carry_m2[l][ct][:, :]), reads=[bcarry_m2[l][ct]], writes=[bcbuf])
            for j in range(NSUB):
                pi = 1 + n % 2
                n += 1
                mm_group(psum[pi][:, :], bps[pi], lambda k: sl[:, k, (ct % 4) * 128:(ct % 4 + 1) * 128], lambda k: H[:, k, hs(j)], 8, [bsl, bH[j]])
                P.op("act", lambda e: e.activation(out=cbuf[:, 3 + j * ST:3 + (j + 1) * ST], in_=psum[pi][:, :], func=AF.Copy), reads=[bps[pi]], writes=[bcbuf])
            for j in range(NSUB):
                qv, bqv = qvs[j % 2]
                P.op("dve", lambda e: e.tensor_scalar(out=qv, in0=cbuf[:, 3 + j * ST:3 + (j + 1) * ST], scalar1=col(l, "m2cw", 24 + ct), scalar2=col(l, "m2cb", ct), op0=ALU.mult, op1=ALU.add),
                     reads=[bcbuf, bcols], writes=[bqv])
                for tap in range(3):
                    P.op("dve", lambda e: e.scalar_tensor_tensor(out=qv, in0=cbuf[:, tap + j * ST:tap + (j + 1) * ST], scalar=col(l, "m2cw", tap * 8 + ct), in1=qv, op0=ALU.mult, op1=ALU.add),
                         reads=[bcbuf, bqv, bcols], writes=[bqv])
                if ct < 4:
                    dst, bd = xTb[:, ct * TP + j * ST:ct * TP + (j + 1) * ST], bxT
                elif ct < 6:
                    dst, bd = BTb[:, (ct - 4) * TP + j * ST:(ct - 4) * TP + (j + 1) * ST], bBT
                else:
                    dst, bd = CTb[:, (ct - 6) * TP + j * ST:(ct - 6) * TP + (j + 1) * ST], bCT
                P.op("act", lambda e: e.activation(out=dst, in_=qv, func=AF.Silu), reads=[bqv], writes=[bd])
            P.op("pool", lambda e: e.tensor_copy(out=carry_m2[l][ct][:, :], in_=cbuf[:, TP:TP + 3]), reads=[bcbuf], writes=[bcarry_m2[l][ct]])
        if CSTOP <= 2:
            return
        P.barrier()
        rB0 = rSc = bps[0]
        EB = (4, 6)
        YB = (5, 1)

        def c_front(qc):
            q2 = qc % 2
            (xdt, bxdt), (xdd, bxdd), (Btok, bBtok), (scm, bscm) = xdts[q2], xdds[q2], Btoks[q2], scms[q2]
            (STn, bSTn) = STbs[1 - q2]
            for ct in range(4):
                P.op("pe", lambda e: e.matmul(psum[7][:, ct * 128:(ct + 1) * 128], xTb[:, ct * TP + qc * 128:ct * TP + (qc + 1) * 128], identb[:, :], start=True, stop=True),
                     reads=[bxT, bidb], writes=[bps[7]])
            for g in range(2):
                P.op("pe", lambda e: e.matmul(psum[0][:, g * 128:(g + 1) * 128], BTb[:, g * TP + qc * 128:g * TP + (qc + 1) * 128], identb[:, :], start=True, stop=True),
                     reads=[bBT, bidb], writes=[rB0])
            for g in range(2):
                P.op("pe", lambda e: e.matmul(psum[0][:, 256 + g * 128:256 + (g + 1) * 128], BTb[:, g * TP + qc * 128:g * TP + (qc + 1) * 128],
                                              CTb[:, g * TP + qc * 128:g * TP + (qc + 1) * 128], start=True, stop=True), reads=[bBT, bCT], writes=[rSc])
            for h in range(8):
                dacol = da[:, qc * 8 + h:qc * 8 + h + 1]
                ltl, bltl = ltls[h]
                rep, brep = reps[h // 2]
                P.op("pool", lambda e: e.tensor_scalar(out=ltl, in0=ltstrict, scalar1=dacol, scalar2=1.0, op0=ALU.mult, op1=ALU.mult), reads=[bda, bconsts], writes=[bltl])
                P.op("pool", lambda e: e.tensor_scalar(out=rep[:, (h % 2) * 64:(h % 2 + 1) * 64], in0=ones[:, 0:64], scalar1=dacol, scalar2=1.0, op0=ALU.mult, op1=ALU.mult), reads=[bda, bconsts], writes=[brep])
            for h in range(8):
                P.op("dve", lambda e: e.tensor_scalar(out=xdt[:, h * 64:(h + 1) * 64], in0=psum[7][:, h * 64:(h + 1) * 64], scalar1=dt[:, qc * 8 + h:qc * 8 + h + 1], scalar2=None, op0=ALU.mult),
                     reads=[bps[7], bdt], writes=[bxdt])
                P.op("dve", lambda e: e.tensor_scalar(out=xdd[:, h * 64:(h + 1) * 64], in0=psum[7][:, h * 64:(h + 1) * 64], scalar1=dtdec[:, qc * 8 + h:qc * 8 + h + 1], scalar2=None, op0=ALU.mult),
                     reads=[bps[7], bdtdec], writes=[bxdd])
            P.op("act", lambda e: e.activation(out=Btok, in_=psum[0][:, 0:256], func=AF.Copy), reads=[rB0], writes=[bBtok])
            P.op("dve", lambda e: e.tensor_tensor(out=scm.rearrange("p (g t) -> p g t", g=2), in0=psum[0][:, 256:512].rearrange("p (g t) -> p g t", g=2),
                                                 in1=triu.unsqueeze(1).broadcast_to([128, 2, 128]), op=ALU.mult), reads=[rSc, bconsts], writes=[bscm])
            for g in range(2):
                P.op("pe", lambda e: e.matmul(psum[1][:, g * 256:(g + 1) * 256], Btok[:, g * 128:(g + 1) * 128], xdd[:, g * 256:(g + 1) * 256], start=True, stop=True),
                     reads=[bBtok, bxdd], writes=[bps[1]])
            for h in range(8):
                P.op("dve", lambda e: e.scalar_tensor_tensor(out=ST_m2[l][:, h * 64:(h + 1) * 64], in0=ST_m2[l][:, h * 64:(h + 1) * 64], scalar=eA[:, qc * 8 + h:qc * 8 + h + 1],
                                                            in1=psum[1][:, h * 64:(h + 1) * 64], op0=ALU.mult, op1=ALU.add), reads=[bST_m2[l], beA, bps[1]], writes=[bST_m2[l]])
            P.op("act", lambda e: e.activation(out=STn, in_=ST_m2[l][:, :], func=AF.Copy), reads=[bST_m2[l], bSTn], writes=[bSTn])

        def c_heads(qc):
            q2 = qc % 2
            (xdt, bxdt), (scm, bscm) = xdts[q2], scms[q2]
            (yq, byq) = yqs[q2]
            (STc, bSTc) = STbs[q2]
            for h in range(8):
                g = h // 4
                hp = h % 2
                pair = h // 2
                pp = pair % 2
                (ltl, bltl), (LTm, bLTm), (MT, bMT) = ltls[h], LTms[hp], MTs[hp]
                (rep, brep), (erow, berow), (t1, bt1) = reps[pair], erows[pp], t1s[pp]
                Lreg = psum[2 + hp][:, pair * 128:(pair + 1) * 128]
                bL = bps[2 + hp]
                eb, yb = EB[pp], YB[pp]
                P.op("pe", lambda e: e.matmul(Lreg, ltl, triu, start=True, stop=True), reads=[bltl, bconsts], writes=[bL])
                P.op("act", lambda e: e.activation(out=LTm, in_=Lreg, func=AF.Exp), reads=[bL], writes=[bLTm])
                P.op("dve", lambda e: e.tensor_tensor(out=MT, in0=scm[:, g * 128:(g + 1) * 128], in1=LTm, op=ALU.mult), reads=[bscm, bLTm], writes=[bMT])
                if hp == 1:
                    P.op("pe", lambda e: e.matmul(psum[eb][:, 0:128], rep, triu, start=True, stop=True), reads=[brep, bconsts], writes=[bps[eb]])
                P.op("pe", lambda e: e.matmul(psum[yb][hp * 64:(hp + 1) * 64, 0:128], xdt[:, h * 64:(h + 1) * 64], MT, start=True, stop=True), reads=[bxdt, bMT], writes=[bps[yb]])
                P.op("pe", lambda e: e.matmul(psum[yb][hp * 64:(hp + 1) * 64, 128:256], STc[:, h * 64:(h + 1) * 64], CTb[:, g * TP + qc * 128:g * TP + (qc + 1) * 128], start=True, stop=True),
                     reads=[bSTc, bCT], writes=[bps[yb]])
                if hp == 1:
                    ct = pair
                    P.op("act", lambda e: e.activation(out=erow, in_=psum[eb][:, 0:128], func=AF.Exp), reads=[bps[eb]], writes=[berow])
                    P.op("dve", lambda e: e.tensor_tensor(out=t1, in0=psum[yb][:, 128:256], in1=erow, op=ALU.mult), reads=[bps[yb], berow], writes=[bt1])
                    P.op("dve", lambda e: e.tensor_tensor(out=t1, in0=psum[yb][:, 0:128], in1=t1, op=ALU.add), reads=[bps[yb], bt1], writes=[bt1])
                    P.op("dve", lambda e: e.scalar_tensor_tensor(out=yq[:, ct * 128:(ct + 1) * 128], in0=xTb[:, ct * TP + qc * 128:ct * TP + (qc + 1) * 128], scalar=col(l, "m2d", ct),
                                                                in1=t1, op0=ALU.mult, op1=ALU.add), reads=[bxT, bt1, bcols, byq], writes=[byq])
                    P.op("pool", lambda e: e.tensor_tensor(out=yq[:, ct * 128:(ct + 1) * 128], in0=yq[:, ct * 128:(ct + 1) * 128],
                                                          in1=ZS[:, ct * TP + qc * 128:ct * TP + (qc + 1) * 128], op=ALU.mult), reads=[byq, bZS], writes=[byq])

        def c_tail(qc):
            q2 = qc % 2
            cs = slice(qc * 128, (qc + 1) * 128)
            (yq, byq), (rstd, brstd) = yqs[q2], rstds[q2]
            rms_stats(lambda k: yq[:, k * 128:(k + 1) * 128], lambda k: [byq], 4, 1.0 / W, rstd, brstd, n=128, lnexp=True)
            for ct in range(4):
                P.op("dve", lambda e: e.scalar_tensor_tensor(out=Y[:, ct, cs], in0=yq[:, ct * 128:(ct + 1) * 128], scalar=col(l, "m2nw", ct), in1=rstd[:, 0:128],
                                                            op0=ALU.mult, op1=ALU.mult), reads=[byq, brstd, bcols], writes=[bY[ct][qc // 4]])

        c_front(0)
        for qc in range(NQ):
            c_heads(qc)
            if qc + 1 < NQ:
                c_front(qc + 1)
            c_tail(qc)
        P.barrier()

    def sincos(ang, bang, n, out_sin, out_cos, tmp, tmpi, btmp):
        for shift, dst in ((0.0, out_sin), (0.5 * np.pi, out_cos)):
            P.op("dve", lambda e: e.tensor_scalar(out=tmp[:, 0:n], in0=ang, scalar1=float(shift), scalar2=float(1.0 / (2 * np.pi)), op0=ALU.add, op1=ALU.mult),
                 reads=[bang], writes=[btmp])
            P.op("dve", lambda e: e.tensor_copy(out=tmpi[:, 0:n], in_=tmp[:, 0:n]), reads=[btmp], writes=[btmp])
            P.op("dve", lambda e: e.tensor_copy(out=tmp[:, n:2 * n], in_=tmpi[:, 0:n]), reads=[btmp], writes=[btmp])
            P.op("dve", lambda e: e.tensor_tensor(out=tmp[:, 0:n], in0=tmp[:, 0:n], in1=tmp[:, n:2 * n], op=ALU.subtract), reads=[btmp], writes=[btmp])
            P.op("dve", lambda e: e.tensor_scalar(out=tmp[:, n:2 * n], in0=tmp[:, 0:n], scalar1=0.5, scalar2=None, op0=ALU.is_gt), reads=[btmp], writes=[btmp])
            P.op("dve", lambda e: e.tensor_tensor(out=tmp[:, 0:n], in0=tmp[:, 0:n], in1=tmp[:, n:2 * n], op=ALU.subtract), reads=[btmp], writes=[btmp])
            P.op("dve", lambda e: e.tensor_scalar(out=tmp[:, n:2 * n], in0=tmp[:, 0:n], scalar1=-0.5, scalar2=None, op0=ALU.is_lt), reads=[btmp], writes=[btmp])
            P.op("dve", lambda e: e.tensor_tensor(out=tmp[:, 0:n], in0=tmp[:, 0:n], in1=tmp[:, n:2 * n], op=ALU.add), reads=[btmp], writes=[btmp])
            P.op("act", lambda e: e.activation(out=dst, in_=tmp[:, 0:n], func=AF.Sin, scale=float(2 * np.pi * (1 - 1e-6))), reads=[btmp], writes=[btmp])

    def s5_lambda(lre, lim, lst, n, bsrc, abre, abim, tmp, tmpi, btmp, scr, bscr):
        step, lrs, lis, mag = scr[:, 0:n], scr[:, n:2 * n], scr[:, 2 * n:3 * n], scr[:, 3 * n:4 * n]
        P.op("act", lambda e: e.activation(out=step, in_=lst, func=AF.Exp), reads=[bsrc], writes=[bscr])
        P.op("dve", lambda e: e.tensor_tensor(out=lrs, in0=lre, in1=step, op=ALU.mult), reads=[bsrc, bscr], writes=[bscr])
        P.op("dve", lambda e: e.tensor_tensor(out=lis, in0=lim, in1=step, op=ALU.mult), reads=[bsrc, bscr], writes=[bscr])
        P.op("act", lambda e: e.activation(out=mag, in_=lrs, func=AF.Exp), reads=[bscr], writes=[bscr])
        sincos(lis, bscr, n, abim, abre, tmp, tmpi, btmp)
        P.op("dve", lambda e: e.tensor_tensor(out=abre, in0=abre, in1=mag, op=ALU.mult), reads=[btmp, bscr], writes=[btmp])
        P.op("dve", lambda e: e.tensor_tensor(out=abim, in0=abim, in1=mag, op=ALU.mult), reads=[btmp, bscr], writes=[btmp])

    def coef_calc(lre, lim, abre, abim, n, cre_, cim_, scr, rd, bscr, bout):
        nr, den, u1, u2 = scr[:, 0:n], scr[:, n:2 * n], scr[:, 2 * n:3 * n], scr[:, 3 * n:4 * n]
        TT = lambda o, a, b, op, r_, w_: P.op("dve", lambda e: e.tensor_tensor(out=o, in0=a, in1=b, op=op), reads=r_, writes=w_)
        P.op("dve", lambda e: e.tensor_scalar(out=nr, in0=abre, scalar1=-1.0, scalar2=None, op0=ALU.add), reads=rd, writes=[bscr])
        TT(den, lre, lre, ALU.mult, rd, [bscr])
        TT(u1, lim, lim, ALU.mult, rd, [bscr])
        TT(den, den, u1, ALU.add, [bscr], [bscr])
        P.op("dve", lambda e: e.reciprocal(out=den, in_=den), reads=[bscr], writes=[bscr])
        TT(u1, nr, lre, ALU.mult, [bscr] + rd, [bscr])
        TT(u2, abim, lim, ALU.mult, rd, [bscr])
        TT(u1, u1, u2, ALU.add, [bscr], [bscr])
        TT(cre_, u1, den, ALU.mult, [bscr], [bout])
        TT(u1, abim, lre, ALU.mult, rd, [bscr])
        TT(u2, nr, lim, ALU.mult, [bscr] + rd, [bscr])
        TT(u1, u1, u2, ALU.subtract, [bscr], [bscr])
        TT(cim_, u1, den, ALU.mult, [bscr], [bout])

    def phase_a(l):
        scratch_reset()
        NSC = 7
        Q8 = 8
        CC = TP // Q8
        TT = lambda o, a, b, op, rd, wr: P.op("dve", lambda e: e.tensor_tensor(out=o, in0=a, in1=b, op=op), reads=rd, writes=wr)
        STT = lambda o, a, sc_, b, rd, wr: P.op("dve", lambda e: e.scalar_tensor_tensor(out=o, in0=a, scalar=sc_, in1=b, op0=ALU.mult, op1=ALU.add), reads=rd, writes=wr)
        TS = lambda o, a, sc_, rd, wr: P.op("dve", lambda e: e.tensor_scalar(out=o, in0=a, scalar1=sc_, scalar2=None, op0=ALU.mult), reads=rd, writes=wr)
        pwc, bpwc = falloc(9 * 3 * 16)
        pws, bpws = falloc(NSC * 3 * 16)
        pcC, bpcC = falloc(2 * 256)
        BD, bBD = balloc(2 * 2048)
        CD, bCD = balloc(2 * 2048)
        BDp, bBDp = balloc(2 * 2048)
        pwcv = pwc.rearrange("p (k a g) -> p k a g", k=9, a=3)
        pwsv = pws.rearrange("p (k a g) -> p k a g", k=NSC, a=3)
        mark = scr_pos[0]
        p5, bp5 = falloc(5 * 256)
        pq, bpq = falloc(3 * 16)
        pbp, bpbp = falloc(2 * 256)
        tmp, btmp = falloc(512)
        tmpi_f, _ = falloc(256)
        tmpi = tmpi_f.bitcast(mybir.dt.int32)
        scr, bscr = falloc(1024)
        ab, bab = falloc(512)
        cf, bcf = falloc(512)
        bb, bbb = falloc(512)
        abp, babp = falloc(32)
        cfp, bcfp = falloc(32)
        bbp, bbbp = falloc(512)
        P.dma("sp", sm, p5.rearrange("p (a n) -> p a n", a=5), s5p_d[l], writes=[bp5])
        P.dma("sp", sm, pq.rearrange("p (a n) -> p a n", a=3), s5q_d[l], writes=[bpq])
        P.dma("sp", sm, pcC.rearrange("p (a n) -> p a n", a=2), s5c_d[l, :, 0:2, :], writes=[bpcC])
        P.dma("sp", sm, pbp.rearrange("p (a n) -> p a n", a=2), s5c_d[l, :, 2:4, :], writes=[bpbp])
        lre, lim, lst, bre, bim = (p5[:, i * 256:(i + 1) * 256] for i in range(5))
        abre, abim = ab[:, 0:256], ab[:, 256:512]
        s5_lambda(lre, lim, lst, 256, bp5, abre, abim, tmp, tmpi, btmp, scr, bscr)
        cre_, cim_ = cf[:, 0:256], cf[:, 256:512]
        coef_calc(lre, lim, abre, abim, 256, cre_, cim_, scr, [bp5, btmp], bscr, bcf)
        u1, u2 = scr[:, 512:768], scr[:, 768:1024]
        bbre, bbim = bb[:, 0:256], bb[:, 256:512]
        TT(u1, cre_, bre, ALU.mult, [bcf, bp5], [bscr])
        TT(u2, cim_, bim, ALU.mult, [bcf, bp5], [bscr])
        TT(bbre, u1, u2, ALU.subtract, [bscr], [bbb])
        TT(u1, cre_, bim, ALU.mult, [bcf, bp5], [bscr])
        TT(u2, cim_, bre, ALU.mult, [bcf, bp5], [bscr])
        TT(bbim, u1, u2, ALU.add, [bscr], [bbb])
        for ri, src in enumerate((bbre, bbim)):
            dstv = BD[:, ri * 2048:(ri + 1) * 2048].rearrange("p (c j g n) -> p c j g n", c=4, j=4, g=2)
            for jj in range(4):
                for g2 in range(2):
                    TS(dstv[:, :, jj, g2, :], src.rearrange("p (c n) -> p c n", c=4), col(l, "mkB", jj * 2 + g2), [bbb, bcols], [bBD])
        P.op("pool", lambda e: e.memset(CD, 0.0), writes=[bCD])
        for ri, nm in enumerate(("mkC", "mkCn")):
            dstv = CD[:, ri * 2048:(ri + 1) * 2048].rearrange("p (c j m) -> p c j m", c=4, j=4)
            srcv = pcC[:, ri * 256:(ri + 1) * 256].rearrange("p (q c j) -> p c j q", q=16, c=4, j=4)
            for jj in range(4):
                for g2 in range(2):
                    gl = 2 * jj + g2
                    TS(dstv[:, :, jj, gl * 16:(gl + 1) * 16], srcv[:, :, jj, :], col(l, nm, g2), [bpcC, bcols, bCD], [bCD])
        s5_lambda(pq[:, 0:16], pq[:, 16:32], pq[:, 32:48], 16, bpq, abp[:, 0:16], abp[:, 16:32], tmp, tmpi, btmp, scr, bscr)
        coef_calc(pq[:, 0:16], pq[:, 16:32], abp[:, 0:16], abp[:, 16:32], 16, cfp[:, 0:16], cfp[:, 16:32], scr, [bpq, btmp], bscr, bcfp)
        bq_re = pbp[:, 0:256].rearrange("p (q g) -> p q g", q=16)
        bq_im = pbp[:, 256:512].rearrange("p (q g) -> p q g", q=16)
        cfr = cfp[:, 0:16].unsqueeze(1).broadcast_to([128, 16, 16])
        cfi = cfp[:, 16:32].unsqueeze(1).broadcast_to([128, 16, 16])
        w1 = scr[:, 0:256].rearrange("p (q g) -> p q g", q=16)
        w2 = scr[:, 256:512].rearrange("p (q g) -> p q g", q=16)
        bbp_re = bbp[:, 0:256].rearrange("p (q g) -> p q g", q=16)
        bbp_im = bbp[:, 256:512].rearrange("p (q g) -> p q g", q=16)
        TT(w1, bq_re, cfr, ALU.mult, [bpbp, bcfp], [bscr])
        TT(w2, bq_im, cfi, ALU.mult, [bpbp, bcfp], [bscr])
        TT(bbp_re, w1, w2, ALU.subtract, [bscr], [bbbp])
        TT(w1, bq_im, cfr, ALU.mult, [bpbp, bcfp], [bscr])
        TT(w2, bq_re, cfi, ALU.mult, [bpbp, bcfp], [bscr])
        TT(bbp_im, w1, w2, ALU.add, [bscr], [bbbp])
        P.op("pool", lambda e: e.memset(BDp, 0.0), writes=[bBDp])
        for ri in range(2):
            dstv = BDp[:, ri * 2048:(ri + 1) * 2048].rearrange("p (c j m) -> p c j m", c=4, j=4)
            srcv = bbp[:, ri * 256:(ri + 1) * 256].rearrange("p (q c j) -> p c j q", q=16, c=4, j=4)
            for jj in range(4):
                for g2 in range(2):
                    gl = 2 * jj + g2
                    TS(dstv[:, :, jj, gl * 16:(gl + 1) * 16], srcv[:, :, jj, :], col(l, "mkC", g2), [bbbp, bcols, bBDp], [bBDp])
        P.op("pool", lambda e: e.memset(pwcv[:, 0, 0, :], 1.0), writes=[bpwc])
        P.op("pool", lambda e: e.memset(pwcv[:, 0, 1:3, :], 0.0), reads=[bpwc], writes=[bpwc])
        P.op("dve", lambda e: e.tensor_copy(out=pwcv[:, 1, 0, :], in_=abp[:, 0:16]), reads=[btmp, bpwc], writes=[bpwc])
        P.op("dve", lambda e: e.tensor_copy(out=pwcv[:, 1, 1, :], in_=abp[:, 16:32]), reads=[btmp, bpwc], writes=[bpwc])
        lr_, li_ = abp[:, 0:16], abp[:, 16:32]
        for k in range(2, 9):
            a_, b_ = pwcv[:, k - 1, 0, :], pwcv[:, k - 1, 1, :]
            TT(scr[:, 0:16], a_, lr_, ALU.mult, [bpwc, btmp], [bscr])
            TT(scr[:, 16:32], b_, li_, ALU.mult, [bpwc, btmp], [bscr])
            TT(pwcv[:, k, 0, :], scr[:, 0:16], scr[:, 16:32], ALU.subtract, [bscr, bpwc], [bpwc])
            TT(scr[:, 32:48], a_, li_, ALU.mult, [bpwc, btmp], [bscr])
            TT(scr[:, 48:64], b_, lr_, ALU.mult, [bpwc, btmp], [bscr])
            TT(pwcv[:, k, 1, :], scr[:, 32:48], scr[:, 48:64], ALU.add, [bscr, bpwc], [bpwc])
        for k in range(1, 9):
            TS(pwcv[:, k, 2, :], pwcv[:, k, 1, :], -1.0, [bpwc], [bpwc])
        P.op("dve", lambda e: e.tensor_copy(out=pwsv[:, 0, :, :], in_=pwcv[:, 8, :, :]), reads=[bpwc], writes=[bpws])
        for k in range(1, NSC):
            a_, b_ = pwsv[:, k - 1, 0, :], pwsv[:, k - 1, 1, :]
            TT(scr[:, 0:16], a_, a_, ALU.mult, [bpws], [bscr])
            TT(scr[:, 16:32], b_, b_, ALU.mult, [bpws], [bscr])
            TT(pwsv[:, k, 0, :], scr[:, 0:16], scr[:, 16:32], ALU.subtract, [bscr, bpws], [bpws])
            TT(scr[:, 32:48], a_, b_, ALU.mult, [bpws], [bscr])
            TS(pwsv[:, k, 1, :], scr[:, 32:48], 2.0, [bscr, bpws], [bpws])
            TS(pwsv[:, k, 2, :], scr[:, 32:48], -2.0, [bscr, bpws], [bpws])
        scratch_reset(mark)
        t2, bt2 = falloc(TP)
        XS, _ = falloc(4 * CC)
        bXS = [[Buf(), Buf()], [Buf(), Buf()]]
        XSv = [[XS[:, (b * 2 + ri) * CC:(b * 2 + ri + 1) * CC] for ri in range(2)] for b in range(2)]
        SP, _ = falloc(2 * CC)
        bSP = [Buf(), Buf()]
        SPv = [SP[:, 0:CC], SP[:, CC:2 * CC]]
        stmp, _ = falloc(4 * CC)
        bstmp = [Buf() for _ in range(4)]
        m12, bm12 = falloc(128)
        U, bU = balloc(4 * TP)
        G1, bG1 = balloc(4 * TP)
        Kc, bKc = balloc(8 * 128)
        Mc, bMc = balloc(8 * 2 * 64)
        SX, bSX = balloc(8 * 2 * CC)
        slu, bslu = load_slot([(w_in[l, :, O_S5U:O_S5U + 512], 8, 0, 512)])
        n = 0
        for c in range(4):
            for j in range(NSUB):
                pi = n % 2
                n += 1
                mm_group(psum[pi][:, :], bps[pi], lambda k: slu[:, k, c * 128:(c + 1) * 128], lambda k: H[:, k, hs(j)], 8, [bslu, bH[j]])
                P.op("act", lambda e: e.activation(out=U[:, c * TP + j * ST:c * TP + (j + 1) * ST], in_=psum[pi][:, :], func=AF.Copy), reads=[bps[pi]], writes=[bU])
        bmv = blockmask.rearrange("p (g q) -> p g q", g=8)
        for c in range(4):
            Uc = U[:, c * TP:(c + 1) * TP]
            Ucv = Uc.rearrange("p (cc r) -> p cc r", r=8)
            Cre = pcC[:, 0:256].rearrange("q (p g) -> q p g", p=16)[:, :, 4 * c:4 * c + 4]
            Cim = pcC[:, 256:512].rearrange("q (p g) -> q p g", p=16)[:, :, 4 * c:4 * c + 4]
            m1 = m12[:, 0:64].rearrange("q (p j) -> q p j", p=16)
            m2 = m12[:, 64:128].rearrange("q (p j) -> q p j", p=16)
            for tau in range(8):
                Mre = Mc[:, (tau * 2) * 64:(tau * 2 + 1) * 64].rearrange("q (p j) -> q p j", p=16)
                Mim = Mc[:, (tau * 2 + 1) * 64:(tau * 2 + 2) * 64].rearrange("q (p j) -> q p j", p=16)
                if tau == 0:
                    P.op("dve", lambda e: e.tensor_copy(out=Mre, in_=Cre), reads=[bpcC, bMc], writes=[bMc])
                    TS(Mim, Cim, -1.0, [bpcC, bMc], [bMc])
                    continue
                Pre = pwcv[:, tau, 0, 4 * c:4 * c + 4].unsqueeze(1).broadcast_to([128, 16, 4])
                Pim = pwcv[:, tau, 1, 4 * c:4 * c + 4].unsqueeze(1).broadcast_to([128, 16, 4])
                nPim = pwcv[:, tau, 2, 4 * c:4 * c + 4].unsqueeze(1).broadcast_to([128, 16, 4])
                TT(m1, Cre, Pre, ALU.mult, [bpcC, bpwc, bm12], [bm12])
                TT(m2, Cim, Pim, ALU.mult, [bpcC, bpwc, bm12], [bm12])
                TT(Mre, m1, m2, ALU.subtract, [bm12, bMc], [bMc])
                TT(m1, Cre, nPim, ALU.mult, [bpcC, bpwc, bm12], [bm12])
                TT(m2, Cim, Pre, ALU.mult, [bpcC, bpwc, bm12], [bm12])
                TT(Mim, m1, m2, ALU.subtract, [bm12, bMc], [bMc])
            for tau in range(8):
                nmm = 0
                for jj in range(4):
                    gp = 4 * c + jj
                    for ri in range(2):
                        rhs = Mc[:, (tau * 2 + ri) * 64:(tau * 2 + ri + 1) * 64].rearrange("q (p j) -> q p j", p=16)[:, :, jj]
                        P.op("pe", lambda e: e.matmul(psum[6][:, tau * 16:(tau + 1) * 16], BDp[:, ri * 2048 + gp * 128:ri * 2048 + (gp + 1) * 128], rhs,
                                                      start=(nmm == 0), stop=(nmm == 7)), reads=[bBDp, bMc], writes=[bps[6]], inc=(nmm == 7))
                        nmm += 1
            for tau in range(8):
                P.op("dve", lambda e: e.tensor_tensor(out=Kc[:, tau * 128:(tau + 1) * 128].rearrange("p (g q) -> p g q", g=8), in0=bmv,
                                                     in1=psum[6][:, tau * 16:(tau + 1) * 16].unsqueeze(1).broadcast_to([128, 8, 16]), op=ALU.mult),
                     reads=[bps[6], bconsts, bKc], writes=[bKc])
            for bk in (4, 5):
                P.op("pe", lambda e: e.matmul(psum[bk][:, :], zerob[:, :], Uc[:, 0:512], start=True, stop=False, skip_group_check=True),
                     reads=[bzerob, bU], writes=[bps[bk]], inc=True)
            for r in range(8):
                for rp in range(r + 1):
                    last = (r == 7 and rp == 7)
                    P.op("pe", lambda e: e.matmul(psum[4 + r // 4][:, (r % 4) * 128:(r % 4 + 1) * 128], Kc[:, (r - rp) * 128:(r - rp + 1) * 128], Ucv[:, :, rp],
                                                  start=False, stop=False, skip_group_check=True), reads=[bKc, bU], writes=[bps[4 + r // 4]], inc=last)
            for jj in range(4):
                gp = 4 * c + jj
                for j in range(NSUB):
                    for ri in range(2):
                        pi = 2 * ri + j
                        P.op("pe", lambda e: e.matmul(psum[pi][:, :], BD[:, ri * 2048 + gp * 128:ri * 2048 + (gp + 1) * 128], Uc[:, j * ST:(j + 1) * ST], start=True, stop=True),
                             reads=[bBD, bU], writes=[bps[pi]])
                bv = [psbig[:, ri * 1024:(ri + 1) * 1024].rearrange("p (cc r) -> p cc r", r=8) for ri in range(2)]
                bpb = [[bps[0], bps[1]], [bps[2], bps[3]]]
                acc = [XSv[0][ri][:, 0:CC] for ri in range(2)]
                for ri in range(2):
                    P.op("act", lambda e: e.activation(out=acc[ri], in_=bv[ri][:, :, 7], func=AF.Copy), reads=bpb[ri] + [bXS[0][ri]], writes=[bXS[0][ri]])
                for r in range(7):
                    k = 7 - r
                    pr, pi_, npi = pwcv[:, k, 0, gp:gp + 1], pwcv[:, k, 1, gp:gp + 1], pwcv[:, k, 2, gp:gp + 1]
                    STT(acc[0], bv[0][:, :, r], pr, acc[0], bpb[0] + [bpwc, bXS[0][0]], [bXS[0][0]])
                    STT(acc[1], bv[1][:, :, r], pr, acc[1], bpb[1] + [bpwc, bXS[0][1]], [bXS[0][1]])
                    STT(acc[0], bv[1][:, :, r], npi, acc[0], bpb[1] + [bpwc, bXS[0][0]], [bXS[0][0]])
                    STT(acc[1], bv[0][:, :, r], pi_, acc[1], bpb[0] + [bpwc, bXS[0][1]], [bXS[0][1]])
                cr, ci = carry_s5[l][:, 0, gp:gp + 1], carry_s5[l][:, 1, gp:gp + 1]
                p8r, p8i, p8n = pwcv[:, 8, 0, gp:gp + 1], pwcv[:, 8, 1, gp:gp + 1], pwcv[:, 8, 2, gp:gp + 1]
                STT(XSv[0][0][:, 0:1], cr, p8r, XSv[0][0][:, 0:1], [bcarry_s5[l], bpwc, bXS[0][0]], [bXS[0][0]])
                STT(XSv[0][1][:, 0:1], ci, p8r, XSv[0][1][:, 0:1], [bcarry_s5[l], bpwc, bXS[0][1]], [bXS[0][1]])
                STT(XSv[0][0][:, 0:1], ci, p8n, XSv[0][0][:, 0:1], [bcarry_s5[l], bpwc, bXS[0][0]], [bXS[0][0]])
                STT(XSv[0][1][:, 0:1], cr, p8i, XSv[0][1][:, 0:1], [bcarry_s5[l], bpwc, bXS[0][1]], [bXS[0][1]])
                sbuf_i = 0
                for k in range(NSC):
                    sh = 1 << k
                    src, dst = XSv[sbuf_i], XSv[1 - sbuf_i]
                    bs_, bd_ = bXS[sbuf_i], bXS[1 - sbuf_i]
                    ar, ai, nai = pwsv[:, k, 0, gp:gp + 1], pwsv[:, k, 1, gp:gp + 1], pwsv[:, k, 2, gp:gp + 1]
                    STT(dst[0][:, sh:CC], src[0][:, 0:CC - sh], ar, src[0][:, sh:CC], [bs_[0], bpws, bd_[0]], [bd_[0]])
                    STT(dst[1][:, sh:CC], src[1][:, 0:CC - sh], ar, src[1][:, sh:CC], [bs_[1], bpws, bd_[1]], [bd_[1]])
                    STT(dst[0][:, sh:CC], src[1][:, 0:CC - sh], nai, dst[0][:, sh:CC], [bs_[1], bpws, bd_[0]], [bd_[0]])
                    STT(dst[1][:, sh:CC], src[0][:, 0:CC - sh], ai, dst[1][:, sh:CC], [bs_[0], bpws, bd_[1]], [bd_[1]])
                    for ri in range(2):
                        P.op("act", lambda e: e.activation(out=dst[ri][:, 0:sh], in_=src[ri][:, 0:sh], func=AF.Copy), reads=[bs_[ri], bd_[ri]], writes=[bd_[ri]])
                    sbuf_i = 1 - sbuf_i
                S_, bS_ = XSv[sbuf_i], bXS[sbuf_i]
                for ri in range(2):
                    P.op("pool", lambda e: e.tensor_copy(out=SPv[ri][:, 1:CC], in_=S_[ri][:, 0:CC - 1]), reads=[bS_[ri], bSP[ri]], writes=[bSP[ri]])
                    P.op("pool", lambda e: e.tensor_copy(out=SPv[ri][:, 0:1], in_=carry_s5[l][:, ri, gp:gp + 1]), reads=[bcarry_s5[l], bSP[ri]], writes=[bSP[ri]])
                for ri in range(2):
                    P.op("pool", lambda e: e.tensor_copy(out=carry_s5[l][:, ri, gp:gp + 1], in_=S_[ri][:, CC - 1:CC]), reads=[bS_[ri], bcarry_s5[l]], writes=[bcarry_s5[l]])
                for x in range(1, 9):
                    pr, pi_, npi = pwcv[:, x, 0, gp:gp + 1], pwcv[:, x, 1, gp:gp + 1], pwcv[:, x, 2, gp:gp + 1]
                    sb2 = (x % 2) * 2
                    t_re, t_im = stmp[:, sb2 * CC:(sb2 + 1) * CC], stmp[:, (sb2 + 1) * CC:(sb2 + 2) * CC]
                    P.op("pool", lambda e: e.tensor_scalar(out=t_re, in0=SPv[0], scalar1=pr, scalar2=1.0, op0=ALU.mult, op1=ALU.mult), reads=[bSP[0], bpwc, bstmp[sb2]], writes=[bstmp[sb2]])
                    P.op("pool", lambda e: e.tensor_scalar(out=t_im, in0=SPv[1], scalar1=pr, scalar2=1.0, op0=ALU.mult, op1=ALU.mult), reads=[bSP[1], bpwc, bstmp[sb2 + 1]], writes=[bstmp[sb2 + 1]])
                    STT(SX[:, ((x - 1) * 2) * CC:((x - 1) * 2 + 1) * CC], SPv[1], npi, t_re, [bSP[1], bpwc, bstmp[sb2], bSX], [bSX])
                    STT(SX[:, ((x - 1) * 2 + 1) * CC:((x - 1) * 2 + 2) * CC], SPv[0], pi_, t_im, [bSP[0], bpwc, bstmp[sb2 + 1], bSX], [bSX])
                for r in range(8):
                    for ri in range(2):
                        last = (r == 7 and ri == 1)
                        P.op("pe", lambda e: e.matmul(psum[4 + r // 4][:, (r % 4) * 128:(r % 4 + 1) * 128], CD[:, ri * 2048 + gp * 128:ri * 2048 + (gp + 1) * 128],
                                                      SX[:, (r * 2 + ri) * CC:(r * 2 + ri + 1) * CC], start=False, stop=(jj == 3 and last), skip_group_check=True),
                             reads=[bCD, bSX], writes=[bps[4 + r // 4]], inc=last)
            t2v = t2.rearrange("p (cc r) -> p cc r", r=8)
            for bk in range(2):
                P.op("dve", lambda e: e.scalar_tensor_tensor(out=t2v[:, :, 4 * bk:4 * bk + 4], in0=Ucv[:, :, 4 * bk:4 * bk + 4], scalar=col(l, "s5d", c),
                                                            in1=psum[4 + bk][:, :].rearrange("p (r cc) -> p cc r", r=4), op0=ALU.mult, op1=ALU.add),
                     reads=[bU, bcols, bps[4 + bk], bt2], writes=[bt2])
            for j in range(NSUB):
                P.op("act", lambda e: e.activation(out=G1[:, c * TP + j * ST:c * TP + (j + 1) * ST], in_=t2[:, hs(j)], func=AF.Gelu), reads=[bt2], writes=[bG1])
        P.barrier()
        sig, bsig = XS, Buf()
        gate_s, bgs = stmp, Buf()
        slw, bslw = load_slot([(w_glu[l, :, :], 4, 0, 512)])
        slg, bslg = load_slot([(w_in[l, :, O_S5G:O_S5G + 512], 8, 0, 512)])
        for co in range(4):
            for j in range(NSUB):
                p1, p2 = (0, 1) if (co * NSUB + j) % 2 == 0 else (2, 3)
                mm_group(psum[p1][:, :], bps[p1], lambda k: slw[:, k, co * 128:(co + 1) * 128], lambda k: G1[:, k * TP + j * ST:k * TP + (j + 1) * ST], 4, [bslw, bG1])
                mm_group(psum[p2][:, :], bps[p2], lambda k: slg[:, k, co * 128:(co + 1) * 128], lambda k: H[:, k, hs(j)], 8, [bslg, bH[j]])
                P.op("act", lambda e: e.activation(out=sig, in_=psum[p1][:, :], func=AF.Sigmoid), reads=[bps[p1]], writes=[bsig])
                P.op("act", lambda e: e.activation(out=gate_s, in_=psum[p2][:, :], func=AF.Silu), reads=[bps[p2]], writes=[bgs])
                P.op("dve", lambda e: e.tensor_tensor(out=t2[:, 0:ST], in0=G1[:, co * TP + j * ST:co * TP + (j + 1) * ST], in1=sig, op=ALU.mult), reads=[bG1, bsig, bt2], writes=[bt2])
                P.op("pool", lambda e: e.tensor_tensor(out=Y[:, co, hs(j)], in0=t2[:, 0:ST], in1=gate_s, op=ALU.mult), reads=[bt2, bgs], writes=[bY[co][j]])

    first_merge = [True]
    mg_t = sb("mg_t", [128, ST], F32)
    mt_t = sb("mt_t", [128, ST], F32)
    bmg, bmt = Buf(), Buf()

    def phase_merge(l, kb):
        g, bg, t, bt = mg_t[:, :], bmg, mt_t[:, :], bmt
        for hh in range(2):
            slg, bslg = load_slot([(w_in[l, :, O_MG + kb * D + hh * 512:O_MG + kb * D + (hh + 1) * 512], 8, 0, 512)])
            slb, bslb = load_slot([(w_br[l, kb, :, hh * 512:(hh + 1) * 512], 4, 0, 512)])
            for dt_ in range(4):
                d = hh * 4 + dt_
                for j in range(NSUB):
                    pg, pbk = (0, 1) if (dt_ * NSUB + j) % 2 == 0 else (2, 3)
                    mm_group(psum[pg][:, :], bps[pg], lambda k: slg[:, k, dt_ * 128:(dt_ + 1) * 128], lambda k: H[:, k, hs(j)], 8, [bslg, bH[j]])
                    mm_group(psum[pbk][:, :], bps[pbk], lambda k: slb[:, k, dt_ * 128:(dt_ + 1) * 128], lambda k: Y[:, k, hs(j)], 4,
                             [bslb] + [bY[k][j] for k in range(4)])
                    P.op("act", lambda e: e.activation(out=g, in_=psum[pg][:, :], func=AF.Sigmoid, bias=col(l, "mb", kb * 8 + d)),
                         reads=[bps[pg], bcols], writes=[bg])
                    if first_merge[0]:
                        P.op("dve", lambda e: e.tensor_tensor(out=ACC[:, d, hs(j)], in0=psum[pbk][:, :], in1=g, op=ALU.mult),
                             reads=[bps[pbk], bg], writes=[bACC[d][j]])
                    else:
                        P.op("dve", lambda e: e.tensor_tensor(out=t, in0=psum[pbk][:, :], in1=g, op=ALU.mult),
                             reads=[bps[pbk], bg], writes=[bt])
                        P.op("pool", lambda e: e.tensor_tensor(out=ACC[:, d, hs(j)], in0=ACC[:, d, hs(j)], in1=t, op=ALU.add),
                             reads=[bt, bACC[d][j]], writes=[bACC[d][j]])
        first_merge[0] = False

    def phase_out(l):
        for j in range(NSUB):
            for k in range(8):
                P.op("act", lambda e: e.activation(out=H[:, k, hs(j)], in_=ACC[:, k, hs(j)], func=AF.Copy),
                     reads=[bACC[k][j]], writes=[bH[j]])
        for hh in range(2):
            sl, bsl = load_slot([(w_out[l, :, hh * 512:(hh + 1) * 512], 8, 0, 512)])
            for dt_ in range(4):
                d = hh * 4 + dt_
                for j in range(NSUB):
                    pi = 2 + (dt_ * NSUB + j) % 4
                    mm_group(psum[pi][:, :], bps[pi], lambda k: sl[:, k, dt_ * 128:(dt_ + 1) * 128], lambda k: H[:, k, hs(j)], 8, [bsl, bH[j]])
                    P.op("dve", lambda e: e.tensor_tensor(out=X[:, d, hs(j)], in0=psum[pi][:, :], in1=X[:, d, hs(j)], op=ALU.add),
                         reads=[bps[pi], bX[d][j]], writes=[bX[d][j]])

    fo_t = sb("fo_t", [128, 2, ST], F32)
    bfo = [Buf(), Buf()]

    def phase_final(p):
        n = 0
        for j in range(NSUB):
            rms_stats(lambda k: X[:, k, hs(j)], lambda k: [bX[k][j]], 8, 1.0 / D, rstd_t, brstd_t)
            for k in range(8):
                i = n % 2
                n += 1
                P.op("dve", lambda e: e.scalar_tensor_tensor(
                    out=fo_t[:, i, :], in0=X[:, k, hs(j)], scalar=col(0, "fw", k),
                    in1=rstd_t[:, :], op0=ALU.mult, op1=ALU.mult),
                    reads=[bX[k][j], brstd_t, bcols], writes=[bfo[i]])
                P.dma("sp", sy, yT[k * 128:(k + 1) * 128, p * TP + j * ST:p * TP + (j + 1) * ST], fo_t[:, i, :], reads=[bfo[i]])

    phases = {"a": phase_a, "b": phase_b, "c": phase_c, "d": phase_d}
    for p in range(NPASS):
        for k in range(8):
            for j in range(NSUB):
                P.dma("sp", sx, X[:, k, hs(j)], xT[k * 128:(k + 1) * 128, p * TP + j * ST:p * TP + (j + 1) * ST], writes=[bX[k][j]])
        for l in range(nlayers):
            phase_norm(l)
            first_merge[0] = True
            for kb, name in enumerate("abcd"):
                if name not in branches:
                    continue
                phases[name](l)
                phase_merge(l, kb)
            phase_out(l)
        phase_final(p)
    P._wait("sp", sy, P.cnt[sy])
    print("program: nins=%d nwaits=%d" % (P.nins, P.nwaits), {k: v for k, v in P.cnt.items() if k in P.eng})
    return nc


_NC_CACHE = {}


def run(inputs, branches=("a", "b", "c", "d"), nlayers=DEPTH, trace=False):
    key = (tuple(branches), nlayers)
    if key not in _NC_CACHE:
        _NC_CACHE[key] = build_nc(branches, nlayers)
    nc = _NC_CACHE[key]
    inp = {k: np.asarray(v) for k, v in inputs.items()}
    x = inp["x"].astype(np.float32)
    s5 = [host_s5(inp, l) for l in range(DEPTH)]
    shared = {
        "w_in": np.ascontiguousarray(inp["w_in"], dtype=np.float32),
        "w_branch": np.ascontiguousarray(inp["w_branch"], dtype=np.float32),
        "w_out": np.ascontiguousarray(inp["w_out"], dtype=np.float32),
        "w_glu": np.ascontiguousarray(inp["s5_w_glu"], dtype=np.float32),
        "cols": np.stack([host_cols(inp, l) for l in range(DEPTH)], 0),
        "consts": host_consts(),
        "rows": np.stack([host_rows(inp, l) for l in range(DEPTH)], 0),
        "sguw": np.ascontiguousarray(inp["sgu_w"].transpose(0, 3, 1, 2), dtype=np.float32),
        "sgub": np.ascontiguousarray(np.repeat(inp["sgu_b"].reshape(DEPTH, 4, 2, 1, 128), 64, axis=3).transpose(0, 2, 3, 1, 4).reshape(DEPTH, 128, 512), dtype=np.float32),
        "s5p": np.stack([s[0] for s in s5], 0),
        "s5q": np.stack([s[1] for s in s5], 0),
        "s5c": np.stack([s[2] for s in s5], 0),
    }
    in_maps = []
    for b in range(8):
        m = dict(shared)
        m["xT"] = np.ascontiguousarray(x[b].T)
        in_maps.append(m)
    res = run_bass_kernel_spmd(nc, in_maps, core_ids=list(range(8)), trace=trace)
    out = np.stack([np.ascontiguousarray(res.results[b]["yT"].T) for b in range(8)], 0).astype(np.float32)
    return out, res


def kernel(**inputs):
    out, _ = run(inputs)
    return out
```

```python
import os
import numpy as np
import concourse.bass as bass
import concourse.mybir as mybir
from concourse.bass_utils import run_bass_kernel_spmd

F32 = mybir.dt.float32
BF16 = mybir.dt.bfloat16
ALU = mybir.AluOpType
AF = mybir.ActivationFunctionType

D = 1024
SEQ = 2048
DEPTH = 2
W = 512
IN_DIM = 10248
TP = 1024
NPASS = SEQ // TP
ST = 512
NSUB = TP // ST
EPS = 1e-6

O_S5U, O_S5G = 0, 512
O_SGU, O_SGV, O_SGG = 1024, 1536, 2048
O_M2Z, O_M2X, O_M2DT = 2560, 3072, 4096
O_SCB, O_SCC, O_SCH, O_SCG = 4104, 4616, 5128, 5640
O_MG = 6152


class Buf:
    __slots__ = ("w", "r")

    def __init__(self):
        self.w = None
        self.r = {}


class Prog:
    def __init__(self, nc):
        self.nc = nc
        self.eng = {"pe": nc.tensor, "act": nc.scalar, "dve": nc.vector, "pool": nc.gpsimd, "sp": nc.sync}
        self.sem = {}
        self.cnt = {}
        self.seen = {e: {} for e in self.eng}
        self.pend = {e: [] for e in self.eng}
        for e in self.eng:
            self.sem[e] = nc.alloc_semaphore("s_" + e)
            self.cnt[e] = 0
        self.nwaits = 0
        self.nins = 0

    def new_sem(self, name):
        self.sem[name] = self.nc.alloc_semaphore(name)
        self.cnt[name] = 0
        return name

    def _wait(self, e, key, val):
        if key not in self.eng:
            val = self.cnt[key]
        if self.seen[e].get(key, 0) >= val:
            return
        self.seen[e][key] = val
        self.eng[e].wait_ge(self.sem[key], val)
        self.nwaits += 1

    def _deps(self, e, reads, writes):
        deps = {}
        for b in reads:
            if b.w is not None:
                k, v = b.w
                if deps.get(k, 0) < v:
                    deps[k] = v
        for b in writes:
            if b.w is not None:
                k, v = b.w
                if deps.get(k, 0) < v:
                    deps[k] = v
            for k, v in b.r.items():
                if deps.get(k, 0) < v:
                    deps[k] = v
        for k, v in deps.items():
            if k == e and (e == "pe" or v > self.cnt[e]):
                continue
            self._wait(e, k, v)

    def _commit(self, k, v, reads, writes):
        for b in reads:
            b.r[k] = v
        for b in writes:
            b.w = (k, v)
            b.r = {}

    def op(self, e, fn, reads=(), writes=(), inc=True):
        self._deps(e, reads, writes)
        ins = fn(self.eng[e])
        self.nins += 1
        if inc:
            self.cnt[e] += 1
            ins.then_inc(self.sem[e], 1)
            self._commit(e, self.cnt[e], reads, writes)
        else:
            v = self.cnt[e] + 1
            self._commit(e, v, reads, writes)

    def barrier(self):
        for e in self.eng:
            for k, v in self.cnt.items():
                if k != e and v > 0:
                    self._wait(e, k, v)

    def dma(self, q, semkey, out, in_, reads=(), writes=(), **kw):
        self._deps(q, reads, writes)
        ins = self.eng[q].dma_start(out=out, in_=in_, **kw)
        self.cnt[semkey] += 16
        ins.then_inc(self.sem[semkey], 16)
        self.nins += 1
        self._commit(semkey, self.cnt[semkey], reads, writes)


def col_layout():
    off = {}
    n = 0

    def add(name, w):
        nonlocal n
        off[name] = n
        n += w
    add("nw", 8)
    add("mb", 32)
    add("scw", 12)
    add("fw", 8)
    add("m2cw", 32)
    add("m2cb", 8)
    add("m2d", 4)
    add("m2nw", 4)
    add("s5d", 4)
    add("mkB", 8)
    add("mkC", 2)
    add("mkCn", 2)
    return off, n


COLOFF, NCOL = col_layout()


def host_cols(inp, l):
    c = np.zeros((128, NCOL), np.float32)
    c[:, COLOFF["nw"]:COLOFF["nw"] + 8] = inp["norm_w"][l].reshape(8, 128).T
    c[:, COLOFF["mb"]:COLOFF["mb"] + 32] = inp["merge_b"][l].reshape(32, 128).T
    c[:, COLOFF["scw"]:COLOFF["scw"] + 12] = inp["sc_conv_w"][l].reshape(12, 128).T
    c[:, COLOFF["fw"]:COLOFF["fw"] + 8] = inp["final_norm_w"].reshape(8, 128).T
    c[:, COLOFF["m2cw"]:COLOFF["m2cw"] + 32] = inp["m2_conv_w"][l].reshape(32, 128).T
    c[:, COLOFF["m2cb"]:COLOFF["m2cb"] + 8] = inp["m2_conv_b"][l].reshape(8, 128).T
    c[:, COLOFF["m2d"]:COLOFF["m2d"] + 4] = np.repeat(inp["m2_d"][l], 64).reshape(4, 128).T
    c[:, COLOFF["m2nw"]:COLOFF["m2nw"] + 4] = inp["m2_norm_w"][l].reshape(4, 128).T
    c[:, COLOFF["s5d"]:COLOFF["s5d"] + 4] = inp["s5_d"][l].reshape(4, 128).T
    gl = np.arange(128) // 16
    for jj in range(4):
        for g2 in range(2):
            c[:, COLOFF["mkB"] + jj * 2 + g2] = (gl == 2 * jj + g2)
    g2p = np.arange(128) // 64
    for g2 in range(2):
        c[:, COLOFF["mkC"] + g2] = (g2p == g2)
        c[:, COLOFF["mkCn"] + g2] = -1.0 * (g2p == g2)
    return c


def host_consts():
    i = np.arange(128)
    k = np.zeros((128, 5, 128), np.float32)
    k[:, 0] = np.eye(128)
    k[:, 1] = (i[:, None] <= i[None, :])
    k[:, 2] = (i[:, None] > i[None, :])
    k[:, 3] = 1.0
    k[:, 4] = (i[:, None] // 16 == i[None, :] // 16)
    return k


def host_rows(inp, l):
    r = np.zeros((128, 1040), np.float32)
    r[:, 0:512] = inp["sgu_ln_w"][l][None, :]
    r[:, 512:1024] = inp["sgu_ln_b"][l][None, :]
    r[:, 1024:1032] = inp["m2_dt_bias"][l][None, :]
    r[:, 1032:1040] = inp["m2_a_log"][l][None, :]
    return r


def host_s5(inp, l):
    G, N, Pq = 32, 64, 16
    def L2(a_gn):
        a = a_gn.reshape(4, 8, N)
        a = np.repeat(a[:, :, None, :], 16, axis=2)
        return a.transpose(1, 2, 0, 3).reshape(128, 256)
    def L2b(b_gnq):
        a = b_gnq.reshape(4, 8, N, Pq)
        return a.transpose(1, 3, 0, 2).reshape(128, 256)
    p5 = np.stack([L2(inp["s5_lambda_re"][l]), L2(inp["s5_lambda_im"][l]),
                   L2(np.repeat(inp["s5_log_step"][l][:, None], N, 1)),
                   L2b(inp["s5_b_re"][l]), L2b(inp["s5_b_im"][l])], 1)
    def PL(a_gn):
        return a_gn.reshape(16, 2, N).transpose(1, 2, 0).reshape(128, 16)
    pq = np.stack([PL(inp["s5_lambda_re"][l]), PL(inp["s5_lambda_im"][l]),
                   PL(np.repeat(inp["s5_log_step"][l][:, None], N, 1))], 1)
    def PLc(c_gpn):
        return c_gpn.reshape(16, 2, Pq, N).transpose(1, 3, 2, 0).reshape(128, 256)
    def PLb(b_gnq):
        return b_gnq.reshape(16, 2, N, Pq).transpose(1, 2, 3, 0).reshape(128, 256)
    pc = np.stack([PLc(inp["s5_c_re"][l]), PLc(inp["s5_c_im"][l]),
                   PLb(inp["s5_b_re"][l]), PLb(inp["s5_b_im"][l])], 1)
    return p5.astype(np.float32), pq.astype(np.float32), pc.astype(np.float32)


def build_nc(branches=("a", "b", "c", "d"), nlayers=DEPTH):
    nc = bass.Bass("TRN2", target_bir_lowering=False)
    xT = nc.dram_tensor("xT", [D, SEQ], F32, kind="ExternalInput").ap()
    w_in = nc.dram_tensor("w_in", [DEPTH, D, IN_DIM], F32, kind="ExternalInput").ap()
    w_br = nc.dram_tensor("w_branch", [DEPTH, 4, W, D], F32, kind="ExternalInput").ap()
    w_out = nc.dram_tensor("w_out", [DEPTH, D, D], F32, kind="ExternalInput").ap()
    w_glu = nc.dram_tensor("w_glu", [DEPTH, W, W], F32, kind="ExternalInput").ap()
    cols_d = nc.dram_tensor("cols", [DEPTH, 128, NCOL], F32, kind="ExternalInput").ap()
    consts_d = nc.dram_tensor("consts", [128, 5, 128], F32, kind="ExternalInput").ap()
    rows_d = nc.dram_tensor("rows", [DEPTH, 128, 1040], F32, kind="ExternalInput").ap()
    sguw_d = nc.dram_tensor("sguw", [DEPTH, 128, 8, 128], F32, kind="ExternalInput").ap()
    sgub_d = nc.dram_tensor("sgub", [DEPTH, 128, 512], F32, kind="ExternalInput").ap()
    s5p_d = nc.dram_tensor("s5p", [DEPTH, 128, 5, 256], F32, kind="ExternalInput").ap()
    s5q_d = nc.dram_tensor("s5q", [DEPTH, 128, 3, 16], F32, kind="ExternalInput").ap()
    s5c_d = nc.dram_tensor("s5c", [DEPTH, 128, 4, 256], F32, kind="ExternalInput").ap()
    yT = nc.dram_tensor("yT", [D, SEQ], F32, kind="ExternalOutput").ap()
    S5_CACHE_N = 9 * 48 + 7 * 48 + 512 + 3 * 2048
    s5cache = nc.dram_tensor("s5cache", [DEPTH, 128, S5_CACHE_N], F32).ap()

    P = Prog(nc)
    sb = nc.alloc_sbuf_tensor
    X = sb("X", [128, 8, TP], F32)
    H = sb("H", [128, 8, TP], BF16)
    ACC = sb("ACC", [128, 8, TP], F32)
    Y = sb("Y", [128, 4, TP], BF16)
    cols = sb("colsb", [128, DEPTH, NCOL], F32)
    consts = sb("constsb", [128, 5, 128], F32)
    identb = sb("identb", [128, 128], BF16)
    mask01b = sb("mask01b", [128, 128], BF16)
    bX = [[Buf() for _ in range(NSUB)] for _ in range(8)]
    bH = [Buf() for _ in range(NSUB)]
    bACC = [[Buf() for _ in range(NSUB)] for _ in range(8)]
    bY = [[Buf() for _ in range(NSUB)] for _ in range(4)]
    bcols, bconsts = Buf(), Buf()
    ident, triu, ltstrict, ones = consts[:, 0, :], consts[:, 1, :], consts[:, 2, :], consts[:, 3, :]
    blockmask = consts[:, 4, :]
    zerob = sb("zerob", [128, 128], BF16)
    bzerob = Buf()
    bones = bconsts

    NS = 4
    slots = [sb("slot%d" % i, [128, 8 * 512], BF16) for i in range(NS)]
    bslot = [Buf() for _ in range(NS)]
    sslot = [P.new_sem("dslot%d" % i) for i in range(NS)]
    slot_rr = [0]

    psbig = nc.alloc_psum_tensor("psbig", [128, 7 * 512], F32)
    psum = [psbig[:, i * 512:(i + 1) * 512] for i in range(7)]
    psT = nc.alloc_psum_tensor("psT", [128, 512], F32)
    bps = [Buf() for _ in range(7)]
    bpsT = Buf()
    psum.append(psT[:, :])
    bps.append(bpsT)

    SCRF = 16600
    scrF = sb("scrF", [128, SCRF], F32)
    scr_pos = [0]

    def scratch_reset(pos=0):
        P.barrier()
        scr_pos[0] = pos

    def falloc(n, parts=128):
        a = scrF[0:parts, scr_pos[0]:scr_pos[0] + n]
        scr_pos[0] += n
        assert scr_pos[0] <= SCRF, scr_pos
        return a, Buf()

    def balloc(n):
        m = (n + 1) // 2
        a = scrF[:, scr_pos[0]:scr_pos[0] + m].bitcast(BF16)[:, 0:n]
        scr_pos[0] += m
        assert scr_pos[0] <= SCRF, scr_pos
        return a, Buf()

    sx = P.new_sem("dx")
    sy = P.new_sem("dy")
    sc = P.new_sem("dc")
    sm = P.new_sem("dm")
    sw8 = P.new_sem("dw8")
    scache = P.new_sem("dcache")
    bcache = [Buf() for _ in range(DEPTH)]
    cur_pass = [0]

    P.dma("sp", sc, cols[:, :, :], cols_d.rearrange("l p n -> p l n"), writes=[bcols])
    P.dma("sp", sc, consts[:, :, :], consts_d, writes=[bconsts])
    bidb = Buf()
    P.op("dve", lambda e: e.tensor_copy(out=identb[:, :], in_=ident), reads=[bconsts], writes=[bidb])
    P.op("dve", lambda e: e.tensor_copy(out=mask01b[:, :], in_=triu), reads=[bconsts], writes=[bidb])
    epsb = sb("epsb", [128, 1], F32)
    bepsb = Buf()
    P.op("pool", lambda e: e.memset(epsb[:, :], EPS), writes=[bepsb])
    P.op("pool", lambda e: e.memset(zerob[:, :], 0.0), writes=[bzerob])

    def col(l, name, j=0):
        o = COLOFF[name] + j
        return cols[:, l, o:o + 1]

    def sched_layer(l):
        out = []

        def mg(kb):
            for hh in range(2):
                out.append([(w_in[l, :, O_MG + kb * D + hh * 512:O_MG + kb * D + (hh + 1) * 512], 8, 0, 512)])
                out.append([(w_br[l, kb, :, hh * 512:(hh + 1) * 512], 4, 0, 512)])
        if "a" in branches:
            out.append([(w_in[l, :, O_S5U:O_S5U + 512], 8, 0, 512)])
            out.append([(w_glu[l, :, :], 4, 0, 512)])
            out.append([(w_in[l, :, O_S5G:O_S5G + 512], 8, 0, 512)])
            mg(0)
        if "b" in branches:
            out.append([(w_in[l, :, O_SGV:O_SGV + 512], 8, 0, 512)])
            for c in range(4):
                out.append([(w_in[l, :, O_SGU + c * 128:O_SGU + (c + 1) * 128], 8, 0, 128),
                            (w_in[l, :, O_SGG + c * 128:O_SGG + (c + 1) * 128], 8, 128, 128)])
            mg(1)
        if "c" in branches:
            out.append([(w_in[l, :, O_M2DT - 120:O_M2DT + 8], 8, 0, 128)])
            out.append([(w_in[l, :, O_M2Z:O_M2Z + 512], 8, 0, 512)])
            for hh in range(2):
                out.append([(w_in[l, :, O_M2X + hh * 512:O_M2X + (hh + 1) * 512], 8, 0, 512)])
            mg(2)
        if "d" in branches:
            for c in range(4):
                out.append([(w_in[l, :, o + c * 128:o + (c + 1) * 128], 8, i * 128, 128) for i, o in enumerate((O_SCB, O_SCC, O_SCH, O_SCG))])
            mg(3)
        for hh in range(2):
            out.append([(w_out[l, :, hh * 512:(hh + 1) * 512], 8, 0, 512)])
        return out

    schedule = [e for _p in range(NPASS) for l in range(nlayers) for e in sched_layer(l)]
    LOOK = 2
    emitted = [0]

    def _emit_load(j):
        i = j % NS
        s = slots[i]
        for src, kt, off, n in schedule[j]:
            dst = s[:, :].rearrange("p (k n) -> p k n", k=8)[:, 0:kt, off:off + n]
            P.dma("pool", sslot[i], dst, src.rearrange("(k p) n -> p k n", p=128), writes=[bslot[i]])

    def load_slot(pieces):
        j = slot_rr[0]
        slot_rr[0] += 1
        assert [(kt, off, n) for _, kt, off, n in pieces] == [(kt, off, n) for _, kt, off, n in schedule[j]], j
        while emitted[0] < min(len(schedule), j + 1 + LOOK):
            _emit_load(emitted[0])
            emitted[0] += 1
        i = j % NS
        return slots[i][:, :].rearrange("p (k n) -> p k n", k=8), bslot[i]

    def mm_group(out_ap, bout, lhs_fn, rhs_fn, nk, reads):
        for k in range(nk):
            P.op("pe", lambda e: e.matmul(out_ap, lhs_fn(k), rhs_fn(k), start=(k == 0), stop=(k == nk - 1)),
                 reads=reads, writes=[bout], inc=(k == nk - 1))

    def hs(j):
        return slice(j * ST, (j + 1) * ST)

    carry_sc = [[sb("csc%d_%d" % (l, c), [128, 2], F32) for c in range(4)] for l in range(DEPTH)]
    bcarry_sc = [[Buf() for c in range(4)] for l in range(DEPTH)]
    carry_m2 = [[sb("cm2%d_%d" % (l, c), [128, 3], F32) for c in range(8)] for l in range(DEPTH)]
    bcarry_m2 = [[Buf() for c in range(8)] for l in range(DEPTH)]
    ST_m2 = [sb("stm2_%d" % l, [128, 512], F32) for l in range(DEPTH)]
    bST_m2 = [Buf() for l in range(DEPTH)]
    carry_s5 = [sb("cs5_%d" % l, [128, 2, 16], F32) for l in range(DEPTH)]
    bcarry_s5 = [Buf() for l in range(DEPTH)]
    for l in range(DEPTH):
        for c in range(4):
            P.op("pool", lambda e: e.memset(carry_sc[l][c][:, :], 0.0), writes=[bcarry_sc[l][c]])
        for c in range(8):
            P.op("pool", lambda e: e.memset(carry_m2[l][c][:, :], 0.0), writes=[bcarry_m2[l][c]])
        P.op("pool", lambda e: e.memset(ST_m2[l][:, :], 0.0), writes=[bST_m2[l]])
        P.op("pool", lambda e: e.memset(carry_s5[l][:, :, :], 0.0), writes=[bcarry_s5[l]])

    def rms_stats(src_fn, breads, nk, scale, rstd, brstd, n=ST, lnexp=False):
        sq, bsq = falloc_sq[0]
        for k in range(nk):
            P.op("act", lambda e: e.activation(out=sq[:, 0:n], in_=src_fn(k), func=AF.Square), reads=breads(k), writes=[bsq])
            P.op("pe", lambda e: e.matmul(psum[6][:, 0:n], ones, sq[:, 0:n], start=(k == 0), stop=(k == nk - 1)),
                 reads=[bsq, bones], writes=[bps[6]])
        if lnexp:
            P.op("act", lambda e: e.activation(out=rstd[:, 0:n], in_=psum[6][:, 0:n], func=AF.Ln, bias=epsb[:, 0:1], scale=scale),
                 reads=[bps[6], bepsb], writes=[brstd])
            P.op("act", lambda e: e.activation(out=rstd[:, 0:n], in_=rstd[:, 0:n], func=AF.Exp, scale=-0.5), reads=[brstd], writes=[brstd])
            return
        P.op("act", lambda e: e.activation(out=rstd[:, 0:n], in_=psum[6][:, 0:n], func=AF.Sqrt, bias=epsb[:, 0:1], scale=scale),
             reads=[bps[6], bepsb], writes=[brstd])
        P.op("dve", lambda e: e.reciprocal(out=rstd[:, 0:n], in_=rstd[:, 0:n]), reads=[brstd], writes=[brstd])

    sq_t = sb("sq_t", [128, ST], F32)
    falloc_sq = [(sq_t, Buf())]
    rstd_t = sb("rstd_t", [128, ST], F32)
    brstd_t = Buf()

    def phase_norm(l):
        for j in range(NSUB):
            rms_stats(lambda k: X[:, k, hs(j)], lambda k: [bX[k][j]], 8, 1.0 / D, rstd_t, brstd_t)
            for k in range(8):
                P.op("dve", lambda e: e.scalar_tensor_tensor(
                    out=H[:, k, hs(j)], in0=X[:, k, hs(j)], scalar=col(l, "nw", k),
                    in1=rstd_t[:, :], op0=ALU.mult, op1=ALU.mult),
                    reads=[bX[k][j], brstd_t, bcols], writes=[bH[j]])

    def phase_d(l):
        scratch_reset()
        pbufs = [falloc(2 + TP) for _ in range(2)]
        hsbs = [falloc(ST) for _ in range(2)]
        qs = [falloc(ST) for _ in range(2)]
        yvs = [falloc(ST) for _ in range(2)]
        sgs = [falloc(ST) for _ in range(2)]
        for c in range(4):
            sl, bsl = load_slot([(w_in[l, :, o + c * 128:o + (c + 1) * 128], 8, i * 128, 128)
                                 for i, o in enumerate((O_SCB, O_SCC, O_SCH, O_SCG))])
            pbuf, bp = pbufs[c % 2]
            P.op("pool", lambda e: e.tensor_copy(out=pbuf[:, 0:2], in_=carry_sc[l][c][:, :]), reads=[bcarry_sc[l][c]], writes=[bp])
            for j in range(NSUB):
                pb = 3 * (j % 2)
                pgate = 6 + (j % 2)
                (hsb, bhsb), (q, bq), (yv, byv), (sg, bsg) = hsbs[j % 2], qs[j % 2], yvs[j % 2], sgs[j % 2]
                for i in range(3):
                    mm_group(psum[pb + i][:, :], bps[pb + i], lambda k: sl[:, k, i * 128:(i + 1) * 128], lambda k: H[:, k, hs(j)], 8, [bsl, bH[j]])
                mm_group(psum[pgate][:, :], bps[pgate], lambda k: sl[:, k, 384:512], lambda k: H[:, k, hs(j)], 8, [bsl, bH[j]])
                P.op("act", lambda e: e.activation(out=hsb, in_=psum[pb + 2][:, :], func=AF.Copy), reads=[bps[pb + 2]], writes=[bhsb])
                P.op("dve", lambda e: e.tensor_tensor(out=pbuf[:, 2 + j * ST:2 + (j + 1) * ST], in0=psum[pb + 1][:, :], in1=hsb, op=ALU.mult),
                     reads=[bps[pb + 1], bhsb], writes=[bp])
                P.op("dve", lambda e: e.tensor_scalar(out=q, in0=pbuf[:, 2 + j * ST:2 + (j + 1) * ST], scalar1=col(l, "scw", 8 + c), scalar2=None, op0=ALU.mult),
                     reads=[bp, bcols], writes=[bq])
                P.op("dve", lambda e: e.scalar_tensor_tensor(out=q, in0=pbuf[:, 1 + j * ST:1 + (j + 1) * ST], scalar=col(l, "scw", 4 + c), in1=q, op0=ALU.mult, op1=ALU.add),
                     reads=[bp, bq, bcols], writes=[bq])
                P.op("dve", lambda e: e.scalar_tensor_tensor(out=q, in0=pbuf[:, j * ST:(j + 1) * ST], scalar=col(l, "scw", c), in1=q, op0=ALU.mult, op1=ALU.add),
                     reads=[bp, bq, bcols], writes=[bq])
                P.op("dve", lambda e: e.tensor_tensor(out=yv, in0=psum[pb][:, :], in1=q, op=ALU.mult), reads=[bps[pb], bq], writes=[byv])
                P.op("act", lambda e: e.activation(out=sg, in_=psum[pgate][:, :], func=AF.Silu), reads=[bps[pgate]], writes=[bsg])
                P.op("pool", lambda e: e.tensor_tensor(out=Y[:, c, hs(j)], in0=yv, in1=sg, op=ALU.mult),
                     reads=[byv, bsg], writes=[bY[c][j]])
            P.op("pool", lambda e: e.tensor_copy(out=carry_sc[l][c][:, :], in_=pbuf[:, TP:TP + 2]), reads=[bp], writes=[bcarry_sc[l][c]])

    def phase_b(l):
        scratch_reset()
        lnw, blnw = falloc(512)
        lnb, blnb = falloc(512)
        wraw, bwraw = falloc(1024)
        bsrow, bbsrow = falloc(512)
        v32s = [falloc(512) for _ in range(2)]
        vns = [falloc(512) for _ in range(2)]
        st6s = [falloc(6) for _ in range(2)]
        mvs = [falloc(2) for _ in range(2)]
        rss = [falloc(1) for _ in range(2)]
        gus = [falloc(ST) for _ in range(2)]
        sgs = [falloc(ST) for _ in range(2)]
        t1s = [falloc(ST) for _ in range(2)]
        wmT, bwmT = balloc(1024)
        VN, bVN = balloc(8 * 512)
        bVNq = [Buf() for _ in range(8)]
        P.dma("sp", sm, lnw, rows_d[l, :, 0:512], writes=[blnw])
        P.dma("sp", sm, lnb, rows_d[l, :, 512:1024], writes=[blnb])
        P.dma("sp", sm, wraw, sguw_d[l].rearrange("s h t -> s (h t)"), writes=[bwraw])
        P.dma("sp", sm, bsrow, sgub_d[l], writes=[bbsrow])
        bsv = bsrow.rearrange("p (c t) -> p c t", c=4)
        P.op("dve", lambda e: e.tensor_tensor(out=wmT.rearrange("p (h t) -> p h t", h=8), in0=wraw.rearrange("p (h t) -> p h t", h=8),
                                             in1=triu.unsqueeze(1).broadcast_to([128, 8, 128]), op=ALU.mult),
             reads=[bwraw, bconsts], writes=[bwmT])
        slv, bslv = load_slot([(w_in[l, :, O_SGV:O_SGV + 512], 8, 0, 512)])
        for qc in range(TP // 128):
            pi = qc % 2
            (v32, bv32), (vn, bvn), (st6, bst6), (mv, bmv), (rs, brs) = v32s[pi], vns[pi], st6s[pi], mvs[pi], rss[pi]
            mm_group(psum[pi][:, :], bps[pi], lambda k: H[:, k, qc * 128:(qc + 1) * 128], lambda k: slv[:, k, 0:512], 8, [bslv, bH[qc // 4]])
            P.op("act", lambda e: e.activation(out=v32, in_=psum[pi][:, :], func=AF.Gelu), reads=[bps[pi]], writes=[bv32])
            P.op("dve", lambda e: e.bn_stats(out=st6, in_=v32), reads=[bv32], writes=[bst6])
            P.op("dve", lambda e: e.bn_aggr(out=mv, in_=st6), reads=[bst6], writes=[bmv])
            P.op("act", lambda e: e.activation(out=rs, in_=mv[:, 1:2], func=AF.Sqrt, bias=epsb[:, 0:1], scale=1.0), reads=[bmv, bepsb], writes=[brs])
            P.op("dve", lambda e: e.reciprocal(out=rs, in_=rs), reads=[brs], writes=[brs])
            P.op("dve", lambda e: e.tensor_scalar(out=vn, in0=v32, scalar1=mv[:, 0:1], scalar2=rs, op0=ALU.subtract, op1=ALU.mult),
                 reads=[bv32, bmv, brs], writes=[bvn])
            P.op("pool", lambda e: e.tensor_tensor(out=vn, in0=vn, in1=lnw, op=ALU.mult), reads=[bvn, blnw], writes=[bvn])
            P.op("pool", lambda e: e.tensor_tensor(out=VN[:, qc * 512:(qc + 1) * 512], in0=vn, in1=lnb, op=ALU.add), reads=[bvn, blnb], writes=[bVNq[qc]])
        for c in range(4):
            sl, bsl = load_slot([(w_in[l, :, O_SGU + c * 128:O_SGU + (c + 1) * 128], 8, 0, 128),
                                 (w_in[l, :, O_SGG + c * 128:O_SGG + (c + 1) * 128], 8, 128, 128)])
            for j in range(NSUB):
                pu, pg, pss = (2, 3, 4) if j % 2 == 0 else (6, 7, 5)
                (gu, bgu), (sg, bsg), (t1, bt1) = gus[j % 2], sgs[j % 2], t1s[j % 2]
                mm_group(psum[pu][:, :], bps[pu], lambda k: sl[:, k, 0:128], lambda k: H[:, k, hs(j)], 8, [bsl, bH[j]])
                mm_group(psum[pg][:, :], bps[pg], lambda k: sl[:, k, 128:256], lambda k: H[:, k, hs(j)], 8, [bsl, bH[j]])
                P.op("act", lambda e: e.activation(out=gu, in_=psum[pu][:, :], func=AF.Gelu), reads=[bps[pu]], writes=[bgu])
                P.op("act", lambda e: e.activation(out=sg, in_=psum[pg][:, :], func=AF.Silu), reads=[bps[pg]], writes=[bsg])
                for qq in range(4):
                    qc = j * 4 + qq
                    for h2 in range(2):
                        h = 2 * c + h2
                        o = psum[pss][h2 * 64:(h2 + 1) * 64, qq * 128:(qq + 1) * 128]
                        P.op("pe", lambda e: e.matmul(o, VN[:, qc * 512 + h * 64:qc * 512 + (h + 1) * 64], wmT[:, h * 128:(h + 1) * 128], start=True, stop=True),
                             reads=[bVNq[qc], bwmT], writes=[bps[pss]], inc=True)
                P.op("dve", lambda e: e.tensor_tensor(out=t1.rearrange("p (q t) -> p q t", q=4), in0=psum[pss][:, :].rearrange("p (q t) -> p q t", q=4),
                                                     in1=bsv[:, c, :].unsqueeze(1).broadcast_to([128, 4, 128]), op=ALU.add), reads=[bps[pss], bbsrow], writes=[bt1])
                P.op("dve", lambda e: e.tensor_tensor(out=t1, in0=t1, in1=gu, op=ALU.mult), reads=[bt1, bgu], writes=[bt1])
                P.op("pool", lambda e: e.tensor_tensor(out=Y[:, c, hs(j)], in0=t1, in1=sg, op=ALU.mult), reads=[bt1, bsg], writes=[bY[c][j]])

    def phase_c(l):
        scratch_reset()
        NQ = TP // 128
        dtb, bdtb = falloc(8)
        alog, balog = falloc(8)
        a_t, ba_t = falloc(8)
        dt, bdt = falloc(64)
        da, bda = falloc(64)
        csc, bcsc = falloc(64)
        dec, bdec = falloc(64)
        eA, beA = falloc(64)
        dtdec, bdtdec = falloc(64)
        cbufs = [falloc(3 + TP) for _ in range(2)]
        qvs = [falloc(ST) for _ in range(2)]
        ltls = [falloc(128) for _ in range(8)]
        reps = [falloc(128) for _ in range(4)]
        erows = [falloc(128) for _ in range(2)]
        t1s = [falloc(128) for _ in range(2)]
        yqs = [falloc(512) for _ in range(2)]
        rstds = [falloc(128) for _ in range(2)]
        xTb, bxT = balloc(4 * TP)
        BTb, bBT = balloc(2 * TP)
        CTb, bCT = balloc(2 * TP)
        ZS, bZS = balloc(4 * TP)
        xdts = [balloc(512) for _ in range(2)]
        xdds = [balloc(512) for _ in range(2)]
        Btoks = [balloc(256) for _ in range(2)]
        scms = [balloc(256) for _ in range(2)]
        LTms = [balloc(128) for _ in range(2)]
        MTs = [balloc(128) for _ in range(2)]
        STbs = [balloc(512) for _ in range(2)]
        STb, bSTb = STbs[0]
        P.dma("sp", sm, dtb, rows_d[l, :, 1024:1032], writes=[bdtb])
        P.dma("sp", sm, alog, rows_d[l, :, 1032:1040], writes=[balog])
        slw8, bwdt = load_slot([(w_in[l, :, O_M2DT - 120:O_M2DT + 8], 8, 0, 128)])
        P.op("act", lambda e: e.activation(out=a_t, in_=alog, func=AF.Exp), reads=[balog], writes=[ba_t])
        P.op("dve", lambda e: e.tensor_scalar(out=a_t, in0=a_t, scalar1=-1.0, scalar2=None, op0=ALU.mult), reads=[ba_t], writes=[ba_t])
        P.op("act", lambda e: e.activation(out=STb, in_=ST_m2[l][:, :], func=AF.Copy), reads=[bST_m2[l]], writes=[bSTb])
        wdtv = slw8[:, :, 120:128]
        for qc in range(NQ):
            mm_group(psum[0][:, qc * 8:(qc + 1) * 8], bps[0], lambda k: H[:, k, qc * 128:(qc + 1) * 128], lambda k: wdtv[:, k, :], 8, [bwdt, bH[qc // 4]])
        P.op("dve", lambda e: e.tensor_tensor(out=dt.rearrange("p (q h) -> p q h", h=8), in0=psum[0][:, 0:64].rearrange("p (q h) -> p q h", h=8),
                                             in1=dtb.unsqueeze(1).broadcast_to([128, NQ, 8]), op=ALU.add), reads=[bps[0], bdtb], writes=[bdt])
        P.op("act", lambda e: e.activation(out=dt, in_=dt, func=AF.Exp), reads=[bdt], writes=[bdt])
        P.op("act", lambda e: e.activation(out=dt, in_=dt, func=AF.Ln, bias=1.0), reads=[bdt], writes=[bdt])
        P.op("dve", lambda e: e.tensor_tensor(out=da.rearrange("p (q h) -> p q h", h=8), in0=dt.rearrange("p (q h) -> p q h", h=8),
                                             in1=a_t.unsqueeze(1).broadcast_to([128, NQ, 8]), op=ALU.mult), reads=[bdt, ba_t], writes=[bda])
        P.op("pe", lambda e: e.matmul(psum[0][:, 64:128], triu, da, start=True, stop=True), reads=[bda, bconsts], writes=[bps[0]])
        P.op("pe", lambda e: e.matmul(psum[0][:, 128:192], ones, da, start=True, stop=True), reads=[bda, bconsts], writes=[bps[0]])
        P.op("act", lambda e: e.activation(out=csc, in_=psum[0][:, 64:128], func=AF.Copy), reads=[bps[0]], writes=[bcsc])
        P.op("dve", lambda e: e.tensor_tensor(out=dec, in0=psum[0][:, 128:192], in1=csc, op=ALU.subtract), reads=[bps[0], bcsc], writes=[bdec])
        P.op("act", lambda e: e.activation(out=dec, in_=dec, func=AF.Exp), reads=[bdec], writes=[bdec])
        P.op("act", lambda e: e.activation(out=eA, in_=psum[0][:, 128:192], func=AF.Exp), reads=[bps[0]], writes=[beA])
        P.op("dve", lambda e: e.tensor_tensor(out=dtdec, in0=dt, in1=dec, op=ALU.mult), reads=[bdt, bdec], writes=[bdtdec])
        CSTOP = int(os.environ.get("CSTOP", "9"))
        if CSTOP <= 1:
            return
        slz, bslz = load_slot([(w_in[l, :, O_M2Z:O_M2Z + 512], 8, 0, 512)])
        n = 0
        for ct in range(4):
            for j in range(NSUB):
                pi = 1 + n % 2
                n += 1
                mm_group(psum[pi][:, :], bps[pi], lambda k: slz[:, k, ct * 128:(ct + 1) * 128], lambda k: H[:, k, hs(j)], 8, [bslz, bH[j]])
                P.op("act", lambda e: e.activation(out=ZS[:, ct * TP + j * ST:ct * TP + (j + 1) * ST], in_=psum[pi][:, :], func=AF.Silu), reads=[bps[pi]], writes=[bZS])
        slx = [load_slot([(w_in[l, :, O_M2X + hh * 512:O_M2X + (hh + 1) * 512], 8, 0, 512)]) for hh in range(2)]
        for ct in range(8):
            sl, bsl = slx[ct // 4]
            cbuf, bcbuf = cbufs[ct % 2]
            P.op("pool", lambda e: e.tensor_copy(out=cbuf[:, 0:3], in_=carry_m2[l][ct][:, :]), reads=[bcarry_m2[l][ct]], writes=[bcbuf])
            for j in range(NSUB):
                pi = 1 + n % 2
                n += 1
                mm_group(psum[pi][:, :], bps[pi], lambda k: sl[:, k, (ct % 4) * 128:(ct % 4 + 1) * 128], lambda k: H[:, k, hs(j)], 8, [bsl, bH[j]])
                P.op("act", lambda e: e.activation(out=cbuf[:, 3 + j * ST:3 + (j + 1) * ST], in_=psum[pi][:, :], func=AF.Copy), reads=[bps[pi]], writes=[bcbuf])
            for j in range(NSUB):
                qv, bqv = qvs[j % 2]
                P.op("dve", lambda e: e.tensor_scalar(out=qv, in0=cbuf[:, 3 + j * ST:3 + (j + 1) * ST], scalar1=col(l, "m2cw", 24 + ct), scalar2=col(l, "m2cb", ct), op0=ALU.mult, op1=ALU.add),
                     reads=[bcbuf, bcols], writes=[bqv])
                for tap in range(3):
                    P.op("dve", lambda e: e.scalar_tensor_tensor(out=qv, in0=cbuf[:, tap + j * ST:tap + (j + 1) * ST], scalar=col(l, "m2cw", tap * 8 + ct), in1=qv, op0=ALU.mult, op1=ALU.add),
                         reads=[bcbuf, bqv, bcols], writes=[bqv])
                if ct < 4:
                    dst, bd = xTb[:, ct * TP + j * ST:ct * TP + (j + 1) * ST], bxT
                elif ct < 6:
                    dst, bd = BTb[:, (ct - 4) * TP + j * ST:(ct - 4) * TP + (j + 1) * ST], bBT
                else:
                    dst, bd = CTb[:, (ct - 6) * TP + j * ST:(ct - 6) * TP + (j + 1) * ST], bCT
                P.op("act", lambda e: e.activation(out=dst, in_=qv, func=AF.Silu), reads=[bqv], writes=[bd])
            P.op("pool", lambda e: e.tensor_copy(out=carry_m2[l][ct][:, :], in_=cbuf[:, TP:TP + 3]), reads=[bcbuf], writes=[bcarry_m2[l][ct]])
        if CSTOP <= 2:
            return
        P.barrier()
        rB0 = rSc = bps[0]
        EB = (4, 6)
        YB = (5, 1)

        def c_front(qc):
            q2 = qc % 2
            (xdt, bxdt), (xdd, bxdd), (Btok, bBtok), (scm, bscm) = xdts[q2], xdds[q2], Btoks[q2], scms[q2]
            (STn, bSTn) = STbs[1 - q2]
            for ct in range(4):
                P.op("pe", lambda e: e.matmul(psum[7][:, ct * 128:(ct + 1) * 128], xTb[:, ct * TP + qc * 128:ct * TP + (qc + 1) * 128], identb[:, :], start=True, stop=True),
                     reads=[bxT, bidb], writes=[bps[7]])
            for g in range(2):
                P.op("pe", lambda e: e.matmul(psum[0][:, g * 128:(g + 1) * 128], BTb[:, g * TP + qc * 128:g * TP + (qc + 1) * 128], identb[:, :], start=True, stop=True),
                     reads=[bBT, bidb], writes=[rB0])
            for g in range(2):
                P.op("pe", lambda e: e.matmul(psum[0][:, 256 + g * 128:256 + (g + 1) * 128], BTb[:, g * TP + qc * 128:g * TP + (qc + 1) * 128],
                                              CTb[:, g * TP + qc * 128:g * TP + (qc + 1) * 128], start=True, stop=True), reads=[bBT, bCT], writes=[rSc])
            for h in range(8):
                dacol = da[:, qc * 8 + h:qc * 8 + h + 1]
                ltl, bltl = ltls[h]
                rep, brep = reps[h // 2]
                P.op("pool", lambda e: e.tensor_scalar(out=ltl, in0=ltstrict, scalar1=dacol, scalar2=1.0, op0=ALU.mult, op1=ALU.mult), reads=[bda, bconsts], writes=[bltl])
                P.op("pool", lambda e: e.tensor_scalar(out=rep[:, (h % 2) * 64:(h % 2 + 1) * 64], in0=ones[:, 0:64], scalar1=dacol, scalar2=1.0, op0=ALU.mult, op1=ALU.mult), reads=[bda, bconsts], writes=[brep])
            for h in range(8):
                P.op("dve", lambda e: e.tensor_scalar(out=xdt[:, h * 64:(h + 1) * 64], in0=psum[7][:, h * 64:(h + 1) * 64], scalar1=dt[:, qc * 8 + h:qc * 8 + h + 1], scalar2=None, op0=ALU.mult),
                     reads=[bps[7], bdt], writes=[bxdt])
                P.op("dve", lambda e: e.tensor_scalar(out=xdd[:, h * 64:(h + 1) * 64], in0=psum[7][:, h * 64:(h + 1) * 64], scalar1=dtdec[:, qc * 8 + h:qc * 8 + h + 1], scalar2=None, op0=ALU.mult),
                     reads=[bps[7], bdtdec], writes=[bxdd])
            P.op("act", lambda e: e.activation(out=Btok, in_=psum[0][:, 0:256], func=AF.Copy), reads=[rB0], writes=[bBtok])
            P.op("dve", lambda e: e.tensor_tensor(out=scm.rearrange("p (g t) -> p g t", g=2), in0=psum[0][:, 256:512].rearrange("p (g t) -> p g t", g=2),
                                                 in1=triu.unsqueeze(1).broadcast_to([128, 2, 128]), op=ALU.mult), reads=[rSc, bconsts], writes=[bscm])
            for g in range(2):
                P.op("pe", lambda e: e.matmul(psum[1][:, g * 256:(g + 1) * 256], Btok[:, g * 128:(g + 1) * 128], xdd[:, g * 256:(g + 1) * 256], start=True, stop=True),
                     reads=[bBtok, bxdd], writes=[bps[1]])
            for h in range(8):
                P.op("dve", lambda e: e.scalar_tensor_tensor(out=ST_m2[l][:, h * 64:(h + 1) * 64], in0=ST_m2[l][:, h * 64:(h + 1) * 64], scalar=eA[:, qc * 8 + h:qc * 8 + h + 1],
                                                            in1=psum[1][:, h * 64:(h + 1) * 64], op0=ALU.mult, op1=ALU.add), reads=[bST_m2[l], beA, bps[1]], writes=[bST_m2[l]])
            P.op("act", lambda e: e.activation(out=STn, in_=ST_m2[l][:, :], func=AF.Copy), reads=[bST_m2[l], bSTn], writes=[bSTn])

        def c_heads(qc):
            q2 = qc % 2
            (xdt, bxdt), (scm, bscm) = xdts[q2], scms[q2]
            (yq, byq) = yqs[q2]
            (STc, bSTc) = STbs[q2]
            for h in range(8):
                g = h // 4
                hp = h % 2
                pair = h // 2
                pp = pair % 2
                (ltl, bltl), (LTm, bLTm), (MT, bMT) = ltls[h], LTms[hp], MTs[hp]
                (rep, brep), (erow, berow), (t1, bt1) = reps[pair], erows[pp], t1s[pp]
                Lreg = psum[2 + hp][:, pair * 128:(pair + 1) * 128]
                bL = bps[2 + hp]
                eb, yb = EB[pp], YB[pp]
                P.op("pe", lambda e: e.matmul(Lreg, ltl, triu, start=True, stop=True), reads=[bltl, bconsts], writes=[bL])
                P.op("act", lambda e: e.activation(out=LTm, in_=Lreg, func=AF.Exp), reads=[bL], writes=[bLTm])
                P.op("dve", lambda e: e.tensor_tensor(out=MT, in0=scm[:, g * 128:(g + 1) * 128], in1=LTm, op=ALU.mult), reads=[bscm, bLTm], writes=[bMT])
                if hp == 1:
                    P.op("pe", lambda e: e.matmul(psum[eb][:, 0:128], rep, triu, start=True, stop=True), reads=[brep, bconsts], writes=[bps[eb]])
                P.op("pe", lambda e: e.matmul(psum[yb][hp * 64:(hp + 1) * 64, 0:128], xdt[:, h * 64:(h + 1) * 64], MT, start=True, stop=True), reads=[bxdt, bMT], writes=[bps[yb]])
                P.op("pe", lambda e: e.matmul(psum[yb][hp * 64:(hp + 1) * 64, 128:256], STc[:, h * 64:(h + 1) * 64], CTb[:, g * TP + qc * 128:g * TP + (qc + 1) * 128], start=True, stop=True),
                     reads=[bSTc, bCT], writes=[bps[yb]])
                if hp == 1:
                    ct = pair
                    P.op("act", lambda e: e.activation(out=erow, in_=psum[eb][:, 0:128], func=AF.Exp), reads=[bps[eb]], writes=[berow])
                    P.op("dve", lambda e: e.tensor_tensor(out=t1, in0=psum[yb][:, 128:256], in1=erow, op=ALU.mult), reads=[bps[yb], berow], writes=[bt1])
                    P.op("dve", lambda e: e.tensor_tensor(out=t1, in0=psum[yb][:, 0:128], in1=t1, op=ALU.add), reads=[bps[yb], bt1], writes=[bt1])
                    P.op("dve", lambda e: e.scalar_tensor_tensor(out=yq[:, ct * 128:(ct + 1) * 128], in0=xTb[:, ct * TP + qc * 128:ct * TP + (qc + 1) * 128], scalar=col(l, "m2d", ct),
                                                                in1=t1, op0=ALU.mult, op1=ALU.add), reads=[bxT, bt1, bcols, byq], writes=[byq])
                    P.op("pool", lambda e: e.tensor_tensor(out=yq[:, ct * 128:(ct + 1) * 128], in0=yq[:, ct * 128:(ct + 1) * 128],
                                                          in1=ZS[:, ct * TP + qc * 128:ct * TP + (qc + 1) * 128], op=ALU.mult), reads=[byq, bZS], writes=[byq])

        def c_tail(qc):
            q2 = qc % 2
            cs = slice(qc * 128, (qc + 1) * 128)
            (yq, byq), (rstd, brstd) = yqs[q2], rstds[q2]
            rms_stats(lambda k: yq[:, k * 128:(k + 1) * 128], lambda k: [byq], 4, 1.0 / W, rstd, brstd, n=128, lnexp=True)
            for ct in range(4):
                P.op("dve", lambda e: e.scalar_tensor_tensor(out=Y[:, ct, cs], in0=yq[:, ct * 128:(ct + 1) * 128], scalar=col(l, "m2nw", ct), in1=rstd[:, 0:128],
                                                            op0=ALU.mult, op1=ALU.mult), reads=[byq, brstd, bcols], writes=[bY[ct][qc // 4]])

        c_front(0)
        for qc in range(NQ):
            c_heads(qc)
            if qc + 1 < NQ:
                c_front(qc + 1)
            c_tail(qc)
        P.barrier()

    def sincos(ang, bang, n, out_sin, out_cos, tmp, tmpi, btmp):
        for shift, dst in ((0.0, out_sin), (0.5 * np.pi, out_cos)):
            P.op("dve", lambda e: e.tensor_scalar(out=tmp[:, 0:n], in0=ang, scalar1=float(shift), scalar2=float(1.0 / (2 * np.pi)), op0=ALU.add, op1=ALU.mult),
                 reads=[bang], writes=[btmp])
            P.op("dve", lambda e: e.tensor_copy(out=tmpi[:, 0:n], in_=tmp[:, 0:n]), reads=[btmp], writes=[btmp])
            P.op("dve", lambda e: e.tensor_copy(out=tmp[:, n:2 * n], in_=tmpi[:, 0:n]), reads=[btmp], writes=[btmp])
            P.op("dve", lambda e: e.tensor_tensor(out=tmp[:, 0:n], in0=tmp[:, 0:n], in1=tmp[:, n:2 * n], op=ALU.subtract), reads=[btmp], writes=[btmp])
            P.op("dve", lambda e: e.tensor_scalar(out=tmp[:, n:2 * n], in0=tmp[:, 0:n], scalar1=0.5, scalar2=None, op0=ALU.is_gt), reads=[btmp], writes=[btmp])
            P.op("dve", lambda e: e.tensor_tensor(out=tmp[:, 0:n], in0=tmp[:, 0:n], in1=tmp[:, n:2 * n], op=ALU.subtract), reads=[btmp], writes=[btmp])
            P.op("dve", lambda e: e.tensor_scalar(out=tmp[:, n:2 * n], in0=tmp[:, 0:n], scalar1=-0.5, scalar2=None, op0=ALU.is_lt), reads=[btmp], writes=[btmp])
            P.op("dve", lambda e: e.tensor_tensor(out=tmp[:, 0:n], in0=tmp[:, 0:n], in1=tmp[:, n:2 * n], op=ALU.add), reads=[btmp], writes=[btmp])
            P.op("act", lambda e: e.activation(out=dst, in_=tmp[:, 0:n], func=AF.Sin, scale=float(2 * np.pi * (1 - 1e-6))), reads=[btmp], writes=[btmp])

    def s5_lambda(lre, lim, lst, n, bsrc, abre, abim, tmp, tmpi, btmp, scr, bscr):
        step, lrs, lis, mag = scr[:, 0:n], scr[:, n:2 * n], scr[:, 2 * n:3 * n], scr[:, 3 * n:4 * n]
        P.op("act", lambda e: e.activation(out=step, in_=lst, func=AF.Exp), reads=[bsrc], writes=[bscr])
        P.op("dve", lambda e: e.tensor_tensor(out=lrs, in0=lre, in1=step, op=ALU.mult), reads=[bsrc, bscr], writes=[bscr])
        P.op("dve", lambda e: e.tensor_tensor(out=lis, in0=lim, in1=step, op=ALU.mult), reads=[bsrc, bscr], writes=[bscr])
        P.op("act", lambda e: e.activation(out=mag, in_=lrs, func=AF.Exp), reads=[bscr], writes=[bscr])
        sincos(lis, bscr, n, abim, abre, tmp, tmpi, btmp)
        P.op("dve", lambda e: e.tensor_tensor(out=abre, in0=abre, in1=mag, op=ALU.mult), reads=[btmp, bscr], writes=[btmp])
        P.op("dve", lambda e: e.tensor_tensor(out=abim, in0=abim, in1=mag, op=ALU.mult), reads=[btmp, bscr], writes=[btmp])

    def coef_calc(lre, lim, abre, abim, n, cre_, cim_, scr, rd, bscr, bout):
        nr, den, u1, u2 = scr[:, 0:n], scr[:, n:2 * n], scr[:, 2 * n:3 * n], scr[:, 3 * n:4 * n]
        TT = lambda o, a, b, op, r_, w_: P.op("dve", lambda e: e.tensor_tensor(out=o, in0=a, in1=b, op=op), reads=r_, writes=w_)
        P.op("dve", lambda e: e.tensor_scalar(out=nr, in0=abre, scalar1=-1.0, scalar2=None, op0=ALU.add), reads=rd, writes=[bscr])
        TT(den, lre, lre, ALU.mult, rd, [bscr])
        TT(u1, lim, lim, ALU.mult, rd, [bscr])
        TT(den, den, u1, ALU.add, [bscr], [bscr])
        P.op("dve", lambda e: e.reciprocal(out=den, in_=den), reads=[bscr], writes=[bscr])
        TT(u1, nr, lre, ALU.mult, [bscr] + rd, [bscr])
        TT(u2, abim, lim, ALU.mult, rd, [bscr])
        TT(u1, u1, u2, ALU.add, [bscr], [bscr])
        TT(cre_, u1, den, ALU.mult, [bscr], [bout])
        TT(u1, abim, lre, ALU.mult, rd, [bscr])
        TT(u2, nr, lim, ALU.mult, [bscr] + rd, [bscr])
        TT(u1, u1, u2, ALU.subtract, [bscr], [bscr])
        TT(cim_, u1, den, ALU.mult, [bscr], [bout])

    def phase_a(l):
        scratch_reset()
        NSC = 7
        Q8 = 8
        CC = TP // Q8
        TT = lambda o, a, b, op, rd, wr: P.op("dve", lambda e: e.tensor_tensor(out=o, in0=a, in1=b, op=op), reads=rd, writes=wr)
        STT = lambda o, a, sc_, b, rd, wr: P.op("dve", lambda e: e.scalar_tensor_tensor(out=o, in0=a, scalar=sc_, in1=b, op0=ALU.mult, op1=ALU.add), reads=rd, writes=wr)
        TS = lambda o, a, sc_, rd, wr: P.op("dve", lambda e: e.tensor_scalar(out=o, in0=a, scalar1=sc_, scalar2=None, op0=ALU.mult), reads=rd, writes=wr)
        pwc, bpwc = falloc(9 * 3 * 16)
        pws, bpws = falloc(NSC * 3 * 16)
        pcC, bpcC = falloc(2 * 256)
        BD, bBD = balloc(2 * 2048)
        CD, bCD = balloc(2 * 2048)
        BDp, bBDp = balloc(2 * 2048)
        pwcv = pwc.rearrange("p (k a g) -> p k a g", k=9, a=3)
        pwsv = pws.rearrange("p (k a g) -> p k a g", k=NSC, a=3)
        mark = scr_pos[0]
        p5, bp5 = falloc(5 * 256)
        pq, bpq = falloc(3 * 16)
        pbp, bpbp = falloc(2 * 256)
        tmp, btmp = falloc(512)
        tmpi_f, _ = falloc(256)
        tmpi = tmpi_f.bitcast(mybir.dt.int32)
        scr, bscr = falloc(1024)
        ab, bab = falloc(512)
        cf, bcf = falloc(512)
        bb, bbb = falloc(512)
        abp, babp = falloc(32)
        cfp, bcfp = falloc(32)
        bbp, bbbp = falloc(512)
        do_prep = (cur_pass[0] == 0)
        assert mark == S5_CACHE_N, mark
        pers = [bpwc, bpws, bpcC, bBD, bCD, bBDp]
        if not do_prep:
            P.dma("sp", scache, scrF[:, 0:mark], s5cache[l], reads=[bcache[l]], writes=pers)
        if do_prep:
            P.dma("sp", sm, p5.rearrange("p (a n) -> p a n", a=5), s5p_d[l], writes=[bp5])
            P.dma("sp", sm, pq.rearrange("p (a n) -> p a n", a=3), s5q_d[l], writes=[bpq])
            P.dma("sp", sm, pcC.rearrange("p (a n) -> p a n", a=2), s5c_d[l, :, 0:2, :], writes=[bpcC])
            P.dma("sp", sm, pbp.rearrange("p (a n) -> p a n", a=2), s5c_d[l, :, 2:4, :], writes=[bpbp])
            lre, lim, lst, bre, bim = (p5[:, i * 256:(i + 1) * 256] for i in range(5))
            abre, abim = ab[:, 0:256], ab[:, 256:512]
            s5_lambda(lre, lim, lst, 256, bp5, abre, abim, tmp, tmpi, btmp, scr, bscr)
            cre_, cim_ = cf[:, 0:256], cf[:, 256:512]
            coef_calc(lre, lim, abre, abim, 256, cre_, cim_, scr, [bp5, btmp], bscr, bcf)
            u1, u2 = scr[:, 512:768], scr[:, 768:1024]
            bbre, bbim = bb[:, 0:256], bb[:, 256:512]
            TT(u1, cre_, bre, ALU.mult, [bcf, bp5], [bscr])
            TT(u2, cim_, bim, ALU.mult, [bcf, bp5], [bscr])
            TT(bbre, u1, u2, ALU.subtract, [bscr], [bbb])
            TT(u1, cre_, bim, ALU.mult, [bcf, bp5], [bscr])
            TT(u2, cim_, bre, ALU.mult, [bcf, bp5], [bscr])
            TT(bbim, u1, u2, ALU.add, [bscr], [bbb])
            for ri, src in enumerate((bbre, bbim)):
                dstv = BD[:, ri * 2048:(ri + 1) * 2048].rearrange("p (c j g n) -> p c j g n", c=4, j=4, g=2)
                for jj in range(4):
                    for g2 in range(2):
                        TS(dstv[:, :, jj, g2, :], src.rearrange("p (c n) -> p c n", c=4), col(l, "mkB", jj * 2 + g2), [bbb, bcols], [bBD])
            P.op("pool", lambda e: e.memset(CD, 0.0), writes=[bCD])
            for ri, nm in enumerate(("mkC", "mkCn")):
                dstv = CD[:, ri * 2048:(ri + 1) * 2048].rearrange("p (c j m) -> p c j m", c=4, j=4)
                srcv = pcC[:, ri * 256:(ri + 1) * 256].rearrange("p (q c j) -> p c j q", q=16, c=4, j=4)
                for jj in range(4):
                    for g2 in range(2):
                        gl = 2 * jj + g2
                        TS(dstv[:, :, jj, gl * 16:(gl + 1) * 16], srcv[:, :, jj, :], col(l, nm, g2), [bpcC, bcols, bCD], [bCD])
            s5_lambda(pq[:, 0:16], pq[:, 16:32], pq[:, 32:48], 16, bpq, abp[:, 0:16], abp[:, 16:32], tmp, tmpi, btmp, scr, bscr)
            coef_calc(pq[:, 0:16], pq[:, 16:32], abp[:, 0:16], abp[:, 16:32], 16, cfp[:, 0:16], cfp[:, 16:32], scr, [bpq, btmp], bscr, bcfp)
            bq_re = pbp[:, 0:256].rearrange("p (q g) -> p q g", q=16)
            bq_im = pbp[:, 256:512].rearrange("p (q g) -> p q g", q=16)
            cfr = cfp[:, 0:16].unsqueeze(1).broadcast_to([128, 16, 16])
            cfi = cfp[:, 16:32].unsqueeze(1).broadcast_to([128, 16, 16])
            w1 = scr[:, 0:256].rearrange("p (q g) -> p q g", q=16)
            w2 = scr[:, 256:512].rearrange("p (q g) -> p q g", q=16)
            bbp_re = bbp[:, 0:256].rearrange("p (q g) -> p q g", q=16)
            bbp_im = bbp[:, 256:512].rearrange("p (q g) -> p q g", q=16)
            TT(w1, bq_re, cfr, ALU.mult, [bpbp, bcfp], [bscr])
            TT(w2, bq_im, cfi, ALU.mult, [bpbp, bcfp], [bscr])
            TT(bbp_re, w1, w2, ALU.subtract, [bscr], [bbbp])
            TT(w1, bq_im, cfr, ALU.mult, [bpbp, bcfp], [bscr])
            TT(w2, bq_re, cfi, ALU.mult, [bpbp, bcfp], [bscr])
            TT(bbp_im, w1, w2, ALU.add, [bscr], [bbbp])
            P.op("pool", lambda e: e.memset(BDp, 0.0), writes=[bBDp])
            for ri in range(2):
                dstv = BDp[:, ri * 2048:(ri + 1) * 2048].rearrange("p (c j m) -> p c j m", c=4, j=4)
                srcv = bbp[:, ri * 256:(ri + 1) * 256].rearrange("p (q c j) -> p c j q", q=16, c=4, j=4)
                for jj in range(4):
                    for g2 in range(2):
                        gl = 2 * jj + g2
                        TS(dstv[:, :, jj, gl * 16:(gl + 1) * 16], srcv[:, :, jj, :], col(l, "mkC", g2), [bbbp, bcols, bBDp], [bBDp])
            P.op("pool", lambda e: e.memset(pwcv[:, 0, 0, :], 1.0), writes=[bpwc])
            P.op("pool", lambda e: e.memset(pwcv[:, 0, 1:3, :], 0.0), reads=[bpwc], writes=[bpwc])
            P.op("dve", lambda e: e.tensor_copy(out=pwcv[:, 1, 0, :], in_=abp[:, 0:16]), reads=[btmp, bpwc], writes=[bpwc])
            P.op("dve", lambda e: e.tensor_copy(out=pwcv[:, 1, 1, :], in_=abp[:, 16:32]), reads=[btmp, bpwc], writes=[bpwc])
            lr_, li_ = abp[:, 0:16], abp[:, 16:32]
            for k in range(2, 9):
                a_, b_ = pwcv[:, k - 1, 0, :], pwcv[:, k - 1, 1, :]
                TT(scr[:, 0:16], a_, lr_, ALU.mult, [bpwc, btmp], [bscr])
                TT(scr[:, 16:32], b_, li_, ALU.mult, [bpwc, btmp], [bscr])
                TT(pwcv[:, k, 0, :], scr[:, 0:16], scr[:, 16:32], ALU.subtract, [bscr, bpwc], [bpwc])
                TT(scr[:, 32:48], a_, li_, ALU.mult, [bpwc, btmp], [bscr])
                TT(scr[:, 48:64], b_, lr_, ALU.mult, [bpwc, btmp], [bscr])
                TT(pwcv[:, k, 1, :], scr[:, 32:48], scr[:, 48:64], ALU.add, [bscr, bpwc], [bpwc])
            for k in range(1, 9):
                TS(pwcv[:, k, 2, :], pwcv[:, k, 1, :], -1.0, [bpwc], [bpwc])
            P.op("dve", lambda e: e.tensor_copy(out=pwsv[:, 0, :, :], in_=pwcv[:, 8, :, :]), reads=[bpwc], writes=[bpws])
            for k in range(1, NSC):
                a_, b_ = pwsv[:, k - 1, 0, :], pwsv[:, k - 1, 1, :]
                TT(scr[:, 0:16], a_, a_, ALU.mult, [bpws], [bscr])
                TT(scr[:, 16:32], b_, b_, ALU.mult, [bpws], [bscr])
                TT(pwsv[:, k, 0, :], scr[:, 0:16], scr[:, 16:32], ALU.subtract, [bscr, bpws], [bpws])
                TT(scr[:, 32:48], a_, b_, ALU.mult, [bpws], [bscr])
                TS(pwsv[:, k, 1, :], scr[:, 32:48], 2.0, [bscr, bpws], [bpws])
                TS(pwsv[:, k, 2, :], scr[:, 32:48], -2.0, [bscr, bpws], [bpws])
            P.dma("sp", scache, s5cache[l], scrF[:, 0:mark], reads=pers, writes=[bcache[l]])
        scratch_reset(mark)
        t2, bt2 = falloc(TP)
        XS, _ = falloc(4 * CC)
        bXS = [[Buf(), Buf()], [Buf(), Buf()]]
        XSv = [[XS[:, (b * 2 + ri) * CC:(b * 2 + ri + 1) * CC] for ri in range(2)] for b in range(2)]
        SP, _ = falloc(2 * CC)
        bSP = [Buf(), Buf()]
        SPv = [SP[:, 0:CC], SP[:, CC:2 * CC]]
        stmp, _ = falloc(4 * CC)
        bstmp = [Buf() for _ in range(4)]
        m12, bm12 = falloc(128)
        U, bU = balloc(4 * TP)
        G1, bG1 = balloc(4 * TP)
        Kc, bKc = balloc(8 * 128)
        Mc, bMc = balloc(8 * 2 * 64)
        SX, bSX = balloc(8 * 2 * CC)
        slu, bslu = load_slot([(w_in[l, :, O_S5U:O_S5U + 512], 8, 0, 512)])
        n = 0
        for c in range(4):
            for j in range(NSUB):
                pi = n % 2
                n += 1
                mm_group(psum[pi][:, :], bps[pi], lambda k: slu[:, k, c * 128:(c + 1) * 128], lambda k: H[:, k, hs(j)], 8, [bslu, bH[j]])
                P.op("act", lambda e: e.activation(out=U[:, c * TP + j * ST:c * TP + (j + 1) * ST], in_=psum[pi][:, :], func=AF.Copy), reads=[bps[pi]], writes=[bU])
        bmv = blockmask.rearrange("p (g q) -> p g q", g=8)
        for c in range(4):
            Uc = U[:, c * TP:(c + 1) * TP]
            Ucv = Uc.rearrange("p (cc r) -> p cc r", r=8)
            Cre = pcC[:, 0:256].rearrange("q (p g) -> q p g", p=16)[:, :, 4 * c:4 * c + 4]
            Cim = pcC[:, 256:512].rearrange("q (p g) -> q p g", p=16)[:, :, 4 * c:4 * c + 4]
            m1 = m12[:, 0:64].rearrange("q (p j) -> q p j", p=16)
            m2 = m12[:, 64:128].rearrange("q (p j) -> q p j", p=16)
            TTp = lambda o, a, b, op, rd, wr: P.op("pool", lambda e: e.tensor_tensor(out=o, in0=a, in1=b, op=op), reads=rd, writes=wr)
            for tau in range(8):
                Mre = Mc[:, (tau * 2) * 64:(tau * 2 + 1) * 64].rearrange("q (p j) -> q p j", p=16)
                Mim = Mc[:, (tau * 2 + 1) * 64:(tau * 2 + 2) * 64].rearrange("q (p j) -> q p j", p=16)
                if tau == 0:
                    P.op("pool", lambda e: e.tensor_copy(out=Mre, in_=Cre), reads=[bpcC, bMc], writes=[bMc])
                    P.op("pool", lambda e: e.tensor_scalar(out=Mim, in0=Cim, scalar1=-1.0, scalar2=1.0, op0=ALU.mult, op1=ALU.mult), reads=[bpcC, bMc], writes=[bMc])
                    continue
                Pre = pwcv[:, tau, 0, 4 * c:4 * c + 4].unsqueeze(1).broadcast_to([128, 16, 4])
                Pim = pwcv[:, tau, 1, 4 * c:4 * c + 4].unsqueeze(1).broadcast_to([128, 16, 4])
                nPim = pwcv[:, tau, 2, 4 * c:4 * c + 4].unsqueeze(1).broadcast_to([128, 16, 4])
                TTp(m1, Cre, Pre, ALU.mult, [bpcC, bpwc, bm12], [bm12])
                TTp(m2, Cim, Pim, ALU.mult, [bpcC, bpwc, bm12], [bm12])
                TTp(Mre, m1, m2, ALU.subtract, [bm12, bMc], [bMc])
                TTp(m1, Cre, nPim, ALU.mult, [bpcC, bpwc, bm12], [bm12])
                TTp(m2, Cim, Pre, ALU.mult, [bpcC, bpwc, bm12], [bm12])
                TTp(Mim, m1, m2, ALU.subtract, [bm12, bMc], [bMc])
            for tau in range(8):
                nmm = 0
                for jj in range(4):
                    gp = 4 * c + jj
                    for ri in range(2):
                        rhs = Mc[:, (tau * 2 + ri) * 64:(tau * 2 + ri + 1) * 64].rearrange("q (p j) -> q p j", p=16)[:, :, jj]
                        P.op("pe", lambda e: e.matmul(psum[6][:, tau * 16:(tau + 1) * 16], BDp[:, ri * 2048 + gp * 128:ri * 2048 + (gp + 1) * 128], rhs,
                                                      start=(nmm == 0), stop=(nmm == 7)), reads=[bBDp, bMc], writes=[bps[6]], inc=(nmm == 7))
                        nmm += 1
            for tau in range(8):
                P.op("dve", lambda e: e.tensor_tensor(out=Kc[:, tau * 128:(tau + 1) * 128].rearrange("p (g q) -> p g q", g=8), in0=bmv,
                                                     in1=psum[6][:, tau * 16:(tau + 1) * 16].unsqueeze(1).broadcast_to([128, 8, 16]), op=ALU.mult),
                     reads=[bps[6], bconsts, bKc], writes=[bKc])
            for bk in (4, 5):
                P.op("pe", lambda e: e.matmul(psum[bk][:, :], zerob[:, :], Uc[:, 0:512], start=True, stop=False, skip_group_check=True),
                     reads=[bzerob, bU], writes=[bps[bk]], inc=True)
            for r in range(8):
                for rp in range(r + 1):
                    last = (r == 7 and rp == 7)
                    P.op("pe", lambda e: e.matmul(psum[4 + r // 4][:, (r % 4) * 128:(r % 4 + 1) * 128], Kc[:, (r - rp) * 128:(r - rp + 1) * 128], Ucv[:, :, rp],
                                                  start=False, stop=False, skip_group_check=True), reads=[bKc, bU], writes=[bps[4 + r // 4]], inc=last)
            for jj in range(4):
                gp = 4 * c + jj
                for j in range(NSUB):
                    for ri in range(2):
                        pi = 2 * ri + j
                        P.op("pe", lambda e: e.matmul(psum[pi][:, :], BD[:, ri * 2048 + gp * 128:ri * 2048 + (gp + 1) * 128], Uc[:, j * ST:(j + 1) * ST], start=True, stop=True),
                             reads=[bBD, bU], writes=[bps[pi]])
                bv = [psbig[:, ri * 1024:(ri + 1) * 1024].rearrange("p (cc r) -> p cc r", r=8) for ri in range(2)]
                bpb = [[bps[0], bps[1]], [bps[2], bps[3]]]
                acc = [XSv[0][ri][:, 0:CC] for ri in range(2)]
                for ri in range(2):
                    P.op("act", lambda e: e.activation(out=acc[ri], in_=bv[ri][:, :, 7], func=AF.Copy), reads=bpb[ri] + [bXS[0][ri]], writes=[bXS[0][ri]])
                for r in range(7):
                    k = 7 - r
                    pr, pi_, npi = pwcv[:, k, 0, gp:gp + 1], pwcv[:, k, 1, gp:gp + 1], pwcv[:, k, 2, gp:gp + 1]
                    STT(acc[0], bv[0][:, :, r], pr, acc[0], bpb[0] + [bpwc, bXS[0][0]], [bXS[0][0]])
                    STT(acc[1], bv[1][:, :, r], pr, acc[1], bpb[1] + [bpwc, bXS[0][1]], [bXS[0][1]])
                    STT(acc[0], bv[1][:, :, r], npi, acc[0], bpb[1] + [bpwc, bXS[0][0]], [bXS[0][0]])
                    STT(acc[1], bv[0][:, :, r], pi_, acc[1], bpb[0] + [bpwc, bXS[0][1]], [bXS[0][1]])
                cr, ci = carry_s5[l][:, 0, gp:gp + 1], carry_s5[l][:, 1, gp:gp + 1]
                p8r, p8i, p8n = pwcv[:, 8, 0, gp:gp + 1], pwcv[:, 8, 1, gp:gp + 1], pwcv[:, 8, 2, gp:gp + 1]
                STT(XSv[0][0][:, 0:1], cr, p8r, XSv[0][0][:, 0:1], [bcarry_s5[l], bpwc, bXS[0][0]], [bXS[0][0]])
                STT(XSv[0][1][:, 0:1], ci, p8r, XSv[0][1][:, 0:1], [bcarry_s5[l], bpwc, bXS[0][1]], [bXS[0][1]])
                STT(XSv[0][0][:, 0:1], ci, p8n, XSv[0][0][:, 0:1], [bcarry_s5[l], bpwc, bXS[0][0]], [bXS[0][0]])
                STT(XSv[0][1][:, 0:1], cr, p8i, XSv[0][1][:, 0:1], [bcarry_s5[l], bpwc, bXS[0][1]], [bXS[0][1]])
                sbuf_i = 0
                for k in range(NSC):
                    sh = 1 << k
                    src, dst = XSv[sbuf_i], XSv[1 - sbuf_i]
                    bs_, bd_ = bXS[sbuf_i], bXS[1 - sbuf_i]
                    ar, ai, nai = pwsv[:, k, 0, gp:gp + 1], pwsv[:, k, 1, gp:gp + 1], pwsv[:, k, 2, gp:gp + 1]
                    STT(dst[0][:, sh:CC], src[0][:, 0:CC - sh], ar, src[0][:, sh:CC], [bs_[0], bpws, bd_[0]], [bd_[0]])
                    STT(dst[1][:, sh:CC], src[1][:, 0:CC - sh], ar, src[1][:, sh:CC], [bs_[1], bpws, bd_[1]], [bd_[1]])
                    STT(dst[0][:, sh:CC], src[1][:, 0:CC - sh], nai, dst[0][:, sh:CC], [bs_[1], bpws, bd_[0]], [bd_[0]])
                    STT(dst[1][:, sh:CC], src[0][:, 0:CC - sh], ai, dst[1][:, sh:CC], [bs_[0], bpws, bd_[1]], [bd_[1]])
                    for ri in range(2):
                        P.op("act", lambda e: e.activation(out=dst[ri][:, 0:sh], in_=src[ri][:, 0:sh], func=AF.Copy), reads=[bs_[ri], bd_[ri]], writes=[bd_[ri]])
                    sbuf_i = 1 - sbuf_i
                S_, bS_ = XSv[sbuf_i], bXS[sbuf_i]
                for ri in range(2):
                    P.op("pool", lambda e: e.tensor_copy(out=SPv[ri][:, 1:CC], in_=S_[ri][:, 0:CC - 1]), reads=[bS_[ri], bSP[ri]], writes=[bSP[ri]])
                    P.op("pool", lambda e: e.tensor_copy(out=SPv[ri][:, 0:1], in_=carry_s5[l][:, ri, gp:gp + 1]), reads=[bcarry_s5[l], bSP[ri]], writes=[bSP[ri]])
                for ri in range(2):
                    P.op("pool", lambda e: e.tensor_copy(out=carry_s5[l][:, ri, gp:gp + 1], in_=S_[ri][:, CC - 1:CC]), reads=[bS_[ri], bcarry_s5[l]], writes=[bcarry_s5[l]])
                for x in range(1, 9):
                    pr, pi_, npi = pwcv[:, x, 0, gp:gp + 1], pwcv[:, x, 1, gp:gp + 1], pwcv[:, x, 2, gp:gp + 1]
                    sb2 = (x % 2) * 2
                    t_re, t_im = stmp[:, sb2 * CC:(sb2 + 1) * CC], stmp[:, (sb2 + 1) * CC:(sb2 + 2) * CC]
                    P.op("pool", lambda e: e.tensor_scalar(out=t_re, in0=SPv[0], scalar1=pr, scalar2=1.0, op0=ALU.mult, op1=ALU.mult), reads=[bSP[0], bpwc, bstmp[sb2]], writes=[bstmp[sb2]])
                    P.op("pool", lambda e: e.tensor_scalar(out=t_im, in0=SPv[1], scalar1=pr, scalar2=1.0, op0=ALU.mult, op1=ALU.mult), reads=[bSP[1], bpwc, bstmp[sb2 + 1]], writes=[bstmp[sb2 + 1]])
                    STT(SX[:, ((x - 1) * 2) * CC:((x - 1) * 2 + 1) * CC], SPv[1], npi, t_re, [bSP[1], bpwc, bstmp[sb2], bSX], [bSX])
                    STT(SX[:, ((x - 1) * 2 + 1) * CC:((x - 1) * 2 + 2) * CC], SPv[0], pi_, t_im, [bSP[0], bpwc, bstmp[sb2 + 1], bSX], [bSX])
                for r in range(8):
                    for ri in range(2):
                        last = (r == 7 and ri == 1)
                        P.op("pe", lambda e: e.matmul(psum[4 + r // 4][:, (r % 4) * 128:(r % 4 + 1) * 128], CD[:, ri * 2048 + gp * 128:ri * 2048 + (gp + 1) * 128],
                                                      SX[:, (r * 2 + ri) * CC:(r * 2 + ri + 1) * CC], start=False, stop=(jj == 3 and last), skip_group_check=True),
                             reads=[bCD, bSX], writes=[bps[4 + r // 4]], inc=last)
            t2v = t2.rearrange("p (cc r) -> p cc r", r=8)
            for bk in range(2):
                P.op("dve", lambda e: e.scalar_tensor_tensor(out=t2v[:, :, 4 * bk:4 * bk + 4], in0=Ucv[:, :, 4 * bk:4 * bk + 4], scalar=col(l, "s5d", c),
                                                            in1=psum[4 + bk][:, :].rearrange("p (r cc) -> p cc r", r=4), op0=ALU.mult, op1=ALU.add),
                     reads=[bU, bcols, bps[4 + bk], bt2], writes=[bt2])
            for j in range(NSUB):
                P.op("act", lambda e: e.activation(out=G1[:, c * TP + j * ST:c * TP + (j + 1) * ST], in_=t2[:, hs(j)], func=AF.Gelu), reads=[bt2], writes=[bG1])
        P.barrier()
        sig, bsig = XS, Buf()
        gate_s, bgs = stmp, Buf()
        slw, bslw = load_slot([(w_glu[l, :, :], 4, 0, 512)])
        slg, bslg = load_slot([(w_in[l, :, O_S5G:O_S5G + 512], 8, 0, 512)])
        for co in range(4):
            for j in range(NSUB):
                p1, p2 = (0, 1) if (co * NSUB + j) % 2 == 0 else (2, 3)
                mm_group(psum[p1][:, :], bps[p1], lambda k: slw[:, k, co * 128:(co + 1) * 128], lambda k: G1[:, k * TP + j * ST:k * TP + (j + 1) * ST], 4, [bslw, bG1])
                mm_group(psum[p2][:, :], bps[p2], lambda k: slg[:, k, co * 128:(co + 1) * 128], lambda k: H[:, k, hs(j)], 8, [bslg, bH[j]])
                P.op("act", lambda e: e.activation(out=sig, in_=psum[p1][:, :], func=AF.Sigmoid), reads=[bps[p1]], writes=[bsig])
                P.op("act", lambda e: e.activation(out=gate_s, in_=psum[p2][:, :], func=AF.Silu), reads=[bps[p2]], writes=[bgs])
                P.op("dve", lambda e: e.tensor_tensor(out=t2[:, 0:ST], in0=G1[:, co * TP + j * ST:co * TP + (j + 1) * ST], in1=sig, op=ALU.mult), reads=[bG1, bsig, bt2], writes=[bt2])
                P.op("pool", lambda e: e.tensor_tensor(out=Y[:, co, hs(j)], in0=t2[:, 0:ST], in1=gate_s, op=ALU.mult), reads=[bt2, bgs], writes=[bY[co][j]])

    first_merge = [True]
    mg_t = sb("mg_t", [128, ST], F32)
    mt_t = sb("mt_t", [128, ST], F32)
    bmg, bmt = Buf(), Buf()

    def phase_merge(l, kb):
        g, bg, t, bt = mg_t[:, :], bmg, mt_t[:, :], bmt
        for hh in range(2):
            slg, bslg = load_slot([(w_in[l, :, O_MG + kb * D + hh * 512:O_MG + kb * D + (hh + 1) * 512], 8, 0, 512)])
            slb, bslb = load_slot([(w_br[l, kb, :, hh * 512:(hh + 1) * 512], 4, 0, 512)])
            for dt_ in range(4):
                d = hh * 4 + dt_
                for j in range(NSUB):
                    pg, pbk = (0, 1) if (dt_ * NSUB + j) % 2 == 0 else (2, 3)
                    mm_group(psum[pg][:, :], bps[pg], lambda k: slg[:, k, dt_ * 128:(dt_ + 1) * 128], lambda k: H[:, k, hs(j)], 8, [bslg, bH[j]])
                    mm_group(psum[pbk][:, :], bps[pbk], lambda k: slb[:, k, dt_ * 128:(dt_ + 1) * 128], lambda k: Y[:, k, hs(j)], 4,
                             [bslb] + [bY[k][j] for k in range(4)])
                    P.op("act", lambda e: e.activation(out=g, in_=psum[pg][:, :], func=AF.Sigmoid, bias=col(l, "mb", kb * 8 + d)),
                         reads=[bps[pg], bcols], writes=[bg])
                    if first_merge[0]:
                        P.op("dve", lambda e: e.tensor_tensor(out=ACC[:, d, hs(j)], in0=psum[pbk][:, :], in1=g, op=ALU.mult),
                             reads=[bps[pbk], bg], writes=[bACC[d][j]])
                    else:
                        P.op("dve", lambda e: e.tensor_tensor(out=t, in0=psum[pbk][:, :], in1=g, op=ALU.mult),
                             reads=[bps[pbk], bg], writes=[bt])
                        P.op("pool", lambda e: e.tensor_tensor(out=ACC[:, d, hs(j)], in0=ACC[:, d, hs(j)], in1=t, op=ALU.add),
                             reads=[bt, bACC[d][j]], writes=[bACC[d][j]])
        first_merge[0] = False

    def phase_out(l):
        for j in range(NSUB):
            for k in range(8):
                P.op("act", lambda e: e.activation(out=H[:, k, hs(j)], in_=ACC[:, k, hs(j)], func=AF.Copy),
                     reads=[bACC[k][j]], writes=[bH[j]])
        for hh in range(2):
            sl, bsl = load_slot([(w_out[l, :, hh * 512:(hh + 1) * 512], 8, 0, 512)])
            for dt_ in range(4):
                d = hh * 4 + dt_
                for j in range(NSUB):
                    pi = 2 + (dt_ * NSUB + j) % 4
                    mm_group(psum[pi][:, :], bps[pi], lambda k: sl[:, k, dt_ * 128:(dt_ + 1) * 128], lambda k: H[:, k, hs(j)], 8, [bsl, bH[j]])
                    P.op("dve", lambda e: e.tensor_tensor(out=X[:, d, hs(j)], in0=psum[pi][:, :], in1=X[:, d, hs(j)], op=ALU.add),
                         reads=[bps[pi], bX[d][j]], writes=[bX[d][j]])

    fo_t = sb("fo_t", [128, 2, ST], F32)
    bfo = [Buf(), Buf()]

    def phase_final(p):
        n = 0
        for j in range(NSUB):
            rms_stats(lambda k: X[:, k, hs(j)], lambda k: [bX[k][j]], 8, 1.0 / D, rstd_t, brstd_t)
            for k in range(8):
                i = n % 2
                n += 1
                P.op("dve", lambda e: e.scalar_tensor_tensor(
                    out=fo_t[:, i, :], in0=X[:, k, hs(j)], scalar=col(0, "fw", k),
                    in1=rstd_t[:, :], op0=ALU.mult, op1=ALU.mult),
                    reads=[bX[k][j], brstd_t, bcols], writes=[bfo[i]])
                P.dma("sp", sy, yT[k * 128:(k + 1) * 128, p * TP + j * ST:p * TP + (j + 1) * ST], fo_t[:, i, :], reads=[bfo[i]])

    phases = {"a": phase_a, "b": phase_b, "c": phase_c, "d": phase_d}
    for p in range(NPASS):
        cur_pass[0] = p
        for k in range(8):
            for j in range(NSUB):
                P.dma("sp", sx, X[:, k, hs(j)], xT[k * 128:(k + 1) * 128, p * TP + j * ST:p * TP + (j + 1) * ST], writes=[bX[k][j]])
        for l in range(nlayers):
            phase_norm(l)
            first_merge[0] = True
            for kb, name in enumerate("abcd"):
                if name not in branches:
                    continue
                phases[name](l)
                phase_merge(l, kb)
            phase_out(l)
        phase_final(p)
    P._wait("sp", sy, P.cnt[sy])
    print("program: nins=%d nwaits=%d" % (P.nins, P.nwaits), {k: v for k, v in P.cnt.items() if k in P.eng})
    return nc


_NC_CACHE = {}


def run(inputs, branches=("a", "b", "c", "d"), nlayers=DEPTH, trace=False):
    key = (tuple(branches), nlayers)
    if key not in _NC_CACHE:
        _NC_CACHE[key] = build_nc(branches, nlayers)
    nc = _NC_CACHE[key]
    inp = {k: np.asarray(v) for k, v in inputs.items()}
    x = inp["x"].astype(np.float32)
    s5 = [host_s5(inp, l) for l in range(DEPTH)]
    shared = {
        "w_in": np.ascontiguousarray(inp["w_in"], dtype=np.float32),
        "w_branch": np.ascontiguousarray(inp["w_branch"], dtype=np.float32),
        "w_out": np.ascontiguousarray(inp["w_out"], dtype=np.float32),
        "w_glu": np.ascontiguousarray(inp["s5_w_glu"], dtype=np.float32),
        "cols": np.stack([host_cols(inp, l) for l in range(DEPTH)], 0),
        "consts": host_consts(),
        "rows": np.stack([host_rows(inp, l) for l in range(DEPTH)], 0),
        "sguw": np.ascontiguousarray(inp["sgu_w"].transpose(0, 3, 1, 2), dtype=np.float32),
        "sgub": np.ascontiguousarray(np.repeat(inp["sgu_b"].reshape(DEPTH, 4, 2, 1, 128), 64, axis=3).transpose(0, 2, 3, 1, 4).reshape(DEPTH, 128, 512), dtype=np.float32),
        "s5p": np.stack([s[0] for s in s5], 0),
        "s5q": np.stack([s[1] for s in s5], 0),
        "s5c": np.stack([s[2] for s in s5], 0),
    }
    in_maps = []
    for b in range(8):
        m = dict(shared)
        m["xT"] = np.ascontiguousarray(x[b].T)
        in_maps.append(m)
    res = run_bass_kernel_spmd(nc, in_maps, core_ids=list(range(8)), trace=trace)
    out = np.stack([np.ascontiguousarray(res.results[b]["yT"].T) for b in range(8)], 0).astype(np.float32)
    return out, res


def kernel(**inputs):
    out, _ = run(inputs)
    return out
```

```python
import os
import numpy as np
import concourse.bass as bass
import concourse.mybir as mybir
from concourse.bass_utils import run_bass_kernel_spmd

F32 = mybir.dt.float32
BF16 = mybir.dt.bfloat16
ALU = mybir.AluOpType
AF = mybir.ActivationFunctionType

D = 1024
SEQ = 2048
DEPTH = 2
W = 512
IN_DIM = 10248
TP = 1024
NPASS = SEQ // TP
ST = 512
NSUB = TP // ST
EPS = 1e-6

O_S5U, O_S5G = 0, 512
O_SGU, O_SGV, O_SGG = 1024, 1536, 2048
O_M2Z, O_M2X, O_M2DT = 2560, 3072, 4096
O_SCB, O_SCC, O_SCH, O_SCG = 4104, 4616, 5128, 5640
O_MG = 6152


class Buf:
    __slots__ = ("w", "r")

    def __init__(self):
        self.w = None
        self.r = {}


class Prog:
    def __init__(self, nc):
        self.nc = nc
        self.eng = {"pe": nc.tensor, "act": nc.scalar, "dve": nc.vector, "pool": nc.gpsimd, "sp": nc.sync}
        self.sem = {}
        self.cnt = {}
        self.seen = {e: {} for e in self.eng}
        self.pend = {e: [] for e in self.eng}
        for e in self.eng:
            self.sem[e] = nc.alloc_semaphore("s_" + e)
            self.cnt[e] = 0
        self.nwaits = 0
        self.nins = 0
        self.barrier_dma = set()

    def new_sem(self, name):
        self.sem[name] = self.nc.alloc_semaphore(name)
        self.cnt[name] = 0
        return name

    def _wait(self, e, key, val):
        if key not in self.eng:
            val = self.cnt[key]
        if self.seen[e].get(key, 0) >= val:
            return
        self.seen[e][key] = val
        self.eng[e].wait_ge(self.sem[key], val)
        self.nwaits += 1

    def _deps(self, e, reads, writes):
        deps = {}
        for b in reads:
            if b.w is not None:
                k, v = b.w
                if deps.get(k, 0) < v:
                    deps[k] = v
        for b in writes:
            if b.w is not None:
                k, v = b.w
                if deps.get(k, 0) < v:
                    deps[k] = v
            for k, v in b.r.items():
                if deps.get(k, 0) < v:
                    deps[k] = v
        for k, v in deps.items():
            if k == e and (e == "pe" or v > self.cnt[e]):
                continue
            self._wait(e, k, v)

    def _commit(self, k, v, reads, writes):
        for b in reads:
            b.r[k] = v
        for b in writes:
            b.w = (k, v)
            b.r = {}

    def op(self, e, fn, reads=(), writes=(), inc=True):
        self._deps(e, reads, writes)
        ins = fn(self.eng[e])
        self.nins += 1
        if inc:
            self.cnt[e] += 1
            ins.then_inc(self.sem[e], 1)
            self._commit(e, self.cnt[e], reads, writes)
        else:
            v = self.cnt[e] + 1
            self._commit(e, v, reads, writes)

    def barrier(self):
        for e in self.eng:
            for k, v in self.cnt.items():
                if k != e and v > 0 and (k in self.eng or k in self.barrier_dma):
                    self._wait(e, k, v)

    def dma(self, q, semkey, out, in_, reads=(), writes=(), **kw):
        self._deps(q, reads, writes)
        ins = self.eng[q].dma_start(out=out, in_=in_, **kw)
        self.cnt[semkey] += 16
        ins.then_inc(self.sem[semkey], 16)
        self.nins += 1
        self._commit(semkey, self.cnt[semkey], reads, writes)


def col_layout():
    off = {}
    n = 0

    def add(name, w):
        nonlocal n
        off[name] = n
        n += w
    add("nw", 8)
    add("mb", 32)
    add("scw", 12)
    add("fw", 8)
    add("m2cw", 32)
    add("m2cb", 8)
    add("m2d", 4)
    add("m2nw", 4)
    add("s5d", 4)
    add("mkB", 8)
    add("mkC", 2)
    add("mkCn", 2)
    return off, n


COLOFF, NCOL = col_layout()


def host_cols(inp, l):
    c = np.zeros((128, NCOL), np.float32)
    c[:, COLOFF["nw"]:COLOFF["nw"] + 8] = inp["norm_w"][l].reshape(8, 128).T
    c[:, COLOFF["mb"]:COLOFF["mb"] + 32] = inp["merge_b"][l].reshape(32, 128).T
    c[:, COLOFF["scw"]:COLOFF["scw"] + 12] = inp["sc_conv_w"][l].reshape(12, 128).T
    c[:, COLOFF["fw"]:COLOFF["fw"] + 8] = inp["final_norm_w"].reshape(8, 128).T
    c[:, COLOFF["m2cw"]:COLOFF["m2cw"] + 32] = inp["m2_conv_w"][l].reshape(32, 128).T
    c[:, COLOFF["m2cb"]:COLOFF["m2cb"] + 8] = inp["m2_conv_b"][l].reshape(8, 128).T
    c[:, COLOFF["m2d"]:COLOFF["m2d"] + 4] = np.repeat(inp["m2_d"][l], 64).reshape(4, 128).T
    c[:, COLOFF["m2nw"]:COLOFF["m2nw"] + 4] = inp["m2_norm_w"][l].reshape(4, 128).T
    c[:, COLOFF["s5d"]:COLOFF["s5d"] + 4] = inp["s5_d"][l].reshape(4, 128).T
    gl = np.arange(128) // 16
    for jj in range(4):
        for g2 in range(2):
            c[:, COLOFF["mkB"] + jj * 2 + g2] = (gl == 2 * jj + g2)
    g2p = np.arange(128) // 64
    for g2 in range(2):
        c[:, COLOFF["mkC"] + g2] = (g2p == g2)
        c[:, COLOFF["mkCn"] + g2] = -1.0 * (g2p == g2)
    return c


def host_consts():
    i = np.arange(128)
    k = np.zeros((128, 5, 128), np.float32)
    k[:, 0] = np.eye(128)
    k[:, 1] = (i[:, None] <= i[None, :])
    k[:, 2] = (i[:, None] > i[None, :])
    k[:, 3] = 1.0
    k[:, 4] = (i[:, None] // 16 == i[None, :] // 16)
    return k


def host_rows(inp, l):
    r = np.zeros((128, 1040), np.float32)
    r[:, 0:512] = inp["sgu_ln_w"][l][None, :]
    r[:, 512:1024] = inp["sgu_ln_b"][l][None, :]
    r[:, 1024:1032] = inp["m2_dt_bias"][l][None, :]
    r[:, 1032:1040] = inp["m2_a_log"][l][None, :]
    return r


def host_s5(inp, l):
    G, N, Pq = 32, 64, 16
    def L2(a_gn):
        a = a_gn.reshape(4, 8, N)
        a = np.repeat(a[:, :, None, :], 16, axis=2)
        return a.transpose(1, 2, 0, 3).reshape(128, 256)
    def L2b(b_gnq):
        a = b_gnq.reshape(4, 8, N, Pq)
        return a.transpose(1, 3, 0, 2).reshape(128, 256)
    p5 = np.stack([L2(inp["s5_lambda_re"][l]), L2(inp["s5_lambda_im"][l]),
                   L2(np.repeat(inp["s5_log_step"][l][:, None], N, 1)),
                   L2b(inp["s5_b_re"][l]), L2b(inp["s5_b_im"][l])], 1)
    def PL(a_gn):
        return a_gn.reshape(16, 2, N).transpose(1, 2, 0).reshape(128, 16)
    pq = np.stack([PL(inp["s5_lambda_re"][l]), PL(inp["s5_lambda_im"][l]),
                   PL(np.repeat(inp["s5_log_step"][l][:, None], N, 1))], 1)
    def PLc(c_gpn):
        return c_gpn.reshape(16, 2, Pq, N).transpose(1, 3, 2, 0).reshape(128, 256)
    def PLb(b_gnq):
        return b_gnq.reshape(16, 2, N, Pq).transpose(1, 2, 3, 0).reshape(128, 256)
    pc = np.stack([PLc(inp["s5_c_re"][l]), PLc(inp["s5_c_im"][l]),
                   PLb(inp["s5_b_re"][l]), PLb(inp["s5_b_im"][l])], 1)
    return p5.astype(np.float32), pq.astype(np.float32), pc.astype(np.float32)


def build_nc(branches=("a", "b", "c", "d"), nlayers=DEPTH):
    nc = bass.Bass("TRN2", target_bir_lowering=False)
    xT = nc.dram_tensor("xT", [D, SEQ], F32, kind="ExternalInput").ap()
    w_in = nc.dram_tensor("w_in", [DEPTH, D, IN_DIM], F32, kind="ExternalInput").ap()
    w_br = nc.dram_tensor("w_branch", [DEPTH, 4, W, D], F32, kind="ExternalInput").ap()
    w_out = nc.dram_tensor("w_out", [DEPTH, D, D], F32, kind="ExternalInput").ap()
    w_glu = nc.dram_tensor("w_glu", [DEPTH, W, W], F32, kind="ExternalInput").ap()
    cols_d = nc.dram_tensor("cols", [DEPTH, 128, NCOL], F32, kind="ExternalInput").ap()
    consts_d = nc.dram_tensor("consts", [128, 5, 128], F32, kind="ExternalInput").ap()
    rows_d = nc.dram_tensor("rows", [DEPTH, 128, 1040], F32, kind="ExternalInput").ap()
    sguw_d = nc.dram_tensor("sguw", [DEPTH, 128, 8, 128], F32, kind="ExternalInput").ap()
    sgub_d = nc.dram_tensor("sgub", [DEPTH, 128, 512], F32, kind="ExternalInput").ap()
    s5p_d = nc.dram_tensor("s5p", [DEPTH, 128, 5, 256], F32, kind="ExternalInput").ap()
    s5q_d = nc.dram_tensor("s5q", [DEPTH, 128, 3, 16], F32, kind="ExternalInput").ap()
    s5c_d = nc.dram_tensor("s5c", [DEPTH, 128, 4, 256], F32, kind="ExternalInput").ap()
    yT = nc.dram_tensor("yT", [D, SEQ], F32, kind="ExternalOutput").ap()
    S5_CACHE_N = 9 * 48 + 7 * 48 + 512 + 3 * 2048
    s5cache = nc.dram_tensor("s5cache", [DEPTH, 128, S5_CACHE_N], F32).ap()

    P = Prog(nc)
    sb = nc.alloc_sbuf_tensor
    X = sb("X", [128, 8, TP], F32)
    H = sb("H", [128, 8, TP], BF16)
    ACC = sb("ACC", [128, 8, TP], F32)
    Y = sb("Y", [128, 4, TP], BF16)
    cols = sb("colsb", [128, DEPTH, NCOL], F32)
    consts = sb("constsb", [128, 5, 128], F32)
    identb = sb("identb", [128, 128], BF16)
    mask01b = sb("mask01b", [128, 128], BF16)
    bX = [[Buf() for _ in range(NSUB)] for _ in range(8)]
    bH = [Buf() for _ in range(NSUB)]
    bACC = [[Buf() for _ in range(NSUB)] for _ in range(8)]
    bY = [[Buf() for _ in range(NSUB)] for _ in range(4)]
    bcols, bconsts = Buf(), Buf()
    ident, triu, ltstrict, ones = consts[:, 0, :], consts[:, 1, :], consts[:, 2, :], consts[:, 3, :]
    blockmask = consts[:, 4, :]
    zerob = sb("zerob", [128, 128], BF16)
    bzerob = Buf()
    bones = bconsts

    NS = 4
    slots = [sb("slot%d" % i, [128, 8 * 512], BF16) for i in range(NS)]
    bslot = [Buf() for _ in range(NS)]
    sslot = [P.new_sem("dslot%d" % i) for i in range(NS)]
    slot_rr = [0]

    psbig = nc.alloc_psum_tensor("psbig", [128, 7 * 512], F32)
    psum = [psbig[:, i * 512:(i + 1) * 512] for i in range(7)]
    psT = nc.alloc_psum_tensor("psT", [128, 512], F32)
    bps = [Buf() for _ in range(7)]
    bpsT = Buf()
    psum.append(psT[:, :])
    bps.append(bpsT)

    SCRF = 16600
    scrF = sb("scrF", [128, SCRF], F32)
    scr_pos = [0]

    def scratch_reset(pos=0):
        P.barrier()
        scr_pos[0] = pos

    def falloc(n, parts=128):
        a = scrF[0:parts, scr_pos[0]:scr_pos[0] + n]
        scr_pos[0] += n
        assert scr_pos[0] <= SCRF, scr_pos
        return a, Buf()

    def balloc(n):
        m = (n + 1) // 2
        a = scrF[:, scr_pos[0]:scr_pos[0] + m].bitcast(BF16)[:, 0:n]
        scr_pos[0] += m
        assert scr_pos[0] <= SCRF, scr_pos
        return a, Buf()

    sx = P.new_sem("dx")
    sy = P.new_sem("dy")
    sc = P.new_sem("dc")
    sm = P.new_sem("dm")
    sw8 = P.new_sem("dw8")
    scache = P.new_sem("dcache")
    P.barrier_dma.update([sm, scache, sw8])
    bcache = [Buf() for _ in range(DEPTH)]
    cur_pass = [0]

    P.dma("sp", sc, cols[:, :, :], cols_d.rearrange("l p n -> p l n"), writes=[bcols])
    P.dma("sp", sc, consts[:, :, :], consts_d, writes=[bconsts])
    bidb = Buf()
    P.op("dve", lambda e: e.tensor_copy(out=identb[:, :], in_=ident), reads=[bconsts], writes=[bidb])
    P.op("dve", lambda e: e.tensor_copy(out=mask01b[:, :], in_=triu), reads=[bconsts], writes=[bidb])
    epsb = sb("epsb", [128, 1], F32)
    bepsb = Buf()
    P.op("pool", lambda e: e.memset(epsb[:, :], EPS), writes=[bepsb])
    P.op("pool", lambda e: e.memset(zerob[:, :], 0.0), writes=[bzerob])

    def col(l, name, j=0):
        o = COLOFF[name] + j
        return cols[:, l, o:o + 1]

    def sched_layer(l):
        out = []

        def mg(kb):
            for hh in range(2):
                out.append([(w_in[l, :, O_MG + kb * D + hh * 512:O_MG + kb * D + (hh + 1) * 512], 8, 0, 512)])
                out.append([(w_br[l, kb, :, hh * 512:(hh + 1) * 512], 4, 0, 512)])
        if "a" in branches:
            out.append([(w_in[l, :, O_S5U:O_S5U + 512], 8, 0, 512)])
            out.append([(w_glu[l, :, :], 4, 0, 512)])
            out.append([(w_in[l, :, O_S5G:O_S5G + 512], 8, 0, 512)])
            mg(0)
        if "b" in branches:
            out.append([(w_in[l, :, O_SGV:O_SGV + 512], 8, 0, 512)])
            for c in range(4):
                out.append([(w_in[l, :, O_SGU + c * 128:O_SGU + (c + 1) * 128], 8, 0, 128),
                            (w_in[l, :, O_SGG + c * 128:O_SGG + (c + 1) * 128], 8, 128, 128)])
            mg(1)
        if "c" in branches:
            out.append([(w_in[l, :, O_M2DT - 120:O_M2DT + 8], 8, 0, 128)])
            out.append([(w_in[l, :, O_M2Z:O_M2Z + 512], 8, 0, 512)])
            for hh in range(2):
                out.append([(w_in[l, :, O_M2X + hh * 512:O_M2X + (hh + 1) * 512], 8, 0, 512)])
            mg(2)
        if "d" in branches:
            for c in range(4):
                out.append([(w_in[l, :, o + c * 128:o + (c + 1) * 128], 8, i * 128, 128) for i, o in enumerate((O_SCB, O_SCC, O_SCH, O_SCG))])
            mg(3)
        for hh in range(2):
            out.append([(w_out[l, :, hh * 512:(hh + 1) * 512], 8, 0, 512)])
        return out

    schedule = [e for _p in range(NPASS) for l in range(nlayers) for e in sched_layer(l)]
    LOOK = 2
    USE_WCACHE = os.environ.get("NO_WCACHE") is None
    emitted = [0]

    NPP = len(schedule) // NPASS
    wcache = nc.dram_tensor("wcache", [NPP, 128, 8 * 512], BF16).ap()
    bwc = [Buf() for _ in range(NPP)]
    sst = [P.new_sem("dst%d" % i) for i in range(NS)]

    def _emit_load(j):
        i = j % NS
        s = slots[i]
        sv = s[:, :].rearrange("p (k n) -> p k n", k=8)
        ktm = max(kt for _, kt, _, _ in schedule[j])
        nm = max(off + n for _, _, off, n in schedule[j])
        jj = j % NPP
        cv = wcache[jj, :, 0:ktm * nm].rearrange("p (k n) -> p k n", k=ktm)
        if j < NPP or not USE_WCACHE:
            for src, kt, off, n in schedule[j]:
                P.dma("pool", sslot[i], sv[:, 0:kt, off:off + n], src.rearrange("(k p) n -> p k n", p=128), writes=[bslot[i]])
            if USE_WCACHE and NPASS > 1:
                P.dma("sp", sst[i], cv, sv[:, 0:ktm, 0:nm], reads=[bslot[i]], writes=[bwc[jj]])
        else:
            P.dma("sp", sslot[i], sv[:, 0:ktm, 0:nm], cv, reads=[bwc[jj]], writes=[bslot[i]])

    def load_slot(pieces):
        j = slot_rr[0]
        slot_rr[0] += 1
        assert [(kt, off, n) for _, kt, off, n in pieces] == [(kt, off, n) for _, kt, off, n in schedule[j]], j
        while emitted[0] < min(len(schedule), j + 1 + LOOK):
            _emit_load(emitted[0])
            emitted[0] += 1
        i = j % NS
        return slots[i][:, :].rearrange("p (k n) -> p k n", k=8), bslot[i]

    def mm_group(out_ap, bout, lhs_fn, rhs_fn, nk, reads):
        for k in range(nk):
            P.op("pe", lambda e: e.matmul(out_ap, lhs_fn(k), rhs_fn(k), start=(k == 0), stop=(k == nk - 1)),
                 reads=reads, writes=[bout], inc=(k == nk - 1))

    def hs(j):
        return slice(j * ST, (j + 1) * ST)

    carry_sc = [[sb("csc%d_%d" % (l, c), [128, 2], F32) for c in range(4)] for l in range(DEPTH)]
    bcarry_sc = [[Buf() for c in range(4)] for l in range(DEPTH)]
    carry_m2 = [[sb("cm2%d_%d" % (l, c), [128, 3], F32) for c in range(8)] for l in range(DEPTH)]
    bcarry_m2 = [[Buf() for c in range(8)] for l in range(DEPTH)]
    ST_m2 = [sb("stm2_%d" % l, [128, 512], F32) for l in range(DEPTH)]
    bST_m2 = [Buf() for l in range(DEPTH)]
    carry_s5 = [sb("cs5_%d" % l, [128, 2, 16], F32) for l in range(DEPTH)]
    bcarry_s5 = [Buf() for l in range(DEPTH)]
    for l in range(DEPTH):
        for c in range(4):
            P.op("pool", lambda e: e.memset(carry_sc[l][c][:, :], 0.0), writes=[bcarry_sc[l][c]])
        for c in range(8):
            P.op("pool", lambda e: e.memset(carry_m2[l][c][:, :], 0.0), writes=[bcarry_m2[l][c]])
        P.op("pool", lambda e: e.memset(ST_m2[l][:, :], 0.0), writes=[bST_m2[l]])
        P.op("pool", lambda e: e.memset(carry_s5[l][:, :, :], 0.0), writes=[bcarry_s5[l]])

    def rms_stats(src_fn, breads, nk, scale, rstd, brstd, n=ST, lnexp=False):
        sq, bsq = falloc_sq[0]
        for k in range(nk):
            P.op("act", lambda e: e.activation(out=sq[:, 0:n], in_=src_fn(k), func=AF.Square), reads=breads(k), writes=[bsq])
            P.op("pe", lambda e: e.matmul(psum[6][:, 0:n], ones, sq[:, 0:n], start=(k == 0), stop=(k == nk - 1)),
                 reads=[bsq, bones], writes=[bps[6]])
        if lnexp:
            P.op("act", lambda e: e.activation(out=rstd[:, 0:n], in_=psum[6][:, 0:n], func=AF.Ln, bias=epsb[:, 0:1], scale=scale),
                 reads=[bps[6], bepsb], writes=[brstd])
            P.op("act", lambda e: e.activation(out=rstd[:, 0:n], in_=rstd[:, 0:n], func=AF.Exp, scale=-0.5), reads=[brstd], writes=[brstd])
            return
        P.op("act", lambda e: e.activation(out=rstd[:, 0:n], in_=psum[6][:, 0:n], func=AF.Sqrt, bias=epsb[:, 0:1], scale=scale),
             reads=[bps[6], bepsb], writes=[brstd])
        P.op("dve", lambda e: e.reciprocal(out=rstd[:, 0:n], in_=rstd[:, 0:n]), reads=[brstd], writes=[brstd])

    sq_t = sb("sq_t", [128, ST], F32)
    falloc_sq = [(sq_t, Buf())]
    rstd_t = sb("rstd_t", [128, ST], F32)
    brstd_t = Buf()

    def phase_norm(l):
        for j in range(NSUB):
            rms_stats(lambda k: X[:, k, hs(j)], lambda k: [bX[k][j]], 8, 1.0 / D, rstd_t, brstd_t)
            for k in range(8):
                P.op("dve", lambda e: e.scalar_tensor_tensor(
                    out=H[:, k, hs(j)], in0=X[:, k, hs(j)], scalar=col(l, "nw", k),
                    in1=rstd_t[:, :], op0=ALU.mult, op1=ALU.mult),
                    reads=[bX[k][j], brstd_t, bcols], writes=[bH[j]])

    def phase_d(l):
        scratch_reset()
        pbufs = [falloc(2 + TP) for _ in range(2)]
        hsbs = [falloc(ST) for _ in range(2)]
        qs = [falloc(ST) for _ in range(2)]
        yvs = [falloc(ST) for _ in range(2)]
        sgs = [falloc(ST) for _ in range(2)]
        for c in range(4):
            sl, bsl = load_slot([(w_in[l, :, o + c * 128:o + (c + 1) * 128], 8, i * 128, 128)
                                 for i, o in enumerate((O_SCB, O_SCC, O_SCH, O_SCG))])
            pbuf, bp = pbufs[c % 2]
            P.op("pool", lambda e: e.tensor_copy(out=pbuf[:, 0:2], in_=carry_sc[l][c][:, :]), reads=[bcarry_sc[l][c]], writes=[bp])
            for j in range(NSUB):
                pb = 3 * (j % 2)
                pgate = 6 + (j % 2)
                (hsb, bhsb), (q, bq), (yv, byv), (sg, bsg) = hsbs[j % 2], qs[j % 2], yvs[j % 2], sgs[j % 2]
                for i in range(3):
                    mm_group(psum[pb + i][:, :], bps[pb + i], lambda k: sl[:, k, i * 128:(i + 1) * 128], lambda k: H[:, k, hs(j)], 8, [bsl, bH[j]])
                mm_group(psum[pgate][:, :], bps[pgate], lambda k: sl[:, k, 384:512], lambda k: H[:, k, hs(j)], 8, [bsl, bH[j]])
                P.op("act", lambda e: e.activation(out=hsb, in_=psum[pb + 2][:, :], func=AF.Copy), reads=[bps[pb + 2]], writes=[bhsb])
                P.op("dve", lambda e: e.tensor_tensor(out=pbuf[:, 2 + j * ST:2 + (j + 1) * ST], in0=psum[pb + 1][:, :], in1=hsb, op=ALU.mult),
                     reads=[bps[pb + 1], bhsb], writes=[bp])
                P.op("dve", lambda e: e.tensor_scalar(out=q, in0=pbuf[:, 2 + j * ST:2 + (j + 1) * ST], scalar1=col(l, "scw", 8 + c), scalar2=None, op0=ALU.mult),
                     reads=[bp, bcols], writes=[bq])
                P.op("dve", lambda e: e.scalar_tensor_tensor(out=q, in0=pbuf[:, 1 + j * ST:1 + (j + 1) * ST], scalar=col(l, "scw", 4 + c), in1=q, op0=ALU.mult, op1=ALU.add),
                     reads=[bp, bq, bcols], writes=[bq])
                P.op("dve", lambda e: e.scalar_tensor_tensor(out=q, in0=pbuf[:, j * ST:(j + 1) * ST], scalar=col(l, "scw", c), in1=q, op0=ALU.mult, op1=ALU.add),
                     reads=[bp, bq, bcols], writes=[bq])
                P.op("dve", lambda e: e.tensor_tensor(out=yv, in0=psum[pb][:, :], in1=q, op=ALU.mult), reads=[bps[pb], bq], writes=[byv])
                P.op("act", lambda e: e.activation(out=sg, in_=psum[pgate][:, :], func=AF.Silu), reads=[bps[pgate]], writes=[bsg])
                P.op("pool", lambda e: e.tensor_tensor(out=Y[:, c, hs(j)], in0=yv, in1=sg, op=ALU.mult),
                     reads=[byv, bsg], writes=[bY[c][j]])
            P.op("pool", lambda e: e.tensor_copy(out=carry_sc[l][c][:, :], in_=pbuf[:, TP:TP + 2]), reads=[bp], writes=[bcarry_sc[l][c]])

    def phase_b(l):
        scratch_reset()
        lnw, blnw = falloc(512)
        lnb, blnb = falloc(512)
        wraw, bwraw = falloc(1024)
        bsrow, bbsrow = falloc(512)
        v32s = [falloc(512) for _ in range(2)]
        vns = [falloc(512) for _ in range(2)]
        st6s = [falloc(6) for _ in range(2)]
        mvs = [falloc(2) for _ in range(2)]
        rss = [falloc(1) for _ in range(2)]
        gus = [falloc(ST) for _ in range(2)]
        sgs = [falloc(ST) for _ in range(2)]
        t1s = [falloc(ST) for _ in range(2)]
        wmT, bwmT = balloc(1024)
        VN, bVN = balloc(8 * 512)
        bVNq = [Buf() for _ in range(8)]
        P.dma("sp", sm, lnw, rows_d[l, :, 0:512], writes=[blnw])
        P.dma("sp", sm, lnb, rows_d[l, :, 512:1024], writes=[blnb])
        P.dma("sp", sm, wraw, sguw_d[l].rearrange("s h t -> s (h t)"), writes=[bwraw])
        P.dma("sp", sm, bsrow, sgub_d[l], writes=[bbsrow])
        bsv = bsrow.rearrange("p (c t) -> p c t", c=4)
        P.op("dve", lambda e: e.tensor_tensor(out=wmT.rearrange("p (h t) -> p h t", h=8), in0=wraw.rearrange("p (h t) -> p h t", h=8),
                                             in1=triu.unsqueeze(1).broadcast_to([128, 8, 128]), op=ALU.mult),
             reads=[bwraw, bconsts], writes=[bwmT])
        slv, bslv = load_slot([(w_in[l, :, O_SGV:O_SGV + 512], 8, 0, 512)])
        for qc in range(TP // 128):
            pi = qc % 2
            (v32, bv32), (vn, bvn), (st6, bst6), (mv, bmv), (rs, brs) = v32s[pi], vns[pi], st6s[pi], mvs[pi], rss[pi]
            mm_group(psum[pi][:, :], bps[pi], lambda k: H[:, k, qc * 128:(qc + 1) * 128], lambda k: slv[:, k, 0:512], 8, [bslv, bH[qc // 4]])
            P.op("act", lambda e: e.activation(out=v32, in_=psum[pi][:, :], func=AF.Gelu), reads=[bps[pi]], writes=[bv32])
            P.op("dve", lambda e: e.bn_stats(out=st6, in_=v32), reads=[bv32], writes=[bst6])
            P.op("dve", lambda e: e.bn_aggr(out=mv, in_=st6), reads=[bst6], writes=[bmv])
            P.op("act", lambda e: e.activation(out=rs, in_=mv[:, 1:2], func=AF.Sqrt, bias=epsb[:, 0:1], scale=1.0), reads=[bmv, bepsb], writes=[brs])
            P.op("dve", lambda e: e.reciprocal(out=rs, in_=rs), reads=[brs], writes=[brs])
            P.op("dve", lambda e: e.tensor_scalar(out=vn, in0=v32, scalar1=mv[:, 0:1], scalar2=rs, op0=ALU.subtract, op1=ALU.mult),
                 reads=[bv32, bmv, brs], writes=[bvn])
            P.op("pool", lambda e: e.tensor_tensor(out=vn, in0=vn, in1=lnw, op=ALU.mult), reads=[bvn, blnw], writes=[bvn])
            P.op("pool", lambda e: e.tensor_tensor(out=VN[:, qc * 512:(qc + 1) * 512], in0=vn, in1=lnb, op=ALU.add), reads=[bvn, blnb], writes=[bVNq[qc]])
        for c in range(4):
            sl, bsl = load_slot([(w_in[l, :, O_SGU + c * 128:O_SGU + (c + 1) * 128], 8, 0, 128),
                                 (w_in[l, :, O_SGG + c * 128:O_SGG + (c + 1) * 128], 8, 128, 128)])
            for j in range(NSUB):
                pu, pg, pss = (2, 3, 4) if j % 2 == 0 else (6, 7, 5)
                (gu, bgu), (sg, bsg), (t1, bt1) = gus[j % 2], sgs[j % 2], t1s[j % 2]
                mm_group(psum[pu][:, :], bps[pu], lambda k: sl[:, k, 0:128], lambda k: H[:, k, hs(j)], 8, [bsl, bH[j]])
                mm_group(psum[pg][:, :], bps[pg], lambda k: sl[:, k, 128:256], lambda k: H[:, k, hs(j)], 8, [bsl, bH[j]])
                P.op("act", lambda e: e.activation(out=gu, in_=psum[pu][:, :], func=AF.Gelu), reads=[bps[pu]], writes=[bgu])
                P.op("act", lambda e: e.activation(out=sg, in_=psum[pg][:, :], func=AF.Silu), reads=[bps[pg]], writes=[bsg])
                for qq in range(4):
                    qc = j * 4 + qq
                    for h2 in range(2):
                        h = 2 * c + h2
                        o = psum[pss][h2 * 64:(h2 + 1) * 64, qq * 128:(qq + 1) * 128]
                        P.op("pe", lambda e: e.matmul(o, VN[:, qc * 512 + h * 64:qc * 512 + (h + 1) * 64], wmT[:, h * 128:(h + 1) * 128], start=True, stop=True),
                             reads=[bVNq[qc], bwmT], writes=[bps[pss]], inc=True)
                P.op("dve", lambda e: e.tensor_tensor(out=t1.rearrange("p (q t) -> p q t", q=4), in0=psum[pss][:, :].rearrange("p (q t) -> p q t", q=4),
                                                     in1=bsv[:, c, :].unsqueeze(1).broadcast_to([128, 4, 128]), op=ALU.add), reads=[bps[pss], bbsrow], writes=[bt1])
                P.op("dve", lambda e: e.tensor_tensor(out=t1, in0=t1, in1=gu, op=ALU.mult), reads=[bt1, bgu], writes=[bt1])
                P.op("pool", lambda e: e.tensor_tensor(out=Y[:, c, hs(j)], in0=t1, in1=sg, op=ALU.mult), reads=[bt1, bsg], writes=[bY[c][j]])

    def phase_c(l):
        scratch_reset()
        NQ = TP // 128
        dtb, bdtb = falloc(8)
        alog, balog = falloc(8)
        a_t, ba_t = falloc(8)
        dt, bdt = falloc(64)
        da, bda = falloc(64)
        csc, bcsc = falloc(64)
        dec, bdec = falloc(64)
        eA, beA = falloc(64)
        dtdec, bdtdec = falloc(64)
        cbufs = [falloc(3 + TP) for _ in range(2)]
        qvs = [falloc(ST) for _ in range(2)]
        ltls = [falloc(128) for _ in range(8)]
        reps = [falloc(128) for _ in range(4)]
        erows = [falloc(128) for _ in range(2)]
        t1s = [falloc(128) for _ in range(2)]
        yqs = [falloc(512) for _ in range(2)]
        rstds = [falloc(128) for _ in range(2)]
        xTb, bxT = balloc(4 * TP)
        BTb, bBT = balloc(2 * TP)
        CTb, bCT = balloc(2 * TP)
        ZS, bZS = balloc(4 * TP)
        xdts = [balloc(512) for _ in range(2)]
        xdds = [balloc(512) for _ in range(2)]
        Btoks = [balloc(256) for _ in range(2)]
        scms = [balloc(256) for _ in range(2)]
        LTms = [balloc(128) for _ in range(2)]
        MTs = [balloc(128) for _ in range(2)]
        STbs = [balloc(512) for _ in range(2)]
        STb, bSTb = STbs[0]
        P.dma("sp", sm, dtb, rows_d[l, :, 1024:1032], writes=[bdtb])
        P.dma("sp", sm, alog, rows_d[l, :, 1032:1040], writes=[balog])
        slw8, bwdt = load_slot([(w_in[l, :, O_M2DT - 120:O_M2DT + 8], 8, 0, 128)])
        P.op("act", lambda e: e.activation(out=a_t, in_=alog, func=AF.Exp), reads=[balog], writes=[ba_t])
        P.op("dve", lambda e: e.tensor_scalar(out=a_t, in0=a_t, scalar1=-1.0, scalar2=None, op0=ALU.mult), reads=[ba_t], writes=[ba_t])
        P.op("act", lambda e: e.activation(out=STb, in_=ST_m2[l][:, :], func=AF.Copy), reads=[bST_m2[l]], writes=[bSTb])
        wdtv = slw8[:, :, 120:128]
        for qc in range(NQ):
            mm_group(psum[0][:, qc * 8:(qc + 1) * 8], bps[0], lambda k: H[:, k, qc * 128:(qc + 1) * 128], lambda k: wdtv[:, k, :], 8, [bwdt, bH[qc // 4]])
        P.op("dve", lambda e: e.tensor_tensor(out=dt.rearrange("p (q h) -> p q h", h=8), in0=psum[0][:, 0:64].rearrange("p (q h) -> p q h", h=8),
                                             in1=dtb.unsqueeze(1).broadcast_to([128, NQ, 8]), op=ALU.add), reads=[bps[0], bdtb], writes=[bdt])
        P.op("act", lambda e: e.activation(out=dt, in_=dt, func=AF.Exp), reads=[bdt], writes=[bdt])
        P.op("act", lambda e: e.activation(out=dt, in_=dt, func=AF.Ln, bias=1.0), reads=[bdt], writes=[bdt])
        P.op("dve", lambda e: e.tensor_tensor(out=da.rearrange("p (q h) -> p q h", h=8), in0=dt.rearrange("p (q h) -> p q h", h=8),
                                             in1=a_t.unsqueeze(1).broadcast_to([128, NQ, 8]), op=ALU.mult), reads=[bdt, ba_t], writes=[bda])
        P.op("pe", lambda e: e.matmul(psum[0][:, 64:128], triu, da, start=True, stop=True), reads=[bda, bconsts], writes=[bps[0]])
        P.op("pe", lambda e: e.matmul(psum[0][:, 128:192], ones, da, start=True, stop=True), reads=[bda, bconsts], writes=[bps[0]])
        P.op("act", lambda e: e.activation(out=csc, in_=psum[0][:, 64:128], func=AF.Copy), reads=[bps[0]], writes=[bcsc])
        P.op("dve", lambda e: e.tensor_tensor(out=dec, in0=psum[0][:, 128:192], in1=csc, op=ALU.subtract), reads=[bps[0], bcsc], writes=[bdec])
        P.op("act", lambda e: e.activation(out=dec, in_=dec, func=AF.Exp), reads=[bdec], writes=[bdec])
        P.op("act", lambda e: e.activation(out=eA, in_=psum[0][:, 128:192], func=AF.Exp), reads=[bps[0]], writes=[beA])
        P.op("dve", lambda e: e.tensor_tensor(out=dtdec, in0=dt, in1=dec, op=ALU.mult), reads=[bdt, bdec], writes=[bdtdec])
        CSTOP = int(os.environ.get("CSTOP", "9"))
        if CSTOP <= 1:
            return
        slz, bslz = load_slot([(w_in[l, :, O_M2Z:O_M2Z + 512], 8, 0, 512)])
        n = 0
        for ct in range(4):
            for j in range(NSUB):
                pi = 1 + n % 2
                n += 1
                mm_group(psum[pi][:, :], bps[pi], lambda k: slz[:, k, ct * 128:(ct + 1) * 128], lambda k: H[:, k, hs(j)], 8, [bslz, bH[j]])
                P.op("act", lambda e: e.activation(out=ZS[:, ct * TP + j * ST:ct * TP + (j + 1) * ST], in_=psum[pi][:, :], func=AF.Silu), reads=[bps[pi]], writes=[bZS])
        slx = [load_slot([(w_in[l, :, O_M2X + hh * 512:O_M2X + (hh + 1) * 512], 8, 0, 512)]) for hh in range(2)]
        for ct in range(8):
            sl, bsl = slx[ct // 4]
            cbuf, bcbuf = cbufs[ct % 2]
            P.op("pool", lambda e: e.tensor_copy(out=cbuf[:, 0:3], in_=carry_m2[l][ct][:, :]), reads=[bcarry_m2[l][ct]], writes=[bcbuf])
            for j in range(NSUB):
                pi = 1 + n % 2
                n += 1
                mm_group(psum[pi][:, :], bps[pi], lambda k: sl[:, k, (ct % 4) * 128:(ct % 4 + 1) * 128], lambda k: H[:, k, hs(j)], 8, [bsl, bH[j]])
                P.op("act", lambda e: e.activation(out=cbuf[:, 3 + j * ST:3 + (j + 1) * ST], in_=psum[pi][:, :], func=AF.Copy), reads=[bps[pi]], writes=[bcbuf])
            for j in range(NSUB):
                qv, bqv = qvs[j % 2]
                P.op("dve", lambda e: e.tensor_scalar(out=qv, in0=cbuf[:, 3 + j * ST:3 + (j + 1) * ST], scalar1=col(l, "m2cw", 24 + ct), scalar2=col(l, "m2cb", ct), op0=ALU.mult, op1=ALU.add),
                     reads=[bcbuf, bcols], writes=[bqv])
                for tap in range(3):
                    P.op("dve", lambda e: e.scalar_tensor_tensor(out=qv, in0=cbuf[:, tap + j * ST:tap + (j + 1) * ST], scalar=col(l, "m2cw", tap * 8 + ct), in1=qv, op0=ALU.mult, op1=ALU.add),
                         reads=[bcbuf, bqv, bcols], writes=[bqv])
                if ct < 4:
                    dst, bd = xTb[:, ct * TP + j * ST:ct * TP + (j + 1) * ST], bxT
                elif ct < 6:
                    dst, bd = BTb[:, (ct - 4) * TP + j * ST:(ct - 4) * TP + (j + 1) * ST], bBT
                else:
                    dst, bd = CTb[:, (ct - 6) * TP + j * ST:(ct - 6) * TP + (j + 1) * ST], bCT
                P.op("act", lambda e: e.activation(out=dst, in_=qv, func=AF.Silu), reads=[bqv], writes=[bd])
            P.op("pool", lambda e: e.tensor_copy(out=carry_m2[l][ct][:, :], in_=cbuf[:, TP:TP + 3]), reads=[bcbuf], writes=[bcarry_m2[l][ct]])
        if CSTOP <= 2:
            return
        rB0 = rSc = bps[0]
        EB = (4, 6)
        YB = (5, 1)

        def c_front(qc):
            q2 = qc % 2
            (xdt, bxdt), (xdd, bxdd), (Btok, bBtok), (scm, bscm) = xdts[q2], xdds[q2], Btoks[q2], scms[q2]
            (STn, bSTn) = STbs[1 - q2]
            for ct in range(4):
                P.op("pe", lambda e: e.matmul(psum[7][:, ct * 128:(ct + 1) * 128], xTb[:, ct * TP + qc * 128:ct * TP + (qc + 1) * 128], identb[:, :], start=True, stop=True),
                     reads=[bxT, bidb], writes=[bps[7]])
            for g in range(2):
                P.op("pe", lambda e: e.matmul(psum[0][:, g * 128:(g + 1) * 128], BTb[:, g * TP + qc * 128:g * TP + (qc + 1) * 128], identb[:, :], start=True, stop=True),
                     reads=[bBT, bidb], writes=[rB0])
            for g in range(2):
                P.op("pe", lambda e: e.matmul(psum[0][:, 256 + g * 128:256 + (g + 1) * 128], BTb[:, g * TP + qc * 128:g * TP + (qc + 1) * 128],
                                              CTb[:, g * TP + qc * 128:g * TP + (qc + 1) * 128], start=True, stop=True), reads=[bBT, bCT], writes=[rSc])
            for h in range(8):
                dacol = da[:, qc * 8 + h:qc * 8 + h + 1]
                ltl, bltl = ltls[h]
                rep, brep = reps[h // 2]
                P.op("pool", lambda e: e.tensor_scalar(out=ltl, in0=ltstrict, scalar1=dacol, scalar2=1.0, op0=ALU.mult, op1=ALU.mult), reads=[bda, bconsts], writes=[bltl])
                P.op("pool", lambda e: e.tensor_scalar(out=rep[:, (h % 2) * 64:(h % 2 + 1) * 64], in0=ones[:, 0:64], scalar1=dacol, scalar2=1.0, op0=ALU.mult, op1=ALU.mult), reads=[bda, bconsts], writes=[brep])
            for h in range(8):
                P.op("dve", lambda e: e.tensor_scalar(out=xdt[:, h * 64:(h + 1) * 64], in0=psum[7][:, h * 64:(h + 1) * 64], scalar1=dt[:, qc * 8 + h:qc * 8 + h + 1], scalar2=None, op0=ALU.mult),
                     reads=[bps[7], bdt], writes=[bxdt])
                P.op("dve", lambda e: e.tensor_scalar(out=xdd[:, h * 64:(h + 1) * 64], in0=psum[7][:, h * 64:(h + 1) * 64], scalar1=dtdec[:, qc * 8 + h:qc * 8 + h + 1], scalar2=None, op0=ALU.mult),
                     reads=[bps[7], bdtdec], writes=[bxdd])
            P.op("act", lambda e: e.activation(out=Btok, in_=psum[0][:, 0:256], func=AF.Copy), reads=[rB0], writes=[bBtok])
            P.op("dve", lambda e: e.tensor_tensor(out=scm.rearrange("p (g t) -> p g t", g=2), in0=psum[0][:, 256:512].rearrange("p (g t) -> p g t", g=2),
                                                 in1=triu.unsqueeze(1).broadcast_to([128, 2, 128]), op=ALU.mult), reads=[rSc, bconsts], writes=[bscm])
            for g in range(2):
                P.op("pe", lambda e: e.matmul(psum[1][:, g * 256:(g + 1) * 256], Btok[:, g * 128:(g + 1) * 128], xdd[:, g * 256:(g + 1) * 256], start=True, stop=True),
                     reads=[bBtok, bxdd], writes=[bps[1]])
            for h in range(8):
                P.op("dve", lambda e: e.scalar_tensor_tensor(out=ST_m2[l][:, h * 64:(h + 1) * 64], in0=ST_m2[l][:, h * 64:(h + 1) * 64], scalar=eA[:, qc * 8 + h:qc * 8 + h + 1],
                                                            in1=psum[1][:, h * 64:(h + 1) * 64], op0=ALU.mult, op1=ALU.add), reads=[bST_m2[l], beA, bps[1]], writes=[bST_m2[l]])
            P.op("act", lambda e: e.activation(out=STn, in_=ST_m2[l][:, :], func=AF.Copy), reads=[bST_m2[l], bSTn], writes=[bSTn])

        def c_heads(qc):
            q2 = qc % 2
            (xdt, bxdt), (scm, bscm) = xdts[q2], scms[q2]
            (yq, byq) = yqs[q2]
            (STc, bSTc) = STbs[q2]
            for h in range(8):
                g = h // 4
                hp = h % 2
                pair = h // 2
                pp = pair % 2
                (ltl, bltl), (LTm, bLTm), (MT, bMT) = ltls[h], LTms[hp], MTs[hp]
                (rep, brep), (erow, berow), (t1, bt1) = reps[pair], erows[pp], t1s[pp]
                Lreg = psum[2 + hp][:, pair * 128:(pair + 1) * 128]
                bL = bps[2 + hp]
                eb, yb = EB[pp], YB[pp]
                P.op("pe", lambda e: e.matmul(Lreg, ltl, triu, start=True, stop=True), reads=[bltl, bconsts], writes=[bL])
                P.op("act", lambda e: e.activation(out=LTm, in_=Lreg, func=AF.Exp), reads=[bL], writes=[bLTm])
                P.op("dve", lambda e: e.tensor_tensor(out=MT, in0=scm[:, g * 128:(g + 1) * 128], in1=LTm, op=ALU.mult), reads=[bscm, bLTm], writes=[bMT])
                if hp == 1:
                    P.op("pe", lambda e: e.matmul(psum[eb][:, 0:128], rep, triu, start=True, stop=True), reads=[brep, bconsts], writes=[bps[eb]])
                P.op("pe", lambda e: e.matmul(psum[yb][hp * 64:(hp + 1) * 64, 0:128], xdt[:, h * 64:(h + 1) * 64], MT, start=True, stop=True), reads=[bxdt, bMT], writes=[bps[yb]])
                P.op("pe", lambda e: e.matmul(psum[yb][hp * 64:(hp + 1) * 64, 128:256], STc[:, h * 64:(h + 1) * 64], CTb[:, g * TP + qc * 128:g * TP + (qc + 1) * 128], start=True, stop=True),
                     reads=[bSTc, bCT], writes=[bps[yb]])
                if hp == 1:
                    ct = pair
                    P.op("act", lambda e: e.activation(out=erow, in_=psum[eb][:, 0:128], func=AF.Exp), reads=[bps[eb]], writes=[berow])
                    P.op("dve", lambda e: e.tensor_tensor(out=t1, in0=psum[yb][:, 128:256], in1=erow, op=ALU.mult), reads=[bps[yb], berow], writes=[bt1])
                    P.op("dve", lambda e: e.tensor_tensor(out=t1, in0=psum[yb][:, 0:128], in1=t1, op=ALU.add), reads=[bps[yb], bt1], writes=[bt1])
                    P.op("dve", lambda e: e.scalar_tensor_tensor(out=yq[:, ct * 128:(ct + 1) * 128], in0=xTb[:, ct * TP + qc * 128:ct * TP + (qc + 1) * 128], scalar=col(l, "m2d", ct),
                                                                in1=t1, op0=ALU.mult, op1=ALU.add), reads=[bxT, bt1, bcols, byq], writes=[byq])
                    P.op("pool", lambda e: e.tensor_tensor(out=yq[:, ct * 128:(ct + 1) * 128], in0=yq[:, ct * 128:(ct + 1) * 128],
                                                          in1=ZS[:, ct * TP + qc * 128:ct * TP + (qc + 1) * 128], op=ALU.mult), reads=[byq, bZS], writes=[byq])

        def c_tail(qc):
            q2 = qc % 2
            cs = slice(qc * 128, (qc + 1) * 128)
            (yq, byq), (rstd, brstd) = yqs[q2], rstds[q2]
            rms_stats(lambda k: yq[:, k * 128:(k + 1) * 128], lambda k: [byq], 4, 1.0 / W, rstd, brstd, n=128, lnexp=True)
            for ct in range(4):
                P.op("dve", lambda e: e.scalar_tensor_tensor(out=Y[:, ct, cs], in0=yq[:, ct * 128:(ct + 1) * 128], scalar=col(l, "m2nw", ct), in1=rstd[:, 0:128],
                                                            op0=ALU.mult, op1=ALU.mult), reads=[byq, brstd, bcols], writes=[bY[ct][qc // 4]])

        c_front(0)
        for qc in range(NQ):
            c_heads(qc)
            if qc + 1 < NQ:
                c_front(qc + 1)
            c_tail(qc)

    def sincos(ang, bang, n, out_sin, out_cos, tmp, tmpi, btmp):
        for shift, dst in ((0.0, out_sin), (0.5 * np.pi, out_cos)):
            P.op("dve", lambda e: e.tensor_scalar(out=tmp[:, 0:n], in0=ang, scalar1=float(shift), scalar2=float(1.0 / (2 * np.pi)), op0=ALU.add, op1=ALU.mult),
                 reads=[bang], writes=[btmp])
            P.op("dve", lambda e: e.tensor_copy(out=tmpi[:, 0:n], in_=tmp[:, 0:n]), reads=[btmp], writes=[btmp])
            P.op("dve", lambda e: e.tensor_copy(out=tmp[:, n:2 * n], in_=tmpi[:, 0:n]), reads=[btmp], writes=[btmp])
            P.op("dve", lambda e: e.tensor_tensor(out=tmp[:, 0:n], in0=tmp[:, 0:n], in1=tmp[:, n:2 * n], op=ALU.subtract), reads=[btmp], writes=[btmp])
            P.op("dve", lambda e: e.tensor_scalar(out=tmp[:, n:2 * n], in0=tmp[:, 0:n], scalar1=0.5, scalar2=None, op0=ALU.is_gt), reads=[btmp], writes=[btmp])
            P.op("dve", lambda e: e.tensor_tensor(out=tmp[:, 0:n], in0=tmp[:, 0:n], in1=tmp[:, n:2 * n], op=ALU.subtract), reads=[btmp], writes=[btmp])
            P.op("dve", lambda e: e.tensor_scalar(out=tmp[:, n:2 * n], in0=tmp[:, 0:n], scalar1=-0.5, scalar2=None, op0=ALU.is_lt), reads=[btmp], writes=[btmp])
            P.op("dve", lambda e: e.tensor_tensor(out=tmp[:, 0:n], in0=tmp[:, 0:n], in1=tmp[:, n:2 * n], op=ALU.add), reads=[btmp], writes=[btmp])
            P.op("act", lambda e: e.activation(out=dst, in_=tmp[:, 0:n], func=AF.Sin, scale=float(2 * np.pi * (1 - 1e-6))), reads=[btmp], writes=[btmp])

    def s5_lambda(lre, lim, lst, n, bsrc, abre, abim, tmp, tmpi, btmp, scr, bscr):
        step, lrs, lis, mag = scr[:, 0:n], scr[:, n:2 * n], scr[:, 2 * n:3 * n], scr[:, 3 * n:4 * n]
        P.op("act", lambda e: e.activation(out=step, in_=lst, func=AF.Exp), reads=[bsrc], writes=[bscr])
        P.op("dve", lambda e: e.tensor_tensor(out=lrs, in0=lre, in1=step, op=ALU.mult), reads=[bsrc, bscr], writes=[bscr])
        P.op("dve", lambda e: e.tensor_tensor(out=lis, in0=lim, in1=step, op=ALU.mult), reads=[bsrc, bscr], writes=[bscr])
        P.op("act", lambda e: e.activation(out=mag, in_=lrs, func=AF.Exp), reads=[bscr], writes=[bscr])
        sincos(lis, bscr, n, abim, abre, tmp, tmpi, btmp)
        P.op("dve", lambda e: e.tensor_tensor(out=abre, in0=abre, in1=mag, op=ALU.mult), reads=[btmp, bscr], writes=[btmp])
        P.op("dve", lambda e: e.tensor_tensor(out=abim, in0=abim, in1=mag, op=ALU.mult), reads=[btmp, bscr], writes=[btmp])

    def coef_calc(lre, lim, abre, abim, n, cre_, cim_, scr, rd, bscr, bout):
        nr, den, u1, u2 = scr[:, 0:n], scr[:, n:2 * n], scr[:, 2 * n:3 * n], scr[:, 3 * n:4 * n]
        TT = lambda o, a, b, op, r_, w_: P.op("dve", lambda e: e.tensor_tensor(out=o, in0=a, in1=b, op=op), reads=r_, writes=w_)
        P.op("dve", lambda e: e.tensor_scalar(out=nr, in0=abre, scalar1=-1.0, scalar2=None, op0=ALU.add), reads=rd, writes=[bscr])
        TT(den, lre, lre, ALU.mult, rd, [bscr])
        TT(u1, lim, lim, ALU.mult, rd, [bscr])
        TT(den, den, u1, ALU.add, [bscr], [bscr])
        P.op("dve", lambda e: e.reciprocal(out=den, in_=den), reads=[bscr], writes=[bscr])
        TT(u1, nr, lre, ALU.mult, [bscr] + rd, [bscr])
        TT(u2, abim, lim, ALU.mult, rd, [bscr])
        TT(u1, u1, u2, ALU.add, [bscr], [bscr])
        TT(cre_, u1, den, ALU.mult, [bscr], [bout])
        TT(u1, abim, lre, ALU.mult, rd, [bscr])
        TT(u2, nr, lim, ALU.mult, [bscr] + rd, [bscr])
        TT(u1, u1, u2, ALU.subtract, [bscr], [bscr])
        TT(cim_, u1, den, ALU.mult, [bscr], [bout])

    def phase_a(l):
        scratch_reset()
        NSC = 7
        Q8 = 8
        CC = TP // Q8
        TT = lambda o, a, b, op, rd, wr: P.op("dve", lambda e: e.tensor_tensor(out=o, in0=a, in1=b, op=op), reads=rd, writes=wr)
        STT = lambda o, a, sc_, b, rd, wr: P.op("dve", lambda e: e.scalar_tensor_tensor(out=o, in0=a, scalar=sc_, in1=b, op0=ALU.mult, op1=ALU.add), reads=rd, writes=wr)
        TS = lambda o, a, sc_, rd, wr: P.op("dve", lambda e: e.tensor_scalar(out=o, in0=a, scalar1=sc_, scalar2=None, op0=ALU.mult), reads=rd, writes=wr)
        pwc, bpwc = falloc(9 * 3 * 16)
        pws, bpws = falloc(NSC * 3 * 16)
        pcC, bpcC = falloc(2 * 256)
        BD, bBD = balloc(2 * 2048)
        CD, bCD = balloc(2 * 2048)
        BDp, bBDp = balloc(2 * 2048)
        pwcv = pwc.rearrange("p (k a g) -> p k a g", k=9, a=3)
        pwsv = pws.rearrange("p (k a g) -> p k a g", k=NSC, a=3)
        mark = scr_pos[0]
        p5, bp5 = falloc(5 * 256)
        pq, bpq = falloc(3 * 16)
        pbp, bpbp = falloc(2 * 256)
        tmp, btmp = falloc(512)
        tmpi_f, _ = falloc(256)
        tmpi = tmpi_f.bitcast(mybir.dt.int32)
        scr, bscr = falloc(1024)
        ab, bab = falloc(512)
        cf, bcf = falloc(512)
        bb, bbb = falloc(512)
        abp, babp = falloc(32)
        cfp, bcfp = falloc(32)
        bbp, bbbp = falloc(512)
        do_prep = (cur_pass[0] == 0)
        assert mark == S5_CACHE_N, mark
        pers = [bpwc, bpws, bpcC, bBD, bCD, bBDp]
        if not do_prep:
            P.dma("sp", scache, scrF[:, 0:mark], s5cache[l], reads=[bcache[l]], writes=pers)
        if do_prep:
            P.dma("sp", sm, p5.rearrange("p (a n) -> p a n", a=5), s5p_d[l], writes=[bp5])
            P.dma("sp", sm, pq.rearrange("p (a n) -> p a n", a=3), s5q_d[l], writes=[bpq])
            P.dma("sp", sm, pcC.rearrange("p (a n) -> p a n", a=2), s5c_d[l, :, 0:2, :], writes=[bpcC])
            P.dma("sp", sm, pbp.rearrange("p (a n) -> p a n", a=2), s5c_d[l, :, 2:4, :], writes=[bpbp])
            lre, lim, lst, bre, bim = (p5[:, i * 256:(i + 1) * 256] for i in range(5))
            abre, abim = ab[:, 0:256], ab[:, 256:512]
            s5_lambda(lre, lim, lst, 256, bp5, abre, abim, tmp, tmpi, btmp, scr, bscr)
            cre_, cim_ = cf[:, 0:256], cf[:, 256:512]
            coef_calc(lre, lim, abre, abim, 256, cre_, cim_, scr, [bp5, btmp], bscr, bcf)
            u1, u2 = scr[:, 512:768], scr[:, 768:1024]
            bbre, bbim = bb[:, 0:256], bb[:, 256:512]
            TT(u1, cre_, bre, ALU.mult, [bcf, bp5], [bscr])
            TT(u2, cim_, bim, ALU.mult, [bcf, bp5], [bscr])
            TT(bbre, u1, u2, ALU.subtract, [bscr], [bbb])
            TT(u1, cre_, bim, ALU.mult, [bcf, bp5], [bscr])
            TT(u2, cim_, bre, ALU.mult, [bcf, bp5], [bscr])
            TT(bbim, u1, u2, ALU.add, [bscr], [bbb])
            for ri, src in enumerate((bbre, bbim)):
                dstv = BD[:, ri * 2048:(ri + 1) * 2048].rearrange("p (c j g n) -> p c j g n", c=4, j=4, g=2)
                for jj in range(4):
                    for g2 in range(2):
                        TS(dstv[:, :, jj, g2, :], src.rearrange("p (c n) -> p c n", c=4), col(l, "mkB", jj * 2 + g2), [bbb, bcols], [bBD])
            P.op("pool", lambda e: e.memset(CD, 0.0), writes=[bCD])
            for ri, nm in enumerate(("mkC", "mkCn")):
                dstv = CD[:, ri * 2048:(ri + 1) * 2048].rearrange("p (c j m) -> p c j m", c=4, j=4)
                srcv = pcC[:, ri * 256:(ri + 1) * 256].rearrange("p (q c j) -> p c j q", q=16, c=4, j=4)
                for jj in range(4):
                    for g2 in range(2):
                        gl = 2 * jj + g2
                        TS(dstv[:, :, jj, gl * 16:(gl + 1) * 16], srcv[:, :, jj, :], col(l, nm, g2), [bpcC, bcols, bCD], [bCD])
            s5_lambda(pq[:, 0:16], pq[:, 16:32], pq[:, 32:48], 16, bpq, abp[:, 0:16], abp[:, 16:32], tmp, tmpi, btmp, scr, bscr)
            coef_calc(pq[:, 0:16], pq[:, 16:32], abp[:, 0:16], abp[:, 16:32], 16, cfp[:, 0:16], cfp[:, 16:32], scr, [bpq, btmp], bscr, bcfp)
            bq_re = pbp[:, 0:256].rearrange("p (q g) -> p q g", q=16)
            bq_im = pbp[:, 256:512].rearrange("p (q g) -> p q g", q=16)
            cfr = cfp[:, 0:16].unsqueeze(1).broadcast_to([128, 16, 16])
            cfi = cfp[:, 16:32].unsqueeze(1).broadcast_to([128, 16, 16])
            w1 = scr[:, 0:256].rearrange("p (q g) -> p q g", q=16)
            w2 = scr[:, 256:512].rearrange("p (q g) -> p q g", q=16)
            bbp_re = bbp[:, 0:256].rearrange("p (q g) -> p q g", q=16)
            bbp_im = bbp[:, 256:512].rearrange("p (q g) -> p q g", q=16)
            TT(w1, bq_re, cfr, ALU.mult, [bpbp, bcfp], [bscr])
            TT(w2, bq_im, cfi, ALU.mult, [bpbp, bcfp], [bscr])
            TT(bbp_re, w1, w2, ALU.subtract, [bscr], [bbbp])
            TT(w1, bq_im, cfr, ALU.mult, [bpbp, bcfp], [bscr])
            TT(w2, bq_re, cfi, ALU.mult, [bpbp, bcfp], [bscr])
            TT(bbp_im, w1, w2, ALU.add, [bscr], [bbbp])
            P.op("pool", lambda e: e.memset(BDp, 0.0), writes=[bBDp])
            for ri in range(2):
                dstv = BDp[:, ri * 2048:(ri + 1) * 2048].rearrange("p (c j m) -> p c j m", c=4, j=4)
                srcv = bbp[:, ri * 256:(ri + 1) * 256].rearrange("p (q c j) -> p c j q", q=16, c=4, j=4)
                for jj in range(4):
                    for g2 in range(2):
                        gl = 2 * jj + g2
                        TS(dstv[:, :, jj, gl * 16:(gl + 1) * 16], srcv[:, :, jj, :], col(l, "mkC", g2), [bbbp, bcols, bBDp], [bBDp])
            P.op("pool", lambda e: e.memset(pwcv[:, 0, 0, :], 1.0), writes=[bpwc])
            P.op("pool", lambda e: e.memset(pwcv[:, 0, 1:3, :], 0.0), reads=[bpwc], writes=[bpwc])
            P.op("dve", lambda e: e.tensor_copy(out=pwcv[:, 1, 0, :], in_=abp[:, 0:16]), reads=[btmp, bpwc], writes=[bpwc])
            P.op("dve", lambda e: e.tensor_copy(out=pwcv[:, 1, 1, :], in_=abp[:, 16:32]), reads=[btmp, bpwc], writes=[bpwc])
            lr_, li_ = abp[:, 0:16], abp[:, 16:32]
            for k in range(2, 9):
                a_, b_ = pwcv[:, k - 1, 0, :], pwcv[:, k - 1, 1, :]
                TT(scr[:, 0:16], a_, lr_, ALU.mult, [bpwc, btmp], [bscr])
                TT(scr[:, 16:32], b_, li_, ALU.mult, [bpwc, btmp], [bscr])
                TT(pwcv[:, k, 0, :], scr[:, 0:16], scr[:, 16:32], ALU.subtract, [bscr, bpwc], [bpwc])
                TT(scr[:, 32:48], a_, li_, ALU.mult, [bpwc, btmp], [bscr])
                TT(scr[:, 48:64], b_, lr_, ALU.mult, [bpwc, btmp], [bscr])
                TT(pwcv[:, k, 1, :], scr[:, 32:48], scr[:, 48:64], ALU.add, [bscr, bpwc], [bpwc])
            for k in range(1, 9):
                TS(pwcv[:, k, 2, :], pwcv[:, k, 1, :], -1.0, [bpwc], [bpwc])
            P.op("dve", lambda e: e.tensor_copy(out=pwsv[:, 0, :, :], in_=pwcv[:, 8, :, :]), reads=[bpwc], writes=[bpws])
            for k in range(1, NSC):
                a_, b_ = pwsv[:, k - 1, 0, :], pwsv[:, k - 1, 1, :]
                TT(scr[:, 0:16], a_, a_, ALU.mult, [bpws], [bscr])
                TT(scr[:, 16:32], b_, b_, ALU.mult, [bpws], [bscr])
                TT(pwsv[:, k, 0, :], scr[:, 0:16], scr[:, 16:32], ALU.subtract, [bscr, bpws], [bpws])
                TT(scr[:, 32:48], a_, b_, ALU.mult, [bpws], [bscr])
                TS(pwsv[:, k, 1, :], scr[:, 32:48], 2.0, [bscr, bpws], [bpws])
                TS(pwsv[:, k, 2, :], scr[:, 32:48], -2.0, [bscr, bpws], [bpws])
            P.dma("sp", scache, s5cache[l], scrF[:, 0:mark], reads=pers, writes=[bcache[l]])
        scratch_reset(mark)
        t2, bt2 = falloc(TP)
        XS, _ = falloc(4 * CC)
        bXS = [[Buf(), Buf()], [Buf(), Buf()]]
        XSv = [[XS[:, (b * 2 + ri) * CC:(b * 2 + ri + 1) * CC] for ri in range(2)] for b in range(2)]
        SP, _ = falloc(2 * CC)
        bSP = [Buf(), Buf()]
        SPv = [SP[:, 0:CC], SP[:, CC:2 * CC]]
        stmp, _ = falloc(4 * CC)
        bstmp = [Buf() for _ in range(4)]
        m12, bm12 = falloc(128)
        U, bU = balloc(4 * TP)
        G1, bG1 = balloc(4 * TP)
        Kc, bKc = balloc(8 * 128)
        Mc, bMc = balloc(8 * 2 * 64)
        SX, bSX = balloc(8 * 2 * CC)
        slu, bslu = load_slot([(w_in[l, :, O_S5U:O_S5U + 512], 8, 0, 512)])
        n = 0
        for c in range(4):
            for j in range(NSUB):
                pi = n % 2
                n += 1
                mm_group(psum[pi][:, :], bps[pi], lambda k: slu[:, k, c * 128:(c + 1) * 128], lambda k: H[:, k, hs(j)], 8, [bslu, bH[j]])
                P.op("act", lambda e: e.activation(out=U[:, c * TP + j * ST:c * TP + (j + 1) * ST], in_=psum[pi][:, :], func=AF.Copy), reads=[bps[pi]], writes=[bU])
        bmv = blockmask.rearrange("p (g q) -> p g q", g=8)
        for c in range(4):
            Uc = U[:, c * TP:(c + 1) * TP]
            Ucv = Uc.rearrange("p (cc r) -> p cc r", r=8)
            Cre = pcC[:, 0:256].rearrange("q (p g) -> q p g", p=16)[:, :, 4 * c:4 * c + 4]
            Cim = pcC[:, 256:512].rearrange("q (p g) -> q p g", p=16)[:, :, 4 * c:4 * c + 4]
            m1 = m12[:, 0:64].rearrange("q (p j) -> q p j", p=16)
            m2 = m12[:, 64:128].rearrange("q (p j) -> q p j", p=16)
            TTp = lambda o, a, b, op, rd, wr: P.op("pool", lambda e: e.tensor_tensor(out=o, in0=a, in1=b, op=op), reads=rd, writes=wr)
            for tau in range(8):
                Mre = Mc[:, (tau * 2) * 64:(tau * 2 + 1) * 64].rearrange("q (p j) -> q p j", p=16)
                Mim = Mc[:, (tau * 2 + 1) * 64:(tau * 2 + 2) * 64].rearrange("q (p j) -> q p j", p=16)
                if tau == 0:
                    P.op("pool", lambda e: e.tensor_copy(out=Mre, in_=Cre), reads=[bpcC, bMc], writes=[bMc])
                    P.op("pool", lambda e: e.tensor_scalar(out=Mim, in0=Cim, scalar1=-1.0, scalar2=1.0, op0=ALU.mult, op1=ALU.mult), reads=[bpcC, bMc], writes=[bMc])
                    continue
                Pre = pwcv[:, tau, 0, 4 * c:4 * c + 4].unsqueeze(1).broadcast_to([128, 16, 4])
                Pim = pwcv[:, tau, 1, 4 * c:4 * c + 4].unsqueeze(1).broadcast_to([128, 16, 4])
                nPim = pwcv[:, tau, 2, 4 * c:4 * c + 4].unsqueeze(1).broadcast_to([128, 16, 4])
                TTp(m1, Cre, Pre, ALU.mult, [bpcC, bpwc, bm12], [bm12])
                TTp(m2, Cim, Pim, ALU.mult, [bpcC, bpwc, bm12], [bm12])
                TTp(Mre, m1, m2, ALU.subtract, [bm12, bMc], [bMc])
                TTp(m1, Cre, nPim, ALU.mult, [bpcC, bpwc, bm12], [bm12])
                TTp(m2, Cim, Pre, ALU.mult, [bpcC, bpwc, bm12], [bm12])
                TTp(Mim, m1, m2, ALU.subtract, [bm12, bMc], [bMc])
            for tau in range(8):
                nmm = 0
                for jj in range(4):
                    gp = 4 * c + jj
                    for ri in range(2):
                        rhs = Mc[:, (tau * 2 + ri) * 64:(tau * 2 + ri + 1) * 64].rearrange("q (p j) -> q p j", p=16)[:, :, jj]
                        P.op("pe", lambda e: e.matmul(psum[6][:, tau * 16:(tau + 1) * 16], BDp[:, ri * 2048 + gp * 128:ri * 2048 + (gp + 1) * 128], rhs,
                                                      start=(nmm == 0), stop=(nmm == 7)), reads=[bBDp, bMc], writes=[bps[6]], inc=(nmm == 7))
                        nmm += 1
            for tau in range(8):
                P.op("dve", lambda e: e.tensor_tensor(out=Kc[:, tau * 128:(tau + 1) * 128].rearrange("p (g q) -> p g q", g=8), in0=bmv,
                                                     in1=psum[6][:, tau * 16:(tau + 1) * 16].unsqueeze(1).broadcast_to([128, 8, 16]), op=ALU.mult),
                     reads=[bps[6], bconsts, bKc], writes=[bKc])
            for bk in (4, 5):
                P.op("pe", lambda e: e.matmul(psum[bk][:, :], zerob[:, :], Uc[:, 0:512], start=True, stop=False, skip_group_check=True),
                     reads=[bzerob, bU], writes=[bps[bk]], inc=True)
            for r in range(8):
                for rp in range(r + 1):
                    last = (r == 7 and rp == 7)
                    P.op("pe", lambda e: e.matmul(psum[4 + r // 4][:, (r % 4) * 128:(r % 4 + 1) * 128], Kc[:, (r - rp) * 128:(r - rp + 1) * 128], Ucv[:, :, rp],
                                                  start=False, stop=False, skip_group_check=True), reads=[bKc, bU], writes=[bps[4 + r // 4]], inc=last)
            for jj in range(4):
                gp = 4 * c + jj
                for j in range(NSUB):
                    for ri in range(2):
                        pi = 2 * ri + j
                        P.op("pe", lambda e: e.matmul(psum[pi][:, :], BD[:, ri * 2048 + gp * 128:ri * 2048 + (gp + 1) * 128], Uc[:, j * ST:(j + 1) * ST], start=True, stop=True),
                             reads=[bBD, bU], writes=[bps[pi]])
                bv = [psbig[:, ri * 1024:(ri + 1) * 1024].rearrange("p (cc r) -> p cc r", r=8) for ri in range(2)]
                bpb = [[bps[0], bps[1]], [bps[2], bps[3]]]
                acc = [XSv[0][ri][:, 0:CC] for ri in range(2)]
                for ri in range(2):
                    P.op("act", lambda e: e.activation(out=acc[ri], in_=bv[ri][:, :, 7], func=AF.Copy), reads=bpb[ri] + [bXS[0][ri]], writes=[bXS[0][ri]])
                for r in range(7):
                    k = 7 - r
                    pr, pi_, npi = pwcv[:, k, 0, gp:gp + 1], pwcv[:, k, 1, gp:gp + 1], pwcv[:, k, 2, gp:gp + 1]
                    STT(acc[0], bv[0][:, :, r], pr, acc[0], bpb[0] + [bpwc, bXS[0][0]], [bXS[0][0]])
                    STT(acc[1], bv[1][:, :, r], pr, acc[1], bpb[1] + [bpwc, bXS[0][1]], [bXS[0][1]])
                    STT(acc[0], bv[1][:, :, r], npi, acc[0], bpb[1] + [bpwc, bXS[0][0]], [bXS[0][0]])
                    STT(acc[1], bv[0][:, :, r], pi_, acc[1], bpb[0] + [bpwc, bXS[0][1]], [bXS[0][1]])
                cr, ci = carry_s5[l][:, 0, gp:gp + 1], carry_s5[l][:, 1, gp:gp + 1]
                p8r, p8i, p8n = pwcv[:, 8, 0, gp:gp + 1], pwcv[:, 8, 1, gp:gp + 1], pwcv[:, 8, 2, gp:gp + 1]
                STT(XSv[0][0][:, 0:1], cr, p8r, XSv[0][0][:, 0:1], [bcarry_s5[l], bpwc, bXS[0][0]], [bXS[0][0]])
                STT(XSv[0][1][:, 0:1], ci, p8r, XSv[0][1][:, 0:1], [bcarry_s5[l], bpwc, bXS[0][1]], [bXS[0][1]])
                STT(XSv[0][0][:, 0:1], ci, p8n, XSv[0][0][:, 0:1], [bcarry_s5[l], bpwc, bXS[0][0]], [bXS[0][0]])
                STT(XSv[0][1][:, 0:1], cr, p8i, XSv[0][1][:, 0:1], [bcarry_s5[l], bpwc, bXS[0][1]], [bXS[0][1]])
                sbuf_i = 0
                for k in range(NSC):
                    sh = 1 << k
                    src, dst = XSv[sbuf_i], XSv[1 - sbuf_i]
                    bs_, bd_ = bXS[sbuf_i], bXS[1 - sbuf_i]
                    ar, ai, nai = pwsv[:, k, 0, gp:gp + 1], pwsv[:, k, 1, gp:gp + 1], pwsv[:, k, 2, gp:gp + 1]
                    STT(dst[0][:, sh:CC], src[0][:, 0:CC - sh], ar, src[0][:, sh:CC], [bs_[0], bpws, bd_[0]], [bd_[0]])
                    STT(dst[1][:, sh:CC], src[1][:, 0:CC - sh], ar, src[1][:, sh:CC], [bs_[1], bpws, bd_[1]], [bd_[1]])
                    STT(dst[0][:, sh:CC], src[1][:, 0:CC - sh], nai, dst[0][:, sh:CC], [bs_[1], bpws, bd_[0]], [bd_[0]])
                    STT(dst[1][:, sh:CC], src[0][:, 0:CC - sh], ai, dst[1][:, sh:CC], [bs_[0], bpws, bd_[1]], [bd_[1]])
                    for ri in range(2):
                        P.op("act", lambda e: e.activation(out=dst[ri][:, 0:sh], in_=src[ri][:, 0:sh], func=AF.Copy), reads=[bs_[ri], bd_[ri]], writes=[bd_[ri]])
                    sbuf_i = 1 - sbuf_i
                S_, bS_ = XSv[sbuf_i], bXS[sbuf_i]
                for ri in range(2):
                    P.op("pool", lambda e: e.tensor_copy(out=SPv[ri][:, 1:CC], in_=S_[ri][:, 0:CC - 1]), reads=[bS_[ri], bSP[ri]], writes=[bSP[ri]])
                    P.op("pool", lambda e: e.tensor_copy(out=SPv[ri][:, 0:1], in_=carry_s5[l][:, ri, gp:gp + 1]), reads=[bcarry_s5[l], bSP[ri]], writes=[bSP[ri]])
                for ri in range(2):
                    P.op("pool", lambda e: e.tensor_copy(out=carry_s5[l][:, ri, gp:gp + 1], in_=S_[ri][:, CC - 1:CC]), reads=[bS_[ri], bcarry_s5[l]], writes=[bcarry_s5[l]])
                for x in range(1, 9):
                    pr, pi_, npi = pwcv[:, x, 0, gp:gp + 1], pwcv[:, x, 1, gp:gp + 1], pwcv[:, x, 2, gp:gp + 1]
                    sb2 = (x % 2) * 2
                    t_re, t_im = stmp[:, sb2 * CC:(sb2 + 1) * CC], stmp[:, (sb2 + 1) * CC:(sb2 + 2) * CC]
                    P.op("pool", lambda e: e.tensor_scalar(out=t_re, in0=SPv[0], scalar1=pr, scalar2=1.0, op0=ALU.mult, op1=ALU.mult), reads=[bSP[0], bpwc, bstmp[sb2]], writes=[bstmp[sb2]])
                    P.op("pool", lambda e: e.tensor_scalar(out=t_im, in0=SPv[1], scalar1=pr, scalar2=1.0, op0=ALU.mult, op1=ALU.mult), reads=[bSP[1], bpwc, bstmp[sb2 + 1]], writes=[bstmp[sb2 + 1]])
                    STT(SX[:, ((x - 1) * 2) * CC:((x - 1) * 2 + 1) * CC], SPv[1], npi, t_re, [bSP[1], bpwc, bstmp[sb2], bSX], [bSX])
                    STT(SX[:, ((x - 1) * 2 + 1) * CC:((x - 1) * 2 + 2) * CC], SPv[0], pi_, t_im, [bSP[0], bpwc, bstmp[sb2 + 1], bSX], [bSX])
                for r in range(8):
                    for ri in range(2):
                        last = (r == 7 and ri == 1)
                        P.op("pe", lambda e: e.matmul(psum[4 + r // 4][:, (r % 4) * 128:(r % 4 + 1) * 128], CD[:, ri * 2048 + gp * 128:ri * 2048 + (gp + 1) * 128],
                                                      SX[:, (r * 2 + ri) * CC:(r * 2 + ri + 1) * CC], start=False, stop=(jj == 3 and last), skip_group_check=True),
                             reads=[bCD, bSX], writes=[bps[4 + r // 4]], inc=last)
            t2v = t2.rearrange("p (cc r) -> p cc r", r=8)
            for bk in range(2):
                P.op("dve", lambda e: e.scalar_tensor_tensor(out=t2v[:, :, 4 * bk:4 * bk + 4], in0=Ucv[:, :, 4 * bk:4 * bk + 4], scalar=col(l, "s5d", c),
                                                            in1=psum[4 + bk][:, :].rearrange("p (r cc) -> p cc r", r=4), op0=ALU.mult, op1=ALU.add),
                     reads=[bU, bcols, bps[4 + bk], bt2], writes=[bt2])
            for j in range(NSUB):
                P.op("act", lambda e: e.activation(out=G1[:, c * TP + j * ST:c * TP + (j + 1) * ST], in_=t2[:, hs(j)], func=AF.Gelu), reads=[bt2], writes=[bG1])
        P.barrier()
        sig, bsig = XS, Buf()
        gate_s, bgs = stmp, Buf()
        slw, bslw = load_slot([(w_glu[l, :, :], 4, 0, 512)])
        slg, bslg = load_slot([(w_in[l, :, O_S5G:O_S5G + 512], 8, 0, 512)])
        for co in range(4):
            for j in range(NSUB):
                p1, p2 = (0, 1) if (co * NSUB + j) % 2 == 0 else (2, 3)
                mm_group(psum[p1][:, :], bps[p1], lambda k: slw[:, k, co * 128:(co + 1) * 128], lambda k: G1[:, k * TP + j * ST:k * TP + (j + 1) * ST], 4, [bslw, bG1])
                mm_group(psum[p2][:, :], bps[p2], lambda k: slg[:, k, co * 128:(co + 1) * 128], lambda k: H[:, k, hs(j)], 8, [bslg, bH[j]])
                P.op("act", lambda e: e.activation(out=sig, in_=psum[p1][:, :], func=AF.Sigmoid), reads=[bps[p1]], writes=[bsig])
                P.op("act", lambda e: e.activation(out=gate_s, in_=psum[p2][:, :], func=AF.Silu), reads=[bps[p2]], writes=[bgs])
                P.op("dve", lambda e: e.tensor_tensor(out=t2[:, 0:ST], in0=G1[:, co * TP + j * ST:co * TP + (j + 1) * ST], in1=sig, op=ALU.mult), reads=[bG1, bsig, bt2], writes=[bt2])
                P.op("pool", lambda e: e.tensor_tensor(out=Y[:, co, hs(j)], in0=t2[:, 0:ST], in1=gate_s, op=ALU.mult), reads=[bt2, bgs], writes=[bY[co][j]])

    first_merge = [True]
    mg_t = sb("mg_t", [128, ST], F32)
    mt_t = sb("mt_t", [128, ST], F32)
    bmg, bmt = Buf(), Buf()

    def phase_merge(l, kb):
        g, bg, t, bt = mg_t[:, :], bmg, mt_t[:, :], bmt
        for hh in range(2):
            slg, bslg = load_slot([(w_in[l, :, O_MG + kb * D + hh * 512:O_MG + kb * D + (hh + 1) * 512], 8, 0, 512)])
            slb, bslb = load_slot([(w_br[l, kb, :, hh * 512:(hh + 1) * 512], 4, 0, 512)])
            for dt_ in range(4):
                d = hh * 4 + dt_
                for j in range(NSUB):
                    pg, pbk = (0, 1) if (dt_ * NSUB + j) % 2 == 0 else (2, 3)
                    mm_group(psum[pg][:, :], bps[pg], lambda k: slg[:, k, dt_ * 128:(dt_ + 1) * 128], lambda k: H[:, k, hs(j)], 8, [bslg, bH[j]])
                    mm_group(psum[pbk][:, :], bps[pbk], lambda k: slb[:, k, dt_ * 128:(dt_ + 1) * 128], lambda k: Y[:, k, hs(j)], 4,
                             [bslb] + [bY[k][j] for k in range(4)])
                    P.op("act", lambda e: e.activation(out=g, in_=psum[pg][:, :], func=AF.Sigmoid, bias=col(l, "mb", kb * 8 + d)),
                         reads=[bps[pg], bcols], writes=[bg])
                    if first_merge[0]:
                        P.op("dve", lambda e: e.tensor_tensor(out=ACC[:, d, hs(j)], in0=psum[pbk][:, :], in1=g, op=ALU.mult),
                             reads=[bps[pbk], bg], writes=[bACC[d][j]])
                    else:
                        P.op("dve", lambda e: e.tensor_tensor(out=t, in0=psum[pbk][:, :], in1=g, op=ALU.mult),
                             reads=[bps[pbk], bg], writes=[bt])
                        P.op("pool", lambda e: e.tensor_tensor(out=ACC[:, d, hs(j)], in0=ACC[:, d, hs(j)], in1=t, op=ALU.add),
                             reads=[bt, bACC[d][j]], writes=[bACC[d][j]])
        first_merge[0] = False

    def phase_out(l):
        for j in range(NSUB):
            for k in range(8):
                P.op("act", lambda e: e.activation(out=H[:, k, hs(j)], in_=ACC[:, k, hs(j)], func=AF.Copy),
                     reads=[bACC[k][j]], writes=[bH[j]])
        for hh in range(2):
            sl, bsl = load_slot([(w_out[l, :, hh * 512:(hh + 1) * 512], 8, 0, 512)])
            for dt_ in range(4):
                d = hh * 4 + dt_
                for j in range(NSUB):
                    pi = 2 + (dt_ * NSUB + j) % 4
                    mm_group(psum[pi][:, :], bps[pi], lambda k: sl[:, k, dt_ * 128:(dt_ + 1) * 128], lambda k: H[:, k, hs(j)], 8, [bsl, bH[j]])
                    P.op("dve", lambda e: e.tensor_tensor(out=X[:, d, hs(j)], in0=psum[pi][:, :], in1=X[:, d, hs(j)], op=ALU.add),
                         reads=[bps[pi], bX[d][j]], writes=[bX[d][j]])

    fo_t = sb("fo_t", [128, 2, ST], F32)
    bfo = [Buf(), Buf()]

    def phase_final(p):
        n = 0
        for j in range(NSUB):
            rms_stats(lambda k: X[:, k, hs(j)], lambda k: [bX[k][j]], 8, 1.0 / D, rstd_t, brstd_t)
            for k in range(8):
                i = n % 2
                n += 1
                P.op("dve", lambda e: e.scalar_tensor_tensor(
                    out=fo_t[:, i, :], in0=X[:, k, hs(j)], scalar=col(0, "fw", k),
                    in1=rstd_t[:, :], op0=ALU.mult, op1=ALU.mult),
                    reads=[bX[k][j], brstd_t, bcols], writes=[bfo[i]])
                P.dma("sp", sy, yT[k * 128:(k + 1) * 128, p * TP + j * ST:p * TP + (j + 1) * ST], fo_t[:, i, :], reads=[bfo[i]])

    phases = {"a": phase_a, "b": phase_b, "c": phase_c, "d": phase_d}
    for p in range(NPASS):
        cur_pass[0] = p
        for k in range(8):
            for j in range(NSUB):
                P.dma("sp", sx, X[:, k, hs(j)], xT[k * 128:(k + 1) * 128, p * TP + j * ST:p * TP + (j + 1) * ST], writes=[bX[k][j]])
        for l in range(nlayers):
            phase_norm(l)
            first_merge[0] = True
            for kb, name in enumerate("abcd"):
                if name not in branches:
                    continue
                phases[name](l)
                phase_merge(l, kb)
            phase_out(l)
        phase_final(p)
    P._wait("sp", sy, P.cnt[sy])
    print("program: nins=%d nwaits=%d" % (P.nins, P.nwaits), {k: v for k, v in P.cnt.items() if k in P.eng})
    return nc


_NC_CACHE = {}


def run(inputs, branches=("a", "b", "c", "d"), nlayers=DEPTH, trace=False):
    key = (tuple(branches), nlayers)
    if key not in _NC_CACHE:
        _NC_CACHE[key] = build_nc(branches, nlayers)
    nc = _NC_CACHE[key]
    inp = {k: np.asarray(v) for k, v in inputs.items()}
    x = inp["x"].astype(np.float32)
    s5 = [host_s5(inp, l) for l in range(DEPTH)]
    shared = {
        "w_in": np.ascontiguousarray(inp["w_in"], dtype=np.float32),
        "w_branch": np.ascontiguousarray(inp["w_branch"], dtype=np.float32),
        "w_out": np.ascontiguousarray(inp["w_out"], dtype=np.float32),
        "w_glu": np.ascontiguousarray(inp["s5_w_glu"], dtype=np.float32),
        "cols": np.stack([host_cols(inp, l) for l in range(DEPTH)], 0),
        "consts": host_consts(),
        "rows": np.stack([host_rows(inp, l) for l in range(DEPTH)], 0),
        "sguw": np.ascontiguousarray(inp["sgu_w"].transpose(0, 3, 1, 2), dtype=np.float32),
        "sgub": np.ascontiguousarray(np.repeat(inp["sgu_b"].reshape(DEPTH, 4, 2, 1, 128), 64, axis=3).transpose(0, 2, 3, 1, 4).reshape(DEPTH, 128, 512), dtype=np.float32),
        "s5p": np.stack([s[0] for s in s5], 0),
        "s5q": np.stack([s[1] for s in s5], 0),
        "s5c": np.stack([s[2] for s in s5], 0),
    }
    in_maps = []
    for b in range(8):
        m = dict(shared)
        m["xT"] = np.ascontiguousarray(x[b].T)
        in_maps.append(m)
    res = run_bass_kernel_spmd(nc, in_maps, core_ids=list(range(8)), trace=trace)
    out = np.stack([np.ascontiguousarray(res.results[b]["yT"].T) for b in range(8)], 0).astype(np.float32)
    return out, res


def kernel(**inputs):
    out, _ = run(inputs)
    return out
```
